# Optimizing a Trainium2 kernel written in Bass

```python
import math
import jax, jax.numpy as jnp
from jax import lax
import numpy as np

D_MODEL = 2048
BATCH = 2
SEQ = 16384
DEPTH = 2

N_HEADS = 8
N_KV_HEADS = 2
HEAD_DIM = 128
GQA_GROUP = N_HEADS // N_KV_HEADS
ATTN_WIDTH = N_HEADS * HEAD_DIM
KV_WIDTH = N_KV_HEADS * HEAD_DIM
Q_BLOCK = 128
ROPE_THETA = 10000.0
ROPE_PAIRS = HEAD_DIM // 4
N_FOURIER_GROUPS = 8
FOURIER_GROUP_CH = 128
FOURIER_WIDTH = N_FOURIER_GROUPS * FOURIER_GROUP_CH
OFF_Q = 0
OFF_K = OFF_Q + ATTN_WIDTH
OFF_V = OFF_K + KV_WIDTH
OFF_F = OFF_V + KV_WIDTH
OFF_GA = OFF_F + FOURIER_WIDTH
OFF_GF = OFF_GA + D_MODEL
IN_WIDTH = OFF_GF + D_MODEL
D_FF = 5632
CONV_W = 3
N_META = 16
GRID_W = 64
NORM_EPS = 1e-6

kernel_name = "hybrid_gqa_fourier_convffn_encoder"


def _rmsnorm(x, g):
    xf = x.astype(jnp.float32)
    y = xf * lax.rsqrt(jnp.mean(xf * xf, axis=-1, keepdims=True) + NORM_EPS)
    return (y * g.astype(jnp.float32)).astype(x.dtype)


def _rope_half(x, cos, sin):
    x1, x2 = jnp.split(x, 2, axis=-1)
    c = cos[:, None, :]
    s = sin[:, None, :]
    return jnp.concatenate([x1 * c - x2 * s, x1 * s + x2 * c], axis=-1)


def _rope_2d(x, cos_r, sin_r, cos_c, sin_c):
    half = HEAD_DIM // 2
    return jnp.concatenate([_rope_half(x[..., :half], cos_r, sin_r),
                            _rope_half(x[..., half:], cos_c, sin_c)], axis=-1)


def _attend_block(qb, k, v):
    scale = 1.0 / math.sqrt(HEAD_DIM)
    s = jnp.einsum('bqkgd,bskd->bkgqs', qb, k, preferred_element_type=jnp.float32) * scale
    p = jax.nn.softmax(s, axis=-1).astype(v.dtype)
    return jnp.einsum('bkgqs,bskd->bqkgd', p, v)


def _dwconv3_centred(u, w, b):
    up = jnp.pad(u, ((0, 0), (1, 1), (0, 0)))
    return up[:, :-2] * w[0] + up[:, 1:-1] * w[1] + up[:, 2:] * w[2] + b


def setup_inputs(seed: int = 0) -> dict:
    key = jax.random.key(seed)
    ks = jax.random.split(key, 16)
    f32 = jnp.float32
    n = lambda k, shape, fan_in: jax.random.normal(k, shape, f32) * (fan_in ** -0.5)
    return {
        "x": jax.random.normal(ks[0], (BATCH, SEQ, D_MODEL), f32),
        "meta_tokens": jax.random.normal(ks[1], (N_META, D_MODEL), f32),
        "norm_mix": 1.0 + 0.05 * jax.random.normal(ks[2], (DEPTH, D_MODEL), f32),
        "norm_ffn": 1.0 + 0.05 * jax.random.normal(ks[3], (DEPTH, D_MODEL), f32),
        "w_in": n(ks[4], (DEPTH, D_MODEL, IN_WIDTH), D_MODEL),
        "b_gate": 0.02 * jax.random.normal(ks[5], (DEPTH, 2 * D_MODEL), f32),
        "q_norm": 1.0 + 0.05 * jax.random.normal(ks[6], (DEPTH, HEAD_DIM), f32),
        "k_norm": 1.0 + 0.05 * jax.random.normal(ks[7], (DEPTH, HEAD_DIM), f32),
        "w_attn_br": n(ks[8], (DEPTH, ATTN_WIDTH, D_MODEL), ATTN_WIDTH),
        "w_four": n(ks[9], (DEPTH, FOURIER_WIDTH, D_MODEL), FOURIER_WIDTH),
        "w_out": n(ks[10], (DEPTH, D_MODEL, D_MODEL), D_MODEL),
        "w_up": n(ks[11], (DEPTH, D_MODEL, 2 * D_FF), D_MODEL),
        "w_conv": n(ks[12], (DEPTH, CONV_W, D_FF), CONV_W),
        "b_conv": 0.02 * jax.random.normal(ks[13], (DEPTH, D_FF), f32),
        "w_down": n(ks[14], (DEPTH, D_FF, D_MODEL), D_FF),
    }


def reference(x, meta_tokens, norm_mix, norm_ffn, w_in, b_gate, q_norm, k_norm,
              w_attn_br, w_four, w_out, w_up, w_conv, b_conv, w_down):
    B, n_tok, D = x.shape
    ROWS = n_tok // GRID_W
    L = n_tok + N_META
    n_blocks = n_tok // Q_BLOCK

    f32 = jnp.float32
    rows = jnp.repeat(jnp.arange(ROWS, dtype=f32), GRID_W)
    cols = jnp.tile(jnp.arange(GRID_W, dtype=f32), ROWS)
    zeros_meta = jnp.zeros((N_META,), f32)
    pos_r = jnp.concatenate([zeros_meta, rows])
    pos_c = jnp.concatenate([zeros_meta, cols])
    inv_freq = 1.0 / (ROPE_THETA ** (jnp.arange(ROPE_PAIRS, dtype=f32) / ROPE_PAIRS))
    ang_r = pos_r[:, None] * inv_freq[None, :]
    ang_c = pos_c[:, None] * inv_freq[None, :]
    cos_r, sin_r = jnp.cos(ang_r).astype(x.dtype), jnp.sin(ang_r).astype(x.dtype)
    cos_c, sin_c = jnp.cos(ang_c).astype(x.dtype), jnp.sin(ang_c).astype(x.dtype)

    meta = jnp.broadcast_to(meta_tokens.astype(x.dtype)[None], (B, N_META, D))
    h_stream = jnp.concatenate([meta, x], axis=1)

    for i in range(DEPTH):
        h = _rmsnorm(h_stream, norm_mix[i])
        proj = h @ w_in[i]
        q = proj[..., OFF_Q:OFF_K].reshape(B, L, N_HEADS, HEAD_DIM)
        k = proj[..., OFF_K:OFF_V].reshape(B, L, N_KV_HEADS, HEAD_DIM)
        v = proj[..., OFF_V:OFF_F].reshape(B, L, N_KV_HEADS, HEAD_DIM)
        f = proj[..., OFF_F:OFF_GA]
        gates = proj[..., OFF_GA:] + b_gate[i]
        g_attn = jax.nn.sigmoid(gates[..., :D])
        g_four = jax.nn.sigmoid(gates[..., D:])

        q = _rope_2d(_rmsnorm(q, q_norm[i]), cos_r, sin_r, cos_c, sin_c)
        k = _rope_2d(_rmsnorm(k, k_norm[i]), cos_r, sin_r, cos_c, sin_c)
        q = q.reshape(B, L, N_KV_HEADS, GQA_GROUP, HEAD_DIM)
        o_meta = _attend_block(q[:, :N_META], k, v).reshape(B, N_META, ATTN_WIDTH)
        q_blocks = q[:, N_META:].reshape(B, n_blocks, Q_BLOCK, N_KV_HEADS, GQA_GROUP, HEAD_DIM)
        q_blocks = jnp.moveaxis(q_blocks, 1, 0)
        o_real = lax.map(lambda qb: _attend_block(qb, k, v), q_blocks)
        o_real = jnp.moveaxis(o_real, 0, 1).reshape(B, n_tok, ATTN_WIDTH)
        attn = jnp.concatenate([o_meta, o_real], axis=1)
        a_br = attn @ w_attn_br[i]

        fg = f.reshape(B, L, N_FOURIER_GROUPS, FOURIER_GROUP_CH).astype(jnp.float32)
        fr = jnp.fft.fft2(fg, axes=(1, 3), norm="ortho").real.astype(x.dtype)
        s_br = fr.reshape(B, L, FOURIER_WIDTH) @ w_four[i]

        merged = g_attn * a_br + g_four * s_br
        h_stream = h_stream + merged @ w_out[i]

        h2 = _rmsnorm(h_stream, norm_ffn[i])
        up = h2 @ w_up[i]
        u_gate = _dwconv3_centred(up[..., :D_FF], w_conv[i], b_conv[i])
        u_val = up[..., D_FF:]
        h_stream = h_stream + (jax.nn.silu(u_gate) * u_val) @ w_down[i]

    return h_stream[:, N_META:]
```

```python
import contextlib
import math
import numpy as np
import concourse.bass as bass
import concourse.mybir as mybir
from concourse.bass_utils import run_bass_kernel_spmd

F32, BF16 = mybir.dt.float32, mybir.dt.bfloat16
AF = mybir.ActivationFunctionType
ALU = mybir.AluOpType
NCORES = 8
GRP = [[0, 1, 2, 3], [4, 5, 6, 7]]
ALL8 = [list(range(8))]


def nchunks_for(n, unit_bytes, limit=1 << 20):
    for k in range(1, n + 1):
        if n % k == 0 and (n // k) * unit_bytes <= limit:
            return k
    raise ValueError


class Cfg:
    def __init__(self, D=2048, SEQ=16384, H=8, KVH=2, FG=8, DFF=5632, TT=512, N1=100, N2=164, DEPTH=2):
        self.D, self.SEQ, self.H, self.KVH, self.FG, self.DFF, self.TT = D, SEQ, H, KVH, FG, DFF, TT
        self.N1, self.N2, self.DEPTH = N1, N2, DEPTH
        self.HD = 128
        self.G = H // KVH
        self.AW = H * 128
        self.KVW = KVH * 128
        self.FW = FG * 128
        self.NMETA = 16
        self.GRIDW = 64
        self.L = SEQ + 16
        assert N1 * N2 == self.L
        self.TPC = SEQ // 4
        self.NT = self.TPC // TT
        self.NCOL = 16 + self.TPC
        self.DC = D // 128
        self.FFC = DFF // 128
        self.FCH = self.FW // 4
        self.FCB = self.FCH // 128
        self.OFF_K = self.AW
        self.OFF_V = self.AW + self.KVW
        self.OFF_F = self.OFF_V + self.KVW
        self.OFF_GA = self.OFF_F + self.FW
        self.INW = self.OFF_GA + 2 * D
        self.EPS = 1e-6
        self.N2C = N2 // 2
        assert self.N2C * 2 == N2 and self.N2C <= 128 and 2 * N2 <= 512 and N1 <= 128
        self.P2B = 512 // self.FCH
        assert N2 % self.P2B == 0
        self.NH = 2 * (self.NT + 1)
        MB = 1 << 20
        self.NFC = nchunks_for(self.TPC, self.FW * 2)
        self.NPC = nchunks_for(self.L, 2 * self.FCH * 2)
        self.LC = self.L // self.NPC
        assert 128 * self.TPC * 2 <= MB
        o = 0
        self.V_GMIX = o; o += self.DC
        self.V_GFFN = o; o += self.DC
        self.V_BG = o; o += 2 * self.DC
        self.V_QN = o; o += 1
        self.V_KN = o; o += 1
        self.V_WCV = o; o += 3 * self.FFC
        self.V_BCV = o; o += self.FFC
        self.NV = o


FULL = Cfg()


def lhsT_layout(W):
    K, N = W.shape
    return np.ascontiguousarray(W.reshape(K // 128, 128, N // 128, 128).transpose(2, 1, 0, 3).reshape(N, K))


def rhs_layout(W):
    K, N = W.shape
    return np.ascontiguousarray(W.reshape(K // 128, 128, N).transpose(1, 0, 2).reshape(128, (K // 128) * N))


WNAMES = ["wqk", "wg", "wvf", "wab", "wfr", "wout", "wup", "wdn"]


def weight_shapes(cf):
    return {
        "wqk": (cf.AW + cf.KVW, cf.D), "wg": (2 * cf.D, cf.D), "wvf": (128, cf.DC * (cf.KVW + cf.FW)),
        "wab": (cf.D, cf.AW), "wfr": (128, cf.FG * cf.D), "wout": (cf.D, cf.D),
        "wup": (2 * cf.DFF, cf.D), "wdn": (cf.D, cf.DFF),
    }


def weight_chunks(cf):
    out = {}
    for n, (R, C) in weight_shapes(cf).items():
        out[n] = nchunks_for(R // 4, C * 2)
    return out


def host_tables(cf):
    f64 = np.float64
    tabs = {}
    ident = np.eye(128, dtype=np.float32)
    rotT = np.zeros((128, 128), np.float32)
    for i in range(128):
        blk = i // 32
        if blk % 2 == 0:
            rotT[i + 32, i] = -1.0
        else:
            rotT[i - 32, i] = 1.0
    N1, N2, L = cf.N1, cf.N2, cf.L
    a1 = 2 * np.pi * np.outer(np.arange(N1), np.arange(N1)).astype(f64) / N1
    t1 = np.concatenate([np.cos(a1), np.sin(a1)], 1).astype(np.float32)
    a2 = 2 * np.pi * np.outer(np.arange(N2), np.arange(N2)).astype(f64) / N2
    C2, S2 = np.cos(a2), np.sin(a2)
    tabR = np.concatenate([C2, S2], 1)
    tabI = np.concatenate([-S2, C2], 1)
    t3 = np.stack([tabR, tabI], 0).reshape(2, 2, cf.N2C, 2 * N2).astype(np.float32)
    at = 2 * np.pi * np.outer(np.arange(N1), np.arange(N2)).astype(f64) / L
    tw = np.stack([np.cos(at), np.sin(at)], 1).astype(np.float32)
    ac = 2 * np.pi * np.outer(np.arange(128), np.arange(128)).astype(f64) / 128
    sc = 1.0 / math.sqrt(L * 128.0)
    cs = np.stack([np.cos(ac) * sc, -np.sin(ac) * sc], 1).astype(np.float32)
    tabs["cmat"] = np.ascontiguousarray(np.concatenate([ident, rotT, cs.reshape(128, 256)], 1))
    tabs["t1"] = t1
    tabs["t3"] = np.ascontiguousarray(t3.transpose(2, 0, 1, 3).reshape(cf.N2C, 4 * 2 * N2))
    tabs["tw"] = np.ascontiguousarray(tw.reshape(N1, 2 * N2))
    return tabs


def core_tables(cf, r):
    inv = 1.0 / (10000.0 ** (np.arange(32, dtype=np.float64) / 32))
    tg = r * cf.TPC + np.arange(cf.TPC)
    rows = (tg // cf.GRIDW).astype(np.float64)
    cols = (tg % cf.GRIDW).astype(np.float64)
    ang = np.zeros((128, cf.NCOL), np.float64)
    ang[0:32, 16:] = inv[:, None] * rows[None]
    ang[32:64, 16:] = inv[:, None] * rows[None]
    ang[64:96, 16:] = inv[:, None] * cols[None]
    ang[96:128, 16:] = inv[:, None] * cols[None]
    rope = np.stack([np.cos(ang), np.sin(ang)], 1).astype(np.float32)
    sel = np.zeros((128, 16), np.float32)
    sel[:, 0] = 1.0 if r == 0 else 0.0
    for j in range(4):
        sel[:, 1 + j] = 1.0 if j == r - 1 else 0.0
        sel[:, 5 + j] = 1.0 if j == r + 1 else 0.0
        sel[:, 9 + j] = 1.0 if j == r else 0.0
    return np.ascontiguousarray(rope.reshape(128, 2 * cf.NCOL)), sel


def prep_inputs(cf, x, meta_tokens, norm_mix, norm_ffn, w_in, b_gate, q_norm, k_norm,
                w_attn_br, w_four, w_out, w_up, w_conv, b_conv, w_down):
    f = lambda a: np.asarray(a, dtype=np.float32)
    x, meta_tokens = f(x), f(meta_tokens)
    w_in, w_attn_br, w_four, w_out, w_up, w_down = map(f, (w_in, w_attn_br, w_four, w_out, w_up, w_down))
    DEPTH = cf.DEPTH
    full = {n: [] for n in WNAMES}
    for l in range(DEPTH):
        full["wqk"].append(lhsT_layout(w_in[l][:, 0:cf.OFF_V]))
        full["wg"].append(lhsT_layout(w_in[l][:, cf.OFF_GA:]))
        full["wvf"].append(rhs_layout(w_in[l][:, cf.OFF_V:cf.OFF_GA]))
        full["wab"].append(lhsT_layout(w_attn_br[l]))
        full["wfr"].append(rhs_layout(w_four[l]))
        full["wout"].append(lhsT_layout(w_out[l]))
        full["wup"].append(lhsT_layout(w_up[l]))
        full["wdn"].append(lhsT_layout(w_down[l]))
    vecs = np.zeros((DEPTH, 128, cf.NV), np.float32)
    for l in range(DEPTH):
        vecs[l, :, cf.V_GMIX:cf.V_GMIX + cf.DC] = f(norm_mix[l]).reshape(cf.DC, 128).T
        vecs[l, :, cf.V_GFFN:cf.V_GFFN + cf.DC] = f(norm_ffn[l]).reshape(cf.DC, 128).T
        vecs[l, :, cf.V_BG:cf.V_BG + 2 * cf.DC] = f(b_gate[l]).reshape(2 * cf.DC, 128).T
        vecs[l, :, cf.V_QN] = f(q_norm[l])
        vecs[l, :, cf.V_KN] = f(k_norm[l])
        wc = f(w_conv[l]).reshape(3, cf.FFC, 128).transpose(2, 1, 0)
        vecs[l, :, cf.V_WCV:cf.V_WCV + 3 * cf.FFC] = wc.reshape(128, 3 * cf.FFC)
        vecs[l, :, cf.V_BCV:cf.V_BCV + cf.FFC] = f(b_conv[l]).reshape(cf.FFC, 128).T
    tabs = host_tables(cf)
    wch = weight_chunks(cf)
    metaT = np.ascontiguousarray(meta_tokens.T)
    in_maps = []
    for c in range(NCORES):
        b, r = c // 4, c % 4
        m = {}
        m["xT"] = np.ascontiguousarray(x[b, r * cf.TPC:(r + 1) * cf.TPC, :].T)
        m["metaT"] = metaT
        for n in WNAMES:
            R = full[n][0].shape[0]
            nch = wch[n]
            rc = R // 4 // nch
            m[n] = np.ascontiguousarray(np.stack(
                [full[n][l].reshape(nch, 4, rc, -1)[:, r].reshape(nch * rc, -1) for l in range(DEPTH)], 0))
        m["vecs"] = vecs
        rope, sel = core_tables(cf, r)
        m["rope"] = rope
        m["sel"] = sel
        for k, v in tabs.items():
            m[k] = v
        in_maps.append(m)
    return in_maps


class _Stop(Exception):
    pass


class Buf:
    def __init__(self, name, t):
        self.name, self.t = name, t
        self.w, self.r = {}, {}


class Eng:
    def __init__(self, name, e, sem):
        self.name, self.e, self.sem = name, e, sem
        self.count = 0
        self.waited = {}


def _merge(d, src):
    for k, (s, v) in src.items():
        if k not in d or d[k][1] < v:
            d[k] = (s, v)


class TR:
    def __init__(self, nc, stack):
        self.nc, self.stack = nc, stack
        mk = lambda n: stack.enter_context(nc.semaphore(n))
        self.pe = Eng("pe", nc.tensor, mk("s_pe"))
        self.act = Eng("act", nc.scalar, mk("s_act"))
        self.dve = Eng("dve", nc.vector, mk("s_dve"))
        self.pool = Eng("pool", nc.gpsimd, mk("s_pool"))
        self.sp = Eng("sp", nc.sync, mk("s_sp"))
        self.engs = [self.pe, self.act, self.dve, self.pool, self.sp]
        self.dsems = {}
        self.csems = []
        self.shsem = {}
        self.nsem = 5
        self.dead = False

    def _sync(self, eng, reads, writes, ignore=None):
        raw = {}
        for b in reads:
            _merge(raw, b.w)
        oth = {}
        for b in writes:
            _merge(oth, b.w)
            _merge(oth, b.r)
        me = id(eng.sem)
        d = dict(raw)
        for k, sv in oth.items():
            if k == me:
                continue
            if k not in d or d[k][1] < sv[1]:
                d[k] = sv
        if eng is self.pe:
            d.pop(me, None)
        if ignore is not None:
            d.pop(ignore, None)
        for k, (s, v) in d.items():
            if eng.waited.get(k, 0) < v:
                eng.e.wait_ge(s, v)
                eng.waited[k] = v

    def _rec(self, ev, reads, writes):
        k, s, v = ev
        for b in reads:
            if k not in b.r or b.r[k][1] < v:
                b.r[k] = (s, v)
        for b in writes:
            if k not in b.w or b.w[k][1] < v:
                b.w[k] = (s, v)

    def op(self, eng, fn, reads=(), writes=()):
        if self.dead:
            return
        self._sync(eng, reads, writes)
        ins = fn()
        eng.count += 1
        ins.then_inc(eng.sem, 1)
        self._rec((id(eng.sem), eng.sem, eng.count), reads, writes)

    def mm(self, fns, reads=(), writes=()):
        if self.dead:
            return
        eng = self.pe
        self._sync(eng, reads, writes)
        ins = None
        for fn in fns:
            ins = fn()
        eng.count += 1
        ins.then_inc(eng.sem, 1)
        self._rec((id(eng.sem), eng.sem, eng.count), reads, writes)

    def dma(self, q, out, in_, key, reads=(), writes=()):
        if self.dead:
            return
        self._sync(q, reads, writes)
        if key not in self.dsems:
            self.nsem += 1
            self.dsems[key] = [self.stack.enter_context(self.nc.semaphore("d_" + key)), 0]
        ent = self.dsems[key]
        ent[1] += 1
        q.e.dma_start(out=out, in_=in_).then_inc(ent[0], 16)
        self._rec((id(ent[0]), ent[0], 16 * ent[1]), reads, writes)

    def coll(self, kind, groups, inb, outb, name, shared=None, in_ap=None, out_ap=None):
        if self.dead:
            return
        q = self.pool
        if shared is not None and shared[0] in self.shsem:
            self._sync(q, [inb], [outb], ignore=id(self.shsem[shared[0]]))
        else:
            self._sync(q, [inb], [outb])
        if shared is None:
            self.nsem += 1
            sem = self.stack.enter_context(self.nc.semaphore("c_" + name))
            val = 1
            self.csems.append((sem, 1))
        else:
            key, val = shared
            if key not in self.shsem:
                self.nsem += 1
                self.shsem[key] = self.stack.enter_context(self.nc.semaphore("c_" + key))
                self.csems.append((self.shsem[key], val))
            sem = self.shsem[key]
        q.e.collective_compute(kind, ALU.bypass, replica_groups=groups,
                               ins=[(in_ap if in_ap is not None else inb.t.ap()).opt()],
                               outs=[(out_ap if out_ap is not None else outb.t.ap()).opt()]).then_inc(sem, 1)
        self._rec((id(sem), sem, val), [inb], [outb])

    def barrier(self):
        if self.dead:
            return
        evs = [(id(e.sem), e.sem, e.count) for e in self.engs if e.count > 0]
        evs += [(id(s), s, 16 * c) for (s, c) in self.dsems.values() if c > 0]
        evs += [(id(s), s, v) for (s, v) in self.csems]
        for eng in self.engs:
            for k, s, v in evs:
                if k == id(eng.sem):
                    continue
                if eng.waited.get(k, 0) < v:
                    eng.e.wait_ge(s, v)
                    eng.waited[k] = v


def build(cf, final_wait=True):
    nc = bass.Bass("TRN2", target_bir_lowering=False)
    D, DC, TT, NT, NCOL, TPC, H, KVH, G = cf.D, cf.DC, cf.TT, cf.NT, cf.NCOL, cf.TPC, cf.H, cf.KVH, cf.G
    FW, FCH, FCB, FFC, KVW, AW, L, N1, N2, N2C = cf.FW, cf.FCH, cf.FCB, cf.FFC, cf.KVW, cf.AW, cf.L, cf.N1, cf.N2, cf.N2C
    DEPTH = cf.DEPTH
    NQK = H + KVH
    VFW = KVW + FW
    PQC = 2 * FW // 128
    wsh = weight_shapes(cf)

    def din(name, shape, dt=F32):
        return nc.dram_tensor(name, list(shape), dt, kind="ExternalInput")

    xT_in = din("xT", [D, TPC])
    metaT_in = din("metaT", [D, 16])
    w_in_sh = {n: din(n, [DEPTH, wsh[n][0] // 4, wsh[n][1]]) for n in WNAMES}
    vecs_in = din("vecs", [DEPTH, 128, cf.NV])
    rope_in = din("rope", [128, 2 * NCOL])
    sel_in = din("sel", [128, 16])
    cmat_in = din("cmat", [128, 512])
    t1_in = din("t1", [N1, 2 * N1])
    t3_in = din("t3", [N2C, 8 * N2])
    tw_in = din("tw", [N1, 2 * N2])
    yT = nc.dram_tensor("yT", [D, TPC], F32, kind="ExternalOutput")

    stack = contextlib.ExitStack()
    with stack:
        stack.enter_context(nc.allow_non_contiguous_dma(reason="small strided scratch transfers"))
        T = TR(nc, stack)
        blk = stack.enter_context(nc.Block())
        PE, ACT, DVE, POOL, SP = T.pe, T.act, T.dve, T.pool, T.sp

        def dram(name, shape, dt):
            return Buf(name, nc.dram_tensor(name, list(shape), dt))

        def ext(tn, name):
            return Buf(name, tn)

        B_xT, B_metaT, B_yT = ext(xT_in, "xT"), ext(metaT_in, "metaT"), ext(yT, "yT")
        B_vecs, B_rope, B_sel, B_cmat = ext(vecs_in, "vecs"), ext(rope_in, "rope"), ext(sel_in, "sel"), ext(cmat_in, "cmat")
        B_t1, B_t3, B_tw = ext(t1_in, "t1"), ext(t3_in, "t3"), ext(tw_in, "tw")
        B_wsh = {n: ext(w_in_sh[n], n) for n in WNAMES}

        wbs = {(n, l): dram(f"wbs_{n}{l}", [wsh[n][0] // 4, wsh[n][1]], BF16) for n in WNAMES for l in range(DEPTH)}
        wbf = {(n, l): dram(f"wbf_{n}{l}", [wsh[n][0], wsh[n][1]], BF16) for n in WNAMES for l in range(DEPTH)}
        wcs_d = [dram(f"wcs{l}", [D, 2 * FW], BF16) for l in range(DEPTH)]
        xa = dram("xa", [D, NCOL], F32)
        xm = dram("xm", [D, NCOL], F32)
        qT_d = dram("qT", [AW, NCOL], BF16)
        gT_d = dram("gT", [2 * D, NCOL], BF16)
        aT_d = dram("aT", [AW, NCOL], BF16)
        kin = [dram(f"kin{l}", [KVW, TPC], BF16) for l in range(DEPTH)]
        kout = [dram(f"kout{l}", [4 * KVW, TPC], BF16) for l in range(DEPTH)]
        kmeta = dram("kmeta", [KVW, 16], BF16)
        vin = [dram(f"vin{l}", [KVH * TPC, 128], BF16) for l in range(DEPTH)]
        vout = [dram(f"vout{l}", [KVH * 4 * TPC, 128], BF16) for l in range(DEPTH)]
        vmeta = dram("vmeta", [16, KVW], BF16)
        fin = [dram(f"fin{l}", [TPC, FW], BF16) for l in range(DEPTH)]
        fout = [dram(f"fout{l}", [4 * TPC, FW], BF16) for l in range(DEPTH)]
        fmeta = dram("fmeta", [16, FW], BF16)
        yd = dram("yd", [2, N1, N2, FCH], BF16)
        fpos = dram("fpos", [L, FW], BF16)
        pqin = [dram(f"pqin{l}", [cf.NPC * 2 * FCH, cf.LC], BF16) for l in range(DEPTH)]
        pqout = [dram(f"pqout{l}", [cf.NPC * 4 * 2 * FCH, cf.LC], BF16) for l in range(DEPTH)]
        hin = [dram(f"hin{l}", [128, 2 * DC], F32) for l in range(DEPTH)]
        hout = [dram(f"hout{l}", [4 * 128, 2 * DC], F32) for l in range(DEPTH)]

        ps0 = stack.enter_context(nc.psum_tensor("ps0", [128, 1024], F32))
        ps1 = stack.enter_context(nc.psum_tensor("ps1", [128, 1024], F32))
        ps4 = stack.enter_context(nc.psum_tensor("ps4", [128, 512], F32))
        ps5 = stack.enter_context(nc.psum_tensor("ps5", [128, 512], F32))
        ps6 = stack.enter_context(nc.psum_tensor("ps6", [128, 512], F32))
        pst = stack.enter_context(nc.psum_tensor("pst", [128, 1024], BF16))
        BK = [Buf(f"bk{i}", None) for i in range(8)]
        bank_aps = [ps0[:, 0:512], ps0[:, 512:1024], ps1[:, 0:512], ps1[:, 512:1024], ps4[:, :], ps5[:, :], ps6[:, :]]

        def bk(i):
            return bank_aps[i]

        uniq = [0]

        def sb(st, name, shape, dt, key=None):
            uniq[0] += 1
            return Buf(key or name, st.enter_context(nc.sbuf_tensor(f"{name}_{uniq[0]}", list(shape), dt)))

        ident = sb(stack, "ident", [128, 128], BF16)
        rotT = sb(stack, "rotT", [128, 128], BF16)
        csm = sb(stack, "csm", [128, 2, 128], BF16)
        ones = sb(stack, "ones", [128, 128], BF16)
        selb = sb(stack, "selb", [128, 16], F32)
        vec = sb(stack, "vec", [128, cf.NV], F32)

        T.dma(POOL, ident.t[:], cmat_in.ap()[:, 0:128], "c0", [B_cmat], [ident])
        T.dma(POOL, rotT.t[:], cmat_in.ap()[:, 128:256], "c0", [B_cmat], [rotT])
        T.dma(POOL, csm.t[:], cmat_in.ap()[:, 256:512].rearrange("p (a b) -> p a b", a=2), "c0", [B_cmat], [csm])
        T.dma(POOL, selb.t[:], sel_in.ap(), "c0", [B_sel], [selb])
        T.op(DVE, lambda: nc.vector.memset(ones.t[:], 1.0), [], [ones])

        T.dma(POOL, xa.t.ap()[:, 0:16], metaT_in.ap(), "xinit", [B_metaT], [xa])
        T.dma(POOL, xa.t.ap()[:, 16:NCOL], xT_in.ap(), "xinit", [B_xT], [xa])

        def ag_chunks(inb, outb, rows_in, nch, key, total=None):
            rc = rows_in // nch
            for j in range(nch):
                if getattr(cf, 'unshare', False) and total is None:
                    T.coll("AllGather", GRP, inb, outb, f"{key}_{j}",
                           in_ap=inb.t.ap()[j * rc:(j + 1) * rc, :], out_ap=outb.t.ap()[j * 4 * rc:(j + 1) * 4 * rc, :])
                else:
                    T.coll("AllGather", GRP, inb, outb, key, shared=(key, total or nch),
                           in_ap=inb.t.ap()[j * rc:(j + 1) * rc, :], out_ap=outb.t.ap()[j * 4 * rc:(j + 1) * 4 * rc, :])

        wch = weight_chunks(cf)
        wtot = sum(wch.values())
        for l in range(DEPTH):
            for n in WNAMES:
                T.dma(POOL, wbs[(n, l)].t.ap(), w_in_sh[n].ap()[l], "wcast", [B_wsh[n]], [wbs[(n, l)]])
                ag_chunks(wbs[(n, l)], wbf[(n, l)], wsh[n][0] // 4, wch[n], f"w{l}", wtot)

        tiles = [(16 + i * TT, TT, i) for i in range(NT)] + [(0, 16, NT)]

        class WRing:
            def __init__(self, st, name, kcmax, nslots):
                self.slots = [sb(st, f"{name}{i}", [128, kcmax, 128], BF16, key=f"{name[2:]}{i}") for i in range(nslots)]
                self.i = 0

            def load(self, wb, chunk, kc):
                s = self.slots[self.i % len(self.slots)]
                self.i += 1
                src = wb.t.ap()[chunk * 128:(chunk + 1) * 128, :].rearrange("p (k n) -> p k n", n=128)
                T.dma(SP, s.t[:, 0:kc, :], src, s.name, [wb], [s])
                return s

        def rmsnorm(xt, n, gcol, sq, hT, rs, rstd, pbank, pbuf):
            T.op(ACT, lambda: nc.scalar.activation(out=sq.t[:, :, 0:n], in_=xt.t[:, :, 0:n], func=AF.Square), [xt], [sq])
            T.mm([(lambda c=c: nc.tensor.matmul(pbank[:, 0:n], ones.t[:, :], sq.t[:, c, 0:n], start=(c == 0), stop=(c == DC - 1)))
                  for c in range(DC)], [ones, sq], [pbuf])
            T.op(ACT, lambda: nc.scalar.activation(out=rs.t[:, 0:n], in_=pbank[:, 0:n], func=AF.Sqrt, bias=float(cf.EPS), scale=1.0 / D), [pbuf], [rs])
            T.op(DVE, lambda: nc.vector.reciprocal(out=rstd.t[:, 0:n], in_=rs.t[:, 0:n]), [rs], [rstd])
            for c in range(DC):
                T.op(DVE, lambda c=c: nc.vector.scalar_tensor_tensor(out=hT.t[:, c, 0:n], in0=xt.t[:, c, 0:n], scalar=vec.t[:, gcol + c:gcol + c + 1],
                                                                    in1=rstd.t[:, 0:n], op0=ALU.mult, op1=ALU.mult), [xt, vec, rstd], [hT])

        def ck(name):
            if getattr(cf, 'stop', None) == name:
                if getattr(cf, 'exc', False):
                    raise _Stop()
                T.barrier()
                T.dead = True

        try:
          for l in range(DEPTH):
              last = (l == DEPTH - 1)
              T.barrier()
              ck('pro')
              T.dma(POOL, vec.t[:], vecs_in.ap()[l], "c_vec", [B_vecs], [vec])

              with contextlib.ExitStack() as ph:
                  xt = sb(ph, "p1_xt", [128, DC, TT], F32, key="xt")
                  sq = sb(ph, "p1_sq", [128, DC, TT], BF16)
                  hT = sb(ph, "p1_hT", [128, DC, TT], BF16)
                  rs = sb(ph, "p1_rs", [128, TT], F32)
                  rstd = sb(ph, "p1_rstd", [128, TT], F32)
                  wvf = sb(ph, "p1_wvf", [128, DC, VFW], BF16)
                  ring = WRing(ph, "p1_w", DC, 4)
                  rope = sb(ph, "p1_rope", [128, 2, TT], F32)
                  sqh = sb(ph, "p1_sqh", [128, TT], BF16)
                  qg = sb(ph, "p1_qg", [128, TT], BF16)
                  rsh = sb(ph, "p1_rsh", [128, TT], F32)
                  rstdh = sb(ph, "p1_rstdh", [128, TT], F32)
                  t1b = sb(ph, "p1_t1", [128, TT], F32)
                  t2b = sb(ph, "p1_t2", [128, TT], F32)
                  qo = [sb(ph, f"p1_qo{i}", [128, TT], BF16) for i in range(2)]
                  vo = [sb(ph, f"p1_vo{i}", [128, 512], BF16) for i in range(2)]
                  go = [sb(ph, f"p1_go{i}", [128, 4, TT], BF16) for i in range(2)]
                  T.dma(SP, wvf.t[:], wbf[("wvf", l)].t.ap().rearrange("p (k n) -> p k n", n=VFW), "p1_wvf", [wbf[("wvf", l)]], [wvf])
                  nq = nv = ng = 0
                  for (c0, n, _) in tiles:
                      meta = (n == 16)
                      T.dma(POOL, xt.t[:, :, 0:n], xa.t.ap()[:, c0:c0 + n].rearrange("(c p) n -> p c n", p=128), xt.name, [xa], [xt])
                      T.dma(POOL, rope.t[:, :, 0:n], rope_in.ap().rearrange("p (a n) -> p a n", a=2)[:, :, c0:c0 + n], "p1_rope", [B_rope], [rope])
                      ck('p1x')
                      rmsnorm(xt, n, cf.V_GMIX, sq, hT, rs, rstd, bk(6), BK[6])
                      ck('p1a')
                      for hd in range(NQK):
                          w = ring.load(wbf[("wqk", l)], hd, DC)
                          pa, pab = bk(hd % 2), BK[hd % 2]
                          T.mm([(lambda c=c: nc.tensor.matmul(pa[:, 0:n], w.t[:, c, :], hT.t[:, c, 0:n], start=(c == 0), stop=(c == DC - 1)))
                                for c in range(DC)], [w, hT], [pab])
                          gcol = cf.V_QN if hd < H else cf.V_KN
                          T.op(ACT, lambda: nc.scalar.activation(out=sqh.t[:, 0:n], in_=pa[:, 0:n], func=AF.Square), [pab], [sqh])
                          T.op(ACT, lambda: nc.scalar.activation(out=qg.t[:, 0:n], in_=pa[:, 0:n], func=AF.Copy, scale=vec.t[:, gcol:gcol + 1]), [pab, vec], [qg])
                          T.mm([lambda: nc.tensor.matmul(bk(4)[:, 0:n], ones.t[:, :], sqh.t[:, 0:n], start=True, stop=True)], [ones, sqh], [BK[4]])
                          T.mm([lambda: nc.tensor.matmul(bk(5)[:, 0:n], rotT.t[:, :], qg.t[:, 0:n], start=True, stop=True)], [rotT, qg], [BK[5]])
                          T.op(ACT, lambda: nc.scalar.activation(out=rsh.t[:, 0:n], in_=bk(4)[:, 0:n], func=AF.Sqrt, bias=float(cf.EPS), scale=1.0 / 128), [BK[4]], [rsh])
                          T.op(DVE, lambda: nc.vector.reciprocal(out=rstdh.t[:, 0:n], in_=rsh.t[:, 0:n]), [rsh], [rstdh])
                          T.op(DVE, lambda: nc.vector.tensor_tensor(out=t1b.t[:, 0:n], in0=qg.t[:, 0:n], in1=rope.t[:, 0, 0:n], op=ALU.mult), [qg, rope], [t1b])
                          T.op(DVE, lambda: nc.vector.tensor_tensor(out=t2b.t[:, 0:n], in0=bk(5)[:, 0:n], in1=rope.t[:, 1, 0:n], op=ALU.mult), [BK[5], rope], [t2b])
                          T.op(DVE, lambda: nc.vector.tensor_tensor(out=t1b.t[:, 0:n], in0=t1b.t[:, 0:n], in1=t2b.t[:, 0:n], op=ALU.add), [t1b, t2b], [t1b])
                          q_o = qo[nq % 2]
                          nq += 1
                          T.op(DVE, lambda: nc.vector.tensor_tensor(out=q_o.t[:, 0:n], in0=t1b.t[:, 0:n], in1=rstdh.t[:, 0:n], op=ALU.mult), [t1b, rstdh], [q_o])
                          if hd < H:
                              T.dma(POOL, qT_d.t.ap()[hd * 128:(hd + 1) * 128, c0:c0 + n], q_o.t[:, 0:n], q_o.name, [q_o], [qT_d])
                          else:
                              kh = hd - H
                              if meta:
                                  T.dma(POOL, kmeta.t.ap()[kh * 128:(kh + 1) * 128, :], q_o.t[:, 0:n], q_o.name, [q_o], [kmeta])
                              else:
                                  T.dma(POOL, kin[l].t.ap()[kh * 128:(kh + 1) * 128, c0 - 16:c0 - 16 + n], q_o.t[:, 0:n], q_o.name, [q_o], [kin[l]])
                      ck('p1b')
                      ntb = max(1, n // 128)
                      for tb in range(ntb):
                          tn = min(128, n)
                          cb0 = 0
                          while cb0 < VFW:
                              if cb0 < KVW:
                                  cw = KVW
                              else:
                                  cw = min(512, VFW - cb0)
                              pv, pvb = bk(2 + nv % 2), BK[2 + nv % 2]
                              T.mm([(lambda c=c: nc.tensor.matmul(pv[0:tn, 0:cw], hT.t[:, c, tb * 128:tb * 128 + tn], wvf.t[:, c, cb0:cb0 + cw],
                                                                  start=(c == 0), stop=(c == DC - 1))) for c in range(DC)], [hT, wvf], [pvb])
                              v_o = vo[nv % 2]
                              nv += 1
                              T.op(ACT, lambda: nc.scalar.copy(out=v_o.t[0:tn, 0:cw], in_=pv[0:tn, 0:cw]), [pvb], [v_o])
                              if cb0 < KVW:
                                  dst, dm = (vmeta, vmeta.t.ap()[:, :]) if meta else (
                                      vin[l], vin[l].t.ap().rearrange("(h t) d -> t h d", h=KVH)[c0 - 16 + tb * 128:c0 - 16 + tb * 128 + tn, :, :])
                              else:
                                  f0 = cb0 - KVW
                                  dst, dm = (fmeta, fmeta.t.ap()[:, f0:f0 + cw]) if meta else (fin[l], fin[l].t.ap()[c0 - 16 + tb * 128:c0 - 16 + tb * 128 + tn, f0:f0 + cw])
                              srcv = v_o.t[0:tn, 0:cw]
                              if cb0 < KVW and not meta:
                                  srcv = srcv.rearrange("p (h d) -> p h d", h=KVH)
                              T.dma(POOL, dm, srcv, v_o.name, [v_o], [dst])
                              cb0 += cw
                      ck('p1c')
                      for gc in range(2 * DC):
                          w = ring.load(wbf[("wg", l)], gc, DC)
                          pa, pab = bk(gc % 2), BK[gc % 2]
                          T.mm([(lambda c=c: nc.tensor.matmul(pa[:, 0:n], w.t[:, c, :], hT.t[:, c, 0:n], start=(c == 0), stop=(c == DC - 1)))
                                for c in range(DC)], [w, hT], [pab])
                          g_o = go[(ng // 4) % 2]
                          T.op(ACT, lambda: nc.scalar.activation(out=g_o.t[:, gc % 4, 0:n], in_=pa[:, 0:n], func=AF.Sigmoid,
                                                                 bias=vec.t[:, cf.V_BG + gc:cf.V_BG + gc + 1], scale=1.0), [pab, vec], [g_o])
                          ng += 1
                          if gc % 4 == 3:
                              g4 = gc // 4
                              T.dma(POOL, gT_d.t.ap()[g4 * 512:(g4 + 1) * 512, c0:c0 + n].rearrange("(a p) n -> p a n", p=128),
                                    g_o.t[:, :, 0:n], g_o.name, [g_o], [gT_d])
              T.barrier()
              ck('p1')
              ag_chunks(kin[l], kout[l], KVW, KVH, f"k{l}")
              ck('agk')
              ag_chunks(vin[l], vout[l], KVH * TPC, KVH, f"v{l}")
              ck('agv')
              ag_chunks(fin[l], fout[l], TPC, cf.NFC, f"f{l}")
              ck('ag1')

              P2B = cf.P2B
              with contextlib.ExitStack() as ph:
                  T.dma(POOL, fpos.t.ap()[0:16, :], fmeta.t.ap(), "fpos_a", [fmeta], [fpos])
                  frc = TPC // cf.NFC
                  for j in range(cf.NFC):
                      for rr in range(4):
                          p0 = 16 + rr * TPC + j * frc
                          T.dma(POOL, fpos.t.ap()[p0:p0 + frc, :], fout[l].t.ap()[(j * 4 + rr) * frc:(j * 4 + rr + 1) * frc, :], "fpos_a", [fout[l]], [fpos])
                  Z = sb(ph, "f_Z", [N1, N2, FCH], BF16)
                  PZ = N2 // 4 if N2 % 4 == 0 else N2 // 2
                  zs = [sb(ph, f"f_zs{i}", [N1, PZ, FCH], BF16) for i in range(2)]
                  t1s = sb(ph, "f_t1", [N1, 2 * N1], BF16)
                  tws = sb(ph, "f_tw", [N1, 2, N2], F32)
                  wfr = sb(ph, "f_wfr", [128, cf.FG, D], BF16)
                  wst = [sb(ph, f"f_wst{i}", [128, 512], BF16) for i in range(2)]
                  fa = sb(ph, "f_a", [N1, 512], F32)
                  fb = sb(ph, "f_b", [N1, 512], F32)
                  yo = [sb(ph, f"f_yo{i}", [N1, 2, 512], BF16) for i in range(2)]
                  T.dma(POOL, t1s.t[:], t1_in.ap(), "f_t1", [B_t1], [t1s])
                  T.dma(POOL, tws.t[:], tw_in.ap().rearrange("p (a n) -> p a n", a=2), "f_tw", [B_tw], [tws])
                  T.dma(SP, wfr.t[:], wbf[("wfr", l)].t.ap().rearrange("p (g n) -> p g n", g=cf.FG), "f_wfr", [wbf[("wfr", l)]], [wfr])
                  nw = 0
                  for g in range(cf.FG):
                      for pq in range(2):
                          kc = (g // FCB) * (2 * FCB) + pq * FCB + (g % FCB)
                          for nb in range(D // 512 if D >= 512 else 1):
                              nbw = min(512, D)
                              pw, pwb = bk(nw % 2), BK[nw % 2]
                              T.mm([lambda: nc.tensor.matmul(pw[:, 0:nbw], csm.t[:, pq, :], wfr.t[:, g, nb * nbw:(nb + 1) * nbw], start=True, stop=True)],
                                   [csm, wfr], [pwb])
                              ws_ = wst[nw % 2]
                              nw += 1
                              T.op(ACT, lambda: nc.scalar.copy(out=ws_.t[:, 0:nbw], in_=pw[:, 0:nbw]), [pwb], [ws_])
                              T.dma(POOL, wcs_d[l].t.ap()[nb * nbw:(nb + 1) * nbw, kc * 128:(kc + 1) * 128].rearrange("(j p) n -> p j n", p=128),
                                    ws_.t[:, 0:nbw].rearrange("p (j n) -> p j n", n=128), ws_.name, [ws_], [wcs_d[l]])
                  fview = fpos.t.ap().rearrange("(a b) c -> a b c", b=N2)
                  nz = 0
                  for pz in range(N2 // PZ):
                      for q in range(4):
                          z_ = zs[nz % 2]
                          nz += 1
                          T.dma(SP, z_.t[:], fview[:, pz * PZ:(pz + 1) * PZ, q * FCH:(q + 1) * FCH], z_.name, [fpos], [z_])
                          if q == 0:
                              T.op(DVE, lambda: nc.vector.tensor_scalar(out=Z.t[:, pz * PZ:(pz + 1) * PZ, :], in0=z_.t[:], scalar1=selb.t[0:N1, 9:10], scalar2=None, op0=ALU.mult),
                                   [z_, selb], [Z])
                          else:
                              T.op(DVE, lambda: nc.vector.scalar_tensor_tensor(out=Z.t[:, pz * PZ:(pz + 1) * PZ, :], in0=z_.t[:], scalar=selb.t[0:N1, 9 + q:10 + q],
                                                                              in1=Z.t[:, pz * PZ:(pz + 1) * PZ, :], op0=ALU.mult, op1=ALU.add), [z_, selb, Z], [Z])
                  Zf = Z.t[:].rearrange("p a c -> p (a c)")
                  nblk = N2 // P2B
                  for b_ in range(nblk):
                      pr, prb = bk(2 * (b_ % 2)), BK[2 * (b_ % 2)]
                      pi, pib = bk(2 * (b_ % 2) + 1), BK[2 * (b_ % 2) + 1]
                      T.mm([lambda: nc.tensor.matmul(pr[0:N1, :], t1s.t[:, 0:N1], Zf[:, b_ * 512:(b_ + 1) * 512], start=True, stop=True)], [t1s, Z], [prb])
                      T.mm([lambda: nc.tensor.matmul(pi[0:N1, :], t1s.t[:, N1:2 * N1], Zf[:, b_ * 512:(b_ + 1) * 512], start=True, stop=True)], [t1s, Z], [pib])
                      y_ = yo[b_ % 2]
                      tcb = tws.t[:, 0, b_ * P2B:(b_ + 1) * P2B].unsqueeze(2).to_broadcast([N1, P2B, FCH])
                      tsb = tws.t[:, 1, b_ * P2B:(b_ + 1) * P2B].unsqueeze(2).to_broadcast([N1, P2B, FCH])
                      v3 = lambda ap: ap.rearrange("p (a c) -> p a c", c=FCH)
                      T.op(DVE, lambda: nc.vector.tensor_tensor(out=v3(fa.t[:, :]), in0=v3(pr[0:N1, :]), in1=tcb, op=ALU.mult), [prb, tws], [fa])
                      T.op(DVE, lambda: nc.vector.tensor_tensor(out=v3(fb.t[:, :]), in0=v3(pi[0:N1, :]), in1=tsb, op=ALU.mult), [pib, tws], [fb])
                      T.op(DVE, lambda: nc.vector.tensor_tensor(out=y_.t[:, 0, :], in0=fa.t[:, :], in1=fb.t[:, :], op=ALU.subtract), [fa, fb], [y_])
                      T.op(DVE, lambda: nc.vector.tensor_tensor(out=v3(fa.t[:, :]), in0=v3(pr[0:N1, :]), in1=tsb, op=ALU.mult), [prb, tws], [fa])
                      T.op(DVE, lambda: nc.vector.tensor_tensor(out=v3(fb.t[:, :]), in0=v3(pi[0:N1, :]), in1=tcb, op=ALU.mult), [pib, tws], [fb])
                      T.op(DVE, lambda: nc.vector.tensor_tensor(out=y_.t[:, 1, :], in0=fa.t[:, :], in1=fb.t[:, :], op=ALU.add), [fa, fb], [y_])
                      ydv = yd.t.ap().rearrange("r k a c -> k r (a c)")
                      T.dma(POOL, ydv[:, :, b_ * 512:(b_ + 1) * 512], y_.t[:], y_.name, [y_], [yd])
              T.barrier()
              ck('f1')
              with contextlib.ExitStack() as ph:
                  t3s = sb(ph, "f_t3", [N2C, 2, 2, 2 * N2], BF16)
                  T.dma(POOL, t3s.t[:], t3_in.ap().rearrange("p (r k n) -> p r k n", r=2, k=2), "f_t3", [B_t3], [t3s])
                  yt = [[sb(ph, f"f_yt{ri}{ch}", [N2C, N1, 128], BF16) for ch in range(2)] for ri in range(2)]
                  pqs = sb(ph, "f_pqs", [128, 2, L], BF16)
                  for cb in range(FCB):
                      for ri in range(2):
                          for ch in range(2):
                              src = yd.t.ap()[ri].rearrange("k a c -> a k c")[ch * N2C:(ch + 1) * N2C, :, cb * 128:(cb + 1) * 128]
                              T.dma(SP, yt[ri][ch].t[:], src, yt[ri][ch].name, [yd], [yt[ri][ch]])
                      for k1 in range(N1):
                          po, pob = bk(4 + k1 % 2), BK[4 + k1 % 2]
                          fns = []
                          idx = 0
                          for ri in range(2):
                              for ch in range(2):
                                  fns.append(lambda ri=ri, ch=ch, idx=idx: nc.tensor.matmul(po[:, 0:2 * N2], yt[ri][ch].t[:, k1, :], t3s.t[:, ri, ch, :],
                                                                                           start=(idx == 0), stop=(idx == 3)))
                                  idx += 1
                          T.mm(fns, [yt[0][0], yt[0][1], yt[1][0], yt[1][1], t3s], [pob])
                          dst = pqs.t[:, :, k1:k1 + N1 * (N2 - 1) + 1:N1]
                          T.op(ACT, lambda: nc.scalar.copy(out=dst, in_=po[:, 0:2 * N2].rearrange("p (a n) -> p a n", a=2)), [pob], [pqs])
                      for j in range(cf.NPC):
                          T.dma(POOL, pqin[l].t.ap()[j * 2 * FCH:(j + 1) * 2 * FCH, :].rearrange("(a c) n -> c a n", a=2)[cb * 128:(cb + 1) * 128, :, :],
                                pqs.t[:, :, j * cf.LC:(j + 1) * cf.LC], "f_pqs", [pqs], [pqin[l]])
              T.barrier()
              ck('f3')
              ag_chunks(pqin[l], pqout[l], cf.NPC * 2 * FCH, cf.NPC, f"pq{l}")
              ck('pq')

              scale = 1.0 / math.sqrt(128.0)
              NKC = 4 * TPC // 128
              with contextlib.ExitStack() as ph:
                  KT = sb(ph, "a_KT", [128, 4 * TPC + 16], BF16)
                  Vt = sb(ph, "a_Vt", [128, NKC + 1, 132], BF16)
                  QT = [sb(ph, f"a_QT{i}", [128, TT], BF16) for i in range(2)]
                  Pb = [sb(ph, f"a_P{i}", [128, 2, TT], BF16) for i in range(2)]
                  rinv = sb(ph, "a_rinv", [128, 4], F32)
                  on = [sb(ph, f"a_on{i}", [128, 128], BF16) for i in range(2)]
                  aT = [sb(ph, f"a_aT{i}", [128, TT], BF16) for i in range(2)]
                  na = 0
                  for kvh in range(KVH):
                      for rr in range(4):
                          T.dma(SP, KT.t[:, rr * TPC:(rr + 1) * TPC], kout[l].t.ap()[(kvh * 4 + rr) * 128:(kvh * 4 + rr + 1) * 128, :], "a_KT", [kout[l]], [KT])
                      T.dma(SP, KT.t[:, 4 * TPC:4 * TPC + 16], kmeta.t.ap()[kvh * 128:(kvh + 1) * 128, :], "a_KT", [kmeta], [KT])
                      T.dma(SP, Vt.t[:, 0:NKC, 0:128], vout[l].t.ap()[kvh * 4 * TPC:(kvh + 1) * 4 * TPC, :].rearrange("(c p) d -> p c d", p=128), "a_Vt", [vout[l]], [Vt])
                      T.dma(SP, Vt.t[0:16, NKC, 0:128], vmeta.t.ap()[:, kvh * 128:(kvh + 1) * 128], "a_Vt", [vmeta], [Vt])
                      T.op(DVE, lambda: nc.vector.memset(Vt.t[:, :, 128:129], 1.0), [], [Vt])
                      items = [(2 * j, 2) for j in range(NKC // 2)] + [(NKC, 1)]
                      for (c0, n, _) in tiles:
                          nqb = max(1, n // 128)
                          qn_ = min(128, n)
                          for gq in range(G):
                              hd = kvh * G + gq
                              Q = QT[na % 2]
                              a_T = aT[na % 2]
                              na += 1
                              T.dma(POOL, Q.t[:, 0:n], qT_d.t.ap()[hd * 128:(hd + 1) * 128, c0:c0 + n], Q.name, [qT_d], [Q])

                              def qk(it, slot):
                                  kc0, cnt = it
                                  for u in range(cnt):
                                      kk = 16 if kc0 == NKC else 128
                                      pS = bk(2 * slot + u)
                                      T.mm([lambda: nc.tensor.matmul(pS[0:kk, 0:n], KT.t[:, (kc0 + u) * 128:(kc0 + u) * 128 + kk], Q.t[:, 0:n], start=True, stop=True)],
                                           [KT, Q], [BK[2 * slot + u]])

                              qk(items[0], 0)
                              for ii, it in enumerate(items):
                                  slot = ii % 2
                                  if ii + 1 < len(items):
                                      qk(items[ii + 1], (ii + 1) % 2)
                                  kc0, cnt = it
                                  kk = 16 if kc0 == NKC else 128
                                  Pt = Pb[slot]
                                  src = (ps0 if slot == 0 else ps1)[0:kk, :].rearrange("p (a n) -> p a n", a=2)[:, 0:cnt, 0:n]
                                  T.op(ACT, lambda: nc.scalar.activation(out=Pt.t[0:kk, 0:cnt, 0:n], in_=src, func=AF.Exp, scale=scale),
                                       [BK[2 * slot], BK[2 * slot + 1]], [Pt])
                                  fns = []
                                  for u in range(cnt):
                                      for qb in range(nqb):
                                          ob = 4 + qb // 2
                                          oc0 = (qb % 2) * 256
                                          first = (ii == 0 and u == 0 and qb % 2 == 0)
                                          lastm = (ii == len(items) - 1 and u == cnt - 1)
                                          fns.append(lambda u=u, qb=qb, ob=ob, oc0=oc0, first=first, lastm=lastm: nc.tensor.matmul(
                                              bk(ob)[0:qn_, oc0:oc0 + 129], Pt.t[0:kk, u, qb * 128:qb * 128 + qn_], Vt.t[0:kk, kc0 + u, 0:129],
                                              start=first, stop=lastm, skip_group_check=True))
                                  T.mm(fns, [Pt, Vt], [BK[4], BK[5]])
                              for qb in range(nqb):
                                  ob = 4 + qb // 2
                                  oc0 = (qb % 2) * 256
                                  T.op(DVE, lambda: nc.vector.reciprocal(out=rinv.t[0:qn_, qb:qb + 1], in_=bk(ob)[0:qn_, oc0 + 128:oc0 + 129]), [BK[ob]], [rinv])
                                  o_n = on[qb % 2]
                                  T.op(DVE, lambda: nc.vector.tensor_scalar(out=o_n.t[0:qn_, :], in0=bk(ob)[0:qn_, oc0:oc0 + 128], scalar1=rinv.t[0:qn_, qb:qb + 1], scalar2=None, op0=ALU.mult),
                                       [BK[ob], rinv], [o_n])
                                  T.mm([lambda: nc.tensor.transpose(pst[:, qb * 128:qb * 128 + qn_], o_n.t[0:qn_, :], ident.t[0:qn_, 0:qn_])], [o_n, ident], [BK[7]])
                              T.op(DVE, lambda: nc.vector.tensor_copy(out=a_T.t[:, 0:n], in_=pst[:, 0:n]), [BK[7]], [a_T])
                              T.dma(POOL, aT_d.t.ap()[hd * 128:(hd + 1) * 128, c0:c0 + n], a_T.t[:, 0:n], a_T.name, [a_T], [aT_d])
              T.barrier()
              ck('attn')
              xh_st = contextlib.ExitStack()
              xh = sb(xh_st, "xh", [128, DC, cf.NH], F32)
              hsb = sb(xh_st, "hsb", [128, 2, DC], F32)
              with contextlib.ExitStack() as ph:
                  xt = sb(ph, "m_xt", [128, DC, TT], F32, key="xt")
                  at = sb(ph, "m_at", [128, H, TT], BF16)
                  pq = sb(ph, "m_pq", [128, PQC, TT], BF16)
                  pqc = [sb(ph, f"m_pqc{i}", [128, PQC, TT], BF16) for i in range(2)]
                  gt = sb(ph, "m_gt", [128, 2 * DC, TT], BF16)
                  mg = sb(ph, "m_mg", [128, DC, TT], BF16)
                  ta = sb(ph, "m_ta", [128, TT], F32)
                  tb_ = sb(ph, "m_tb", [128, TT], F32)
                  xo = xt
                  ring = WRing(ph, "m_w", max(DC, PQC, H), 4)
                  T.op(DVE, lambda: nc.vector.memset(xh.t[:], 0.0), [], [xh])
                  npc = 0
                  for (c0, n, hj) in tiles:
                      meta = (n == 16)
                      T.dma(POOL, xt.t[:, :, 0:n], xa.t.ap()[:, c0:c0 + n].rearrange("(c p) n -> p c n", p=128), xt.name, [xa], [xt])
                      T.dma(POOL, at.t[:, :, 0:n], aT_d.t.ap()[:, c0:c0 + n].rearrange("(c p) n -> p c n", p=128), "m_at", [aT_d], [at])
                      T.dma(POOL, gt.t[:, :, 0:n], gT_d.t.ap()[:, c0:c0 + n].rearrange("(c p) n -> p c n", p=128), "m_gt", [gT_d], [gt])
                      def pq_load(dst_t, pos0, nn, key, dbuf):
                          a = pos0
                          while a < pos0 + nn:
                              j = a // cf.LC
                              b = min(pos0 + nn, (j + 1) * cf.LC)
                              v_ = pqout[l].t.ap()[j * 8 * FCH:(j + 1) * 8 * FCH, :].rearrange("(c p) n -> p c n", p=128)
                              T.dma(POOL, dst_t[:, :, a - pos0:b - pos0], v_[:, :, a - j * cf.LC:b - j * cf.LC], key, [pqout[l]], [dbuf])
                              a = b
                      if meta:
                          pq_load(pq.t, 0, 16, "m_pq", pq)
                      else:
                          for q in range(4):
                              pc = pqc[npc % 2]
                              npc += 1
                              pq_load(pc.t, q * TPC + c0, n, pc.name, pc)
                              if q == 0:
                                  T.op(DVE, lambda: nc.vector.tensor_scalar(out=pq.t[:, :, 0:n], in0=pc.t[:, :, 0:n], scalar1=selb.t[:, 9:10], scalar2=None, op0=ALU.mult), [pc, selb], [pq])
                              else:
                                  T.op(DVE, lambda: nc.vector.scalar_tensor_tensor(out=pq.t[:, :, 0:n], in0=pc.t[:, :, 0:n], scalar=selb.t[:, 9 + q:10 + q], in1=pq.t[:, :, 0:n],
                                                                                  op0=ALU.mult, op1=ALU.add), [pc, selb, pq], [pq])
                      for oc in range(DC):
                          w1 = ring.load(wbf[("wab", l)], oc, H)
                          w2 = ring.load(wcs_d[l], oc, PQC)
                          T.mm([(lambda c=c: nc.tensor.matmul(bk(0)[:, 0:n], w1.t[:, c, :], at.t[:, c, 0:n], start=(c == 0), stop=(c == H - 1))) for c in range(H)], [w1, at], [BK[0]])
                          T.mm([(lambda c=c: nc.tensor.matmul(bk(1)[:, 0:n], w2.t[:, c, :], pq.t[:, c, 0:n], start=(c == 0), stop=(c == PQC - 1))) for c in range(PQC)], [w2, pq], [BK[1]])
                          T.op(DVE, lambda: nc.vector.tensor_tensor(out=ta.t[:, 0:n], in0=bk(0)[:, 0:n], in1=gt.t[:, oc, 0:n], op=ALU.mult), [BK[0], gt], [ta])
                          T.op(DVE, lambda: nc.vector.tensor_tensor(out=tb_.t[:, 0:n], in0=bk(1)[:, 0:n], in1=gt.t[:, DC + oc, 0:n], op=ALU.mult), [BK[1], gt], [tb_])
                          T.op(DVE, lambda: nc.vector.tensor_tensor(out=mg.t[:, oc, 0:n], in0=ta.t[:, 0:n], in1=tb_.t[:, 0:n], op=ALU.add), [ta, tb_], [mg])
                      for oc in range(DC):
                          w = ring.load(wbf[("wout", l)], oc, DC)
                          pb_, pbb = bk(2 + oc % 2), BK[2 + oc % 2]
                          T.mm([(lambda c=c: nc.tensor.matmul(pb_[:, 0:n], w.t[:, c, :], mg.t[:, c, 0:n], start=(c == 0), stop=(c == DC - 1))) for c in range(DC)], [w, mg], [pbb])
                          T.op(DVE, lambda: nc.vector.tensor_tensor(out=xo.t[:, oc, 0:n], in0=pb_[:, 0:n], in1=xt.t[:, oc, 0:n], op=ALU.add), [pbb, xt], [xo])
                      T.dma(POOL, xm.t.ap()[:, c0:c0 + n].rearrange("(c p) n -> p c n", p=128), xo.t[:, :, 0:n], "xo", [xo], [xm])
                      if meta:
                          T.op(DVE, lambda: nc.vector.tensor_copy(out=xh.t[:, :, 0], in_=xo.t[:, :, 15]), [xo], [xh])
                      else:
                          if hj + 1 < NT:
                              T.op(DVE, lambda: nc.vector.tensor_copy(out=xh.t[:, :, 2 * (hj + 1)], in_=xo.t[:, :, n - 1]), [xo], [xh])
                          else:
                              T.op(DVE, lambda: nc.vector.tensor_copy(out=hsb.t[:, 1, :], in_=xo.t[:, :, n - 1]), [xo], [hsb])
                          if hj >= 1:
                              T.op(DVE, lambda: nc.vector.tensor_copy(out=xh.t[:, :, 2 * (hj - 1) + 1], in_=xo.t[:, :, 0]), [xo], [xh])
                          else:
                              T.op(DVE, lambda: nc.vector.tensor_copy(out=hsb.t[:, 0, :], in_=xo.t[:, :, 0]), [xo], [hsb])
                  T.dma(POOL, hin[l].t.ap().rearrange("p (a c) -> p a c", a=2), hsb.t[:], "hsb", [hsb], [hin[l]])
              T.barrier()
              T.coll("AllGather", GRP, hin[l], hout[l], f"h{l}")
              ck('halo')
              with contextlib.ExitStack() as ph:
                  hb = sb(ph, "n_hb", [128, 4, 2, DC], F32)
                  T.dma(POOL, hb.t[:], hout[l].t.ap().rearrange("(r p) (a c) -> p r a c", p=128, a=2), "n_hb", [hout[l]], [hb])
                  jl, jr, jm = 0, 2 * (NT - 1) + 1, 2 * NT + 1
                  T.op(DVE, lambda: nc.vector.tensor_scalar(out=xh.t[:, :, jl], in0=xh.t[:, :, jl], scalar1=selb.t[:, 0:1], scalar2=None, op0=ALU.mult), [xh, selb], [xh])
                  for j in range(4):
                      T.op(DVE, lambda j=j: nc.vector.scalar_tensor_tensor(out=xh.t[:, :, jl], in0=hb.t[:, j, 1, :], scalar=selb.t[:, 1 + j:2 + j], in1=xh.t[:, :, jl],
                                                                          op0=ALU.mult, op1=ALU.add), [hb, selb, xh], [xh])
                      T.op(DVE, lambda j=j: nc.vector.scalar_tensor_tensor(out=xh.t[:, :, jr], in0=hb.t[:, j, 0, :], scalar=selb.t[:, 5 + j:6 + j], in1=xh.t[:, :, jr],
                                                                          op0=ALU.mult, op1=ALU.add), [hb, selb, xh], [xh])
                  T.op(DVE, lambda: nc.vector.tensor_copy(out=xh.t[:, :, jm], in_=hb.t[:, 0, 0, :]), [hb], [xh])
                  NH = cf.NH
                  sqh_ = sb(ph, "n_sqh", [128, DC, NH], BF16)
                  h2h = sb(ph, "n_h2h", [128, DC, NH], BF16)
                  rsx = sb(ph, "n_rsx", [128, NH], F32)
                  rstx = sb(ph, "n_rstx", [128, NH], F32)
                  ugh = sb(ph, "n_ugh", [128, FFC, NH], F32)
                  xt = sb(ph, "n_xt", [128, DC, TT], F32, key="xt")
                  sq = sb(ph, "n_sq", [128, DC, TT], BF16)
                  h2 = sb(ph, "n_h2", [128, DC, TT], BF16)
                  rs = sb(ph, "n_rs", [128, TT], F32)
                  rstd = sb(ph, "n_rstd", [128, TT], F32)
                  cc = sb(ph, "n_cc", [128, TT], F32)
                  sg = sb(ph, "n_sg", [128, TT], F32)
                  uT = sb(ph, "n_uT", [128, FFC, TT], BF16)
                  xo = xt
                  ring = WRing(ph, "n_w", DC, 4)
                  ringd = WRing(ph, "n_wd", FFC, 2)
                  rmsnorm(xh, NH, cf.V_GFFN, sqh_, h2h, rsx, rstx, bk(6), BK[6])
                  for fc in range(FFC):
                      w = ring.load(wbf[("wup", l)], fc, DC)
                      T.mm([(lambda c=c: nc.tensor.matmul(bk(fc % 2)[:, 0:NH], w.t[:, c, :], h2h.t[:, c, :], start=(c == 0), stop=(c == DC - 1))) for c in range(DC)], [w, h2h], [BK[fc % 2]])
                      T.op(ACT, lambda: nc.scalar.copy(out=ugh.t[:, fc, :], in_=bk(fc % 2)[:, 0:NH]), [BK[fc % 2]], [ugh])
                  wv = lambda fc, k: vec.t[:, cf.V_WCV + 3 * fc + k:cf.V_WCV + 3 * fc + k + 1]
                  for (c0, n, hj) in tiles:
                      meta = (n == 16)
                      T.dma(POOL, xt.t[:, :, 0:n], xm.t.ap()[:, c0:c0 + n].rearrange("(c p) n -> p c n", p=128), xt.name, [xm], [xt])
                      rmsnorm(xt, n, cf.V_GFFN, sq, h2, rs, rstd, bk(6), BK[6])
                      for fc in range(FFC):
                          wg_ = ring.load(wbf[("wup", l)], fc, DC)
                          wv_ = ring.load(wbf[("wup", l)], FFC + fc, DC)
                          pg, pgb = bk(2 * (fc % 2)), BK[2 * (fc % 2)]
                          pu, pub = bk(2 * (fc % 2) + 1), BK[2 * (fc % 2) + 1]
                          T.mm([(lambda c=c: nc.tensor.matmul(pg[:, 0:n], wg_.t[:, c, :], h2.t[:, c, 0:n], start=(c == 0), stop=(c == DC - 1))) for c in range(DC)], [wg_, h2], [pgb])
                          T.mm([(lambda c=c: nc.tensor.matmul(pu[:, 0:n], wv_.t[:, c, :], h2.t[:, c, 0:n], start=(c == 0), stop=(c == DC - 1))) for c in range(DC)], [wv_, h2], [pub])
                          bcol = vec.t[:, cf.V_BCV + fc:cf.V_BCV + fc + 1]
                          T.op(DVE, lambda: nc.vector.tensor_scalar(out=cc.t[:, 0:n], in0=pg[:, 0:n], scalar1=wv(fc, 1), scalar2=bcol, op0=ALU.mult, op1=ALU.add), [pgb, vec], [cc])
                          T.op(DVE, lambda: nc.vector.scalar_tensor_tensor(out=cc.t[:, 1:n], in0=pg[:, 0:n - 1], scalar=wv(fc, 0), in1=cc.t[:, 1:n], op0=ALU.mult, op1=ALU.add), [pgb, vec, cc], [cc])
                          T.op(DVE, lambda: nc.vector.scalar_tensor_tensor(out=cc.t[:, 0:n - 1], in0=pg[:, 1:n], scalar=wv(fc, 2), in1=cc.t[:, 0:n - 1], op0=ALU.mult, op1=ALU.add), [pgb, vec, cc], [cc])
                          T.op(DVE, lambda: nc.vector.scalar_tensor_tensor(out=cc.t[:, 0:1], in0=ugh.t[:, fc, 2 * hj:2 * hj + 1], scalar=wv(fc, 0), in1=cc.t[:, 0:1], op0=ALU.mult, op1=ALU.add), [ugh, vec, cc], [cc])
                          T.op(DVE, lambda: nc.vector.scalar_tensor_tensor(out=cc.t[:, n - 1:n], in0=ugh.t[:, fc, 2 * hj + 1:2 * hj + 2], scalar=wv(fc, 2), in1=cc.t[:, n - 1:n], op0=ALU.mult, op1=ALU.add), [ugh, vec, cc], [cc])
                          T.op(ACT, lambda: nc.scalar.activation(out=sg.t[:, 0:n], in_=cc.t[:, 0:n], func=AF.Silu), [cc], [sg])
                          T.op(DVE, lambda: nc.vector.tensor_tensor(out=uT.t[:, fc, 0:n], in0=sg.t[:, 0:n], in1=pu[:, 0:n], op=ALU.mult), [sg, pub], [uT])
                      for oc in range(DC):
                          w = ringd.load(wbf[("wdn", l)], oc, FFC)
                          pb_, pbb = bk(4 + oc % 2), BK[4 + oc % 2]
                          T.mm([(lambda c=c: nc.tensor.matmul(pb_[:, 0:n], w.t[:, c, :], uT.t[:, c, 0:n], start=(c == 0), stop=(c == FFC - 1))) for c in range(FFC)], [w, uT], [pbb])
                          T.op(DVE, lambda: nc.vector.tensor_tensor(out=xo.t[:, oc, 0:n], in0=pb_[:, 0:n], in1=xt.t[:, oc, 0:n], op=ALU.add), [pbb, xt], [xo])
                      if last and not meta:
                          T.dma(POOL, yT.ap()[:, c0 - 16:c0 - 16 + n].rearrange("(c p) n -> p c n", p=128), xo.t[:, :, 0:n], "xo", [xo], [B_yT])
                      elif not last:
                          T.dma(POOL, xa.t.ap()[:, c0:c0 + n].rearrange("(c p) n -> p c n", p=128), xo.t[:, :, 0:n], "xo", [xo], [xa])
              xh_st.close()
        except _Stop:
            pass
        T.dead = False
        T.barrier()
        if getattr(cf, 'endclear', False):
            fin_sem = stack.enter_context(nc.semaphore("s_fin"))
            for eng in (T.pe, T.act, T.dve, T.sp):
                eng.e.sem_inc(fin_sem, 1)
            nc.gpsimd.wait_ge(fin_sem, 4)
            allsems = [e.sem for e in T.engs] + [v[0] for v in T.dsems.values()] + [sv[0] for sv in T.csems] + [fin_sem]
            for sm in allsems:
                nc.gpsimd.sem_clear(sm)
    return nc, T.nsem


_CACHE = {}


def run(cf, inputs):
    in_maps = prep_inputs(cf, **inputs)
    if "nc" not in _CACHE or _CACHE.get("cf") is not cf:
        _CACHE["nc"], nsem = build(cf)
        _CACHE["cf"] = cf
    res = run_bass_kernel_spmd(_CACHE["nc"], in_maps, core_ids=list(range(NCORES)))
    out = np.empty((2, cf.SEQ, cf.D), np.float32)
    for c in range(NCORES):
        b, r = c // 4, c % 4
        out[b, r * cf.TPC:(r + 1) * cf.TPC, :] = res.results[c]["yT"].T
    return out


def kernel(**inputs):
    return run(FULL, inputs)
```

```python
import contextlib
import math
import numpy as np
import concourse.bass as bass
import concourse.mybir as mybir
from concourse.bass_utils import run_bass_kernel_spmd

F32, BF16 = mybir.dt.float32, mybir.dt.bfloat16
AF = mybir.ActivationFunctionType
ALU = mybir.AluOpType
NCORES = 8
GRP = [[0, 1, 2, 3], [4, 5, 6, 7]]
ALL8 = [list(range(8))]


def nchunks_for(n, unit_bytes, limit=1 << 20):
    for k in range(1, n + 1):
        if n % k == 0 and (n // k) * unit_bytes <= limit:
            return k
    raise ValueError


class Cfg:
    def __init__(self, D=2048, SEQ=16384, H=8, KVH=2, FG=8, DFF=5632, TT=512, N1=100, N2=164, DEPTH=2):
        self.D, self.SEQ, self.H, self.KVH, self.FG, self.DFF, self.TT = D, SEQ, H, KVH, FG, DFF, TT
        self.N1, self.N2, self.DEPTH = N1, N2, DEPTH
        self.HD = 128
        self.G = H // KVH
        self.AW = H * 128
        self.KVW = KVH * 128
        self.FW = FG * 128
        self.NMETA = 16
        self.GRIDW = 64
        self.L = SEQ + 16
        assert N1 * N2 == self.L
        self.TPC = SEQ // 4
        self.NT = self.TPC // TT
        self.NCOL = 16 + self.TPC
        self.DC = D // 128
        self.FFC = DFF // 128
        self.FCH = self.FW // 4
        self.FCB = self.FCH // 128
        self.OFF_K = self.AW
        self.OFF_V = self.AW + self.KVW
        self.OFF_F = self.OFF_V + self.KVW
        self.OFF_GA = self.OFF_F + self.FW
        self.INW = self.OFF_GA + 2 * D
        self.EPS = 1e-6
        self.N2C = N2 // 2
        assert self.N2C * 2 == N2 and self.N2C <= 128 and 2 * N2 <= 512 and N1 <= 128
        self.P2B = 512 // self.FCH
        assert N2 % self.P2B == 0
        self.NH = 2 * (self.NT + 1)
        MB = 1 << 20
        self.NFC = nchunks_for(self.TPC, self.FW * 2)
        self.NPC = nchunks_for(self.L, 2 * self.FCH * 2)
        self.LC = self.L // self.NPC
        assert 128 * self.TPC * 2 <= MB
        o = 0
        self.V_GMIX = o; o += self.DC
        self.V_GFFN = o; o += self.DC
        self.V_BG = o; o += 2 * self.DC
        self.V_QN = o; o += 1
        self.V_KN = o; o += 1
        self.V_WCV = o; o += 3 * self.FFC
        self.V_BCV = o; o += self.FFC
        self.NV = o


FULL = Cfg()


def lhsT_layout(W):
    K, N = W.shape
    return np.ascontiguousarray(W.reshape(K // 128, 128, N // 128, 128).transpose(2, 1, 0, 3).reshape(N, K))


def rhs_layout(W):
    K, N = W.shape
    return np.ascontiguousarray(W.reshape(K // 128, 128, N).transpose(1, 0, 2).reshape(128, (K // 128) * N))


WNAMES = ["wqk", "wg", "wvf", "wab", "wfr", "wout", "wup", "wdn"]


def weight_shapes(cf):
    return {
        "wqk": (cf.AW + cf.KVW, cf.D), "wg": (2 * cf.D, cf.D), "wvf": (128, cf.DC * (cf.KVW + cf.FW)),
        "wab": (cf.D, cf.AW), "wfr": (128, cf.FG * cf.D), "wout": (cf.D, cf.D),
        "wup": (2 * cf.DFF, cf.D), "wdn": (cf.D, cf.DFF),
    }


def weight_chunks(cf):
    out = {}
    for n, (R, C) in weight_shapes(cf).items():
        out[n] = nchunks_for(R // 4, C * 2)
    return out


def host_tables(cf):
    f64 = np.float64
    tabs = {}
    ident = np.eye(128, dtype=np.float32)
    rotT = np.zeros((128, 128), np.float32)
    for i in range(128):
        blk = i // 32
        if blk % 2 == 0:
            rotT[i + 32, i] = -1.0
        else:
            rotT[i - 32, i] = 1.0
    N1, N2, L = cf.N1, cf.N2, cf.L
    a1 = 2 * np.pi * np.outer(np.arange(N1), np.arange(N1)).astype(f64) / N1
    t1 = np.concatenate([np.cos(a1), np.sin(a1)], 1).astype(np.float32)
    a2 = 2 * np.pi * np.outer(np.arange(N2), np.arange(N2)).astype(f64) / N2
    C2, S2 = np.cos(a2), np.sin(a2)
    tabR = np.concatenate([C2, S2], 1)
    tabI = np.concatenate([-S2, C2], 1)
    t3 = np.stack([tabR, tabI], 0).reshape(2, 2, cf.N2C, 2 * N2).astype(np.float32)
    at = 2 * np.pi * np.outer(np.arange(N1), np.arange(N2)).astype(f64) / L
    tw = np.stack([np.cos(at), np.sin(at)], 1).astype(np.float32)
    ac = 2 * np.pi * np.outer(np.arange(128), np.arange(128)).astype(f64) / 128
    sc = 1.0 / math.sqrt(L * 128.0)
    cs = np.stack([np.cos(ac) * sc, -np.sin(ac) * sc], 1).astype(np.float32)
    tabs["cmat"] = np.ascontiguousarray(np.concatenate([ident, rotT, cs.reshape(128, 256)], 1))
    tabs["t1"] = t1
    tabs["t3"] = np.ascontiguousarray(t3.transpose(2, 0, 1, 3).reshape(cf.N2C, 4 * 2 * N2))
    tabs["tw"] = np.ascontiguousarray(tw.reshape(N1, 2 * N2))
    return tabs


def core_tables(cf, r):
    inv = 1.0 / (10000.0 ** (np.arange(32, dtype=np.float64) / 32))
    tg = r * cf.TPC + np.arange(cf.TPC)
    rows = (tg // cf.GRIDW).astype(np.float64)
    cols = (tg % cf.GRIDW).astype(np.float64)
    ang = np.zeros((128, cf.NCOL), np.float64)
    ang[0:32, 16:] = inv[:, None] * rows[None]
    ang[32:64, 16:] = inv[:, None] * rows[None]
    ang[64:96, 16:] = inv[:, None] * cols[None]
    ang[96:128, 16:] = inv[:, None] * cols[None]
    rope = np.stack([np.cos(ang), np.sin(ang)], 1).astype(np.float32)
    sel = np.zeros((128, 16), np.float32)
    sel[:, 0] = 1.0 if r == 0 else 0.0
    for j in range(4):
        sel[:, 1 + j] = 1.0 if j == r - 1 else 0.0
        sel[:, 5 + j] = 1.0 if j == r + 1 else 0.0
        sel[:, 9 + j] = 1.0 if j == r else 0.0
    return np.ascontiguousarray(rope.reshape(128, 2 * cf.NCOL)), sel


def prep_inputs(cf, x, meta_tokens, norm_mix, norm_ffn, w_in, b_gate, q_norm, k_norm,
                w_attn_br, w_four, w_out, w_up, w_conv, b_conv, w_down):
    f = lambda a: np.asarray(a, dtype=np.float32)
    x, meta_tokens = f(x), f(meta_tokens)
    w_in, w_attn_br, w_four, w_out, w_up, w_down = map(f, (w_in, w_attn_br, w_four, w_out, w_up, w_down))
    DEPTH = cf.DEPTH
    full = {n: [] for n in WNAMES}
    for l in range(DEPTH):
        full["wqk"].append(lhsT_layout(w_in[l][:, 0:cf.OFF_V]))
        full["wg"].append(lhsT_layout(w_in[l][:, cf.OFF_GA:]))
        full["wvf"].append(rhs_layout(w_in[l][:, cf.OFF_V:cf.OFF_GA]))
        full["wab"].append(lhsT_layout(w_attn_br[l]))
        full["wfr"].append(rhs_layout(w_four[l]))
        full["wout"].append(lhsT_layout(w_out[l]))
        full["wup"].append(lhsT_layout(w_up[l]))
        full["wdn"].append(lhsT_layout(w_down[l]))
    vecs = np.zeros((DEPTH, 128, cf.NV), np.float32)
    for l in range(DEPTH):
        vecs[l, :, cf.V_GMIX:cf.V_GMIX + cf.DC] = f(norm_mix[l]).reshape(cf.DC, 128).T
        vecs[l, :, cf.V_GFFN:cf.V_GFFN + cf.DC] = f(norm_ffn[l]).reshape(cf.DC, 128).T
        vecs[l, :, cf.V_BG:cf.V_BG + 2 * cf.DC] = f(b_gate[l]).reshape(2 * cf.DC, 128).T
        vecs[l, :, cf.V_QN] = f(q_norm[l])
        vecs[l, :, cf.V_KN] = f(k_norm[l])
        wc = f(w_conv[l]).reshape(3, cf.FFC, 128).transpose(2, 1, 0)
        vecs[l, :, cf.V_WCV:cf.V_WCV + 3 * cf.FFC] = wc.reshape(128, 3 * cf.FFC)
        vecs[l, :, cf.V_BCV:cf.V_BCV + cf.FFC] = f(b_conv[l]).reshape(cf.FFC, 128).T
    tabs = host_tables(cf)
    wch = weight_chunks(cf)
    metaT = np.ascontiguousarray(meta_tokens.T)
    in_maps = []
    for c in range(NCORES):
        b, r = c // 4, c % 4
        m = {}
        m["xT"] = np.ascontiguousarray(x[b, r * cf.TPC:(r + 1) * cf.TPC, :].T)
        m["metaT"] = metaT
        for n in WNAMES:
            R = full[n][0].shape[0]
            nch = wch[n]
            rc = R // 4 // nch
            m[n] = np.ascontiguousarray(np.stack(
                [full[n][l].reshape(nch, 4, rc, -1)[:, r].reshape(nch * rc, -1) for l in range(DEPTH)], 0))
        m["vecs"] = vecs
        rope, sel = core_tables(cf, r)
        m["rope"] = rope
        m["sel"] = sel
        for k, v in tabs.items():
            m[k] = v
        in_maps.append(m)
    return in_maps


class _Stop(Exception):
    pass


class Buf:
    def __init__(self, name, t):
        self.name, self.t = name, t
        self.w, self.r = {}, {}


class Eng:
    def __init__(self, name, e, sem):
        self.name, self.e, self.sem = name, e, sem
        self.count = 0
        self.waited = {}


def _merge(d, src):
    for k, (s, v) in src.items():
        if k not in d or d[k][1] < v:
            d[k] = (s, v)


class TR:
    def __init__(self, nc, stack):
        self.nc, self.stack = nc, stack
        mk = lambda n: stack.enter_context(nc.semaphore(n))
        self.pe = Eng("pe", nc.tensor, mk("s_pe"))
        self.act = Eng("act", nc.scalar, mk("s_act"))
        self.dve = Eng("dve", nc.vector, mk("s_dve"))
        self.pool = Eng("pool", nc.gpsimd, mk("s_pool"))
        self.sp = Eng("sp", nc.sync, mk("s_sp"))
        self.engs = [self.pe, self.act, self.dve, self.pool, self.sp]
        self.dsems = {}
        self.csems = []
        self.shsem = {}
        self.nsem = 5
        self.dead = False

    def _sync(self, eng, reads, writes, ignore=None):
        raw = {}
        for b in reads:
            _merge(raw, b.w)
        oth = {}
        for b in writes:
            _merge(oth, b.w)
            _merge(oth, b.r)
        me = id(eng.sem)
        d = dict(raw)
        for k, sv in oth.items():
            if k == me:
                continue
            if k not in d or d[k][1] < sv[1]:
                d[k] = sv
        if eng is self.pe:
            d.pop(me, None)
        if ignore is not None:
            d.pop(ignore, None)
        for k, (s, v) in d.items():
            if eng.waited.get(k, 0) < v:
                eng.e.wait_ge(s, v)
                eng.waited[k] = v

    def _rec(self, ev, reads, writes):
        k, s, v = ev
        for b in reads:
            if k not in b.r or b.r[k][1] < v:
                b.r[k] = (s, v)
        for b in writes:
            if k not in b.w or b.w[k][1] < v:
                b.w[k] = (s, v)

    def op(self, eng, fn, reads=(), writes=()):
        if self.dead:
            return
        self._sync(eng, reads, writes)
        ins = fn()
        eng.count += 1
        ins.then_inc(eng.sem, 1)
        self._rec((id(eng.sem), eng.sem, eng.count), reads, writes)

    def mm(self, fns, reads=(), writes=()):
        if self.dead:
            return
        eng = self.pe
        self._sync(eng, reads, writes)
        ins = None
        for fn in fns:
            ins = fn()
        eng.count += 1
        ins.then_inc(eng.sem, 1)
        self._rec((id(eng.sem), eng.sem, eng.count), reads, writes)

    def dma(self, q, out, in_, key, reads=(), writes=()):
        if self.dead:
            return
        self._sync(q, reads, writes)
        if key not in self.dsems:
            self.nsem += 1
            self.dsems[key] = [self.stack.enter_context(self.nc.semaphore("d_" + key)), 0]
        ent = self.dsems[key]
        ent[1] += 1
        q.e.dma_start(out=out, in_=in_).then_inc(ent[0], 16)
        self._rec((id(ent[0]), ent[0], 16 * ent[1]), reads, writes)

    def coll(self, kind, groups, inb, outb, name, shared=None, in_ap=None, out_ap=None):
        if self.dead:
            return
        q = self.pool
        if shared is not None and shared[0] in self.shsem:
            self._sync(q, [inb], [outb], ignore=id(self.shsem[shared[0]]))
        else:
            self._sync(q, [inb], [outb])
        if shared is None:
            self.nsem += 1
            sem = self.stack.enter_context(self.nc.semaphore("c_" + name))
            val = 1
            self.csems.append((sem, 1))
        else:
            key, val = shared
            if key not in self.shsem:
                self.nsem += 1
                self.shsem[key] = self.stack.enter_context(self.nc.semaphore("c_" + key))
                self.csems.append((self.shsem[key], val))
            sem = self.shsem[key]
        q.e.collective_compute(kind, ALU.bypass, replica_groups=groups,
                               ins=[(in_ap if in_ap is not None else inb.t.ap()).opt()],
                               outs=[(out_ap if out_ap is not None else outb.t.ap()).opt()]).then_inc(sem, 1)
        self._rec((id(sem), sem, val), [inb], [outb])

    def barrier(self, full=False):
        if self.dead:
            return
        evs = [(id(e.sem), e.sem, e.count) for e in self.engs if e.count > 0]
        evs += [(id(s), s, 16 * c) for (s, c) in self.dsems.values() if c > 0]
        if full:
            evs += [(id(s), s, v) for (s, v) in self.csems]
        for eng in self.engs:
            for k, s, v in evs:
                if k == id(eng.sem):
                    continue
                if eng.waited.get(k, 0) < v:
                    eng.e.wait_ge(s, v)
                    eng.waited[k] = v


def build(cf, final_wait=True):
    nc = bass.Bass("TRN2", target_bir_lowering=False)
    D, DC, TT, NT, NCOL, TPC, H, KVH, G = cf.D, cf.DC, cf.TT, cf.NT, cf.NCOL, cf.TPC, cf.H, cf.KVH, cf.G
    FW, FCH, FCB, FFC, KVW, AW, L, N1, N2, N2C = cf.FW, cf.FCH, cf.FCB, cf.FFC, cf.KVW, cf.AW, cf.L, cf.N1, cf.N2, cf.N2C
    DEPTH = cf.DEPTH
    NQK = H + KVH
    VFW = KVW + FW
    PQC = 2 * FW // 128
    wsh = weight_shapes(cf)

    def din(name, shape, dt=F32):
        return nc.dram_tensor(name, list(shape), dt, kind="ExternalInput")

    xT_in = din("xT", [D, TPC])
    metaT_in = din("metaT", [D, 16])
    w_in_sh = {n: din(n, [DEPTH, wsh[n][0] // 4, wsh[n][1]]) for n in WNAMES}
    vecs_in = din("vecs", [DEPTH, 128, cf.NV])
    rope_in = din("rope", [128, 2 * NCOL])
    sel_in = din("sel", [128, 16])
    cmat_in = din("cmat", [128, 512])
    t1_in = din("t1", [N1, 2 * N1])
    t3_in = din("t3", [N2C, 8 * N2])
    tw_in = din("tw", [N1, 2 * N2])
    yT = nc.dram_tensor("yT", [D, TPC], F32, kind="ExternalOutput")

    stack = contextlib.ExitStack()
    with stack:
        stack.enter_context(nc.allow_non_contiguous_dma(reason="small strided scratch transfers"))
        T = TR(nc, stack)
        blk = stack.enter_context(nc.Block())
        PE, ACT, DVE, POOL, SP = T.pe, T.act, T.dve, T.pool, T.sp

        def dram(name, shape, dt):
            return Buf(name, nc.dram_tensor(name, list(shape), dt))

        def ext(tn, name):
            return Buf(name, tn)

        B_xT, B_metaT, B_yT = ext(xT_in, "xT"), ext(metaT_in, "metaT"), ext(yT, "yT")
        B_vecs, B_rope, B_sel, B_cmat = ext(vecs_in, "vecs"), ext(rope_in, "rope"), ext(sel_in, "sel"), ext(cmat_in, "cmat")
        B_t1, B_t3, B_tw = ext(t1_in, "t1"), ext(t3_in, "t3"), ext(tw_in, "tw")
        B_wsh = {n: ext(w_in_sh[n], n) for n in WNAMES}

        wbs = {(n, l): dram(f"wbs_{n}{l}", [wsh[n][0] // 4, wsh[n][1]], BF16) for n in WNAMES for l in range(DEPTH)}
        wbf = {(n, l): dram(f"wbf_{n}{l}", [wsh[n][0], wsh[n][1]], BF16) for n in WNAMES for l in range(DEPTH)}
        wcs_d = [dram(f"wcs{l}", [D, 2 * FW], BF16) for l in range(DEPTH)]
        xa = dram("xa", [D, NCOL], F32)
        xm = dram("xm", [D, NCOL], F32)
        qT_d = dram("qT", [AW, NCOL], BF16)
        gT_d = dram("gT", [2 * D, NCOL], BF16)
        aT_d = dram("aT", [AW, NCOL], BF16)
        kin = [dram(f"kin{l}", [KVW, TPC], BF16) for l in range(DEPTH)]
        kout = [dram(f"kout{l}", [4 * KVW, TPC], BF16) for l in range(DEPTH)]
        kmeta = dram("kmeta", [KVW, 16], BF16)
        vin = [dram(f"vin{l}", [KVH * TPC, 128], BF16) for l in range(DEPTH)]
        vout = [dram(f"vout{l}", [KVH * 4 * TPC, 128], BF16) for l in range(DEPTH)]
        vmeta = dram("vmeta", [16, KVW], BF16)
        fin = [dram(f"fin{l}", [TPC, FW], BF16) for l in range(DEPTH)]
        fout = [dram(f"fout{l}", [4 * TPC, FW], BF16) for l in range(DEPTH)]
        fmeta = dram("fmeta", [16, FW], BF16)
        yd = dram("yd", [2, N1, N2, FCH], BF16)
        fpos = dram("fpos", [L, FW], BF16)
        pqin = [dram(f"pqin{l}", [cf.NPC * 2 * FCH, cf.LC], BF16) for l in range(DEPTH)]
        pqout = [dram(f"pqout{l}", [cf.NPC * 4 * 2 * FCH, cf.LC], BF16) for l in range(DEPTH)]
        hin = [dram(f"hin{l}", [128, 2 * DC], F32) for l in range(DEPTH)]
        hout = [dram(f"hout{l}", [4 * 128, 2 * DC], F32) for l in range(DEPTH)]

        ps0 = stack.enter_context(nc.psum_tensor("ps0", [128, 1024], F32))
        ps1 = stack.enter_context(nc.psum_tensor("ps1", [128, 1024], F32))
        ps4 = stack.enter_context(nc.psum_tensor("ps4", [128, 512], F32))
        ps5 = stack.enter_context(nc.psum_tensor("ps5", [128, 512], F32))
        ps6 = stack.enter_context(nc.psum_tensor("ps6", [128, 512], F32))
        pst = stack.enter_context(nc.psum_tensor("pst", [128, 1024], BF16))
        BK = [Buf(f"bk{i}", None) for i in range(8)]
        bank_aps = [ps0[:, 0:512], ps0[:, 512:1024], ps1[:, 0:512], ps1[:, 512:1024], ps4[:, :], ps5[:, :], ps6[:, :]]

        def bk(i):
            return bank_aps[i]

        uniq = [0]

        def sb(st, name, shape, dt, key=None):
            uniq[0] += 1
            return Buf(key or name, st.enter_context(nc.sbuf_tensor(f"{name}_{uniq[0]}", list(shape), dt)))

        ident = sb(stack, "ident", [128, 128], BF16)
        rotT = sb(stack, "rotT", [128, 128], BF16)
        csm = sb(stack, "csm", [128, 2, 128], BF16)
        ones = sb(stack, "ones", [128, 128], BF16)
        selb = sb(stack, "selb", [128, 16], F32)
        vec = sb(stack, "vec", [128, cf.NV], F32)

        T.dma(POOL, ident.t[:], cmat_in.ap()[:, 0:128], "c0", [B_cmat], [ident])
        T.dma(POOL, rotT.t[:], cmat_in.ap()[:, 128:256], "c0", [B_cmat], [rotT])
        T.dma(POOL, csm.t[:], cmat_in.ap()[:, 256:512].rearrange("p (a b) -> p a b", a=2), "c0", [B_cmat], [csm])
        T.dma(POOL, selb.t[:], sel_in.ap(), "c0", [B_sel], [selb])
        T.op(DVE, lambda: nc.vector.memset(ones.t[:], 1.0), [], [ones])

        T.dma(POOL, xa.t.ap()[:, 0:16], metaT_in.ap(), "xinit", [B_metaT], [xa])
        T.dma(POOL, xa.t.ap()[:, 16:NCOL], xT_in.ap(), "xinit", [B_xT], [xa])

        def ag_chunks(inb, outb, rows_in, nch, key, total=None):
            rc = rows_in // nch
            for j in range(nch):
                if getattr(cf, 'unshare', False) and total is None:
                    T.coll("AllGather", GRP, inb, outb, f"{key}_{j}",
                           in_ap=inb.t.ap()[j * rc:(j + 1) * rc, :], out_ap=outb.t.ap()[j * 4 * rc:(j + 1) * 4 * rc, :])
                else:
                    T.coll("AllGather", GRP, inb, outb, key, shared=(key, total or nch),
                           in_ap=inb.t.ap()[j * rc:(j + 1) * rc, :], out_ap=outb.t.ap()[j * 4 * rc:(j + 1) * 4 * rc, :])

        wch = weight_chunks(cf)
        wtot = sum(wch.values())
        def weight_ags(l):
            for n in WNAMES:
                ag_chunks(wbs[(n, l)], wbf[(n, l)], wsh[n][0] // 4, wch[n], f"w{l}", wtot)

        for l in range(DEPTH):
            for n in WNAMES:
                T.dma(POOL, wbs[(n, l)].t.ap(), w_in_sh[n].ap()[l], "wcast", [B_wsh[n]], [wbs[(n, l)]])
            if l == 0:
                weight_ags(0)

        tiles = [(16 + i * TT, TT, i) for i in range(NT)] + [(0, 16, NT)]

        class WRing:
            def __init__(self, st, name, kcmax, nslots):
                self.slots = [sb(st, f"{name}{i}", [128, kcmax, 128], BF16, key=f"{name[2:]}{i}") for i in range(nslots)]
                self.i = 0

            def load(self, wb, chunk, kc):
                s = self.slots[self.i % len(self.slots)]
                self.i += 1
                src = wb.t.ap()[chunk * 128:(chunk + 1) * 128, :].rearrange("p (k n) -> p k n", n=128)
                T.dma(SP, s.t[:, 0:kc, :], src, s.name, [wb], [s])
                return s

        def rmsnorm(xt, n, gcol, sq, hT, rs, rstd, pbank, pbuf):
            T.op(ACT, lambda: nc.scalar.activation(out=sq.t[:, :, 0:n], in_=xt.t[:, :, 0:n], func=AF.Square), [xt], [sq])
            T.mm([(lambda c=c: nc.tensor.matmul(pbank[:, 0:n], ones.t[:, :], sq.t[:, c, 0:n], start=(c == 0), stop=(c == DC - 1)))
                  for c in range(DC)], [ones, sq], [pbuf])
            T.op(ACT, lambda: nc.scalar.activation(out=rs.t[:, 0:n], in_=pbank[:, 0:n], func=AF.Sqrt, bias=float(cf.EPS), scale=1.0 / D), [pbuf], [rs])
            T.op(DVE, lambda: nc.vector.reciprocal(out=rstd.t[:, 0:n], in_=rs.t[:, 0:n]), [rs], [rstd])
            for c in range(DC):
                T.op(DVE, lambda c=c: nc.vector.scalar_tensor_tensor(out=hT.t[:, c, 0:n], in0=xt.t[:, c, 0:n], scalar=vec.t[:, gcol + c:gcol + c + 1],
                                                                    in1=rstd.t[:, 0:n], op0=ALU.mult, op1=ALU.mult), [xt, vec, rstd], [hT])

        def ck(name):
            if getattr(cf, 'stop', None) == name:
                if getattr(cf, 'exc', False):
                    raise _Stop()
                T.barrier(full=True)
                T.dead = True

        try:
          for l in range(DEPTH):
              last = (l == DEPTH - 1)
              T.barrier()
              ck('pro')
              T.dma(POOL, vec.t[:], vecs_in.ap()[l], "c_vec", [B_vecs], [vec])

              with contextlib.ExitStack() as ph:
                  xt = sb(ph, "p1_xt", [128, DC, TT], F32, key="xt")
                  sq = sb(ph, "p1_sq", [128, DC, TT], BF16)
                  hT = sb(ph, "p1_hT", [128, DC, TT], BF16)
                  rs = sb(ph, "p1_rs", [128, TT], F32)
                  rstd = sb(ph, "p1_rstd", [128, TT], F32)
                  wvf = sb(ph, "p1_wvf", [128, DC, VFW], BF16)
                  ring = WRing(ph, "p1_w", DC, 4)
                  rope = sb(ph, "p1_rope", [128, 2, TT], F32)
                  sqh = sb(ph, "p1_sqh", [128, TT], BF16)
                  qg = sb(ph, "p1_qg", [128, TT], BF16)
                  rsh = sb(ph, "p1_rsh", [128, TT], F32)
                  rstdh = sb(ph, "p1_rstdh", [128, TT], F32)
                  t1b = sb(ph, "p1_t1", [128, TT], F32)
                  t2b = sb(ph, "p1_t2", [128, TT], F32)
                  qo = [sb(ph, f"p1_qo{i}", [128, TT], BF16) for i in range(2)]
                  vo = [sb(ph, f"p1_vo{i}", [128, 512], BF16) for i in range(2)]
                  go = [sb(ph, f"p1_go{i}", [128, 4, TT], BF16) for i in range(2)]
                  T.dma(SP, wvf.t[:], wbf[("wvf", l)].t.ap().rearrange("p (k n) -> p k n", n=VFW), "p1_wvf", [wbf[("wvf", l)]], [wvf])
                  nq = nv = ng = 0
                  frc = TPC // cf.NFC
                  f_ag = [0]
                  f_cp = [0]

                  def f_copy(j):
                      for rr in range(4):
                          p0 = 16 + rr * TPC + j * frc
                          T.dma(POOL, fpos.t.ap()[p0:p0 + frc, :], fout[l].t.ap()[(j * 4 + rr) * frc:(j * 4 + rr + 1) * frc, :], "fpos_a", [fout[l]], [fpos])

                  def f_progress(done_tokens, flush=False):
                      while f_ag[0] < cf.NFC and (f_ag[0] + 1) * frc <= done_tokens:
                          j = f_ag[0]
                          T.coll("AllGather", GRP, fin[l], fout[l], f"f{l}", shared=(f"f{l}", cf.NFC),
                                 in_ap=fin[l].t.ap()[j * frc:(j + 1) * frc, :], out_ap=fout[l].t.ap()[j * 4 * frc:(j + 1) * 4 * frc, :])
                          f_ag[0] += 1
                          while f_cp[0] < f_ag[0] - 1:
                              f_copy(f_cp[0])
                              f_cp[0] += 1
                      if flush:
                          while f_cp[0] < f_ag[0]:
                              f_copy(f_cp[0])
                              f_cp[0] += 1

                  for (c0, n, _) in tiles:
                      meta = (n == 16)
                      T.dma(POOL, xt.t[:, :, 0:n], xa.t.ap()[:, c0:c0 + n].rearrange("(c p) n -> p c n", p=128), xt.name, [xa], [xt])
                      T.dma(POOL, rope.t[:, :, 0:n], rope_in.ap().rearrange("p (a n) -> p a n", a=2)[:, :, c0:c0 + n], "p1_rope", [B_rope], [rope])
                      ck('p1x')
                      rmsnorm(xt, n, cf.V_GMIX, sq, hT, rs, rstd, bk(6), BK[6])
                      ck('p1a')
                      def head_mm(hd):
                          w = ring.load(wbf[("wqk", l)], hd, DC)
                          pa, pab = bk(hd % 2), BK[hd % 2]
                          T.mm([(lambda c=c: nc.tensor.matmul(pa[:, 0:n], w.t[:, c, :], hT.t[:, c, 0:n], start=(c == 0), stop=(c == DC - 1)))
                                for c in range(DC)], [w, hT], [pab])

                      def head_p1(hd):
                          pa, pab = bk(hd % 2), BK[hd % 2]
                          gcol = cf.V_QN if hd < H else cf.V_KN
                          T.op(ACT, lambda: nc.scalar.activation(out=sqh.t[:, 0:n], in_=pa[:, 0:n], func=AF.Square), [pab], [sqh])
                          T.op(ACT, lambda: nc.scalar.activation(out=qg.t[:, 0:n], in_=pa[:, 0:n], func=AF.Copy, scale=vec.t[:, gcol:gcol + 1]), [pab, vec], [qg])

                      def head_p2(hd, q_o):
                          T.mm([lambda: nc.tensor.matmul(bk(4)[:, 0:n], ones.t[:, :], sqh.t[:, 0:n], start=True, stop=True)], [ones, sqh], [BK[4]])
                          T.mm([lambda: nc.tensor.matmul(bk(5)[:, 0:n], rotT.t[:, :], qg.t[:, 0:n], start=True, stop=True)], [rotT, qg], [BK[5]])
                          T.op(ACT, lambda: nc.scalar.activation(out=rsh.t[:, 0:n], in_=bk(4)[:, 0:n], func=AF.Sqrt, bias=float(cf.EPS), scale=1.0 / 128), [BK[4]], [rsh])
                          T.op(DVE, lambda: nc.vector.reciprocal(out=rstdh.t[:, 0:n], in_=rsh.t[:, 0:n]), [rsh], [rstdh])
                          T.op(DVE, lambda: nc.vector.tensor_tensor(out=t1b.t[:, 0:n], in0=qg.t[:, 0:n], in1=rope.t[:, 0, 0:n], op=ALU.mult), [qg, rope], [t1b])
                          T.op(DVE, lambda: nc.vector.tensor_tensor(out=t2b.t[:, 0:n], in0=bk(5)[:, 0:n], in1=rope.t[:, 1, 0:n], op=ALU.mult), [BK[5], rope], [t2b])
                          T.op(DVE, lambda: nc.vector.tensor_tensor(out=t1b.t[:, 0:n], in0=t1b.t[:, 0:n], in1=t2b.t[:, 0:n], op=ALU.add), [t1b, t2b], [t1b])
                          T.op(DVE, lambda: nc.vector.tensor_tensor(out=q_o.t[:, 0:n], in0=t1b.t[:, 0:n], in1=rstdh.t[:, 0:n], op=ALU.mult), [t1b, rstdh], [q_o])
                          if hd < H:
                              T.dma(POOL, qT_d.t.ap()[hd * 128:(hd + 1) * 128, c0:c0 + n], q_o.t[:, 0:n], q_o.name, [q_o], [qT_d])
                          else:
                              kh = hd - H
                              if meta:
                                  T.dma(POOL, kmeta.t.ap()[kh * 128:(kh + 1) * 128, :], q_o.t[:, 0:n], q_o.name, [q_o], [kmeta])
                              else:
                                  T.dma(POOL, kin[l].t.ap()[kh * 128:(kh + 1) * 128, c0 - 16:c0 - 16 + n], q_o.t[:, 0:n], q_o.name, [q_o], [kin[l]])

                      head_mm(0)
                      head_p1(0)
                      for hd in range(1, NQK):
                          head_mm(hd)
                          head_p2(hd - 1, qo[nq % 2])
                          nq += 1
                          head_p1(hd)
                      head_p2(NQK - 1, qo[nq % 2])
                      nq += 1
                      ck('p1b')
                      ntb = max(1, n // 128)
                      for tb in range(ntb):
                          tn = min(128, n)
                          cb0 = 0
                          while cb0 < VFW:
                              if cb0 < KVW:
                                  cw = KVW
                              else:
                                  cw = min(512, VFW - cb0)
                              pv, pvb = bk(2 + nv % 2), BK[2 + nv % 2]
                              T.mm([(lambda c=c: nc.tensor.matmul(pv[0:tn, 0:cw], hT.t[:, c, tb * 128:tb * 128 + tn], wvf.t[:, c, cb0:cb0 + cw],
                                                                  start=(c == 0), stop=(c == DC - 1))) for c in range(DC)], [hT, wvf], [pvb])
                              v_o = vo[nv % 2]
                              nv += 1
                              T.op(ACT, lambda: nc.scalar.copy(out=v_o.t[0:tn, 0:cw], in_=pv[0:tn, 0:cw]), [pvb], [v_o])
                              if cb0 < KVW:
                                  dst, dm = (vmeta, vmeta.t.ap()[:, :]) if meta else (
                                      vin[l], vin[l].t.ap().rearrange("(h t) d -> t h d", h=KVH)[c0 - 16 + tb * 128:c0 - 16 + tb * 128 + tn, :, :])
                              else:
                                  f0 = cb0 - KVW
                                  dst, dm = (fmeta, fmeta.t.ap()[:, f0:f0 + cw]) if meta else (fin[l], fin[l].t.ap()[c0 - 16 + tb * 128:c0 - 16 + tb * 128 + tn, f0:f0 + cw])
                              srcv = v_o.t[0:tn, 0:cw]
                              if cb0 < KVW and not meta:
                                  srcv = srcv.rearrange("p (h d) -> p h d", h=KVH)
                              T.dma(POOL, dm, srcv, v_o.name, [v_o], [dst])
                              cb0 += cw
                      ck('p1c')
                      for gc in range(2 * DC):
                          w = ring.load(wbf[("wg", l)], gc, DC)
                          pa, pab = bk(gc % 2), BK[gc % 2]
                          T.mm([(lambda c=c: nc.tensor.matmul(pa[:, 0:n], w.t[:, c, :], hT.t[:, c, 0:n], start=(c == 0), stop=(c == DC - 1)))
                                for c in range(DC)], [w, hT], [pab])
                          g_o = go[(ng // 4) % 2]
                          T.op(ACT, lambda: nc.scalar.activation(out=g_o.t[:, gc % 4, 0:n], in_=pa[:, 0:n], func=AF.Sigmoid,
                                                                 bias=vec.t[:, cf.V_BG + gc:cf.V_BG + gc + 1], scale=1.0), [pab, vec], [g_o])
                          ng += 1
                          if gc % 4 == 3:
                              g4 = gc // 4
                              T.dma(POOL, gT_d.t.ap()[g4 * 512:(g4 + 1) * 512, c0:c0 + n].rearrange("(a p) n -> p a n", p=128),
                                    g_o.t[:, :, 0:n], g_o.name, [g_o], [gT_d])
              T.barrier()
              ck('p1')
              ag_chunks(kin[l], kout[l], KVW, KVH, f"k{l}")
              ck('agk')
              ag_chunks(vin[l], vout[l], KVH * TPC, KVH, f"v{l}")
              ck('agv')
              ag_chunks(fin[l], fout[l], TPC, cf.NFC, f"f{l}")
              ck('ag1')

              P2B = cf.P2B
              with contextlib.ExitStack() as ph:
                  T.dma(POOL, fpos.t.ap()[0:16, :], fmeta.t.ap(), "fpos_a", [fmeta], [fpos])
                  frc_ = TPC // cf.NFC
                  for j in range(cf.NFC):
                      for rr in range(4):
                          p0 = 16 + rr * TPC + j * frc_
                          T.dma(POOL, fpos.t.ap()[p0:p0 + frc_, :], fout[l].t.ap()[(j * 4 + rr) * frc_:(j * 4 + rr + 1) * frc_, :], "fpos_a", [fout[l]], [fpos])
                  Z = sb(ph, "f_Z", [N1, N2, FCH], BF16)
                  PZ = N2 // 4 if N2 % 4 == 0 else N2 // 2
                  zs = [sb(ph, f"f_zs{i}", [N1, PZ, FCH], BF16) for i in range(2)]
                  t1s = sb(ph, "f_t1", [N1, 2 * N1], BF16)
                  tws = sb(ph, "f_tw", [N1, 2, N2], F32)
                  wfr = sb(ph, "f_wfr", [128, cf.FG, D], BF16)
                  wst = [sb(ph, f"f_wst{i}", [128, 512], BF16) for i in range(2)]
                  fa = sb(ph, "f_a", [N1, 512], F32)
                  fb = sb(ph, "f_b", [N1, 512], F32)
                  yo = [sb(ph, f"f_yo{i}", [N1, 2, 512], BF16) for i in range(2)]
                  T.dma(POOL, t1s.t[:], t1_in.ap(), "f_t1", [B_t1], [t1s])
                  T.dma(POOL, tws.t[:], tw_in.ap().rearrange("p (a n) -> p a n", a=2), "f_tw", [B_tw], [tws])
                  T.dma(SP, wfr.t[:], wbf[("wfr", l)].t.ap().rearrange("p (g n) -> p g n", g=cf.FG), "f_wfr", [wbf[("wfr", l)]], [wfr])
                  nw = 0
                  for g in range(cf.FG):
                      for pq in range(2):
                          kc = (g // FCB) * (2 * FCB) + pq * FCB + (g % FCB)
                          for nb in range(D // 512 if D >= 512 else 1):
                              nbw = min(512, D)
                              pw, pwb = bk(nw % 2), BK[nw % 2]
                              T.mm([lambda: nc.tensor.matmul(pw[:, 0:nbw], csm.t[:, pq, :], wfr.t[:, g, nb * nbw:(nb + 1) * nbw], start=True, stop=True)],
                                   [csm, wfr], [pwb])
                              ws_ = wst[nw % 2]
                              nw += 1
                              T.op(ACT, lambda: nc.scalar.copy(out=ws_.t[:, 0:nbw], in_=pw[:, 0:nbw]), [pwb], [ws_])
                              T.dma(POOL, wcs_d[l].t.ap()[nb * nbw:(nb + 1) * nbw, kc * 128:(kc + 1) * 128].rearrange("(j p) n -> p j n", p=128),
                                    ws_.t[:, 0:nbw].rearrange("p (j n) -> p j n", n=128), ws_.name, [ws_], [wcs_d[l]])
                  fview = fpos.t.ap().rearrange("(a b) c -> a b c", b=N2)
                  nz = 0
                  for pz in range(N2 // PZ):
                      for q in range(4):
                          z_ = zs[nz % 2]
                          nz += 1
                          T.dma(SP, z_.t[:], fview[:, pz * PZ:(pz + 1) * PZ, q * FCH:(q + 1) * FCH], z_.name, [fpos], [z_])
                          if q == 0:
                              T.op(DVE, lambda: nc.vector.tensor_scalar(out=Z.t[:, pz * PZ:(pz + 1) * PZ, :], in0=z_.t[:], scalar1=selb.t[0:N1, 9:10], scalar2=None, op0=ALU.mult),
                                   [z_, selb], [Z])
                          else:
                              T.op(DVE, lambda: nc.vector.scalar_tensor_tensor(out=Z.t[:, pz * PZ:(pz + 1) * PZ, :], in0=z_.t[:], scalar=selb.t[0:N1, 9 + q:10 + q],
                                                                              in1=Z.t[:, pz * PZ:(pz + 1) * PZ, :], op0=ALU.mult, op1=ALU.add), [z_, selb, Z], [Z])
                  Zf = Z.t[:].rearrange("p a c -> p (a c)")
                  nblk = N2 // P2B
                  for b_ in range(nblk):
                      pr, prb = bk(2 * (b_ % 2)), BK[2 * (b_ % 2)]
                      pi, pib = bk(2 * (b_ % 2) + 1), BK[2 * (b_ % 2) + 1]
                      T.mm([lambda: nc.tensor.matmul(pr[0:N1, :], t1s.t[:, 0:N1], Zf[:, b_ * 512:(b_ + 1) * 512], start=True, stop=True)], [t1s, Z], [prb])
                      T.mm([lambda: nc.tensor.matmul(pi[0:N1, :], t1s.t[:, N1:2 * N1], Zf[:, b_ * 512:(b_ + 1) * 512], start=True, stop=True)], [t1s, Z], [pib])
                      y_ = yo[b_ % 2]
                      tcb = tws.t[:, 0, b_ * P2B:(b_ + 1) * P2B].unsqueeze(2).to_broadcast([N1, P2B, FCH])
                      tsb = tws.t[:, 1, b_ * P2B:(b_ + 1) * P2B].unsqueeze(2).to_broadcast([N1, P2B, FCH])
                      v3 = lambda ap: ap.rearrange("p (a c) -> p a c", c=FCH)
                      T.op(DVE, lambda: nc.vector.tensor_tensor(out=v3(fa.t[:, :]), in0=v3(pr[0:N1, :]), in1=tcb, op=ALU.mult), [prb, tws], [fa])
                      T.op(DVE, lambda: nc.vector.tensor_tensor(out=v3(fb.t[:, :]), in0=v3(pi[0:N1, :]), in1=tsb, op=ALU.mult), [pib, tws], [fb])
                      T.op(DVE, lambda: nc.vector.tensor_tensor(out=y_.t[:, 0, :], in0=fa.t[:, :], in1=fb.t[:, :], op=ALU.subtract), [fa, fb], [y_])
                      T.op(DVE, lambda: nc.vector.tensor_tensor(out=v3(fa.t[:, :]), in0=v3(pr[0:N1, :]), in1=tsb, op=ALU.mult), [prb, tws], [fa])
                      T.op(DVE, lambda: nc.vector.tensor_tensor(out=v3(fb.t[:, :]), in0=v3(pi[0:N1, :]), in1=tcb, op=ALU.mult), [pib, tws], [fb])
                      T.op(DVE, lambda: nc.vector.tensor_tensor(out=y_.t[:, 1, :], in0=fa.t[:, :], in1=fb.t[:, :], op=ALU.add), [fa, fb], [y_])
                      ydv = yd.t.ap().rearrange("r k a c -> k r (a c)")
                      T.dma(POOL, ydv[:, :, b_ * 512:(b_ + 1) * 512], y_.t[:], y_.name, [y_], [yd])
              T.barrier()
              ck('f1')
              with contextlib.ExitStack() as ph:
                  t3s = sb(ph, "f_t3", [N2C, 2, 2, 2 * N2], BF16)
                  T.dma(POOL, t3s.t[:], t3_in.ap().rearrange("p (r k n) -> p r k n", r=2, k=2), "f_t3", [B_t3], [t3s])
                  yt = [[sb(ph, f"f_yt{ri}{ch}", [N2C, N1, 128], BF16) for ch in range(2)] for ri in range(2)]
                  pqs = sb(ph, "f_pqs", [128, 2, L], BF16)
                  for cb in range(FCB):
                      for ri in range(2):
                          for ch in range(2):
                              src = yd.t.ap()[ri].rearrange("k a c -> a k c")[ch * N2C:(ch + 1) * N2C, :, cb * 128:(cb + 1) * 128]
                              T.dma(SP, yt[ri][ch].t[:], src, yt[ri][ch].name, [yd], [yt[ri][ch]])
                      for k1 in range(N1):
                          po, pob = bk(4 + k1 % 2), BK[4 + k1 % 2]
                          fns = []
                          idx = 0
                          for ri in range(2):
                              for ch in range(2):
                                  fns.append(lambda ri=ri, ch=ch, idx=idx: nc.tensor.matmul(po[:, 0:2 * N2], yt[ri][ch].t[:, k1, :], t3s.t[:, ri, ch, :],
                                                                                           start=(idx == 0), stop=(idx == 3)))
                                  idx += 1
                          T.mm(fns, [yt[0][0], yt[0][1], yt[1][0], yt[1][1], t3s], [pob])
                          dst = pqs.t[:, :, k1:k1 + N1 * (N2 - 1) + 1:N1]
                          T.op(ACT, lambda: nc.scalar.copy(out=dst, in_=po[:, 0:2 * N2].rearrange("p (a n) -> p a n", a=2)), [pob], [pqs])
                      for j in range(cf.NPC):
                          T.dma(POOL, pqin[l].t.ap()[j * 2 * FCH:(j + 1) * 2 * FCH, :].rearrange("(a c) n -> c a n", a=2)[cb * 128:(cb + 1) * 128, :, :],
                                pqs.t[:, :, j * cf.LC:(j + 1) * cf.LC], "f_pqs", [pqs], [pqin[l]])
              T.barrier()
              ck('f3')
              ag_chunks(pqin[l], pqout[l], cf.NPC * 2 * FCH, cf.NPC, f"pq{l}")
              ck('pq')

              scale = 1.0 / math.sqrt(128.0)
              NKC = 4 * TPC // 128
              with contextlib.ExitStack() as ph:
                  KT = sb(ph, "a_KT", [128, 4 * TPC + 16], BF16)
                  Vt = sb(ph, "a_Vt", [128, NKC + 1, 132], BF16)
                  QT = [sb(ph, f"a_QT{i}", [128, TT], BF16) for i in range(2)]
                  Pb = [sb(ph, f"a_P{i}", [128, 2, TT], BF16) for i in range(2)]
                  rinv = sb(ph, "a_rinv", [128, 4], F32)
                  on = [sb(ph, f"a_on{i}", [128, 128], BF16) for i in range(2)]
                  aT = [sb(ph, f"a_aT{i}", [128, TT], BF16) for i in range(2)]
                  na = 0
                  if l + 1 < DEPTH:
                      weight_ags(l + 1)
                  for kvh in range(KVH):
                      for rr in range(4):
                          T.dma(SP, KT.t[:, rr * TPC:(rr + 1) * TPC], kout[l].t.ap()[(kvh * 4 + rr) * 128:(kvh * 4 + rr + 1) * 128, :], "a_KT", [kout[l]], [KT])
                      T.dma(SP, KT.t[:, 4 * TPC:4 * TPC + 16], kmeta.t.ap()[kvh * 128:(kvh + 1) * 128, :], "a_KT", [kmeta], [KT])
                      T.dma(SP, Vt.t[:, 0:NKC, 0:128], vout[l].t.ap()[kvh * 4 * TPC:(kvh + 1) * 4 * TPC, :].rearrange("(c p) d -> p c d", p=128), "a_Vt", [vout[l]], [Vt])
                      T.dma(SP, Vt.t[0:16, NKC, 0:128], vmeta.t.ap()[:, kvh * 128:(kvh + 1) * 128], "a_Vt", [vmeta], [Vt])
                      T.op(DVE, lambda: nc.vector.memset(Vt.t[:, :, 128:129], 1.0), [], [Vt])
                      items = [(2 * j, 2) for j in range(NKC // 2)] + [(NKC, 1)]
                      for (c0, n, _) in tiles:
                          nqb = max(1, n // 128)
                          qn_ = min(128, n)
                          for gq in range(G):
                              hd = kvh * G + gq
                              Q = QT[na % 2]
                              a_T = aT[na % 2]
                              na += 1
                              T.dma(POOL, Q.t[:, 0:n], qT_d.t.ap()[hd * 128:(hd + 1) * 128, c0:c0 + n], Q.name, [qT_d], [Q])

                              def qk(it, slot):
                                  kc0, cnt = it
                                  for u in range(cnt):
                                      kk = 16 if kc0 == NKC else 128
                                      pS = bk(2 * slot + u)
                                      T.mm([lambda: nc.tensor.matmul(pS[0:kk, 0:n], KT.t[:, (kc0 + u) * 128:(kc0 + u) * 128 + kk], Q.t[:, 0:n], start=True, stop=True)],
                                           [KT, Q], [BK[2 * slot + u]])

                              qk(items[0], 0)
                              for ii, it in enumerate(items):
                                  slot = ii % 2
                                  if ii + 1 < len(items):
                                      qk(items[ii + 1], (ii + 1) % 2)
                                  kc0, cnt = it
                                  kk = 16 if kc0 == NKC else 128
                                  Pt = Pb[slot]
                                  src = (ps0 if slot == 0 else ps1)[0:kk, :].rearrange("p (a n) -> p a n", a=2)[:, 0:cnt, 0:n]
                                  T.op(ACT, lambda: nc.scalar.activation(out=Pt.t[0:kk, 0:cnt, 0:n], in_=src, func=AF.Exp, scale=scale),
                                       [BK[2 * slot], BK[2 * slot + 1]], [Pt])
                                  fns = []
                                  for u in range(cnt):
                                      for qb in range(nqb):
                                          ob = 4 + qb // 2
                                          oc0 = (qb % 2) * 256
                                          first = (ii == 0 and u == 0 and qb % 2 == 0)
                                          lastm = (ii == len(items) - 1 and u == cnt - 1)
                                          fns.append(lambda u=u, qb=qb, ob=ob, oc0=oc0, first=first, lastm=lastm: nc.tensor.matmul(
                                              bk(ob)[0:qn_, oc0:oc0 + 129], Pt.t[0:kk, u, qb * 128:qb * 128 + qn_], Vt.t[0:kk, kc0 + u, 0:129],
                                              start=first, stop=lastm, skip_group_check=True))
                                  T.mm(fns, [Pt, Vt], [BK[4], BK[5]])
                              for qb in range(nqb):
                                  ob = 4 + qb // 2
                                  oc0 = (qb % 2) * 256
                                  T.op(DVE, lambda: nc.vector.reciprocal(out=rinv.t[0:qn_, qb:qb + 1], in_=bk(ob)[0:qn_, oc0 + 128:oc0 + 129]), [BK[ob]], [rinv])
                                  o_n = on[qb % 2]
                                  T.op(DVE, lambda: nc.vector.tensor_scalar(out=o_n.t[0:qn_, :], in0=bk(ob)[0:qn_, oc0:oc0 + 128], scalar1=rinv.t[0:qn_, qb:qb + 1], scalar2=None, op0=ALU.mult),
                                       [BK[ob], rinv], [o_n])
                                  T.mm([lambda: nc.tensor.transpose(pst[:, qb * 128:qb * 128 + qn_], o_n.t[0:qn_, :], ident.t[0:qn_, 0:qn_])], [o_n, ident], [BK[7]])
                              T.op(DVE, lambda: nc.vector.tensor_copy(out=a_T.t[:, 0:n], in_=pst[:, 0:n]), [BK[7]], [a_T])
                              T.dma(POOL, aT_d.t.ap()[hd * 128:(hd + 1) * 128, c0:c0 + n], a_T.t[:, 0:n], a_T.name, [a_T], [aT_d])
              T.barrier()
              ck('attn')
              xh_st = contextlib.ExitStack()
              xh = sb(xh_st, "xh", [128, DC, cf.NH], F32)
              hsb = sb(xh_st, "hsb", [128, 2, DC], F32)
              with contextlib.ExitStack() as ph:
                  xt = sb(ph, "m_xt", [128, DC, TT], F32, key="xt")
                  at = sb(ph, "m_at", [128, H, TT], BF16)
                  pq = sb(ph, "m_pq", [128, PQC, TT], BF16)
                  pqc = [sb(ph, f"m_pqc{i}", [128, PQC, TT], BF16) for i in range(2)]
                  gt = sb(ph, "m_gt", [128, 2 * DC, TT], BF16)
                  mg = sb(ph, "m_mg", [128, DC, TT], BF16)
                  ta = sb(ph, "m_ta", [128, TT], F32)
                  tb_ = sb(ph, "m_tb", [128, TT], F32)
                  xo = xt
                  ring = WRing(ph, "m_w", max(DC, PQC, H), 4)
                  T.op(DVE, lambda: nc.vector.memset(xh.t[:], 0.0), [], [xh])
                  npc = 0
                  for (c0, n, hj) in tiles:
                      meta = (n == 16)
                      T.dma(POOL, xt.t[:, :, 0:n], xa.t.ap()[:, c0:c0 + n].rearrange("(c p) n -> p c n", p=128), xt.name, [xa], [xt])
                      T.dma(POOL, at.t[:, :, 0:n], aT_d.t.ap()[:, c0:c0 + n].rearrange("(c p) n -> p c n", p=128), "m_at", [aT_d], [at])
                      T.dma(POOL, gt.t[:, :, 0:n], gT_d.t.ap()[:, c0:c0 + n].rearrange("(c p) n -> p c n", p=128), "m_gt", [gT_d], [gt])
                      def pq_load(dst_t, pos0, nn, key, dbuf):
                          a = pos0
                          while a < pos0 + nn:
                              j = a // cf.LC
                              b = min(pos0 + nn, (j + 1) * cf.LC)
                              v_ = pqout[l].t.ap()[j * 8 * FCH:(j + 1) * 8 * FCH, :].rearrange("(c p) n -> p c n", p=128)
                              T.dma(POOL, dst_t[:, :, a - pos0:b - pos0], v_[:, :, a - j * cf.LC:b - j * cf.LC], key, [pqout[l]], [dbuf])
                              a = b
                      if meta:
                          pq_load(pq.t, 0, 16, "m_pq", pq)
                      else:
                          for q in range(4):
                              pc = pqc[npc % 2]
                              npc += 1
                              pq_load(pc.t, q * TPC + c0, n, pc.name, pc)
                              if q == 0:
                                  T.op(DVE, lambda: nc.vector.tensor_scalar(out=pq.t[:, :, 0:n], in0=pc.t[:, :, 0:n], scalar1=selb.t[:, 9:10], scalar2=None, op0=ALU.mult), [pc, selb], [pq])
                              else:
                                  T.op(DVE, lambda: nc.vector.scalar_tensor_tensor(out=pq.t[:, :, 0:n], in0=pc.t[:, :, 0:n], scalar=selb.t[:, 9 + q:10 + q], in1=pq.t[:, :, 0:n],
                                                                                  op0=ALU.mult, op1=ALU.add), [pc, selb, pq], [pq])
                      for oc in range(DC):
                          w1 = ring.load(wbf[("wab", l)], oc, H)
                          w2 = ring.load(wcs_d[l], oc, PQC)
                          ba, bb = 2 * (oc % 2), 2 * (oc % 2) + 1
                          T.mm([(lambda c=c: nc.tensor.matmul(bk(ba)[:, 0:n], w1.t[:, c, :], at.t[:, c, 0:n], start=(c == 0), stop=(c == H - 1))) for c in range(H)], [w1, at], [BK[ba]])
                          T.mm([(lambda c=c: nc.tensor.matmul(bk(bb)[:, 0:n], w2.t[:, c, :], pq.t[:, c, 0:n], start=(c == 0), stop=(c == PQC - 1))) for c in range(PQC)], [w2, pq], [BK[bb]])
                          T.op(DVE, lambda: nc.vector.tensor_tensor(out=ta.t[:, 0:n], in0=bk(ba)[:, 0:n], in1=gt.t[:, oc, 0:n], op=ALU.mult), [BK[ba], gt], [ta])
                          T.op(DVE, lambda: nc.vector.tensor_tensor(out=tb_.t[:, 0:n], in0=bk(bb)[:, 0:n], in1=gt.t[:, DC + oc, 0:n], op=ALU.mult), [BK[bb], gt], [tb_])
                          T.op(DVE, lambda: nc.vector.tensor_tensor(out=mg.t[:, oc, 0:n], in0=ta.t[:, 0:n], in1=tb_.t[:, 0:n], op=ALU.add), [ta, tb_], [mg])
                      for oc in range(DC):
                          w = ring.load(wbf[("wout", l)], oc, DC)
                          pb_, pbb = bk(4 + oc % 2), BK[4 + oc % 2]
                          T.mm([(lambda c=c: nc.tensor.matmul(pb_[:, 0:n], w.t[:, c, :], mg.t[:, c, 0:n], start=(c == 0), stop=(c == DC - 1))) for c in range(DC)], [w, mg], [pbb])
                          T.op(DVE, lambda: nc.vector.tensor_tensor(out=xo.t[:, oc, 0:n], in0=pb_[:, 0:n], in1=xt.t[:, oc, 0:n], op=ALU.add), [pbb, xt], [xo])
                      T.dma(POOL, xm.t.ap()[:, c0:c0 + n].rearrange("(c p) n -> p c n", p=128), xo.t[:, :, 0:n], "xo", [xo], [xm])
                      if meta:
                          T.op(DVE, lambda: nc.vector.tensor_copy(out=xh.t[:, :, 0], in_=xo.t[:, :, 15]), [xo], [xh])
                      else:
                          if hj + 1 < NT:
                              T.op(DVE, lambda: nc.vector.tensor_copy(out=xh.t[:, :, 2 * (hj + 1)], in_=xo.t[:, :, n - 1]), [xo], [xh])
                          else:
                              T.op(DVE, lambda: nc.vector.tensor_copy(out=hsb.t[:, 1, :], in_=xo.t[:, :, n - 1]), [xo], [hsb])
                          if hj >= 1:
                              T.op(DVE, lambda: nc.vector.tensor_copy(out=xh.t[:, :, 2 * (hj - 1) + 1], in_=xo.t[:, :, 0]), [xo], [xh])
                          else:
                              T.op(DVE, lambda: nc.vector.tensor_copy(out=hsb.t[:, 0, :], in_=xo.t[:, :, 0]), [xo], [hsb])
                  T.dma(POOL, hin[l].t.ap().rearrange("p (a c) -> p a c", a=2), hsb.t[:], "hsb", [hsb], [hin[l]])
              T.barrier()
              T.coll("AllGather", GRP, hin[l], hout[l], f"h{l}")
              ck('halo')
              with contextlib.ExitStack() as ph:
                  hb = sb(ph, "n_hb", [128, 4, 2, DC], F32)
                  T.dma(POOL, hb.t[:], hout[l].t.ap().rearrange("(r p) (a c) -> p r a c", p=128, a=2), "n_hb", [hout[l]], [hb])
                  jl, jr, jm = 0, 2 * (NT - 1) + 1, 2 * NT + 1
                  T.op(DVE, lambda: nc.vector.tensor_scalar(out=xh.t[:, :, jl], in0=xh.t[:, :, jl], scalar1=selb.t[:, 0:1], scalar2=None, op0=ALU.mult), [xh, selb], [xh])
                  for j in range(4):
                      T.op(DVE, lambda j=j: nc.vector.scalar_tensor_tensor(out=xh.t[:, :, jl], in0=hb.t[:, j, 1, :], scalar=selb.t[:, 1 + j:2 + j], in1=xh.t[:, :, jl],
                                                                          op0=ALU.mult, op1=ALU.add), [hb, selb, xh], [xh])
                      T.op(DVE, lambda j=j: nc.vector.scalar_tensor_tensor(out=xh.t[:, :, jr], in0=hb.t[:, j, 0, :], scalar=selb.t[:, 5 + j:6 + j], in1=xh.t[:, :, jr],
                                                                          op0=ALU.mult, op1=ALU.add), [hb, selb, xh], [xh])
                  T.op(DVE, lambda: nc.vector.tensor_copy(out=xh.t[:, :, jm], in_=hb.t[:, 0, 0, :]), [hb], [xh])
                  NH = cf.NH
                  sqh_ = sb(ph, "n_sqh", [128, DC, NH], BF16)
                  h2h = sb(ph, "n_h2h", [128, DC, NH], BF16)
                  rsx = sb(ph, "n_rsx", [128, NH], F32)
                  rstx = sb(ph, "n_rstx", [128, NH], F32)
                  ugh = sb(ph, "n_ugh", [128, FFC, NH], F32)
                  xt = sb(ph, "n_xt", [128, DC, TT], F32, key="xt")
                  sq = sb(ph, "n_sq", [128, DC, TT], BF16)
                  h2 = sb(ph, "n_h2", [128, DC, TT], BF16)
                  rs = sb(ph, "n_rs", [128, TT], F32)
                  rstd = sb(ph, "n_rstd", [128, TT], F32)
                  cc = sb(ph, "n_cc", [128, TT], F32)
                  sg = sb(ph, "n_sg", [128, TT], F32)
                  uT = sb(ph, "n_uT", [128, FFC, TT], BF16)
                  xo = xt
                  ring = WRing(ph, "n_w", DC, 4)
                  ringd = WRing(ph, "n_wd", FFC, 2)
                  rmsnorm(xh, NH, cf.V_GFFN, sqh_, h2h, rsx, rstx, bk(6), BK[6])
                  for fc in range(FFC):
                      w = ring.load(wbf[("wup", l)], fc, DC)
                      T.mm([(lambda c=c: nc.tensor.matmul(bk(fc % 2)[:, 0:NH], w.t[:, c, :], h2h.t[:, c, :], start=(c == 0), stop=(c == DC - 1))) for c in range(DC)], [w, h2h], [BK[fc % 2]])
                      T.op(ACT, lambda: nc.scalar.copy(out=ugh.t[:, fc, :], in_=bk(fc % 2)[:, 0:NH]), [BK[fc % 2]], [ugh])
                  wv = lambda fc, k: vec.t[:, cf.V_WCV + 3 * fc + k:cf.V_WCV + 3 * fc + k + 1]
                  for (c0, n, hj) in tiles:
                      meta = (n == 16)
                      T.dma(POOL, xt.t[:, :, 0:n], xm.t.ap()[:, c0:c0 + n].rearrange("(c p) n -> p c n", p=128), xt.name, [xm], [xt])
                      rmsnorm(xt, n, cf.V_GFFN, sq, h2, rs, rstd, bk(6), BK[6])
                      for fc in range(FFC):
                          wg_ = ring.load(wbf[("wup", l)], fc, DC)
                          wv_ = ring.load(wbf[("wup", l)], FFC + fc, DC)
                          pg, pgb = bk(2 * (fc % 2)), BK[2 * (fc % 2)]
                          pu, pub = bk(2 * (fc % 2) + 1), BK[2 * (fc % 2) + 1]
                          T.mm([(lambda c=c: nc.tensor.matmul(pg[:, 0:n], wg_.t[:, c, :], h2.t[:, c, 0:n], start=(c == 0), stop=(c == DC - 1))) for c in range(DC)], [wg_, h2], [pgb])
                          T.mm([(lambda c=c: nc.tensor.matmul(pu[:, 0:n], wv_.t[:, c, :], h2.t[:, c, 0:n], start=(c == 0), stop=(c == DC - 1))) for c in range(DC)], [wv_, h2], [pub])
                          bcol = vec.t[:, cf.V_BCV + fc:cf.V_BCV + fc + 1]
                          T.op(DVE, lambda: nc.vector.tensor_scalar(out=cc.t[:, 0:n], in0=pg[:, 0:n], scalar1=wv(fc, 1), scalar2=bcol, op0=ALU.mult, op1=ALU.add), [pgb, vec], [cc])
                          T.op(DVE, lambda: nc.vector.scalar_tensor_tensor(out=cc.t[:, 1:n], in0=pg[:, 0:n - 1], scalar=wv(fc, 0), in1=cc.t[:, 1:n], op0=ALU.mult, op1=ALU.add), [pgb, vec, cc], [cc])
                          T.op(DVE, lambda: nc.vector.scalar_tensor_tensor(out=cc.t[:, 0:n - 1], in0=pg[:, 1:n], scalar=wv(fc, 2), in1=cc.t[:, 0:n - 1], op0=ALU.mult, op1=ALU.add), [pgb, vec, cc], [cc])
                          T.op(DVE, lambda: nc.vector.scalar_tensor_tensor(out=cc.t[:, 0:1], in0=ugh.t[:, fc, 2 * hj:2 * hj + 1], scalar=wv(fc, 0), in1=cc.t[:, 0:1], op0=ALU.mult, op1=ALU.add), [ugh, vec, cc], [cc])
                          T.op(DVE, lambda: nc.vector.scalar_tensor_tensor(out=cc.t[:, n - 1:n], in0=ugh.t[:, fc, 2 * hj + 1:2 * hj + 2], scalar=wv(fc, 2), in1=cc.t[:, n - 1:n], op0=ALU.mult, op1=ALU.add), [ugh, vec, cc], [cc])
                          T.op(ACT, lambda: nc.scalar.activation(out=sg.t[:, 0:n], in_=cc.t[:, 0:n], func=AF.Silu), [cc], [sg])
                          T.op(DVE, lambda: nc.vector.tensor_tensor(out=uT.t[:, fc, 0:n], in0=sg.t[:, 0:n], in1=pu[:, 0:n], op=ALU.mult), [sg, pub], [uT])
                      for oc in range(DC):
                          w = ringd.load(wbf[("wdn", l)], oc, FFC)
                          pb_, pbb = bk(4 + oc % 2), BK[4 + oc % 2]
                          T.mm([(lambda c=c: nc.tensor.matmul(pb_[:, 0:n], w.t[:, c, :], uT.t[:, c, 0:n], start=(c == 0), stop=(c == FFC - 1))) for c in range(FFC)], [w, uT], [pbb])
                          T.op(DVE, lambda: nc.vector.tensor_tensor(out=xo.t[:, oc, 0:n], in0=pb_[:, 0:n], in1=xt.t[:, oc, 0:n], op=ALU.add), [pbb, xt], [xo])
                      if last and not meta:
                          T.dma(POOL, yT.ap()[:, c0 - 16:c0 - 16 + n].rearrange("(c p) n -> p c n", p=128), xo.t[:, :, 0:n], "xo", [xo], [B_yT])
                      elif not last:
                          T.dma(POOL, xa.t.ap()[:, c0:c0 + n].rearrange("(c p) n -> p c n", p=128), xo.t[:, :, 0:n], "xo", [xo], [xa])
              xh_st.close()
        except _Stop:
            pass
        T.dead = False
        T.barrier(full=True)
        if getattr(cf, 'endclear', False):
            fin_sem = stack.enter_context(nc.semaphore("s_fin"))
            for eng in (T.pe, T.act, T.dve, T.sp):
                eng.e.sem_inc(fin_sem, 1)
            nc.gpsimd.wait_ge(fin_sem, 4)
            allsems = [e.sem for e in T.engs] + [v[0] for v in T.dsems.values()] + [sv[0] for sv in T.csems] + [fin_sem]
            for sm in allsems:
                nc.gpsimd.sem_clear(sm)
    return nc, T.nsem


_CACHE = {}


def run(cf, inputs):
    in_maps = prep_inputs(cf, **inputs)
    if "nc" not in _CACHE or _CACHE.get("cf") is not cf:
        _CACHE["nc"], nsem = build(cf)
        _CACHE["cf"] = cf
    res = run_bass_kernel_spmd(_CACHE["nc"], in_maps, core_ids=list(range(NCORES)))
    out = np.empty((2, cf.SEQ, cf.D), np.float32)
    for c in range(NCORES):
        b, r = c // 4, c % 4
        out[b, r * cf.TPC:(r + 1) * cf.TPC, :] = res.results[c]["yT"].T
    return out


def kernel(**inputs):
    return run(FULL, inputs)
```

```python
import contextlib
import math
import numpy as np
import concourse.bass as bass
import concourse.mybir as mybir
from concourse.bass_utils import run_bass_kernel_spmd

F32, BF16 = mybir.dt.float32, mybir.dt.bfloat16
AF = mybir.ActivationFunctionType
ALU = mybir.AluOpType
NCORES = 8
GRP = [[0, 1, 2, 3], [4, 5, 6, 7]]
ALL8 = [list(range(8))]


def nchunks_for(n, unit_bytes, limit=1 << 20):
    for k in range(1, n + 1):
        if n % k == 0 and (n // k) * unit_bytes <= limit:
            return k
    raise ValueError


class Cfg:
    def __init__(self, D=2048, SEQ=16384, H=8, KVH=2, FG=8, DFF=5632, TT=512, N1=100, N2=164, DEPTH=2):
        self.D, self.SEQ, self.H, self.KVH, self.FG, self.DFF, self.TT = D, SEQ, H, KVH, FG, DFF, TT
        self.N1, self.N2, self.DEPTH = N1, N2, DEPTH
        self.HD = 128
        self.G = H // KVH
        self.AW = H * 128
        self.KVW = KVH * 128
        self.FW = FG * 128
        self.NMETA = 16
        self.GRIDW = 64
        self.L = SEQ + 16
        assert N1 * N2 == self.L
        self.TPC = SEQ // 4
        self.NT = self.TPC // TT
        self.NCOL = 16 + self.TPC
        self.DC = D // 128
        self.FFC = DFF // 128
        self.FCH = self.FW // 4
        self.FCB = self.FCH // 128
        self.OFF_K = self.AW
        self.OFF_V = self.AW + self.KVW
        self.OFF_F = self.OFF_V + self.KVW
        self.OFF_GA = self.OFF_F + self.FW
        self.INW = self.OFF_GA + 2 * D
        self.EPS = 1e-6
        self.N2C = N2 // 2
        assert self.N2C * 2 == N2 and self.N2C <= 128 and 2 * N2 <= 512 and N1 <= 128
        self.P2B = 512 // self.FCH
        assert N2 % self.P2B == 0
        self.NH = 2 * (self.NT + 1)
        MB = 1 << 20
        self.NFC = nchunks_for(self.TPC, self.FW * 2)
        self.NPC = nchunks_for(self.L, 2 * self.FCH * 2)
        self.LC = self.L // self.NPC
        assert 128 * self.TPC * 2 <= MB
        o = 0
        self.V_GMIX = o; o += self.DC
        self.V_GFFN = o; o += self.DC
        self.V_BG = o; o += 2 * self.DC
        self.V_QN = o; o += 1
        self.V_KN = o; o += 1
        self.V_WCV = o; o += 3 * self.FFC
        self.V_BCV = o; o += self.FFC
        self.NV = o


FULL = Cfg()


def lhsT_layout(W):
    K, N = W.shape
    return np.ascontiguousarray(W.reshape(K // 128, 128, N // 128, 128).transpose(2, 1, 0, 3).reshape(N, K))


def rhs_layout(W):
    K, N = W.shape
    return np.ascontiguousarray(W.reshape(K // 128, 128, N).transpose(1, 0, 2).reshape(128, (K // 128) * N))


WNAMES = ["wqk", "wg", "wvf", "wab", "wfr", "wout", "wup", "wdn"]


def weight_shapes(cf):
    return {
        "wqk": (cf.AW + cf.KVW, cf.D), "wg": (2 * cf.D, cf.D), "wvf": (128, cf.DC * (cf.KVW + cf.FW)),
        "wab": (cf.D, cf.AW), "wfr": (128, cf.FG * cf.D), "wout": (cf.D, cf.D),
        "wup": (2 * cf.DFF, cf.D), "wdn": (cf.D, cf.DFF),
    }


def weight_chunks(cf):
    out = {}
    for n, (R, C) in weight_shapes(cf).items():
        out[n] = nchunks_for(R // 4, C * 2)
    return out


def host_tables(cf):
    f64 = np.float64
    tabs = {}
    ident = np.eye(128, dtype=np.float32)
    rotT = np.zeros((128, 128), np.float32)
    for i in range(128):
        blk = i // 32
        if blk % 2 == 0:
            rotT[i + 32, i] = -1.0
        else:
            rotT[i - 32, i] = 1.0
    N1, N2, L = cf.N1, cf.N2, cf.L
    a1 = 2 * np.pi * np.outer(np.arange(N1), np.arange(N1)).astype(f64) / N1
    t1 = np.concatenate([np.cos(a1), np.sin(a1)], 1).astype(np.float32)
    a2 = 2 * np.pi * np.outer(np.arange(N2), np.arange(N2)).astype(f64) / N2
    C2, S2 = np.cos(a2), np.sin(a2)
    tabR = np.concatenate([C2, S2], 1)
    tabI = np.concatenate([-S2, C2], 1)
    t3 = np.stack([tabR, tabI], 0).reshape(2, 2, cf.N2C, 2 * N2).astype(np.float32)
    at = 2 * np.pi * np.outer(np.arange(N1), np.arange(N2)).astype(f64) / L
    tw = np.stack([np.cos(at), np.sin(at)], 1).astype(np.float32)
    ac = 2 * np.pi * np.outer(np.arange(128), np.arange(128)).astype(f64) / 128
    sc = 1.0 / math.sqrt(L * 128.0)
    cs = np.stack([np.cos(ac) * sc, -np.sin(ac) * sc], 1).astype(np.float32)
    tabs["cmat"] = np.ascontiguousarray(np.concatenate([ident, rotT, cs.reshape(128, 256)], 1))
    tabs["t1"] = t1
    tabs["t3"] = np.ascontiguousarray(t3.transpose(2, 0, 1, 3).reshape(cf.N2C, 4 * 2 * N2))
    tabs["tw"] = np.ascontiguousarray(tw.reshape(N1, 2 * N2))
    return tabs


def core_tables(cf, r):
    inv = 1.0 / (10000.0 ** (np.arange(32, dtype=np.float64) / 32))
    tg = r * cf.TPC + np.arange(cf.TPC)
    rows = (tg // cf.GRIDW).astype(np.float64)
    cols = (tg % cf.GRIDW).astype(np.float64)
    ang = np.zeros((128, cf.NCOL), np.float64)
    ang[0:32, 16:] = inv[:, None] * rows[None]
    ang[32:64, 16:] = inv[:, None] * rows[None]
    ang[64:96, 16:] = inv[:, None] * cols[None]
    ang[96:128, 16:] = inv[:, None] * cols[None]
    rope = np.stack([np.cos(ang), np.sin(ang)], 1).astype(np.float32)
    sel = np.zeros((128, 16), np.float32)
    sel[:, 0] = 1.0 if r == 0 else 0.0
    for j in range(4):
        sel[:, 1 + j] = 1.0 if j == r - 1 else 0.0
        sel[:, 5 + j] = 1.0 if j == r + 1 else 0.0
        sel[:, 9 + j] = 1.0 if j == r else 0.0
    return np.ascontiguousarray(rope.reshape(128, 2 * cf.NCOL)), sel


def prep_inputs(cf, x, meta_tokens, norm_mix, norm_ffn, w_in, b_gate, q_norm, k_norm,
                w_attn_br, w_four, w_out, w_up, w_conv, b_conv, w_down):
    f = lambda a: np.asarray(a, dtype=np.float32)
    x, meta_tokens = f(x), f(meta_tokens)
    w_in, w_attn_br, w_four, w_out, w_up, w_down = map(f, (w_in, w_attn_br, w_four, w_out, w_up, w_down))
    DEPTH = cf.DEPTH
    full = {n: [] for n in WNAMES}
    for l in range(DEPTH):
        full["wqk"].append(lhsT_layout(w_in[l][:, 0:cf.OFF_V]))
        full["wg"].append(lhsT_layout(w_in[l][:, cf.OFF_GA:]))
        full["wvf"].append(rhs_layout(w_in[l][:, cf.OFF_V:cf.OFF_GA]))
        full["wab"].append(lhsT_layout(w_attn_br[l]))
        full["wfr"].append(rhs_layout(w_four[l]))
        full["wout"].append(lhsT_layout(w_out[l]))
        full["wup"].append(lhsT_layout(w_up[l]))
        full["wdn"].append(lhsT_layout(w_down[l]))
    vecs = np.zeros((DEPTH, 128, cf.NV), np.float32)
    for l in range(DEPTH):
        vecs[l, :, cf.V_GMIX:cf.V_GMIX + cf.DC] = f(norm_mix[l]).reshape(cf.DC, 128).T
        vecs[l, :, cf.V_GFFN:cf.V_GFFN + cf.DC] = f(norm_ffn[l]).reshape(cf.DC, 128).T
        vecs[l, :, cf.V_BG:cf.V_BG + 2 * cf.DC] = f(b_gate[l]).reshape(2 * cf.DC, 128).T
        vecs[l, :, cf.V_QN] = f(q_norm[l])
        vecs[l, :, cf.V_KN] = f(k_norm[l])
        wc = f(w_conv[l]).reshape(3, cf.FFC, 128).transpose(2, 1, 0)
        vecs[l, :, cf.V_WCV:cf.V_WCV + 3 * cf.FFC] = wc.reshape(128, 3 * cf.FFC)
        vecs[l, :, cf.V_BCV:cf.V_BCV + cf.FFC] = f(b_conv[l]).reshape(cf.FFC, 128).T
    tabs = host_tables(cf)
    wch = weight_chunks(cf)
    metaT = np.ascontiguousarray(meta_tokens.T)
    in_maps = []
    for c in range(NCORES):
        b, r = c // 4, c % 4
        m = {}
        m["xT"] = np.ascontiguousarray(x[b, r * cf.TPC:(r + 1) * cf.TPC, :].T)
        m["metaT"] = metaT
        for n in WNAMES:
            R = full[n][0].shape[0]
            nch = wch[n]
            rc = R // 4 // nch
            m[n] = np.ascontiguousarray(np.stack(
                [full[n][l].reshape(nch, 4, rc, -1)[:, r].reshape(nch * rc, -1) for l in range(DEPTH)], 0))
        m["vecs"] = vecs
        rope, sel = core_tables(cf, r)
        m["rope"] = rope
        m["sel"] = sel
        for k, v in tabs.items():
            m[k] = v
        in_maps.append(m)
    return in_maps


class _Stop(Exception):
    pass


class Buf:
    def __init__(self, name, t):
        self.name, self.t = name, t
        self.w, self.r = {}, {}


class Eng:
    def __init__(self, name, e, sem):
        self.name, self.e, self.sem = name, e, sem
        self.count = 0
        self.waited = {}


def _merge(d, src):
    for k, (s, v) in src.items():
        if k not in d or d[k][1] < v:
            d[k] = (s, v)


class TR:
    def __init__(self, nc, stack):
        self.nc, self.stack = nc, stack
        mk = lambda n: stack.enter_context(nc.semaphore(n))
        self.pe = Eng("pe", nc.tensor, mk("s_pe"))
        self.act = Eng("act", nc.scalar, mk("s_act"))
        self.dve = Eng("dve", nc.vector, mk("s_dve"))
        self.pool = Eng("pool", nc.gpsimd, mk("s_pool"))
        self.sp = Eng("sp", nc.sync, mk("s_sp"))
        self.engs = [self.pe, self.act, self.dve, self.pool, self.sp]
        self.dsems = {}
        self.csems = []
        self.shsem = {}
        self.nsem = 5
        self.dead = False

    def _sync(self, eng, reads, writes, ignore=None):
        raw = {}
        for b in reads:
            _merge(raw, b.w)
        oth = {}
        for b in writes:
            _merge(oth, b.w)
            _merge(oth, b.r)
        me = id(eng.sem)
        d = dict(raw)
        for k, sv in oth.items():
            if k == me:
                continue
            if k not in d or d[k][1] < sv[1]:
                d[k] = sv
        if eng is self.pe:
            d.pop(me, None)
        if ignore is not None:
            d.pop(ignore, None)
        for k, (s, v) in d.items():
            if eng.waited.get(k, 0) < v:
                eng.e.wait_ge(s, v)
                eng.waited[k] = v

    def _rec(self, ev, reads, writes):
        k, s, v = ev
        for b in reads:
            if k not in b.r or b.r[k][1] < v:
                b.r[k] = (s, v)
        for b in writes:
            if k not in b.w or b.w[k][1] < v:
                b.w[k] = (s, v)

    def op(self, eng, fn, reads=(), writes=()):
        if self.dead:
            return
        self._sync(eng, reads, writes)
        ins = fn()
        eng.count += 1
        ins.then_inc(eng.sem, 1)
        self._rec((id(eng.sem), eng.sem, eng.count), reads, writes)

    def mm(self, fns, reads=(), writes=()):
        if self.dead:
            return
        eng = self.pe
        self._sync(eng, reads, writes)
        ins = None
        for fn in fns:
            ins = fn()
        eng.count += 1
        ins.then_inc(eng.sem, 1)
        self._rec((id(eng.sem), eng.sem, eng.count), reads, writes)

    def dma(self, q, out, in_, key, reads=(), writes=()):
        if self.dead:
            return
        self._sync(q, reads, writes)
        if key not in self.dsems:
            self.nsem += 1
            self.dsems[key] = [self.stack.enter_context(self.nc.semaphore("d_" + key)), 0]
        ent = self.dsems[key]
        ent[1] += 1
        q.e.dma_start(out=out, in_=in_).then_inc(ent[0], 16)
        self._rec((id(ent[0]), ent[0], 16 * ent[1]), reads, writes)

    def coll(self, kind, groups, inb, outb, name, shared=None, in_ap=None, out_ap=None):
        if self.dead:
            return
        q = self.pool
        if shared is not None and shared[0] in self.shsem:
            self._sync(q, [inb], [outb], ignore=id(self.shsem[shared[0]]))
        else:
            self._sync(q, [inb], [outb])
        if shared is None:
            self.nsem += 1
            sem = self.stack.enter_context(self.nc.semaphore("c_" + name))
            val = 1
            self.csems.append((sem, 1))
        else:
            key, val = shared
            if key not in self.shsem:
                self.nsem += 1
                self.shsem[key] = self.stack.enter_context(self.nc.semaphore("c_" + key))
                self.csems.append((self.shsem[key], val))
            sem = self.shsem[key]
        q.e.collective_compute(kind, ALU.bypass, replica_groups=groups,
                               ins=[(in_ap if in_ap is not None else inb.t.ap()).opt()],
                               outs=[(out_ap if out_ap is not None else outb.t.ap()).opt()]).then_inc(sem, 1)
        self._rec((id(sem), sem, val), [inb], [outb])

    def barrier(self, full=False):
        if self.dead:
            return
        evs = [(id(e.sem), e.sem, e.count) for e in self.engs if e.count > 0]
        evs += [(id(s), s, 16 * c) for (s, c) in self.dsems.values() if c > 0]
        if full:
            evs += [(id(s), s, v) for (s, v) in self.csems]
        for eng in self.engs:
            for k, s, v in evs:
                if k == id(eng.sem):
                    continue
                if eng.waited.get(k, 0) < v:
                    eng.e.wait_ge(s, v)
                    eng.waited[k] = v


def build(cf, final_wait=True):
    nc = bass.Bass("TRN2", target_bir_lowering=False)
    D, DC, TT, NT, NCOL, TPC, H, KVH, G = cf.D, cf.DC, cf.TT, cf.NT, cf.NCOL, cf.TPC, cf.H, cf.KVH, cf.G
    FW, FCH, FCB, FFC, KVW, AW, L, N1, N2, N2C = cf.FW, cf.FCH, cf.FCB, cf.FFC, cf.KVW, cf.AW, cf.L, cf.N1, cf.N2, cf.N2C
    DEPTH = cf.DEPTH
    NQK = H + KVH
    VFW = KVW + FW
    PQC = 2 * FW // 128
    wsh = weight_shapes(cf)

    def din(name, shape, dt=F32):
        return nc.dram_tensor(name, list(shape), dt, kind="ExternalInput")

    xT_in = din("xT", [D, TPC])
    metaT_in = din("metaT", [D, 16])
    w_in_sh = {n: din(n, [DEPTH, wsh[n][0] // 4, wsh[n][1]]) for n in WNAMES}
    vecs_in = din("vecs", [DEPTH, 128, cf.NV])
    rope_in = din("rope", [128, 2 * NCOL])
    sel_in = din("sel", [128, 16])
    cmat_in = din("cmat", [128, 512])
    t1_in = din("t1", [N1, 2 * N1])
    t3_in = din("t3", [N2C, 8 * N2])
    tw_in = din("tw", [N1, 2 * N2])
    yT = nc.dram_tensor("yT", [D, TPC], F32, kind="ExternalOutput")

    stack = contextlib.ExitStack()
    with stack:
        stack.enter_context(nc.allow_non_contiguous_dma(reason="small strided scratch transfers"))
        T = TR(nc, stack)
        blk = stack.enter_context(nc.Block())
        PE, ACT, DVE, POOL, SP = T.pe, T.act, T.dve, T.pool, T.sp

        def dram(name, shape, dt):
            return Buf(name, nc.dram_tensor(name, list(shape), dt))

        def ext(tn, name):
            return Buf(name, tn)

        B_xT, B_metaT, B_yT = ext(xT_in, "xT"), ext(metaT_in, "metaT"), ext(yT, "yT")
        B_vecs, B_rope, B_sel, B_cmat = ext(vecs_in, "vecs"), ext(rope_in, "rope"), ext(sel_in, "sel"), ext(cmat_in, "cmat")
        B_t1, B_t3, B_tw = ext(t1_in, "t1"), ext(t3_in, "t3"), ext(tw_in, "tw")
        B_wsh = {n: ext(w_in_sh[n], n) for n in WNAMES}

        wbs = {(n, l): dram(f"wbs_{n}{l}", [wsh[n][0] // 4, wsh[n][1]], BF16) for n in WNAMES for l in range(DEPTH)}
        wbf = {(n, l): dram(f"wbf_{n}{l}", [wsh[n][0], wsh[n][1]], BF16) for n in WNAMES for l in range(DEPTH)}
        wcs_d = [dram(f"wcs{l}", [D, 2 * FW], BF16) for l in range(DEPTH)]
        xa = dram("xa", [D, NCOL], F32)
        xm = dram("xm", [D, NCOL], F32)
        qT_d = dram("qT", [AW, NCOL], BF16)
        gT_d = dram("gT", [2 * D, NCOL], BF16)
        aT_d = dram("aT", [AW, NCOL], BF16)
        kin = [dram(f"kin{l}", [KVW, TPC], BF16) for l in range(DEPTH)]
        kout = [dram(f"kout{l}", [4 * KVW, TPC], BF16) for l in range(DEPTH)]
        kmeta = dram("kmeta", [KVW, 16], BF16)
        vin = [dram(f"vin{l}", [KVH * TPC, 128], BF16) for l in range(DEPTH)]
        vout = [dram(f"vout{l}", [KVH * 4 * TPC, 128], BF16) for l in range(DEPTH)]
        vmeta = dram("vmeta", [16, KVW], BF16)
        fin = [dram(f"fin{l}", [TPC, FW], BF16) for l in range(DEPTH)]
        fout = [dram(f"fout{l}", [4 * TPC, FW], BF16) for l in range(DEPTH)]
        fmeta = dram("fmeta", [16, FW], BF16)
        yd = dram("yd", [2, N1, N2, FCH], BF16)
        fpos = dram("fpos", [L, FW], BF16)
        pqin = [dram(f"pqin{l}", [cf.NPC * 2 * FCH, cf.LC], BF16) for l in range(DEPTH)]
        pqout = [dram(f"pqout{l}", [cf.NPC * 4 * 2 * FCH, cf.LC], BF16) for l in range(DEPTH)]
        hin = [dram(f"hin{l}", [128, 2 * DC], F32) for l in range(DEPTH)]
        hout = [dram(f"hout{l}", [4 * 128, 2 * DC], F32) for l in range(DEPTH)]

        ps0 = stack.enter_context(nc.psum_tensor("ps0", [128, 1024], F32))
        ps1 = stack.enter_context(nc.psum_tensor("ps1", [128, 1024], F32))
        ps4 = stack.enter_context(nc.psum_tensor("ps4", [128, 512], F32))
        ps5 = stack.enter_context(nc.psum_tensor("ps5", [128, 512], F32))
        ps6 = stack.enter_context(nc.psum_tensor("ps6", [128, 512], F32))
        pst = stack.enter_context(nc.psum_tensor("pst", [128, 1024], BF16))
        BK = [Buf(f"bk{i}", None) for i in range(8)]
        bank_aps = [ps0[:, 0:512], ps0[:, 512:1024], ps1[:, 0:512], ps1[:, 512:1024], ps4[:, :], ps5[:, :], ps6[:, :]]

        def bk(i):
            return bank_aps[i]

        uniq = [0]

        def sb(st, name, shape, dt, key=None):
            uniq[0] += 1
            return Buf(key or name, st.enter_context(nc.sbuf_tensor(f"{name}_{uniq[0]}", list(shape), dt)))

        ident = sb(stack, "ident", [128, 128], BF16)
        rotT = sb(stack, "rotT", [128, 128], BF16)
        csm = sb(stack, "csm", [128, 2, 128], BF16)
        ones = sb(stack, "ones", [128, 128], BF16)
        selb = sb(stack, "selb", [128, 16], F32)
        vec = sb(stack, "vec", [128, cf.NV], F32)

        T.dma(POOL, ident.t[:], cmat_in.ap()[:, 0:128], "c0", [B_cmat], [ident])
        T.dma(POOL, rotT.t[:], cmat_in.ap()[:, 128:256], "c0", [B_cmat], [rotT])
        T.dma(POOL, csm.t[:], cmat_in.ap()[:, 256:512].rearrange("p (a b) -> p a b", a=2), "c0", [B_cmat], [csm])
        T.dma(POOL, selb.t[:], sel_in.ap(), "c0", [B_sel], [selb])
        T.op(DVE, lambda: nc.vector.memset(ones.t[:], 1.0), [], [ones])

        T.dma(POOL, xa.t.ap()[:, 0:16], metaT_in.ap(), "xinit", [B_metaT], [xa])
        T.dma(POOL, xa.t.ap()[:, 16:NCOL], xT_in.ap(), "xinit", [B_xT], [xa])

        def ag_chunks(inb, outb, rows_in, nch, key, total=None):
            rc = rows_in // nch
            for j in range(nch):
                if getattr(cf, 'unshare', False) and total is None:
                    T.coll("AllGather", GRP, inb, outb, f"{key}_{j}",
                           in_ap=inb.t.ap()[j * rc:(j + 1) * rc, :], out_ap=outb.t.ap()[j * 4 * rc:(j + 1) * 4 * rc, :])
                else:
                    T.coll("AllGather", GRP, inb, outb, key, shared=(key, total or nch),
                           in_ap=inb.t.ap()[j * rc:(j + 1) * rc, :], out_ap=outb.t.ap()[j * 4 * rc:(j + 1) * 4 * rc, :])

        wch = weight_chunks(cf)
        wtot = sum(wch.values())
        WGA = ["wqk", "wg", "wvf"]
        wtot_a = sum(wch[n] for n in WGA)
        wtot_b = wtot - wtot_a

        def weight_ags(l):
            for n in WNAMES:
                if n in WGA:
                    ag_chunks(wbs[(n, l)], wbf[(n, l)], wsh[n][0] // 4, wch[n], f"w{l}a", wtot_a)
                else:
                    ag_chunks(wbs[(n, l)], wbf[(n, l)], wsh[n][0] // 4, wch[n], f"w{l}b", wtot_b)

        for l in range(DEPTH):
            for n in WNAMES:
                T.dma(POOL, wbs[(n, l)].t.ap(), w_in_sh[n].ap()[l], "wcast", [B_wsh[n]], [wbs[(n, l)]])
            if l == 0:
                weight_ags(0)

        tiles = [(16 + i * TT, TT, i) for i in range(NT)] + [(0, 16, NT)]

        class WRing:
            def __init__(self, st, name, kcmax, nslots):
                self.slots = [sb(st, f"{name}{i}", [128, kcmax, 128], BF16, key=f"{name[2:]}{i}") for i in range(nslots)]
                self.i = 0

            def load(self, wb, chunk, kc):
                s = self.slots[self.i % len(self.slots)]
                self.i += 1
                src = wb.t.ap()[chunk * 128:(chunk + 1) * 128, :].rearrange("p (k n) -> p k n", n=128)
                T.dma(SP, s.t[:, 0:kc, :], src, s.name, [wb], [s])
                return s

        def rmsnorm(xt, n, gcol, sq, hT, rs, rstd, pbank, pbuf):
            T.op(ACT, lambda: nc.scalar.activation(out=sq.t[:, :, 0:n], in_=xt.t[:, :, 0:n], func=AF.Square), [xt], [sq])
            T.mm([(lambda c=c: nc.tensor.matmul(pbank[:, 0:n], ones.t[:, :], sq.t[:, c, 0:n], start=(c == 0), stop=(c == DC - 1)))
                  for c in range(DC)], [ones, sq], [pbuf])
            T.op(ACT, lambda: nc.scalar.activation(out=rs.t[:, 0:n], in_=pbank[:, 0:n], func=AF.Sqrt, bias=float(cf.EPS), scale=1.0 / D), [pbuf], [rs])
            T.op(DVE, lambda: nc.vector.reciprocal(out=rstd.t[:, 0:n], in_=rs.t[:, 0:n]), [rs], [rstd])
            for c in range(DC):
                T.op(DVE, lambda c=c: nc.vector.scalar_tensor_tensor(out=hT.t[:, c, 0:n], in0=xt.t[:, c, 0:n], scalar=vec.t[:, gcol + c:gcol + c + 1],
                                                                    in1=rstd.t[:, 0:n], op0=ALU.mult, op1=ALU.mult), [xt, vec, rstd], [hT])

        def ck(name):
            if getattr(cf, 'stop', None) == name:
                if getattr(cf, 'exc', False):
                    raise _Stop()
                T.barrier(full=True)
                T.dead = True

        try:
          for l in range(DEPTH):
              last = (l == DEPTH - 1)
              T.barrier()
              ck('pro')
              T.dma(POOL, vec.t[:], vecs_in.ap()[l], "c_vec", [B_vecs], [vec])

              with contextlib.ExitStack() as ph:
                  xt = sb(ph, "p1_xt", [128, DC, TT], F32, key="xt")
                  sq = sb(ph, "p1_sq", [128, DC, TT], BF16)
                  hT = sb(ph, "p1_hT", [128, DC, TT], BF16)
                  rs = sb(ph, "p1_rs", [128, TT], F32)
                  rstd = sb(ph, "p1_rstd", [128, TT], F32)
                  wvf = sb(ph, "p1_wvf", [128, DC, VFW], BF16)
                  ring = WRing(ph, "p1_w", DC, 4)
                  rope = sb(ph, "p1_rope", [128, 2, TT], F32)
                  sqh = sb(ph, "p1_sqh", [128, TT], BF16)
                  qg = sb(ph, "p1_qg", [128, TT], BF16)
                  rsh = sb(ph, "p1_rsh", [128, TT], F32)
                  rstdh = sb(ph, "p1_rstdh", [128, TT], F32)
                  t1b = sb(ph, "p1_t1", [128, TT], F32)
                  t2b = sb(ph, "p1_t2", [128, TT], F32)
                  qo = [sb(ph, f"p1_qo{i}", [128, TT], BF16) for i in range(2)]
                  vo = [sb(ph, f"p1_vo{i}", [128, 512], BF16) for i in range(2)]
                  go = [sb(ph, f"p1_go{i}", [128, 4, TT], BF16) for i in range(2)]
                  T.dma(SP, wvf.t[:], wbf[("wvf", l)].t.ap().rearrange("p (k n) -> p k n", n=VFW), "p1_wvf", [wbf[("wvf", l)]], [wvf])
                  nq = nv = ng = 0
                  frc = TPC // cf.NFC
                  f_ag = [0]
                  f_cp = [0]

                  def f_copy(j):
                      for rr in range(4):
                          p0 = 16 + rr * TPC + j * frc
                          T.dma(POOL, fpos.t.ap()[p0:p0 + frc, :], fout[l].t.ap()[(j * 4 + rr) * frc:(j * 4 + rr + 1) * frc, :], "fpos_a", [fout[l]], [fpos])

                  def f_progress(done_tokens, flush=False):
                      while f_ag[0] < cf.NFC and (f_ag[0] + 1) * frc <= done_tokens:
                          j = f_ag[0]
                          T.coll("AllGather", GRP, fin[l], fout[l], f"f{l}", shared=(f"f{l}", cf.NFC),
                                 in_ap=fin[l].t.ap()[j * frc:(j + 1) * frc, :], out_ap=fout[l].t.ap()[j * 4 * frc:(j + 1) * 4 * frc, :])
                          f_ag[0] += 1
                          while f_cp[0] < f_ag[0] - 1:
                              f_copy(f_cp[0])
                              f_cp[0] += 1
                      if flush:
                          while f_cp[0] < f_ag[0]:
                              f_copy(f_cp[0])
                              f_cp[0] += 1

                  for (c0, n, _) in tiles:
                      meta = (n == 16)
                      T.dma(POOL, xt.t[:, :, 0:n], xa.t.ap()[:, c0:c0 + n].rearrange("(c p) n -> p c n", p=128), xt.name, [xa], [xt])
                      T.dma(POOL, rope.t[:, :, 0:n], rope_in.ap().rearrange("p (a n) -> p a n", a=2)[:, :, c0:c0 + n], "p1_rope", [B_rope], [rope])
                      ck('p1x')
                      rmsnorm(xt, n, cf.V_GMIX, sq, hT, rs, rstd, bk(6), BK[6])
                      ck('p1a')
                      def head_mm(hd):
                          w = ring.load(wbf[("wqk", l)], hd, DC)
                          pa, pab = bk(hd % 2), BK[hd % 2]
                          T.mm([(lambda c=c: nc.tensor.matmul(pa[:, 0:n], w.t[:, c, :], hT.t[:, c, 0:n], start=(c == 0), stop=(c == DC - 1)))
                                for c in range(DC)], [w, hT], [pab])

                      def head_p1(hd):
                          pa, pab = bk(hd % 2), BK[hd % 2]
                          gcol = cf.V_QN if hd < H else cf.V_KN
                          T.op(ACT, lambda: nc.scalar.activation(out=sqh.t[:, 0:n], in_=pa[:, 0:n], func=AF.Square), [pab], [sqh])
                          T.op(ACT, lambda: nc.scalar.activation(out=qg.t[:, 0:n], in_=pa[:, 0:n], func=AF.Copy, scale=vec.t[:, gcol:gcol + 1]), [pab, vec], [qg])

                      def head_p2(hd, q_o):
                          T.mm([lambda: nc.tensor.matmul(bk(4)[:, 0:n], ones.t[:, :], sqh.t[:, 0:n], start=True, stop=True)], [ones, sqh], [BK[4]])
                          T.mm([lambda: nc.tensor.matmul(bk(5)[:, 0:n], rotT.t[:, :], qg.t[:, 0:n], start=True, stop=True)], [rotT, qg], [BK[5]])
                          T.op(ACT, lambda: nc.scalar.activation(out=rsh.t[:, 0:n], in_=bk(4)[:, 0:n], func=AF.Sqrt, bias=float(cf.EPS), scale=1.0 / 128), [BK[4]], [rsh])
                          T.op(DVE, lambda: nc.vector.reciprocal(out=rstdh.t[:, 0:n], in_=rsh.t[:, 0:n]), [rsh], [rstdh])
                          T.op(DVE, lambda: nc.vector.tensor_tensor(out=t1b.t[:, 0:n], in0=qg.t[:, 0:n], in1=rope.t[:, 0, 0:n], op=ALU.mult), [qg, rope], [t1b])
                          T.op(DVE, lambda: nc.vector.tensor_tensor(out=t2b.t[:, 0:n], in0=bk(5)[:, 0:n], in1=rope.t[:, 1, 0:n], op=ALU.mult), [BK[5], rope], [t2b])
                          T.op(DVE, lambda: nc.vector.tensor_tensor(out=t1b.t[:, 0:n], in0=t1b.t[:, 0:n], in1=t2b.t[:, 0:n], op=ALU.add), [t1b, t2b], [t1b])
                          T.op(DVE, lambda: nc.vector.tensor_tensor(out=q_o.t[:, 0:n], in0=t1b.t[:, 0:n], in1=rstdh.t[:, 0:n], op=ALU.mult), [t1b, rstdh], [q_o])
                          if hd < H:
                              T.dma(POOL, qT_d.t.ap()[hd * 128:(hd + 1) * 128, c0:c0 + n], q_o.t[:, 0:n], q_o.name, [q_o], [qT_d])
                          else:
                              kh = hd - H
                              if meta:
                                  T.dma(POOL, kmeta.t.ap()[kh * 128:(kh + 1) * 128, :], q_o.t[:, 0:n], q_o.name, [q_o], [kmeta])
                              else:
                                  T.dma(POOL, kin[l].t.ap()[kh * 128:(kh + 1) * 128, c0 - 16:c0 - 16 + n], q_o.t[:, 0:n], q_o.name, [q_o], [kin[l]])

                      head_mm(0)
                      head_p1(0)
                      for hd in range(1, NQK):
                          head_mm(hd)
                          head_p2(hd - 1, qo[nq % 2])
                          nq += 1
                          head_p1(hd)
                      head_p2(NQK - 1, qo[nq % 2])
                      nq += 1
                      ck('p1b')
                      ntb = max(1, n // 128)
                      for tb in range(ntb):
                          tn = min(128, n)
                          cb0 = 0
                          while cb0 < VFW:
                              if cb0 < KVW:
                                  cw = KVW
                              else:
                                  cw = min(512, VFW - cb0)
                              pv, pvb = bk(2 + nv % 2), BK[2 + nv % 2]
                              T.mm([(lambda c=c: nc.tensor.matmul(pv[0:tn, 0:cw], hT.t[:, c, tb * 128:tb * 128 + tn], wvf.t[:, c, cb0:cb0 + cw],
                                                                  start=(c == 0), stop=(c == DC - 1))) for c in range(DC)], [hT, wvf], [pvb])
                              v_o = vo[nv % 2]
                              nv += 1
                              T.op(ACT, lambda: nc.scalar.copy(out=v_o.t[0:tn, 0:cw], in_=pv[0:tn, 0:cw]), [pvb], [v_o])
                              if cb0 < KVW:
                                  dst, dm = (vmeta, vmeta.t.ap()[:, :]) if meta else (
                                      vin[l], vin[l].t.ap().rearrange("(h t) d -> t h d", h=KVH)[c0 - 16 + tb * 128:c0 - 16 + tb * 128 + tn, :, :])
                              else:
                                  f0 = cb0 - KVW
                                  dst, dm = (fmeta, fmeta.t.ap()[:, f0:f0 + cw]) if meta else (fin[l], fin[l].t.ap()[c0 - 16 + tb * 128:c0 - 16 + tb * 128 + tn, f0:f0 + cw])
                              srcv = v_o.t[0:tn, 0:cw]
                              if cb0 < KVW and not meta:
                                  srcv = srcv.rearrange("p (h d) -> p h d", h=KVH)
                              T.dma(POOL, dm, srcv, v_o.name, [v_o], [dst])
                              cb0 += cw
                      ck('p1c')
                      for gc in range(2 * DC):
                          w = ring.load(wbf[("wg", l)], gc, DC)
                          pa, pab = bk(gc % 2), BK[gc % 2]
                          T.mm([(lambda c=c: nc.tensor.matmul(pa[:, 0:n], w.t[:, c, :], hT.t[:, c, 0:n], start=(c == 0), stop=(c == DC - 1)))
                                for c in range(DC)], [w, hT], [pab])
                          g_o = go[(ng // 4) % 2]
                          T.op(ACT, lambda: nc.scalar.activation(out=g_o.t[:, gc % 4, 0:n], in_=pa[:, 0:n], func=AF.Sigmoid,
                                                                 bias=vec.t[:, cf.V_BG + gc:cf.V_BG + gc + 1], scale=1.0), [pab, vec], [g_o])
                          ng += 1
                          if gc % 4 == 3:
                              g4 = gc // 4
                              T.dma(POOL, gT_d.t.ap()[g4 * 512:(g4 + 1) * 512, c0:c0 + n].rearrange("(a p) n -> p a n", p=128),
                                    g_o.t[:, :, 0:n], g_o.name, [g_o], [gT_d])
              T.barrier()
              ck('p1')
              ag_chunks(kin[l], kout[l], KVW, KVH, f"k{l}")
              ck('agk')
              ag_chunks(vin[l], vout[l], KVH * TPC, KVH, f"v{l}")
              ck('agv')
              ag_chunks(fin[l], fout[l], TPC, cf.NFC, f"f{l}")
              ck('ag1')

              P2B = cf.P2B
              with contextlib.ExitStack() as ph:
                  T.dma(POOL, fpos.t.ap()[0:16, :], fmeta.t.ap(), "fpos_a", [fmeta], [fpos])
                  frc_ = TPC // cf.NFC
                  for j in range(cf.NFC):
                      for rr in range(4):
                          p0 = 16 + rr * TPC + j * frc_
                          T.dma(POOL, fpos.t.ap()[p0:p0 + frc_, :], fout[l].t.ap()[(j * 4 + rr) * frc_:(j * 4 + rr + 1) * frc_, :], "fpos_a", [fout[l]], [fpos])
                  Z = sb(ph, "f_Z", [N1, N2, FCH], BF16)
                  PZ = N2 // 4 if N2 % 4 == 0 else N2 // 2
                  zs = [sb(ph, f"f_zs{i}", [N1, PZ, FCH], BF16) for i in range(2)]
                  t1s = sb(ph, "f_t1", [N1, 2 * N1], BF16)
                  tws = sb(ph, "f_tw", [N1, 2, N2], F32)
                  wfr = sb(ph, "f_wfr", [128, cf.FG, D], BF16)
                  wst = [sb(ph, f"f_wst{i}", [128, 512], BF16) for i in range(2)]
                  fa = sb(ph, "f_a", [N1, 512], F32)
                  fb = sb(ph, "f_b", [N1, 512], F32)
                  yo = [sb(ph, f"f_yo{i}", [N1, 2, 512], BF16) for i in range(2)]
                  T.dma(POOL, t1s.t[:], t1_in.ap(), "f_t1", [B_t1], [t1s])
                  T.dma(POOL, tws.t[:], tw_in.ap().rearrange("p (a n) -> p a n", a=2), "f_tw", [B_tw], [tws])
                  T.dma(SP, wfr.t[:], wbf[("wfr", l)].t.ap().rearrange("p (g n) -> p g n", g=cf.FG), "f_wfr", [wbf[("wfr", l)]], [wfr])
                  nw = 0
                  for g in range(cf.FG):
                      for pq in range(2):
                          kc = (g // FCB) * (2 * FCB) + pq * FCB + (g % FCB)
                          for nb in range(D // 512 if D >= 512 else 1):
                              nbw = min(512, D)
                              pw, pwb = bk(nw % 2), BK[nw % 2]
                              T.mm([lambda: nc.tensor.matmul(pw[:, 0:nbw], csm.t[:, pq, :], wfr.t[:, g, nb * nbw:(nb + 1) * nbw], start=True, stop=True)],
                                   [csm, wfr], [pwb])
                              ws_ = wst[nw % 2]
                              nw += 1
                              T.op(ACT, lambda: nc.scalar.copy(out=ws_.t[:, 0:nbw], in_=pw[:, 0:nbw]), [pwb], [ws_])
                              T.dma(POOL, wcs_d[l].t.ap()[nb * nbw:(nb + 1) * nbw, kc * 128:(kc + 1) * 128].rearrange("(j p) n -> p j n", p=128),
                                    ws_.t[:, 0:nbw].rearrange("p (j n) -> p j n", n=128), ws_.name, [ws_], [wcs_d[l]])
                  fview = fpos.t.ap().rearrange("(a b) c -> a b c", b=N2)
                  nz = 0
                  for pz in range(N2 // PZ):
                      for q in range(4):
                          z_ = zs[nz % 2]
                          nz += 1
                          T.dma(SP, z_.t[:], fview[:, pz * PZ:(pz + 1) * PZ, q * FCH:(q + 1) * FCH], z_.name, [fpos], [z_])
                          if q == 0:
                              T.op(DVE, lambda: nc.vector.tensor_scalar(out=Z.t[:, pz * PZ:(pz + 1) * PZ, :], in0=z_.t[:], scalar1=selb.t[0:N1, 9:10], scalar2=None, op0=ALU.mult),
                                   [z_, selb], [Z])
                          else:
                              T.op(DVE, lambda: nc.vector.scalar_tensor_tensor(out=Z.t[:, pz * PZ:(pz + 1) * PZ, :], in0=z_.t[:], scalar=selb.t[0:N1, 9 + q:10 + q],
                                                                              in1=Z.t[:, pz * PZ:(pz + 1) * PZ, :], op0=ALU.mult, op1=ALU.add), [z_, selb, Z], [Z])
                  Zf = Z.t[:].rearrange("p a c -> p (a c)")
                  nblk = N2 // P2B
                  for b_ in range(nblk):
                      pr, prb = bk(2 * (b_ % 2)), BK[2 * (b_ % 2)]
                      pi, pib = bk(2 * (b_ % 2) + 1), BK[2 * (b_ % 2) + 1]
                      T.mm([lambda: nc.tensor.matmul(pr[0:N1, :], t1s.t[:, 0:N1], Zf[:, b_ * 512:(b_ + 1) * 512], start=True, stop=True)], [t1s, Z], [prb])
                      T.mm([lambda: nc.tensor.matmul(pi[0:N1, :], t1s.t[:, N1:2 * N1], Zf[:, b_ * 512:(b_ + 1) * 512], start=True, stop=True)], [t1s, Z], [pib])
                      y_ = yo[b_ % 2]
                      tcb = tws.t[:, 0, b_ * P2B:(b_ + 1) * P2B].unsqueeze(2).to_broadcast([N1, P2B, FCH])
                      tsb = tws.t[:, 1, b_ * P2B:(b_ + 1) * P2B].unsqueeze(2).to_broadcast([N1, P2B, FCH])
                      v3 = lambda ap: ap.rearrange("p (a c) -> p a c", c=FCH)
                      T.op(DVE, lambda: nc.vector.tensor_tensor(out=v3(fa.t[:, :]), in0=v3(pr[0:N1, :]), in1=tcb, op=ALU.mult), [prb, tws], [fa])
                      T.op(DVE, lambda: nc.vector.tensor_tensor(out=v3(fb.t[:, :]), in0=v3(pi[0:N1, :]), in1=tsb, op=ALU.mult), [pib, tws], [fb])
                      T.op(DVE, lambda: nc.vector.tensor_tensor(out=y_.t[:, 0, :], in0=fa.t[:, :], in1=fb.t[:, :], op=ALU.subtract), [fa, fb], [y_])
                      T.op(DVE, lambda: nc.vector.tensor_tensor(out=v3(fa.t[:, :]), in0=v3(pr[0:N1, :]), in1=tsb, op=ALU.mult), [prb, tws], [fa])
                      T.op(DVE, lambda: nc.vector.tensor_tensor(out=v3(fb.t[:, :]), in0=v3(pi[0:N1, :]), in1=tcb, op=ALU.mult), [pib, tws], [fb])
                      T.op(DVE, lambda: nc.vector.tensor_tensor(out=y_.t[:, 1, :], in0=fa.t[:, :], in1=fb.t[:, :], op=ALU.add), [fa, fb], [y_])
                      ydv = yd.t.ap().rearrange("r k a c -> k r (a c)")
                      T.dma(POOL, ydv[:, :, b_ * 512:(b_ + 1) * 512], y_.t[:], y_.name, [y_], [yd])
              T.barrier()
              ck('f1')
              with contextlib.ExitStack() as ph:
                  t3s = sb(ph, "f_t3", [N2C, 2, 2, 2 * N2], BF16)
                  T.dma(POOL, t3s.t[:], t3_in.ap().rearrange("p (r k n) -> p r k n", r=2, k=2), "f_t3", [B_t3], [t3s])
                  yt = [[sb(ph, f"f_yt{ri}{ch}", [N2C, N1, 128], BF16) for ch in range(2)] for ri in range(2)]
                  pqs = sb(ph, "f_pqs", [128, 2, L], BF16)
                  for cb in range(FCB):
                      for ri in range(2):
                          for ch in range(2):
                              src = yd.t.ap()[ri].rearrange("k a c -> a k c")[ch * N2C:(ch + 1) * N2C, :, cb * 128:(cb + 1) * 128]
                              T.dma(SP, yt[ri][ch].t[:], src, yt[ri][ch].name, [yd], [yt[ri][ch]])
                      for k1 in range(N1):
                          po, pob = bk(4 + k1 % 2), BK[4 + k1 % 2]
                          fns = []
                          idx = 0
                          for ri in range(2):
                              for ch in range(2):
                                  fns.append(lambda ri=ri, ch=ch, idx=idx: nc.tensor.matmul(po[:, 0:2 * N2], yt[ri][ch].t[:, k1, :], t3s.t[:, ri, ch, :],
                                                                                           start=(idx == 0), stop=(idx == 3)))
                                  idx += 1
                          T.mm(fns, [yt[0][0], yt[0][1], yt[1][0], yt[1][1], t3s], [pob])
                          dst = pqs.t[:, :, k1:k1 + N1 * (N2 - 1) + 1:N1]
                          T.op(ACT, lambda: nc.scalar.copy(out=dst, in_=po[:, 0:2 * N2].rearrange("p (a n) -> p a n", a=2)), [pob], [pqs])
                      for j in range(cf.NPC):
                          T.dma(POOL, pqin[l].t.ap()[j * 2 * FCH:(j + 1) * 2 * FCH, :].rearrange("(a c) n -> c a n", a=2)[cb * 128:(cb + 1) * 128, :, :],
                                pqs.t[:, :, j * cf.LC:(j + 1) * cf.LC], "f_pqs", [pqs], [pqin[l]])
              T.barrier()
              ck('f3')
              ag_chunks(pqin[l], pqout[l], cf.NPC * 2 * FCH, cf.NPC, f"pq{l}")
              ck('pq')

              scale = 1.0 / math.sqrt(128.0)
              NKC = 4 * TPC // 128
              with contextlib.ExitStack() as ph:
                  KT = sb(ph, "a_KT", [128, 4 * TPC + 16], BF16)
                  Vt = sb(ph, "a_Vt", [128, NKC + 1, 132], BF16)
                  QT = [sb(ph, f"a_QT{i}", [128, TT], BF16) for i in range(2)]
                  Pb = [sb(ph, f"a_P{i}", [128, 2, TT], BF16) for i in range(2)]
                  rinv = sb(ph, "a_rinv", [128, 4], F32)
                  on = [sb(ph, f"a_on{i}", [128, 128], BF16) for i in range(2)]
                  aT = [sb(ph, f"a_aT{i}", [128, TT], BF16) for i in range(2)]
                  na = 0
                  if l + 1 < DEPTH:
                      weight_ags(l + 1)
                  for kvh in range(KVH):
                      for rr in range(4):
                          T.dma(SP, KT.t[:, rr * TPC:(rr + 1) * TPC], kout[l].t.ap()[(kvh * 4 + rr) * 128:(kvh * 4 + rr + 1) * 128, :], "a_KT", [kout[l]], [KT])
                      T.dma(SP, KT.t[:, 4 * TPC:4 * TPC + 16], kmeta.t.ap()[kvh * 128:(kvh + 1) * 128, :], "a_KT", [kmeta], [KT])
                      T.dma(SP, Vt.t[:, 0:NKC, 0:128], vout[l].t.ap()[kvh * 4 * TPC:(kvh + 1) * 4 * TPC, :].rearrange("(c p) d -> p c d", p=128), "a_Vt", [vout[l]], [Vt])
                      T.dma(SP, Vt.t[0:16, NKC, 0:128], vmeta.t.ap()[:, kvh * 128:(kvh + 1) * 128], "a_Vt", [vmeta], [Vt])
                      T.op(DVE, lambda: nc.vector.memset(Vt.t[:, :, 128:129], 1.0), [], [Vt])
                      items = [(2 * j, 2) for j in range(NKC // 2)] + [(NKC, 1)]
                      for (c0, n, _) in tiles:
                          nqb = max(1, n // 128)
                          qn_ = min(128, n)
                          for gq in range(G):
                              hd = kvh * G + gq
                              Q = QT[na % 2]
                              a_T = aT[na % 2]
                              na += 1
                              T.dma(POOL, Q.t[:, 0:n], qT_d.t.ap()[hd * 128:(hd + 1) * 128, c0:c0 + n], Q.name, [qT_d], [Q])

                              def qk(it, slot):
                                  kc0, cnt = it
                                  for u in range(cnt):
                                      kk = 16 if kc0 == NKC else 128
                                      pS = bk(2 * slot + u)
                                      T.mm([lambda: nc.tensor.matmul(pS[0:kk, 0:n], KT.t[:, (kc0 + u) * 128:(kc0 + u) * 128 + kk], Q.t[:, 0:n], start=True, stop=True)],
                                           [KT, Q], [BK[2 * slot + u]])

                              qk(items[0], 0)
                              for ii, it in enumerate(items):
                                  slot = ii % 2
                                  if ii + 1 < len(items):
                                      qk(items[ii + 1], (ii + 1) % 2)
                                  kc0, cnt = it
                                  kk = 16 if kc0 == NKC else 128
                                  Pt = Pb[slot]
                                  src = (ps0 if slot == 0 else ps1)[0:kk, :].rearrange("p (a n) -> p a n", a=2)[:, 0:cnt, 0:n]
                                  T.op(ACT, lambda: nc.scalar.activation(out=Pt.t[0:kk, 0:cnt, 0:n], in_=src, func=AF.Exp, scale=scale),
                                       [BK[2 * slot], BK[2 * slot + 1]], [Pt])
                                  fns = []
                                  for u in range(cnt):
                                      for qb in range(nqb):
                                          ob = 4 + qb // 2
                                          oc0 = (qb % 2) * 256
                                          first = (ii == 0 and u == 0 and qb % 2 == 0)
                                          lastm = (ii == len(items) - 1 and u == cnt - 1)
                                          fns.append(lambda u=u, qb=qb, ob=ob, oc0=oc0, first=first, lastm=lastm: nc.tensor.matmul(
                                              bk(ob)[0:qn_, oc0:oc0 + 129], Pt.t[0:kk, u, qb * 128:qb * 128 + qn_], Vt.t[0:kk, kc0 + u, 0:129],
                                              start=first, stop=lastm, skip_group_check=True))
                                  T.mm(fns, [Pt, Vt], [BK[4], BK[5]])
                              for qb in range(nqb):
                                  ob = 4 + qb // 2
                                  oc0 = (qb % 2) * 256
                                  T.op(DVE, lambda: nc.vector.reciprocal(out=rinv.t[0:qn_, qb:qb + 1], in_=bk(ob)[0:qn_, oc0 + 128:oc0 + 129]), [BK[ob]], [rinv])
                                  o_n = on[qb % 2]
                                  T.op(DVE, lambda: nc.vector.tensor_scalar(out=o_n.t[0:qn_, :], in0=bk(ob)[0:qn_, oc0:oc0 + 128], scalar1=rinv.t[0:qn_, qb:qb + 1], scalar2=None, op0=ALU.mult),
                                       [BK[ob], rinv], [o_n])
                                  T.mm([lambda: nc.tensor.transpose(pst[:, qb * 128:qb * 128 + qn_], o_n.t[0:qn_, :], ident.t[0:qn_, 0:qn_])], [o_n, ident], [BK[7]])
                              T.op(DVE, lambda: nc.vector.tensor_copy(out=a_T.t[:, 0:n], in_=pst[:, 0:n]), [BK[7]], [a_T])
                              T.dma(POOL, aT_d.t.ap()[hd * 128:(hd + 1) * 128, c0:c0 + n], a_T.t[:, 0:n], a_T.name, [a_T], [aT_d])
              T.barrier()
              ck('attn')
              xh_st = contextlib.ExitStack()
              xh = sb(xh_st, "xh", [128, DC, cf.NH], F32)
              hsb = sb(xh_st, "hsb", [128, 2, DC], F32)
              with contextlib.ExitStack() as ph:
                  xt = sb(ph, "m_xt", [128, DC, TT], F32, key="xt")
                  at = sb(ph, "m_at", [128, H, TT], BF16)
                  pq = sb(ph, "m_pq", [128, PQC, TT], BF16)
                  pqc = [sb(ph, f"m_pqc{i}", [128, PQC, TT], BF16) for i in range(2)]
                  gt = sb(ph, "m_gt", [128, 2 * DC, TT], BF16)
                  mg = sb(ph, "m_mg", [128, DC, TT], BF16)
                  ta = sb(ph, "m_ta", [128, TT], F32)
                  tb_ = sb(ph, "m_tb", [128, TT], F32)
                  xo = xt
                  ring = WRing(ph, "m_w", max(DC, PQC, H), 4)
                  T.op(DVE, lambda: nc.vector.memset(xh.t[:], 0.0), [], [xh])
                  npc = 0
                  for (c0, n, hj) in tiles:
                      meta = (n == 16)
                      T.dma(POOL, xt.t[:, :, 0:n], xa.t.ap()[:, c0:c0 + n].rearrange("(c p) n -> p c n", p=128), xt.name, [xa], [xt])
                      T.dma(POOL, at.t[:, :, 0:n], aT_d.t.ap()[:, c0:c0 + n].rearrange("(c p) n -> p c n", p=128), "m_at", [aT_d], [at])
                      T.dma(POOL, gt.t[:, :, 0:n], gT_d.t.ap()[:, c0:c0 + n].rearrange("(c p) n -> p c n", p=128), "m_gt", [gT_d], [gt])
                      def pq_load(dst_t, pos0, nn, key, dbuf):
                          a = pos0
                          while a < pos0 + nn:
                              j = a // cf.LC
                              b = min(pos0 + nn, (j + 1) * cf.LC)
                              v_ = pqout[l].t.ap()[j * 8 * FCH:(j + 1) * 8 * FCH, :].rearrange("(c p) n -> p c n", p=128)
                              T.dma(POOL, dst_t[:, :, a - pos0:b - pos0], v_[:, :, a - j * cf.LC:b - j * cf.LC], key, [pqout[l]], [dbuf])
                              a = b
                      if meta:
                          pq_load(pq.t, 0, 16, "m_pq", pq)
                      else:
                          for q in range(4):
                              pc = pqc[npc % 2]
                              npc += 1
                              pq_load(pc.t, q * TPC + c0, n, pc.name, pc)
                              if q == 0:
                                  T.op(DVE, lambda: nc.vector.tensor_scalar(out=pq.t[:, :, 0:n], in0=pc.t[:, :, 0:n], scalar1=selb.t[:, 9:10], scalar2=None, op0=ALU.mult), [pc, selb], [pq])
                              else:
                                  T.op(DVE, lambda: nc.vector.scalar_tensor_tensor(out=pq.t[:, :, 0:n], in0=pc.t[:, :, 0:n], scalar=selb.t[:, 9 + q:10 + q], in1=pq.t[:, :, 0:n],
                                                                                  op0=ALU.mult, op1=ALU.add), [pc, selb, pq], [pq])
                      for oc in range(DC):
                          w1 = ring.load(wbf[("wab", l)], oc, H)
                          w2 = ring.load(wcs_d[l], oc, PQC)
                          ba, bb = 2 * (oc % 2), 2 * (oc % 2) + 1
                          T.mm([(lambda c=c: nc.tensor.matmul(bk(ba)[:, 0:n], w1.t[:, c, :], at.t[:, c, 0:n], start=(c == 0), stop=(c == H - 1))) for c in range(H)], [w1, at], [BK[ba]])
                          T.mm([(lambda c=c: nc.tensor.matmul(bk(bb)[:, 0:n], w2.t[:, c, :], pq.t[:, c, 0:n], start=(c == 0), stop=(c == PQC - 1))) for c in range(PQC)], [w2, pq], [BK[bb]])
                          T.op(DVE, lambda: nc.vector.tensor_tensor(out=ta.t[:, 0:n], in0=bk(ba)[:, 0:n], in1=gt.t[:, oc, 0:n], op=ALU.mult), [BK[ba], gt], [ta])
                          T.op(DVE, lambda: nc.vector.tensor_tensor(out=tb_.t[:, 0:n], in0=bk(bb)[:, 0:n], in1=gt.t[:, DC + oc, 0:n], op=ALU.mult), [BK[bb], gt], [tb_])
                          T.op(DVE, lambda: nc.vector.tensor_tensor(out=mg.t[:, oc, 0:n], in0=ta.t[:, 0:n], in1=tb_.t[:, 0:n], op=ALU.add), [ta, tb_], [mg])
                      for oc in range(DC):
                          w = ring.load(wbf[("wout", l)], oc, DC)
                          pb_, pbb = bk(4 + oc % 2), BK[4 + oc % 2]
                          T.mm([(lambda c=c: nc.tensor.matmul(pb_[:, 0:n], w.t[:, c, :], mg.t[:, c, 0:n], start=(c == 0), stop=(c == DC - 1))) for c in range(DC)], [w, mg], [pbb])
                          T.op(DVE, lambda: nc.vector.tensor_tensor(out=xo.t[:, oc, 0:n], in0=pb_[:, 0:n], in1=xt.t[:, oc, 0:n], op=ALU.add), [pbb, xt], [xo])
                      T.dma(POOL, xm.t.ap()[:, c0:c0 + n].rearrange("(c p) n -> p c n", p=128), xo.t[:, :, 0:n], "xo", [xo], [xm])
                      if meta:
                          T.op(DVE, lambda: nc.vector.tensor_copy(out=xh.t[:, :, 0], in_=xo.t[:, :, 15]), [xo], [xh])
                      else:
                          if hj + 1 < NT:
                              T.op(DVE, lambda: nc.vector.tensor_copy(out=xh.t[:, :, 2 * (hj + 1)], in_=xo.t[:, :, n - 1]), [xo], [xh])
                          else:
                              T.op(DVE, lambda: nc.vector.tensor_copy(out=hsb.t[:, 1, :], in_=xo.t[:, :, n - 1]), [xo], [hsb])
                          if hj >= 1:
                              T.op(DVE, lambda: nc.vector.tensor_copy(out=xh.t[:, :, 2 * (hj - 1) + 1], in_=xo.t[:, :, 0]), [xo], [xh])
                          else:
                              T.op(DVE, lambda: nc.vector.tensor_copy(out=hsb.t[:, 0, :], in_=xo.t[:, :, 0]), [xo], [hsb])
                  T.dma(POOL, hin[l].t.ap().rearrange("p (a c) -> p a c", a=2), hsb.t[:], "hsb", [hsb], [hin[l]])
              T.barrier()
              T.coll("AllGather", GRP, hin[l], hout[l], f"h{l}")
              ck('halo')
              with contextlib.ExitStack() as ph:
                  hb = sb(ph, "n_hb", [128, 4, 2, DC], F32)
                  T.dma(POOL, hb.t[:], hout[l].t.ap().rearrange("(r p) (a c) -> p r a c", p=128, a=2), "n_hb", [hout[l]], [hb])
                  jl, jr, jm = 0, 2 * (NT - 1) + 1, 2 * NT + 1
                  T.op(DVE, lambda: nc.vector.tensor_scalar(out=xh.t[:, :, jl], in0=xh.t[:, :, jl], scalar1=selb.t[:, 0:1], scalar2=None, op0=ALU.mult), [xh, selb], [xh])
                  for j in range(4):
                      T.op(DVE, lambda j=j: nc.vector.scalar_tensor_tensor(out=xh.t[:, :, jl], in0=hb.t[:, j, 1, :], scalar=selb.t[:, 1 + j:2 + j], in1=xh.t[:, :, jl],
                                                                          op0=ALU.mult, op1=ALU.add), [hb, selb, xh], [xh])
                      T.op(DVE, lambda j=j: nc.vector.scalar_tensor_tensor(out=xh.t[:, :, jr], in0=hb.t[:, j, 0, :], scalar=selb.t[:, 5 + j:6 + j], in1=xh.t[:, :, jr],
                                                                          op0=ALU.mult, op1=ALU.add), [hb, selb, xh], [xh])
                  T.op(DVE, lambda: nc.vector.tensor_copy(out=xh.t[:, :, jm], in_=hb.t[:, 0, 0, :]), [hb], [xh])
                  NH = cf.NH
                  sqh_ = sb(ph, "n_sqh", [128, DC, NH], BF16)
                  h2h = sb(ph, "n_h2h", [128, DC, NH], BF16)
                  rsx = sb(ph, "n_rsx", [128, NH], F32)
                  rstx = sb(ph, "n_rstx", [128, NH], F32)
                  ugh = sb(ph, "n_ugh", [128, FFC, NH], F32)
                  xt = sb(ph, "n_xt", [128, DC, TT], F32, key="xt")
                  sq = sb(ph, "n_sq", [128, DC, TT], BF16)
                  h2 = sb(ph, "n_h2", [128, DC, TT], BF16)
                  rs = sb(ph, "n_rs", [128, TT], F32)
                  rstd = sb(ph, "n_rstd", [128, TT], F32)
                  cc = sb(ph, "n_cc", [128, TT], F32)
                  sg = sb(ph, "n_sg", [128, TT], F32)
                  uT = sb(ph, "n_uT", [128, FFC, TT], BF16)
                  xo = xt
                  ring = WRing(ph, "n_w", DC, 4)
                  ringd = WRing(ph, "n_wd", FFC, 2)
                  rmsnorm(xh, NH, cf.V_GFFN, sqh_, h2h, rsx, rstx, bk(6), BK[6])
                  for fc in range(FFC):
                      w = ring.load(wbf[("wup", l)], fc, DC)
                      T.mm([(lambda c=c: nc.tensor.matmul(bk(fc % 2)[:, 0:NH], w.t[:, c, :], h2h.t[:, c, :], start=(c == 0), stop=(c == DC - 1))) for c in range(DC)], [w, h2h], [BK[fc % 2]])
                      T.op(ACT, lambda: nc.scalar.copy(out=ugh.t[:, fc, :], in_=bk(fc % 2)[:, 0:NH]), [BK[fc % 2]], [ugh])
                  wv = lambda fc, k: vec.t[:, cf.V_WCV + 3 * fc + k:cf.V_WCV + 3 * fc + k + 1]
                  for (c0, n, hj) in tiles:
                      meta = (n == 16)
                      if last and meta:
                          continue
                      T.dma(POOL, xt.t[:, :, 0:n], xm.t.ap()[:, c0:c0 + n].rearrange("(c p) n -> p c n", p=128), xt.name, [xm], [xt])
                      rmsnorm(xt, n, cf.V_GFFN, sq, h2, rs, rstd, bk(6), BK[6])
                      for fc in range(FFC):
                          wg_ = ring.load(wbf[("wup", l)], fc, DC)
                          wv_ = ring.load(wbf[("wup", l)], FFC + fc, DC)
                          pg, pgb = bk(2 * (fc % 2)), BK[2 * (fc % 2)]
                          pu, pub = bk(2 * (fc % 2) + 1), BK[2 * (fc % 2) + 1]
                          T.mm([(lambda c=c: nc.tensor.matmul(pg[:, 0:n], wg_.t[:, c, :], h2.t[:, c, 0:n], start=(c == 0), stop=(c == DC - 1))) for c in range(DC)], [wg_, h2], [pgb])
                          T.mm([(lambda c=c: nc.tensor.matmul(pu[:, 0:n], wv_.t[:, c, :], h2.t[:, c, 0:n], start=(c == 0), stop=(c == DC - 1))) for c in range(DC)], [wv_, h2], [pub])
                          bcol = vec.t[:, cf.V_BCV + fc:cf.V_BCV + fc + 1]
                          T.op(DVE, lambda: nc.vector.tensor_scalar(out=cc.t[:, 0:n], in0=pg[:, 0:n], scalar1=wv(fc, 1), scalar2=bcol, op0=ALU.mult, op1=ALU.add), [pgb, vec], [cc])
                          T.op(DVE, lambda: nc.vector.scalar_tensor_tensor(out=cc.t[:, 1:n], in0=pg[:, 0:n - 1], scalar=wv(fc, 0), in1=cc.t[:, 1:n], op0=ALU.mult, op1=ALU.add), [pgb, vec, cc], [cc])
                          T.op(DVE, lambda: nc.vector.scalar_tensor_tensor(out=cc.t[:, 0:n - 1], in0=pg[:, 1:n], scalar=wv(fc, 2), in1=cc.t[:, 0:n - 1], op0=ALU.mult, op1=ALU.add), [pgb, vec, cc], [cc])
                          T.op(DVE, lambda: nc.vector.scalar_tensor_tensor(out=cc.t[:, 0:1], in0=ugh.t[:, fc, 2 * hj:2 * hj + 1], scalar=wv(fc, 0), in1=cc.t[:, 0:1], op0=ALU.mult, op1=ALU.add), [ugh, vec, cc], [cc])
                          T.op(DVE, lambda: nc.vector.scalar_tensor_tensor(out=cc.t[:, n - 1:n], in0=ugh.t[:, fc, 2 * hj + 1:2 * hj + 2], scalar=wv(fc, 2), in1=cc.t[:, n - 1:n], op0=ALU.mult, op1=ALU.add), [ugh, vec, cc], [cc])
                          T.op(ACT, lambda: nc.scalar.activation(out=sg.t[:, 0:n], in_=cc.t[:, 0:n], func=AF.Silu), [cc], [sg])
                          T.op(DVE, lambda: nc.vector.tensor_tensor(out=uT.t[:, fc, 0:n], in0=sg.t[:, 0:n], in1=pu[:, 0:n], op=ALU.mult), [sg, pub], [uT])
                      for oc in range(DC):
                          w = ringd.load(wbf[("wdn", l)], oc, FFC)
                          pb_, pbb = bk(4 + oc % 2), BK[4 + oc % 2]
                          T.mm([(lambda c=c: nc.tensor.matmul(pb_[:, 0:n], w.t[:, c, :], uT.t[:, c, 0:n], start=(c == 0), stop=(c == FFC - 1))) for c in range(FFC)], [w, uT], [pbb])
                          T.op(DVE, lambda: nc.vector.tensor_tensor(out=xo.t[:, oc, 0:n], in0=pb_[:, 0:n], in1=xt.t[:, oc, 0:n], op=ALU.add), [pbb, xt], [xo])
                      if last and not meta:
                          T.dma(POOL, yT.ap()[:, c0 - 16:c0 - 16 + n].rearrange("(c p) n -> p c n", p=128), xo.t[:, :, 0:n], "xo", [xo], [B_yT])
                      elif not last:
                          T.dma(POOL, xa.t.ap()[:, c0:c0 + n].rearrange("(c p) n -> p c n", p=128), xo.t[:, :, 0:n], "xo", [xo], [xa])
              xh_st.close()
        except _Stop:
            pass
        T.dead = False
        T.barrier(full=True)
        if getattr(cf, 'endclear', False):
            fin_sem = stack.enter_context(nc.semaphore("s_fin"))
            for eng in (T.pe, T.act, T.dve, T.sp):
                eng.e.sem_inc(fin_sem, 1)
            nc.gpsimd.wait_ge(fin_sem, 4)
            allsems = [e.sem for e in T.engs] + [v[0] for v in T.dsems.values()] + [sv[0] for sv in T.csems] + [fin_sem]
            for sm in allsems:
                nc.gpsimd.sem_clear(sm)
    return nc, T.nsem


_CACHE = {}


def run(cf, inputs):
    in_maps = prep_inputs(cf, **inputs)
    if "nc" not in _CACHE or _CACHE.get("cf") is not cf:
        _CACHE["nc"], nsem = build(cf)
        _CACHE["cf"] = cf
    res = run_bass_kernel_spmd(_CACHE["nc"], in_maps, core_ids=list(range(NCORES)))
    out = np.empty((2, cf.SEQ, cf.D), np.float32)
    for c in range(NCORES):
        b, r = c // 4, c % 4
        out[b, r * cf.TPC:(r + 1) * cf.TPC, :] = res.results[c]["yT"].T
    return out


def kernel(**inputs):
    return run(FULL, inputs)
```

```python
import contextlib
import math
import numpy as np
import concourse.bass as bass
import concourse.mybir as mybir
from concourse.bass_utils import run_bass_kernel_spmd

F32, BF16 = mybir.dt.float32, mybir.dt.bfloat16
AF = mybir.ActivationFunctionType
ALU = mybir.AluOpType
NCORES = 8
GRP = [[0, 1, 2, 3], [4, 5, 6, 7]]
ALL8 = [list(range(8))]


def nchunks_for(n, unit_bytes, limit=1 << 20):
    for k in range(1, n + 1):
        if n % k == 0 and (n // k) * unit_bytes <= limit:
            return k
    raise ValueError


class Cfg:
    def __init__(self, D=2048, SEQ=16384, H=8, KVH=2, FG=8, DFF=5632, TT=512, N1=100, N2=164, DEPTH=2):
        self.D, self.SEQ, self.H, self.KVH, self.FG, self.DFF, self.TT = D, SEQ, H, KVH, FG, DFF, TT
        self.N1, self.N2, self.DEPTH = N1, N2, DEPTH
        self.HD = 128
        self.G = H // KVH
        self.AW = H * 128
        self.KVW = KVH * 128
        self.FW = FG * 128
        self.NMETA = 16
        self.GRIDW = 64
        self.L = SEQ + 16
        assert N1 * N2 == self.L
        self.TPC = SEQ // 4
        self.NT = self.TPC // TT
        self.NCOL = 16 + self.TPC
        self.DC = D // 128
        self.FFC = DFF // 128
        self.FCH = self.FW // 4
        self.FCB = self.FCH // 128
        self.OFF_K = self.AW
        self.OFF_V = self.AW + self.KVW
        self.OFF_F = self.OFF_V + self.KVW
        self.OFF_GA = self.OFF_F + self.FW
        self.INW = self.OFF_GA + 2 * D
        self.EPS = 1e-6
        self.N2C = N2 // 2
        assert self.N2C * 2 == N2 and self.N2C <= 128 and 2 * N2 <= 512 and N1 <= 128
        self.P2B = 512 // self.FCH
        assert N2 % self.P2B == 0
        self.NH = 2 * (self.NT + 1)
        MB = 1 << 20
        self.NFC = nchunks_for(self.TPC, self.FW * 2)
        self.NPC = nchunks_for(self.L, 2 * self.FCH * 2)
        self.LC = self.L // self.NPC
        assert 128 * self.TPC * 2 <= MB
        o = 0
        self.V_GMIX = o; o += self.DC
        self.V_GFFN = o; o += self.DC
        self.V_BG = o; o += 2 * self.DC
        self.V_QN = o; o += 1
        self.V_KN = o; o += 1
        self.V_WCV = o; o += 3 * self.FFC
        self.V_BCV = o; o += self.FFC
        self.NV = o


FULL = Cfg()


def lhsT_layout(W):
    K, N = W.shape
    return np.ascontiguousarray(W.reshape(K // 128, 128, N // 128, 128).transpose(2, 1, 0, 3).reshape(N, K))


def rhs_layout(W):
    K, N = W.shape
    return np.ascontiguousarray(W.reshape(K // 128, 128, N).transpose(1, 0, 2).reshape(128, (K // 128) * N))


WNAMES = ["wqk", "wg", "wvf", "wab", "wfr", "wout", "wup", "wdn"]


def weight_shapes(cf):
    return {
        "wqk": (cf.AW + cf.KVW, cf.D), "wg": (2 * cf.D, cf.D), "wvf": (128, cf.DC * (cf.KVW + cf.FW)),
        "wab": (cf.D, cf.AW), "wfr": (128, cf.FG * cf.D), "wout": (cf.D, cf.D),
        "wup": (2 * cf.DFF, cf.D), "wdn": (cf.D, cf.DFF),
    }


def weight_chunks(cf):
    out = {}
    for n, (R, C) in weight_shapes(cf).items():
        out[n] = nchunks_for(R // 4, C * 2)
    return out


def host_tables(cf):
    f64 = np.float64
    tabs = {}
    ident = np.eye(128, dtype=np.float32)
    rotT = np.zeros((128, 128), np.float32)
    for i in range(128):
        blk = i // 32
        if blk % 2 == 0:
            rotT[i + 32, i] = -1.0
        else:
            rotT[i - 32, i] = 1.0
    N1, N2, L = cf.N1, cf.N2, cf.L
    a1 = 2 * np.pi * np.outer(np.arange(N1), np.arange(N1)).astype(f64) / N1
    t1 = np.concatenate([np.cos(a1), np.sin(a1)], 1).astype(np.float32)
    a2 = 2 * np.pi * np.outer(np.arange(N2), np.arange(N2)).astype(f64) / N2
    C2, S2 = np.cos(a2), np.sin(a2)
    tabR = np.concatenate([C2, S2], 1)
    tabI = np.concatenate([-S2, C2], 1)
    t3 = np.stack([tabR, tabI], 0).reshape(2, 2, cf.N2C, 2 * N2).astype(np.float32)
    at = 2 * np.pi * np.outer(np.arange(N1), np.arange(N2)).astype(f64) / L
    tw = np.stack([np.cos(at), np.sin(at)], 1).astype(np.float32)
    ac = 2 * np.pi * np.outer(np.arange(128), np.arange(128)).astype(f64) / 128
    sc = 1.0 / math.sqrt(L * 128.0)
    cs = np.stack([np.cos(ac) * sc, -np.sin(ac) * sc], 1).astype(np.float32)
    tabs["cmat"] = np.ascontiguousarray(np.concatenate([ident, rotT, cs.reshape(128, 256)], 1))
    tabs["t1"] = t1
    tabs["t3"] = np.ascontiguousarray(t3.transpose(2, 0, 1, 3).reshape(cf.N2C, 4 * 2 * N2))
    tabs["tw"] = np.ascontiguousarray(tw.reshape(N1, 2 * N2))
    return tabs


def core_tables(cf, r):
    inv = 1.0 / (10000.0 ** (np.arange(32, dtype=np.float64) / 32))
    tg = r * cf.TPC + np.arange(cf.TPC)
    rows = (tg // cf.GRIDW).astype(np.float64)
    cols = (tg % cf.GRIDW).astype(np.float64)
    ang = np.zeros((128, cf.NCOL), np.float64)
    ang[0:32, 16:] = inv[:, None] * rows[None]
    ang[32:64, 16:] = inv[:, None] * rows[None]
    ang[64:96, 16:] = inv[:, None] * cols[None]
    ang[96:128, 16:] = inv[:, None] * cols[None]
    rope = np.stack([np.cos(ang), np.sin(ang)], 1).astype(np.float32)
    sel = np.zeros((128, 16), np.float32)
    sel[:, 0] = 1.0 if r == 0 else 0.0
    for j in range(4):
        sel[:, 1 + j] = 1.0 if j == r - 1 else 0.0
        sel[:, 5 + j] = 1.0 if j == r + 1 else 0.0
        sel[:, 9 + j] = 1.0 if j == r else 0.0
    return np.ascontiguousarray(rope.reshape(128, 2 * cf.NCOL)), sel


def prep_inputs(cf, x, meta_tokens, norm_mix, norm_ffn, w_in, b_gate, q_norm, k_norm,
                w_attn_br, w_four, w_out, w_up, w_conv, b_conv, w_down):
    f = lambda a: np.asarray(a, dtype=np.float32)
    x, meta_tokens = f(x), f(meta_tokens)
    w_in, w_attn_br, w_four, w_out, w_up, w_down = map(f, (w_in, w_attn_br, w_four, w_out, w_up, w_down))
    DEPTH = cf.DEPTH
    full = {n: [] for n in WNAMES}
    for l in range(DEPTH):
        full["wqk"].append(lhsT_layout(w_in[l][:, 0:cf.OFF_V]))
        full["wg"].append(lhsT_layout(w_in[l][:, cf.OFF_GA:]))
        full["wvf"].append(rhs_layout(w_in[l][:, cf.OFF_V:cf.OFF_GA]))
        full["wab"].append(lhsT_layout(w_attn_br[l]))
        full["wfr"].append(rhs_layout(w_four[l]))
        full["wout"].append(lhsT_layout(w_out[l]))
        full["wup"].append(lhsT_layout(w_up[l]))
        full["wdn"].append(lhsT_layout(w_down[l]))
    vecs = np.zeros((DEPTH, 128, cf.NV), np.float32)
    for l in range(DEPTH):
        vecs[l, :, cf.V_GMIX:cf.V_GMIX + cf.DC] = f(norm_mix[l]).reshape(cf.DC, 128).T
        vecs[l, :, cf.V_GFFN:cf.V_GFFN + cf.DC] = f(norm_ffn[l]).reshape(cf.DC, 128).T
        vecs[l, :, cf.V_BG:cf.V_BG + 2 * cf.DC] = f(b_gate[l]).reshape(2 * cf.DC, 128).T
        vecs[l, :, cf.V_QN] = f(q_norm[l])
        vecs[l, :, cf.V_KN] = f(k_norm[l])
        wc = f(w_conv[l]).reshape(3, cf.FFC, 128).transpose(2, 1, 0)
        vecs[l, :, cf.V_WCV:cf.V_WCV + 3 * cf.FFC] = wc.reshape(128, 3 * cf.FFC)
        vecs[l, :, cf.V_BCV:cf.V_BCV + cf.FFC] = f(b_conv[l]).reshape(cf.FFC, 128).T
    tabs = host_tables(cf)
    wch = weight_chunks(cf)
    metaT = np.ascontiguousarray(meta_tokens.T)
    in_maps = []
    for c in range(NCORES):
        b, r = c // 4, c % 4
        m = {}
        m["xT"] = np.ascontiguousarray(x[b, r * cf.TPC:(r + 1) * cf.TPC, :].T)
        m["metaT"] = metaT
        for n in WNAMES:
            R = full[n][0].shape[0]
            nch = wch[n]
            rc = R // 4 // nch
            m[n] = np.ascontiguousarray(np.stack(
                [full[n][l].reshape(nch, 4, rc, -1)[:, r].reshape(nch * rc, -1) for l in range(DEPTH)], 0))
        m["vecs"] = vecs
        rope, sel = core_tables(cf, r)
        m["rope"] = rope
        m["sel"] = sel
        for k, v in tabs.items():
            m[k] = v
        in_maps.append(m)
    return in_maps


class _Stop(Exception):
    pass


class Buf:
    def __init__(self, name, t):
        self.name, self.t = name, t
        self.w, self.r = {}, {}


class Eng:
    def __init__(self, name, e, sem):
        self.name, self.e, self.sem = name, e, sem
        self.count = 0
        self.waited = {}


def _merge(d, src):
    for k, (s, v) in src.items():
        if k not in d or d[k][1] < v:
            d[k] = (s, v)


class TR:
    def __init__(self, nc, stack):
        self.nc, self.stack = nc, stack
        mk = lambda n: stack.enter_context(nc.semaphore(n))
        self.pe = Eng("pe", nc.tensor, mk("s_pe"))
        self.act = Eng("act", nc.scalar, mk("s_act"))
        self.dve = Eng("dve", nc.vector, mk("s_dve"))
        self.pool = Eng("pool", nc.gpsimd, mk("s_pool"))
        self.sp = Eng("sp", nc.sync, mk("s_sp"))
        self.engs = [self.pe, self.act, self.dve, self.pool, self.sp]
        self.dsems = {}
        self.csems = []
        self.shsem = {}
        self.nsem = 5
        self.dead = False

    def _sync(self, eng, reads, writes, ignore=None):
        raw = {}
        for b in reads:
            _merge(raw, b.w)
        oth = {}
        for b in writes:
            _merge(oth, b.w)
            _merge(oth, b.r)
        me = id(eng.sem)
        d = dict(raw)
        for k, sv in oth.items():
            if k == me:
                continue
            if k not in d or d[k][1] < sv[1]:
                d[k] = sv
        if eng is self.pe:
            d.pop(me, None)
        if ignore is not None:
            d.pop(ignore, None)
        for k, (s, v) in d.items():
            if eng.waited.get(k, 0) < v:
                eng.e.wait_ge(s, v)
                eng.waited[k] = v

    def _rec(self, ev, reads, writes):
        k, s, v = ev
        for b in reads:
            if k not in b.r or b.r[k][1] < v:
                b.r[k] = (s, v)
        for b in writes:
            if k not in b.w or b.w[k][1] < v:
                b.w[k] = (s, v)

    def op(self, eng, fn, reads=(), writes=()):
        if self.dead:
            return
        self._sync(eng, reads, writes)
        ins = fn()
        eng.count += 1
        ins.then_inc(eng.sem, 1)
        self._rec((id(eng.sem), eng.sem, eng.count), reads, writes)

    def mm(self, fns, reads=(), writes=()):
        if self.dead:
            return
        eng = self.pe
        self._sync(eng, reads, writes)
        ins = None
        for fn in fns:
            ins = fn()
        eng.count += 1
        ins.then_inc(eng.sem, 1)
        self._rec((id(eng.sem), eng.sem, eng.count), reads, writes)

    def dma(self, q, out, in_, key, reads=(), writes=()):
        if self.dead:
            return
        self._sync(q, reads, writes)
        if key not in self.dsems:
            self.nsem += 1
            self.dsems[key] = [self.stack.enter_context(self.nc.semaphore("d_" + key)), 0]
        ent = self.dsems[key]
        ent[1] += 1
        q.e.dma_start(out=out, in_=in_).then_inc(ent[0], 16)
        self._rec((id(ent[0]), ent[0], 16 * ent[1]), reads, writes)

    def coll(self, kind, groups, inb, outb, name, shared=None, in_ap=None, out_ap=None):
        if self.dead:
            return
        q = self.pool
        if shared is not None and shared[0] in self.shsem:
            self._sync(q, [inb], [outb], ignore=id(self.shsem[shared[0]]))
        else:
            self._sync(q, [inb], [outb])
        if shared is None:
            self.nsem += 1
            sem = self.stack.enter_context(self.nc.semaphore("c_" + name))
            val = 1
            self.csems.append((sem, 1))
        else:
            key, val = shared
            if key not in self.shsem:
                self.nsem += 1
                self.shsem[key] = self.stack.enter_context(self.nc.semaphore("c_" + key))
                self.csems.append((self.shsem[key], val))
            sem = self.shsem[key]
        q.e.collective_compute(kind, ALU.bypass, replica_groups=groups,
                               ins=[(in_ap if in_ap is not None else inb.t.ap()).opt()],
                               outs=[(out_ap if out_ap is not None else outb.t.ap()).opt()]).then_inc(sem, 1)
        self._rec((id(sem), sem, val), [inb], [outb])

    def barrier(self, full=False):
        if self.dead:
            return
        evs = [(id(e.sem), e.sem, e.count) for e in self.engs if e.count > 0]
        evs += [(id(s), s, 16 * c) for (s, c) in self.dsems.values() if c > 0]
        if full:
            evs += [(id(s), s, v) for (s, v) in self.csems]
        for eng in self.engs:
            for k, s, v in evs:
                if k == id(eng.sem):
                    continue
                if eng.waited.get(k, 0) < v:
                    eng.e.wait_ge(s, v)
                    eng.waited[k] = v


def build(cf, final_wait=True):
    nc = bass.Bass("TRN2", target_bir_lowering=False)
    D, DC, TT, NT, NCOL, TPC, H, KVH, G = cf.D, cf.DC, cf.TT, cf.NT, cf.NCOL, cf.TPC, cf.H, cf.KVH, cf.G
    FW, FCH, FCB, FFC, KVW, AW, L, N1, N2, N2C = cf.FW, cf.FCH, cf.FCB, cf.FFC, cf.KVW, cf.AW, cf.L, cf.N1, cf.N2, cf.N2C
    DEPTH = cf.DEPTH
    NQK = H + KVH
    VFW = KVW + FW
    PQC = 2 * FW // 128
    wsh = weight_shapes(cf)

    def din(name, shape, dt=F32):
        return nc.dram_tensor(name, list(shape), dt, kind="ExternalInput")

    xT_in = din("xT", [D, TPC])
    metaT_in = din("metaT", [D, 16])
    w_in_sh = {n: din(n, [DEPTH, wsh[n][0] // 4, wsh[n][1]]) for n in WNAMES}
    vecs_in = din("vecs", [DEPTH, 128, cf.NV])
    rope_in = din("rope", [128, 2 * NCOL])
    sel_in = din("sel", [128, 16])
    cmat_in = din("cmat", [128, 512])
    t1_in = din("t1", [N1, 2 * N1])
    t3_in = din("t3", [N2C, 8 * N2])
    tw_in = din("tw", [N1, 2 * N2])
    yT = nc.dram_tensor("yT", [D, TPC], F32, kind="ExternalOutput")

    stack = contextlib.ExitStack()
    with stack:
        stack.enter_context(nc.allow_non_contiguous_dma(reason="small strided scratch transfers"))
        T = TR(nc, stack)
        blk = stack.enter_context(nc.Block())
        PE, ACT, DVE, POOL, SP = T.pe, T.act, T.dve, T.pool, T.sp

        def dram(name, shape, dt):
            return Buf(name, nc.dram_tensor(name, list(shape), dt))

        def ext(tn, name):
            return Buf(name, tn)

        B_xT, B_metaT, B_yT = ext(xT_in, "xT"), ext(metaT_in, "metaT"), ext(yT, "yT")
        B_vecs, B_rope, B_sel, B_cmat = ext(vecs_in, "vecs"), ext(rope_in, "rope"), ext(sel_in, "sel"), ext(cmat_in, "cmat")
        B_t1, B_t3, B_tw = ext(t1_in, "t1"), ext(t3_in, "t3"), ext(tw_in, "tw")
        B_wsh = {n: ext(w_in_sh[n], n) for n in WNAMES}

        wbs = {(n, l): dram(f"wbs_{n}{l}", [wsh[n][0] // 4, wsh[n][1]], BF16) for n in WNAMES for l in range(DEPTH)}
        wbf = {(n, l): dram(f"wbf_{n}{l}", [wsh[n][0], wsh[n][1]], BF16) for n in WNAMES for l in range(DEPTH)}
        wcs_d = [dram(f"wcs{l}", [D, 2 * FW], BF16) for l in range(DEPTH)]
        xa = dram("xa", [D, NCOL], F32)
        xm = dram("xm", [D, NCOL], F32)
        qT_d = dram("qT", [AW, NCOL], BF16)
        gT_d = dram("gT", [2 * D, NCOL], BF16)
        aT_d = dram("aT", [AW, NCOL], BF16)
        pqsel_d = dram("pqsel", [PQC * 128, NCOL], BF16)
        kin = [dram(f"kin{l}", [KVW, TPC], BF16) for l in range(DEPTH)]
        kout = [dram(f"kout{l}", [4 * KVW, TPC], BF16) for l in range(DEPTH)]
        kmeta = dram("kmeta", [KVW, 16], BF16)
        vin = [dram(f"vin{l}", [KVH * TPC, 128], BF16) for l in range(DEPTH)]
        vout = [dram(f"vout{l}", [KVH * 4 * TPC, 128], BF16) for l in range(DEPTH)]
        vmeta = dram("vmeta", [16, KVW], BF16)
        fin = [dram(f"fin{l}", [TPC, FW], BF16) for l in range(DEPTH)]
        fout = [dram(f"fout{l}", [4 * TPC, FW], BF16) for l in range(DEPTH)]
        fmeta = dram("fmeta", [16, FW], BF16)
        yd = dram("yd", [2, N1, N2, FCH], BF16)
        fpos = dram("fpos", [L, FW], BF16)
        pqin = [dram(f"pqin{l}", [cf.NPC * 2 * FCH, cf.LC], BF16) for l in range(DEPTH)]
        pqout = [dram(f"pqout{l}", [cf.NPC * 4 * 2 * FCH, cf.LC], BF16) for l in range(DEPTH)]
        hin = [dram(f"hin{l}", [128, 2 * DC], F32) for l in range(DEPTH)]
        hout = [dram(f"hout{l}", [4 * 128, 2 * DC], F32) for l in range(DEPTH)]

        ps0 = stack.enter_context(nc.psum_tensor("ps0", [128, 1024], F32))
        ps1 = stack.enter_context(nc.psum_tensor("ps1", [128, 1024], F32))
        ps4 = stack.enter_context(nc.psum_tensor("ps4", [128, 512], F32))
        ps5 = stack.enter_context(nc.psum_tensor("ps5", [128, 512], F32))
        ps6 = stack.enter_context(nc.psum_tensor("ps6", [128, 512], F32))
        pst = stack.enter_context(nc.psum_tensor("pst", [128, 1024], BF16))
        BK = [Buf(f"bk{i}", None) for i in range(8)]
        bank_aps = [ps0[:, 0:512], ps0[:, 512:1024], ps1[:, 0:512], ps1[:, 512:1024], ps4[:, :], ps5[:, :], ps6[:, :]]

        def bk(i):
            return bank_aps[i]

        uniq = [0]

        def sb(st, name, shape, dt, key=None):
            uniq[0] += 1
            return Buf(key or name, st.enter_context(nc.sbuf_tensor(f"{name}_{uniq[0]}", list(shape), dt)))

        ident = sb(stack, "ident", [128, 128], BF16)
        rotT = sb(stack, "rotT", [128, 128], BF16)
        csm = sb(stack, "csm", [128, 2, 128], BF16)
        ones = sb(stack, "ones", [128, 128], BF16)
        selb = sb(stack, "selb", [128, 16], F32)
        vec = sb(stack, "vec", [128, cf.NV], F32)

        T.dma(POOL, ident.t[:], cmat_in.ap()[:, 0:128], "c0", [B_cmat], [ident])
        T.dma(POOL, rotT.t[:], cmat_in.ap()[:, 128:256], "c0", [B_cmat], [rotT])
        T.dma(POOL, csm.t[:], cmat_in.ap()[:, 256:512].rearrange("p (a b) -> p a b", a=2), "c0", [B_cmat], [csm])
        T.dma(POOL, selb.t[:], sel_in.ap(), "c0", [B_sel], [selb])
        T.op(DVE, lambda: nc.vector.memset(ones.t[:], 1.0), [], [ones])

        T.dma(POOL, xa.t.ap()[:, 0:16], metaT_in.ap(), "xinit", [B_metaT], [xa])
        T.dma(POOL, xa.t.ap()[:, 16:NCOL], xT_in.ap(), "xinit", [B_xT], [xa])

        def ag_chunks(inb, outb, rows_in, nch, key, total=None):
            rc = rows_in // nch
            for j in range(nch):
                if getattr(cf, 'unshare', False) and total is None:
                    T.coll("AllGather", GRP, inb, outb, f"{key}_{j}",
                           in_ap=inb.t.ap()[j * rc:(j + 1) * rc, :], out_ap=outb.t.ap()[j * 4 * rc:(j + 1) * 4 * rc, :])
                else:
                    T.coll("AllGather", GRP, inb, outb, key, shared=(key, total or nch),
                           in_ap=inb.t.ap()[j * rc:(j + 1) * rc, :], out_ap=outb.t.ap()[j * 4 * rc:(j + 1) * 4 * rc, :])

        wch = weight_chunks(cf)
        wtot = sum(wch.values())
        WGA = ["wqk", "wg", "wvf"]
        wtot_a = sum(wch[n] for n in WGA)
        wtot_b = wtot - wtot_a

        def weight_ags(l):
            for n in WNAMES:
                if n in WGA:
                    ag_chunks(wbs[(n, l)], wbf[(n, l)], wsh[n][0] // 4, wch[n], f"w{l}a", wtot_a)
                else:
                    ag_chunks(wbs[(n, l)], wbf[(n, l)], wsh[n][0] // 4, wch[n], f"w{l}b", wtot_b)

        for l in range(DEPTH):
            for n in WNAMES:
                T.dma(POOL, wbs[(n, l)].t.ap(), w_in_sh[n].ap()[l], "wcast", [B_wsh[n]], [wbs[(n, l)]])
            if l == 0:
                weight_ags(0)

        tiles = [(16 + i * TT, TT, i) for i in range(NT)] + [(0, 16, NT)]

        class WRing:
            def __init__(self, st, name, kcmax, nslots):
                self.slots = [sb(st, f"{name}{i}", [128, kcmax, 128], BF16, key=f"{name[2:]}{i}") for i in range(nslots)]
                self.i = 0

            def load(self, wb, chunk, kc):
                s = self.slots[self.i % len(self.slots)]
                self.i += 1
                src = wb.t.ap()[chunk * 128:(chunk + 1) * 128, :].rearrange("p (k n) -> p k n", n=128)
                T.dma(SP, s.t[:, 0:kc, :], src, s.name, [wb], [s])
                return s

        def rmsnorm(xt, n, gcol, sq, hT, rs, rstd, pbank, pbuf):
            T.op(ACT, lambda: nc.scalar.activation(out=sq.t[:, :, 0:n], in_=xt.t[:, :, 0:n], func=AF.Square), [xt], [sq])
            T.mm([(lambda c=c: nc.tensor.matmul(pbank[:, 0:n], ones.t[:, :], sq.t[:, c, 0:n], start=(c == 0), stop=(c == DC - 1)))
                  for c in range(DC)], [ones, sq], [pbuf])
            T.op(ACT, lambda: nc.scalar.activation(out=rs.t[:, 0:n], in_=pbank[:, 0:n], func=AF.Sqrt, bias=float(cf.EPS), scale=1.0 / D), [pbuf], [rs])
            T.op(DVE, lambda: nc.vector.reciprocal(out=rstd.t[:, 0:n], in_=rs.t[:, 0:n]), [rs], [rstd])
            for c in range(DC):
                T.op(DVE, lambda c=c: nc.vector.scalar_tensor_tensor(out=hT.t[:, c, 0:n], in0=xt.t[:, c, 0:n], scalar=vec.t[:, gcol + c:gcol + c + 1],
                                                                    in1=rstd.t[:, 0:n], op0=ALU.mult, op1=ALU.mult), [xt, vec, rstd], [hT])

        def ck(name):
            if getattr(cf, 'stop', None) == name:
                if getattr(cf, 'exc', False):
                    raise _Stop()
                T.barrier(full=True)
                T.dead = True

        try:
          for l in range(DEPTH):
              last = (l == DEPTH - 1)
              T.barrier()
              ck('pro')
              T.dma(POOL, vec.t[:], vecs_in.ap()[l], "c_vec", [B_vecs], [vec])

              with contextlib.ExitStack() as ph:
                  xt = sb(ph, "p1_xt", [128, DC, TT], F32, key="xt")
                  sq = sb(ph, "p1_sq", [128, DC, TT], BF16)
                  hT = sb(ph, "p1_hT", [128, DC, TT], BF16)
                  rs = sb(ph, "p1_rs", [128, TT], F32)
                  rstd = sb(ph, "p1_rstd", [128, TT], F32)
                  wvf = sb(ph, "p1_wvf", [128, DC, VFW], BF16)
                  ring = WRing(ph, "p1_w", DC, 4)
                  rope = sb(ph, "p1_rope", [128, 2, TT], F32)
                  sqh = sb(ph, "p1_sqh", [128, TT], BF16)
                  qg = sb(ph, "p1_qg", [128, TT], BF16)
                  rsh = sb(ph, "p1_rsh", [128, TT], F32)
                  rstdh = sb(ph, "p1_rstdh", [128, TT], F32)
                  t1b = sb(ph, "p1_t1", [128, TT], F32)
                  t2b = sb(ph, "p1_t2", [128, TT], F32)
                  qo = [sb(ph, f"p1_qo{i}", [128, TT], BF16) for i in range(2)]
                  vo = [sb(ph, f"p1_vo{i}", [128, 512], BF16) for i in range(2)]
                  go = [sb(ph, f"p1_go{i}", [128, 4, TT], BF16) for i in range(2)]
                  T.dma(SP, wvf.t[:], wbf[("wvf", l)].t.ap().rearrange("p (k n) -> p k n", n=VFW), "p1_wvf", [wbf[("wvf", l)]], [wvf])
                  nq = nv = ng = 0
                  frc = TPC // cf.NFC
                  f_ag = [0]
                  f_cp = [0]

                  def f_copy(j):
                      for rr in range(4):
                          p0 = 16 + rr * TPC + j * frc
                          T.dma(POOL, fpos.t.ap()[p0:p0 + frc, :], fout[l].t.ap()[(j * 4 + rr) * frc:(j * 4 + rr + 1) * frc, :], "fpos_a", [fout[l]], [fpos])

                  def f_progress(done_tokens, flush=False):
                      while f_ag[0] < cf.NFC and (f_ag[0] + 1) * frc <= done_tokens:
                          j = f_ag[0]
                          T.coll("AllGather", GRP, fin[l], fout[l], f"f{l}", shared=(f"f{l}", cf.NFC),
                                 in_ap=fin[l].t.ap()[j * frc:(j + 1) * frc, :], out_ap=fout[l].t.ap()[j * 4 * frc:(j + 1) * 4 * frc, :])
                          f_ag[0] += 1
                          while f_cp[0] < f_ag[0] - 1:
                              f_copy(f_cp[0])
                              f_cp[0] += 1
                      if flush:
                          while f_cp[0] < f_ag[0]:
                              f_copy(f_cp[0])
                              f_cp[0] += 1

                  for (c0, n, _) in tiles:
                      meta = (n == 16)
                      T.dma(POOL, xt.t[:, :, 0:n], xa.t.ap()[:, c0:c0 + n].rearrange("(c p) n -> p c n", p=128), xt.name, [xa], [xt])
                      T.dma(POOL, rope.t[:, :, 0:n], rope_in.ap().rearrange("p (a n) -> p a n", a=2)[:, :, c0:c0 + n], "p1_rope", [B_rope], [rope])
                      ck('p1x')
                      rmsnorm(xt, n, cf.V_GMIX, sq, hT, rs, rstd, bk(6), BK[6])
                      ck('p1a')
                      def head_mm(hd):
                          w = ring.load(wbf[("wqk", l)], hd, DC)
                          pa, pab = bk(hd % 2), BK[hd % 2]
                          T.mm([(lambda c=c: nc.tensor.matmul(pa[:, 0:n], w.t[:, c, :], hT.t[:, c, 0:n], start=(c == 0), stop=(c == DC - 1)))
                                for c in range(DC)], [w, hT], [pab])

                      def head_p1(hd):
                          pa, pab = bk(hd % 2), BK[hd % 2]
                          gcol = cf.V_QN if hd < H else cf.V_KN
                          T.op(ACT, lambda: nc.scalar.activation(out=sqh.t[:, 0:n], in_=pa[:, 0:n], func=AF.Square), [pab], [sqh])
                          T.op(ACT, lambda: nc.scalar.activation(out=qg.t[:, 0:n], in_=pa[:, 0:n], func=AF.Copy, scale=vec.t[:, gcol:gcol + 1]), [pab, vec], [qg])

                      def head_p2(hd, q_o):
                          T.mm([lambda: nc.tensor.matmul(bk(4)[:, 0:n], ones.t[:, :], sqh.t[:, 0:n], start=True, stop=True)], [ones, sqh], [BK[4]])
                          T.mm([lambda: nc.tensor.matmul(bk(5)[:, 0:n], rotT.t[:, :], qg.t[:, 0:n], start=True, stop=True)], [rotT, qg], [BK[5]])
                          T.op(ACT, lambda: nc.scalar.activation(out=rsh.t[:, 0:n], in_=bk(4)[:, 0:n], func=AF.Sqrt, bias=float(cf.EPS), scale=1.0 / 128), [BK[4]], [rsh])
                          T.op(DVE, lambda: nc.vector.reciprocal(out=rstdh.t[:, 0:n], in_=rsh.t[:, 0:n]), [rsh], [rstdh])
                          T.op(DVE, lambda: nc.vector.tensor_tensor(out=t1b.t[:, 0:n], in0=qg.t[:, 0:n], in1=rope.t[:, 0, 0:n], op=ALU.mult), [qg, rope], [t1b])
                          T.op(DVE, lambda: nc.vector.tensor_tensor(out=t2b.t[:, 0:n], in0=bk(5)[:, 0:n], in1=rope.t[:, 1, 0:n], op=ALU.mult), [BK[5], rope], [t2b])
                          T.op(DVE, lambda: nc.vector.tensor_tensor(out=t1b.t[:, 0:n], in0=t1b.t[:, 0:n], in1=t2b.t[:, 0:n], op=ALU.add), [t1b, t2b], [t1b])
                          T.op(DVE, lambda: nc.vector.tensor_tensor(out=q_o.t[:, 0:n], in0=t1b.t[:, 0:n], in1=rstdh.t[:, 0:n], op=ALU.mult), [t1b, rstdh], [q_o])
                          if hd < H:
                              T.dma(POOL, qT_d.t.ap()[hd * 128:(hd + 1) * 128, c0:c0 + n], q_o.t[:, 0:n], q_o.name, [q_o], [qT_d])
                          else:
                              kh = hd - H
                              if meta:
                                  T.dma(POOL, kmeta.t.ap()[kh * 128:(kh + 1) * 128, :], q_o.t[:, 0:n], q_o.name, [q_o], [kmeta])
                              else:
                                  T.dma(POOL, kin[l].t.ap()[kh * 128:(kh + 1) * 128, c0 - 16:c0 - 16 + n], q_o.t[:, 0:n], q_o.name, [q_o], [kin[l]])

                      head_mm(0)
                      head_p1(0)
                      for hd in range(1, NQK):
                          head_mm(hd)
                          head_p2(hd - 1, qo[nq % 2])
                          nq += 1
                          head_p1(hd)
                      head_p2(NQK - 1, qo[nq % 2])
                      nq += 1
                      ck('p1b')
                      ntb = max(1, n // 128)
                      for tb in range(ntb):
                          tn = min(128, n)
                          cb0 = 0
                          while cb0 < VFW:
                              if cb0 < KVW:
                                  cw = KVW
                              else:
                                  cw = min(512, VFW - cb0)
                              pv, pvb = bk(2 + nv % 2), BK[2 + nv % 2]
                              T.mm([(lambda c=c: nc.tensor.matmul(pv[0:tn, 0:cw], hT.t[:, c, tb * 128:tb * 128 + tn], wvf.t[:, c, cb0:cb0 + cw],
                                                                  start=(c == 0), stop=(c == DC - 1))) for c in range(DC)], [hT, wvf], [pvb])
                              v_o = vo[nv % 2]
                              nv += 1
                              T.op(ACT, lambda: nc.scalar.copy(out=v_o.t[0:tn, 0:cw], in_=pv[0:tn, 0:cw]), [pvb], [v_o])
                              if cb0 < KVW:
                                  dst, dm = (vmeta, vmeta.t.ap()[:, :]) if meta else (
                                      vin[l], vin[l].t.ap().rearrange("(h t) d -> t h d", h=KVH)[c0 - 16 + tb * 128:c0 - 16 + tb * 128 + tn, :, :])
                              else:
                                  f0 = cb0 - KVW
                                  dst, dm = (fmeta, fmeta.t.ap()[:, f0:f0 + cw]) if meta else (fin[l], fin[l].t.ap()[c0 - 16 + tb * 128:c0 - 16 + tb * 128 + tn, f0:f0 + cw])
                              srcv = v_o.t[0:tn, 0:cw]
                              if cb0 < KVW and not meta:
                                  srcv = srcv.rearrange("p (h d) -> p h d", h=KVH)
                              T.dma(POOL, dm, srcv, v_o.name, [v_o], [dst])
                              cb0 += cw
                      ck('p1c')
                      for gc in range(2 * DC):
                          w = ring.load(wbf[("wg", l)], gc, DC)
                          pa, pab = bk(gc % 2), BK[gc % 2]
                          T.mm([(lambda c=c: nc.tensor.matmul(pa[:, 0:n], w.t[:, c, :], hT.t[:, c, 0:n], start=(c == 0), stop=(c == DC - 1)))
                                for c in range(DC)], [w, hT], [pab])
                          g_o = go[(ng // 4) % 2]
                          T.op(ACT, lambda: nc.scalar.activation(out=g_o.t[:, gc % 4, 0:n], in_=pa[:, 0:n], func=AF.Sigmoid,
                                                                 bias=vec.t[:, cf.V_BG + gc:cf.V_BG + gc + 1], scale=1.0), [pab, vec], [g_o])
                          ng += 1
                          if gc % 4 == 3:
                              g4 = gc // 4
                              T.dma(POOL, gT_d.t.ap()[g4 * 512:(g4 + 1) * 512, c0:c0 + n].rearrange("(a p) n -> p a n", p=128),
                                    g_o.t[:, :, 0:n], g_o.name, [g_o], [gT_d])
              T.barrier()
              ck('p1')
              ag_chunks(fin[l], fout[l], TPC, cf.NFC, f"f{l}")
              ag_chunks(kin[l], kout[l], KVW, KVH, f"k{l}")
              ag_chunks(vin[l], vout[l], KVH * TPC, KVH, f"v{l}")
              ck('ag1')

              P2B = cf.P2B
              with contextlib.ExitStack() as ph:
                  T.dma(POOL, fpos.t.ap()[0:16, :], fmeta.t.ap(), "fpos_a", [fmeta], [fpos])
                  frc_ = TPC // cf.NFC
                  for j in range(cf.NFC):
                      for rr in range(4):
                          p0 = 16 + rr * TPC + j * frc_
                          T.dma(POOL, fpos.t.ap()[p0:p0 + frc_, :], fout[l].t.ap()[(j * 4 + rr) * frc_:(j * 4 + rr + 1) * frc_, :], "fpos_a", [fout[l]], [fpos])
                  Z = sb(ph, "f_Z", [N1, N2, FCH], BF16)
                  PZ = N2 // 4 if N2 % 4 == 0 else N2 // 2
                  zs = [sb(ph, f"f_zs{i}", [N1, PZ, FCH], BF16) for i in range(2)]
                  t1s = sb(ph, "f_t1", [N1, 2 * N1], BF16)
                  tws = sb(ph, "f_tw", [N1, 2, N2], F32)
                  wfr = sb(ph, "f_wfr", [128, cf.FG, D], BF16)
                  wst = [sb(ph, f"f_wst{i}", [128, 512], BF16) for i in range(2)]
                  fa = sb(ph, "f_a", [N1, 512], F32)
                  fb = sb(ph, "f_b", [N1, 512], F32)
                  yo = [sb(ph, f"f_yo{i}", [N1, 2, 512], BF16) for i in range(2)]
                  T.dma(POOL, t1s.t[:], t1_in.ap(), "f_t1", [B_t1], [t1s])
                  T.dma(POOL, tws.t[:], tw_in.ap().rearrange("p (a n) -> p a n", a=2), "f_tw", [B_tw], [tws])
                  T.dma(SP, wfr.t[:], wbf[("wfr", l)].t.ap().rearrange("p (g n) -> p g n", g=cf.FG), "f_wfr", [wbf[("wfr", l)]], [wfr])
                  nw = 0
                  for g in range(cf.FG):
                      for pq in range(2):
                          kc = (g // FCB) * (2 * FCB) + pq * FCB + (g % FCB)
                          for nb in range(D // 512 if D >= 512 else 1):
                              nbw = min(512, D)
                              pw, pwb = bk(nw % 2), BK[nw % 2]
                              T.mm([lambda: nc.tensor.matmul(pw[:, 0:nbw], csm.t[:, pq, :], wfr.t[:, g, nb * nbw:(nb + 1) * nbw], start=True, stop=True)],
                                   [csm, wfr], [pwb])
                              ws_ = wst[nw % 2]
                              nw += 1
                              T.op(ACT, lambda: nc.scalar.copy(out=ws_.t[:, 0:nbw], in_=pw[:, 0:nbw]), [pwb], [ws_])
                              T.dma(POOL, wcs_d[l].t.ap()[nb * nbw:(nb + 1) * nbw, kc * 128:(kc + 1) * 128].rearrange("(j p) n -> p j n", p=128),
                                    ws_.t[:, 0:nbw].rearrange("p (j n) -> p j n", n=128), ws_.name, [ws_], [wcs_d[l]])
                  fview = fpos.t.ap().rearrange("(a b) c -> a b c", b=N2)
                  nz = 0
                  for pz in range(N2 // PZ):
                      for q in range(4):
                          z_ = zs[nz % 2]
                          nz += 1
                          T.dma(SP, z_.t[:], fview[:, pz * PZ:(pz + 1) * PZ, q * FCH:(q + 1) * FCH], z_.name, [fpos], [z_])
                          if q == 0:
                              T.op(DVE, lambda: nc.vector.tensor_scalar(out=Z.t[:, pz * PZ:(pz + 1) * PZ, :], in0=z_.t[:], scalar1=selb.t[0:N1, 9:10], scalar2=None, op0=ALU.mult),
                                   [z_, selb], [Z])
                          else:
                              T.op(DVE, lambda: nc.vector.scalar_tensor_tensor(out=Z.t[:, pz * PZ:(pz + 1) * PZ, :], in0=z_.t[:], scalar=selb.t[0:N1, 9 + q:10 + q],
                                                                              in1=Z.t[:, pz * PZ:(pz + 1) * PZ, :], op0=ALU.mult, op1=ALU.add), [z_, selb, Z], [Z])
                  Zf = Z.t[:].rearrange("p a c -> p (a c)")
                  nblk = N2 // P2B
                  for b_ in range(nblk):
                      pr, prb = bk(2 * (b_ % 2)), BK[2 * (b_ % 2)]
                      pi, pib = bk(2 * (b_ % 2) + 1), BK[2 * (b_ % 2) + 1]
                      T.mm([lambda: nc.tensor.matmul(pr[0:N1, :], t1s.t[:, 0:N1], Zf[:, b_ * 512:(b_ + 1) * 512], start=True, stop=True)], [t1s, Z], [prb])
                      T.mm([lambda: nc.tensor.matmul(pi[0:N1, :], t1s.t[:, N1:2 * N1], Zf[:, b_ * 512:(b_ + 1) * 512], start=True, stop=True)], [t1s, Z], [pib])
                      y_ = yo[b_ % 2]
                      tcb = tws.t[:, 0, b_ * P2B:(b_ + 1) * P2B].unsqueeze(2).to_broadcast([N1, P2B, FCH])
                      tsb = tws.t[:, 1, b_ * P2B:(b_ + 1) * P2B].unsqueeze(2).to_broadcast([N1, P2B, FCH])
                      v3 = lambda ap: ap.rearrange("p (a c) -> p a c", c=FCH)
                      T.op(DVE, lambda: nc.vector.tensor_tensor(out=v3(fa.t[:, :]), in0=v3(pr[0:N1, :]), in1=tcb, op=ALU.mult), [prb, tws], [fa])
                      T.op(DVE, lambda: nc.vector.tensor_tensor(out=v3(fb.t[:, :]), in0=v3(pi[0:N1, :]), in1=tsb, op=ALU.mult), [pib, tws], [fb])
                      T.op(DVE, lambda: nc.vector.tensor_tensor(out=y_.t[:, 0, :], in0=fa.t[:, :], in1=fb.t[:, :], op=ALU.subtract), [fa, fb], [y_])
                      T.op(DVE, lambda: nc.vector.tensor_tensor(out=v3(fa.t[:, :]), in0=v3(pr[0:N1, :]), in1=tsb, op=ALU.mult), [prb, tws], [fa])
                      T.op(DVE, lambda: nc.vector.tensor_tensor(out=v3(fb.t[:, :]), in0=v3(pi[0:N1, :]), in1=tcb, op=ALU.mult), [pib, tws], [fb])
                      T.op(DVE, lambda: nc.vector.tensor_tensor(out=y_.t[:, 1, :], in0=fa.t[:, :], in1=fb.t[:, :], op=ALU.add), [fa, fb], [y_])
                      ydv = yd.t.ap().rearrange("r k a c -> k r (a c)")
                      T.dma(POOL, ydv[:, :, b_ * 512:(b_ + 1) * 512], y_.t[:], y_.name, [y_], [yd])
              T.barrier()
              ck('f1')
              with contextlib.ExitStack() as ph:
                  t3s = sb(ph, "f_t3", [N2C, 2, 2, 2 * N2], BF16)
                  T.dma(POOL, t3s.t[:], t3_in.ap().rearrange("p (r k n) -> p r k n", r=2, k=2), "f_t3", [B_t3], [t3s])
                  yt = [[sb(ph, f"f_yt{ri}{ch}", [N2C, N1, 128], BF16) for ch in range(2)] for ri in range(2)]
                  pqs = sb(ph, "f_pqs", [128, 2, L], BF16)
                  for cb in range(FCB):
                      for ri in range(2):
                          for ch in range(2):
                              src = yd.t.ap()[ri].rearrange("k a c -> a k c")[ch * N2C:(ch + 1) * N2C, :, cb * 128:(cb + 1) * 128]
                              T.dma(SP, yt[ri][ch].t[:], src, yt[ri][ch].name, [yd], [yt[ri][ch]])
                      for k1 in range(N1):
                          po, pob = bk(4 + k1 % 2), BK[4 + k1 % 2]
                          fns = []
                          idx = 0
                          for ri in range(2):
                              for ch in range(2):
                                  fns.append(lambda ri=ri, ch=ch, idx=idx: nc.tensor.matmul(po[:, 0:2 * N2], yt[ri][ch].t[:, k1, :], t3s.t[:, ri, ch, :],
                                                                                           start=(idx == 0), stop=(idx == 3)))
                                  idx += 1
                          T.mm(fns, [yt[0][0], yt[0][1], yt[1][0], yt[1][1], t3s], [pob])
                          dst = pqs.t[:, :, k1:k1 + N1 * (N2 - 1) + 1:N1]
                          T.op(ACT, lambda: nc.scalar.copy(out=dst, in_=po[:, 0:2 * N2].rearrange("p (a n) -> p a n", a=2)), [pob], [pqs])
                      for j in range(cf.NPC):
                          T.dma(POOL, pqin[l].t.ap()[j * 2 * FCH:(j + 1) * 2 * FCH, :].rearrange("(a c) n -> c a n", a=2)[cb * 128:(cb + 1) * 128, :, :],
                                pqs.t[:, :, j * cf.LC:(j + 1) * cf.LC], "f_pqs", [pqs], [pqin[l]])
              T.barrier()
              ck('f3')
              ag_chunks(pqin[l], pqout[l], cf.NPC * 2 * FCH, cf.NPC, f"pq{l}")
              ck('pq')

              scale = 1.0 / math.sqrt(128.0)
              NKC = 4 * TPC // 128
              with contextlib.ExitStack() as ph:
                  KT = sb(ph, "a_KT", [128, 4 * TPC + 16], BF16)
                  Vt = sb(ph, "a_Vt", [128, NKC + 1, 132], BF16)
                  QT = [sb(ph, f"a_QT{i}", [128, TT], BF16) for i in range(2)]
                  Pb = [sb(ph, f"a_P{i}", [128, 2, TT], BF16) for i in range(2)]
                  rinv = sb(ph, "a_rinv", [128, 4], F32)
                  on = [sb(ph, f"a_on{i}", [128, 128], BF16) for i in range(2)]
                  aT = [sb(ph, f"a_aT{i}", [128, TT], BF16) for i in range(2)]
                  pqa = sb(ph, "a_pq", [128, PQC, TT], BF16)
                  pqc_a = [sb(ph, f"a_pqc{i}", [128, PQC, TT], BF16) for i in range(2)]
                  npc_a = [0]

                  def pq_load(dst_t, pos0, nn, key, dbuf):
                      a = pos0
                      while a < pos0 + nn:
                          j = a // cf.LC
                          b = min(pos0 + nn, (j + 1) * cf.LC)
                          v_ = pqout[l].t.ap()[j * 8 * FCH:(j + 1) * 8 * FCH, :].rearrange("(c p) n -> p c n", p=128)
                          T.dma(POOL, dst_t[:, :, a - pos0:b - pos0], v_[:, :, a - j * cf.LC:b - j * cf.LC], key, [pqout[l]], [dbuf])
                          a = b

                  na = 0
                  if l + 1 < DEPTH:
                      weight_ags(l + 1)
                  for kvh in range(KVH):
                      for rr in range(4):
                          T.dma(SP, KT.t[:, rr * TPC:(rr + 1) * TPC], kout[l].t.ap()[(kvh * 4 + rr) * 128:(kvh * 4 + rr + 1) * 128, :], "a_KT", [kout[l]], [KT])
                      T.dma(SP, KT.t[:, 4 * TPC:4 * TPC + 16], kmeta.t.ap()[kvh * 128:(kvh + 1) * 128, :], "a_KT", [kmeta], [KT])
                      T.dma(SP, Vt.t[:, 0:NKC, 0:128], vout[l].t.ap()[kvh * 4 * TPC:(kvh + 1) * 4 * TPC, :].rearrange("(c p) d -> p c d", p=128), "a_Vt", [vout[l]], [Vt])
                      T.dma(SP, Vt.t[0:16, NKC, 0:128], vmeta.t.ap()[:, kvh * 128:(kvh + 1) * 128], "a_Vt", [vmeta], [Vt])
                      T.op(DVE, lambda: nc.vector.memset(Vt.t[:, :, 128:129], 1.0), [], [Vt])
                      items = [(2 * j, 2) for j in range(NKC // 2)] + [(NKC, 1)]
                      for (c0, n, _) in tiles:
                          nqb = max(1, n // 128)
                          qn_ = min(128, n)
                          if kvh == KVH - 1:
                              if n == 16:
                                  pq_load(pqa.t, 0, 16, "a_pq", pqa)
                              else:
                                  for q in range(4):
                                      pc = pqc_a[npc_a[0] % 2]
                                      npc_a[0] += 1
                                      pq_load(pc.t, q * TPC + c0, n, pc.name, pc)
                                      if q == 0:
                                          T.op(DVE, lambda: nc.vector.tensor_scalar(out=pqa.t[:, :, 0:n], in0=pc.t[:, :, 0:n], scalar1=selb.t[:, 9:10], scalar2=None, op0=ALU.mult), [pc, selb], [pqa])
                                      else:
                                          T.op(DVE, lambda: nc.vector.scalar_tensor_tensor(out=pqa.t[:, :, 0:n], in0=pc.t[:, :, 0:n], scalar=selb.t[:, 9 + q:10 + q], in1=pqa.t[:, :, 0:n],
                                                                                          op0=ALU.mult, op1=ALU.add), [pc, selb, pqa], [pqa])
                              T.dma(POOL, pqsel_d.t.ap()[:, c0:c0 + n].rearrange("(c p) n -> p c n", p=128), pqa.t[:, :, 0:n], "a_pq", [pqa], [pqsel_d])
                          for gq in range(G):
                              hd = kvh * G + gq
                              Q = QT[na % 2]
                              a_T = aT[na % 2]
                              na += 1
                              T.dma(POOL, Q.t[:, 0:n], qT_d.t.ap()[hd * 128:(hd + 1) * 128, c0:c0 + n], Q.name, [qT_d], [Q])

                              def qk(it, slot):
                                  kc0, cnt = it
                                  for u in range(cnt):
                                      kk = 16 if kc0 == NKC else 128
                                      pS = bk(2 * slot + u)
                                      T.mm([lambda: nc.tensor.matmul(pS[0:kk, 0:n], KT.t[:, (kc0 + u) * 128:(kc0 + u) * 128 + kk], Q.t[:, 0:n], start=True, stop=True)],
                                           [KT, Q], [BK[2 * slot + u]])

                              qk(items[0], 0)
                              for ii, it in enumerate(items):
                                  slot = ii % 2
                                  if ii + 1 < len(items):
                                      qk(items[ii + 1], (ii + 1) % 2)
                                  kc0, cnt = it
                                  kk = 16 if kc0 == NKC else 128
                                  Pt = Pb[slot]
                                  src = (ps0 if slot == 0 else ps1)[0:kk, :].rearrange("p (a n) -> p a n", a=2)[:, 0:cnt, 0:n]
                                  T.op(ACT, lambda: nc.scalar.activation(out=Pt.t[0:kk, 0:cnt, 0:n], in_=src, func=AF.Exp, scale=scale),
                                       [BK[2 * slot], BK[2 * slot + 1]], [Pt])
                                  fns = []
                                  for u in range(cnt):
                                      for qb in range(nqb):
                                          ob = 4 + qb // 2
                                          oc0 = (qb % 2) * 256
                                          first = (ii == 0 and u == 0 and qb % 2 == 0)
                                          lastm = (ii == len(items) - 1 and u == cnt - 1)
                                          fns.append(lambda u=u, qb=qb, ob=ob, oc0=oc0, first=first, lastm=lastm: nc.tensor.matmul(
                                              bk(ob)[0:qn_, oc0:oc0 + 129], Pt.t[0:kk, u, qb * 128:qb * 128 + qn_], Vt.t[0:kk, kc0 + u, 0:129],
                                              start=first, stop=lastm, skip_group_check=True))
                                  T.mm(fns, [Pt, Vt], [BK[4], BK[5]])
                              for qb in range(nqb):
                                  ob = 4 + qb // 2
                                  oc0 = (qb % 2) * 256
                                  T.op(DVE, lambda: nc.vector.reciprocal(out=rinv.t[0:qn_, qb:qb + 1], in_=bk(ob)[0:qn_, oc0 + 128:oc0 + 129]), [BK[ob]], [rinv])
                                  o_n = on[qb % 2]
                                  T.op(DVE, lambda: nc.vector.tensor_scalar(out=o_n.t[0:qn_, :], in0=bk(ob)[0:qn_, oc0:oc0 + 128], scalar1=rinv.t[0:qn_, qb:qb + 1], scalar2=None, op0=ALU.mult),
                                       [BK[ob], rinv], [o_n])
                                  T.mm([lambda: nc.tensor.transpose(pst[:, qb * 128:qb * 128 + qn_], o_n.t[0:qn_, :], ident.t[0:qn_, 0:qn_])], [o_n, ident], [BK[7]])
                              T.op(DVE, lambda: nc.vector.tensor_copy(out=a_T.t[:, 0:n], in_=pst[:, 0:n]), [BK[7]], [a_T])
                              T.dma(POOL, aT_d.t.ap()[hd * 128:(hd + 1) * 128, c0:c0 + n], a_T.t[:, 0:n], a_T.name, [a_T], [aT_d])
              T.barrier()
              ck('attn')
              xh_st = contextlib.ExitStack()
              xh = sb(xh_st, "xh", [128, DC, cf.NH], F32)
              hsb = sb(xh_st, "hsb", [128, 2, DC], F32)
              with contextlib.ExitStack() as ph:
                  xt = sb(ph, "m_xt", [128, DC, TT], F32, key="xt")
                  at = sb(ph, "m_at", [128, H, TT], BF16)
                  pq = sb(ph, "m_pq", [128, PQC, TT], BF16)
                  gt = sb(ph, "m_gt", [128, 2 * DC, TT], BF16)
                  mg = sb(ph, "m_mg", [128, DC, TT], BF16)
                  ta = sb(ph, "m_ta", [128, TT], F32)
                  tb_ = sb(ph, "m_tb", [128, TT], F32)
                  xo = xt
                  ring = WRing(ph, "m_w", max(DC, PQC, H), 4)
                  T.op(DVE, lambda: nc.vector.memset(xh.t[:], 0.0), [], [xh])
                  npc = 0
                  for (c0, n, hj) in tiles:
                      meta = (n == 16)
                      T.dma(POOL, xt.t[:, :, 0:n], xa.t.ap()[:, c0:c0 + n].rearrange("(c p) n -> p c n", p=128), xt.name, [xa], [xt])
                      T.dma(POOL, at.t[:, :, 0:n], aT_d.t.ap()[:, c0:c0 + n].rearrange("(c p) n -> p c n", p=128), "m_at", [aT_d], [at])
                      T.dma(POOL, gt.t[:, :, 0:n], gT_d.t.ap()[:, c0:c0 + n].rearrange("(c p) n -> p c n", p=128), "m_gt", [gT_d], [gt])
                      T.dma(POOL, pq.t[:, :, 0:n], pqsel_d.t.ap()[:, c0:c0 + n].rearrange("(c p) n -> p c n", p=128), "m_pq", [pqsel_d], [pq])
                      for oc in range(DC):
                          w1 = ring.load(wbf[("wab", l)], oc, H)
                          w2 = ring.load(wcs_d[l], oc, PQC)
                          ba, bb = 2 * (oc % 2), 2 * (oc % 2) + 1
                          T.mm([(lambda c=c: nc.tensor.matmul(bk(ba)[:, 0:n], w1.t[:, c, :], at.t[:, c, 0:n], start=(c == 0), stop=(c == H - 1))) for c in range(H)], [w1, at], [BK[ba]])
                          T.mm([(lambda c=c: nc.tensor.matmul(bk(bb)[:, 0:n], w2.t[:, c, :], pq.t[:, c, 0:n], start=(c == 0), stop=(c == PQC - 1))) for c in range(PQC)], [w2, pq], [BK[bb]])
                          T.op(DVE, lambda: nc.vector.tensor_tensor(out=ta.t[:, 0:n], in0=bk(ba)[:, 0:n], in1=gt.t[:, oc, 0:n], op=ALU.mult), [BK[ba], gt], [ta])
                          T.op(DVE, lambda: nc.vector.tensor_tensor(out=tb_.t[:, 0:n], in0=bk(bb)[:, 0:n], in1=gt.t[:, DC + oc, 0:n], op=ALU.mult), [BK[bb], gt], [tb_])
                          T.op(DVE, lambda: nc.vector.tensor_tensor(out=mg.t[:, oc, 0:n], in0=ta.t[:, 0:n], in1=tb_.t[:, 0:n], op=ALU.add), [ta, tb_], [mg])
                      for oc in range(DC):
                          w = ring.load(wbf[("wout", l)], oc, DC)
                          pb_, pbb = bk(4 + oc % 2), BK[4 + oc % 2]
                          T.mm([(lambda c=c: nc.tensor.matmul(pb_[:, 0:n], w.t[:, c, :], mg.t[:, c, 0:n], start=(c == 0), stop=(c == DC - 1))) for c in range(DC)], [w, mg], [pbb])
                          T.op(DVE, lambda: nc.vector.tensor_tensor(out=xo.t[:, oc, 0:n], in0=pb_[:, 0:n], in1=xt.t[:, oc, 0:n], op=ALU.add), [pbb, xt], [xo])
                      T.dma(POOL, xm.t.ap()[:, c0:c0 + n].rearrange("(c p) n -> p c n", p=128), xo.t[:, :, 0:n], "xo", [xo], [xm])
                      if meta:
                          T.op(DVE, lambda: nc.vector.tensor_copy(out=xh.t[:, :, 0], in_=xo.t[:, :, 15]), [xo], [xh])
                      else:
                          if hj + 1 < NT:
                              T.op(DVE, lambda: nc.vector.tensor_copy(out=xh.t[:, :, 2 * (hj + 1)], in_=xo.t[:, :, n - 1]), [xo], [xh])
                          else:
                              T.op(DVE, lambda: nc.vector.tensor_copy(out=hsb.t[:, 1, :], in_=xo.t[:, :, n - 1]), [xo], [hsb])
                          if hj >= 1:
                              T.op(DVE, lambda: nc.vector.tensor_copy(out=xh.t[:, :, 2 * (hj - 1) + 1], in_=xo.t[:, :, 0]), [xo], [xh])
                          else:
                              T.op(DVE, lambda: nc.vector.tensor_copy(out=hsb.t[:, 0, :], in_=xo.t[:, :, 0]), [xo], [hsb])
                  T.dma(POOL, hin[l].t.ap().rearrange("p (a c) -> p a c", a=2), hsb.t[:], "hsb", [hsb], [hin[l]])
              T.barrier()
              T.coll("AllGather", GRP, hin[l], hout[l], f"h{l}")
              ck('halo')
              with contextlib.ExitStack() as ph:
                  hb = sb(ph, "n_hb", [128, 4, 2, DC], F32)
                  T.dma(POOL, hb.t[:], hout[l].t.ap().rearrange("(r p) (a c) -> p r a c", p=128, a=2), "n_hb", [hout[l]], [hb])
                  jl, jr, jm = 0, 2 * (NT - 1) + 1, 2 * NT + 1
                  T.op(DVE, lambda: nc.vector.tensor_scalar(out=xh.t[:, :, jl], in0=xh.t[:, :, jl], scalar1=selb.t[:, 0:1], scalar2=None, op0=ALU.mult), [xh, selb], [xh])
                  for j in range(4):
                      T.op(DVE, lambda j=j: nc.vector.scalar_tensor_tensor(out=xh.t[:, :, jl], in0=hb.t[:, j, 1, :], scalar=selb.t[:, 1 + j:2 + j], in1=xh.t[:, :, jl],
                                                                          op0=ALU.mult, op1=ALU.add), [hb, selb, xh], [xh])
                      T.op(DVE, lambda j=j: nc.vector.scalar_tensor_tensor(out=xh.t[:, :, jr], in0=hb.t[:, j, 0, :], scalar=selb.t[:, 5 + j:6 + j], in1=xh.t[:, :, jr],
                                                                          op0=ALU.mult, op1=ALU.add), [hb, selb, xh], [xh])
                  T.op(DVE, lambda: nc.vector.tensor_copy(out=xh.t[:, :, jm], in_=hb.t[:, 0, 0, :]), [hb], [xh])
                  NH = cf.NH
                  sqh_ = sb(ph, "n_sqh", [128, DC, NH], BF16)
                  h2h = sb(ph, "n_h2h", [128, DC, NH], BF16)
                  rsx = sb(ph, "n_rsx", [128, NH], F32)
                  rstx = sb(ph, "n_rstx", [128, NH], F32)
                  ugh = sb(ph, "n_ugh", [128, FFC, NH], F32)
                  xt = sb(ph, "n_xt", [128, DC, TT], F32, key="xt")
                  sq = sb(ph, "n_sq", [128, DC, TT], BF16)
                  h2 = sb(ph, "n_h2", [128, DC, TT], BF16)
                  rs = sb(ph, "n_rs", [128, TT], F32)
                  rstd = sb(ph, "n_rstd", [128, TT], F32)
                  cc = sb(ph, "n_cc", [128, TT], F32)
                  sg = sb(ph, "n_sg", [128, TT], F32)
                  uT = sb(ph, "n_uT", [128, FFC, TT], BF16)
                  xo = xt
                  ring = WRing(ph, "n_w", DC, 4)
                  ringd = WRing(ph, "n_wd", FFC, 2)
                  rmsnorm(xh, NH, cf.V_GFFN, sqh_, h2h, rsx, rstx, bk(6), BK[6])
                  for fc in range(FFC):
                      w = ring.load(wbf[("wup", l)], fc, DC)
                      T.mm([(lambda c=c: nc.tensor.matmul(bk(fc % 2)[:, 0:NH], w.t[:, c, :], h2h.t[:, c, :], start=(c == 0), stop=(c == DC - 1))) for c in range(DC)], [w, h2h], [BK[fc % 2]])
                      T.op(ACT, lambda: nc.scalar.copy(out=ugh.t[:, fc, :], in_=bk(fc % 2)[:, 0:NH]), [BK[fc % 2]], [ugh])
                  wv = lambda fc, k: vec.t[:, cf.V_WCV + 3 * fc + k:cf.V_WCV + 3 * fc + k + 1]
                  for (c0, n, hj) in tiles:
                      meta = (n == 16)
                      if last and meta:
                          continue
                      T.dma(POOL, xt.t[:, :, 0:n], xm.t.ap()[:, c0:c0 + n].rearrange("(c p) n -> p c n", p=128), xt.name, [xm], [xt])
                      rmsnorm(xt, n, cf.V_GFFN, sq, h2, rs, rstd, bk(6), BK[6])
                      for fc in range(FFC):
                          wg_ = ring.load(wbf[("wup", l)], fc, DC)
                          wv_ = ring.load(wbf[("wup", l)], FFC + fc, DC)
                          pg, pgb = bk(2 * (fc % 2)), BK[2 * (fc % 2)]
                          pu, pub = bk(2 * (fc % 2) + 1), BK[2 * (fc % 2) + 1]
                          T.mm([(lambda c=c: nc.tensor.matmul(pg[:, 0:n], wg_.t[:, c, :], h2.t[:, c, 0:n], start=(c == 0), stop=(c == DC - 1))) for c in range(DC)], [wg_, h2], [pgb])
                          T.mm([(lambda c=c: nc.tensor.matmul(pu[:, 0:n], wv_.t[:, c, :], h2.t[:, c, 0:n], start=(c == 0), stop=(c == DC - 1))) for c in range(DC)], [wv_, h2], [pub])
                          bcol = vec.t[:, cf.V_BCV + fc:cf.V_BCV + fc + 1]
                          T.op(DVE, lambda: nc.vector.tensor_scalar(out=cc.t[:, 0:n], in0=pg[:, 0:n], scalar1=wv(fc, 1), scalar2=bcol, op0=ALU.mult, op1=ALU.add), [pgb, vec], [cc])
                          T.op(DVE, lambda: nc.vector.scalar_tensor_tensor(out=cc.t[:, 1:n], in0=pg[:, 0:n - 1], scalar=wv(fc, 0), in1=cc.t[:, 1:n], op0=ALU.mult, op1=ALU.add), [pgb, vec, cc], [cc])
                          T.op(DVE, lambda: nc.vector.scalar_tensor_tensor(out=cc.t[:, 0:n - 1], in0=pg[:, 1:n], scalar=wv(fc, 2), in1=cc.t[:, 0:n - 1], op0=ALU.mult, op1=ALU.add), [pgb, vec, cc], [cc])
                          T.op(DVE, lambda: nc.vector.scalar_tensor_tensor(out=cc.t[:, 0:1], in0=ugh.t[:, fc, 2 * hj:2 * hj + 1], scalar=wv(fc, 0), in1=cc.t[:, 0:1], op0=ALU.mult, op1=ALU.add), [ugh, vec, cc], [cc])
                          T.op(DVE, lambda: nc.vector.scalar_tensor_tensor(out=cc.t[:, n - 1:n], in0=ugh.t[:, fc, 2 * hj + 1:2 * hj + 2], scalar=wv(fc, 2), in1=cc.t[:, n - 1:n], op0=ALU.mult, op1=ALU.add), [ugh, vec, cc], [cc])
                          T.op(ACT, lambda: nc.scalar.activation(out=sg.t[:, 0:n], in_=cc.t[:, 0:n], func=AF.Silu), [cc], [sg])
                          T.op(DVE, lambda: nc.vector.tensor_tensor(out=uT.t[:, fc, 0:n], in0=sg.t[:, 0:n], in1=pu[:, 0:n], op=ALU.mult), [sg, pub], [uT])
                      for oc in range(DC):
                          w = ringd.load(wbf[("wdn", l)], oc, FFC)
                          pb_, pbb = bk(4 + oc % 2), BK[4 + oc % 2]
                          T.mm([(lambda c=c: nc.tensor.matmul(pb_[:, 0:n], w.t[:, c, :], uT.t[:, c, 0:n], start=(c == 0), stop=(c == FFC - 1))) for c in range(FFC)], [w, uT], [pbb])
                          T.op(DVE, lambda: nc.vector.tensor_tensor(out=xo.t[:, oc, 0:n], in0=pb_[:, 0:n], in1=xt.t[:, oc, 0:n], op=ALU.add), [pbb, xt], [xo])
                      if last and not meta:
                          T.dma(POOL, yT.ap()[:, c0 - 16:c0 - 16 + n].rearrange("(c p) n -> p c n", p=128), xo.t[:, :, 0:n], "xo", [xo], [B_yT])
                      elif not last:
                          T.dma(POOL, xa.t.ap()[:, c0:c0 + n].rearrange("(c p) n -> p c n", p=128), xo.t[:, :, 0:n], "xo", [xo], [xa])
              xh_st.close()
        except _Stop:
            pass
        T.dead = False
        T.barrier(full=True)
        if getattr(cf, 'endclear', False):
            fin_sem = stack.enter_context(nc.semaphore("s_fin"))
            for eng in (T.pe, T.act, T.dve, T.sp):
                eng.e.sem_inc(fin_sem, 1)
            nc.gpsimd.wait_ge(fin_sem, 4)
            allsems = [e.sem for e in T.engs] + [v[0] for v in T.dsems.values()] + [sv[0] for sv in T.csems] + [fin_sem]
            for sm in allsems:
                nc.gpsimd.sem_clear(sm)
    return nc, T.nsem


_CACHE = {}


def run(cf, inputs):
    in_maps = prep_inputs(cf, **inputs)
    if "nc" not in _CACHE or _CACHE.get("cf") is not cf:
        _CACHE["nc"], nsem = build(cf)
        _CACHE["cf"] = cf
    res = run_bass_kernel_spmd(_CACHE["nc"], in_maps, core_ids=list(range(NCORES)))
    out = np.empty((2, cf.SEQ, cf.D), np.float32)
    for c in range(NCORES):
        b, r = c // 4, c % 4
        out[b, r * cf.TPC:(r + 1) * cf.TPC, :] = res.results[c]["yT"].T
    return out


def kernel(**inputs):
    return run(FULL, inputs)
```

```python
import contextlib
import math
import numpy as np
import concourse.bass as bass
import concourse.mybir as mybir
from concourse.bass_utils import run_bass_kernel_spmd

F32, BF16 = mybir.dt.float32, mybir.dt.bfloat16
AF = mybir.ActivationFunctionType
ALU = mybir.AluOpType
NCORES = 8
GRP = [[0, 1, 2, 3], [4, 5, 6, 7]]
ALL8 = [list(range(8))]


def nchunks_for(n, unit_bytes, limit=1 << 20):
    for k in range(1, n + 1):
        if n % k == 0 and (n // k) * unit_bytes <= limit:
            return k
    raise ValueError


class Cfg:
    def __init__(self, D=2048, SEQ=16384, H=8, KVH=2, FG=8, DFF=5632, TT=512, N1=100, N2=164, DEPTH=2):
        self.D, self.SEQ, self.H, self.KVH, self.FG, self.DFF, self.TT = D, SEQ, H, KVH, FG, DFF, TT
        self.N1, self.N2, self.DEPTH = N1, N2, DEPTH
        self.HD = 128
        self.G = H // KVH
        self.AW = H * 128
        self.KVW = KVH * 128
        self.FW = FG * 128
        self.NMETA = 16
        self.GRIDW = 64
        self.L = SEQ + 16
        assert N1 * N2 == self.L
        self.TPC = SEQ // 4
        self.NT = self.TPC // TT
        self.NCOL = 16 + self.TPC
        self.DC = D // 128
        self.FFC = DFF // 128
        self.FCH = self.FW // 4
        self.FCB = self.FCH // 128
        self.OFF_K = self.AW
        self.OFF_V = self.AW + self.KVW
        self.OFF_F = self.OFF_V + self.KVW
        self.OFF_GA = self.OFF_F + self.FW
        self.INW = self.OFF_GA + 2 * D
        self.EPS = 1e-6
        self.N2C = N2 // 2
        assert self.N2C * 2 == N2 and self.N2C <= 128 and 2 * N2 <= 512 and N1 <= 128
        self.P2B = 512 // self.FCH
        assert N2 % self.P2B == 0
        self.NH = 2 * (self.NT + 1)
        MB = 1 << 20
        self.NFC = nchunks_for(self.TPC, self.FW * 2)
        self.NPC = nchunks_for(self.L, 2 * self.FCH * 2)
        self.LC = self.L // self.NPC
        assert 128 * self.TPC * 2 <= MB
        o = 0
        self.V_GMIX = o; o += self.DC
        self.V_GFFN = o; o += self.DC
        self.V_BG = o; o += 2 * self.DC
        self.V_QN = o; o += 1
        self.V_KN = o; o += 1
        self.V_WCV = o; o += 3 * self.FFC
        self.V_BCV = o; o += self.FFC
        self.NV = o


FULL = Cfg()


def lhsT_layout(W):
    K, N = W.shape
    return np.ascontiguousarray(W.reshape(K // 128, 128, N // 128, 128).transpose(2, 1, 0, 3).reshape(N, K))


def rhs_layout(W):
    K, N = W.shape
    return np.ascontiguousarray(W.reshape(K // 128, 128, N).transpose(1, 0, 2).reshape(128, (K // 128) * N))


WNAMES = ["wqk", "wg", "wvf", "wab", "wfr", "wout", "wup", "wdn"]


def weight_shapes(cf):
    return {
        "wqk": (cf.AW + cf.KVW, cf.D), "wg": (2 * cf.D, cf.D), "wvf": (128, cf.DC * (cf.KVW + cf.FW)),
        "wab": (cf.D, cf.AW), "wfr": (128, cf.FG * cf.D), "wout": (cf.D, cf.D),
        "wup": (2 * cf.DFF, cf.D), "wdn": (cf.D, cf.DFF),
    }


def weight_chunks(cf):
    out = {}
    for n, (R, C) in weight_shapes(cf).items():
        out[n] = nchunks_for(R // 4, C * 2)
    return out


def host_tables(cf):
    f64 = np.float64
    tabs = {}
    ident = np.eye(128, dtype=np.float32)
    rotT = np.zeros((128, 128), np.float32)
    for i in range(128):
        blk = i // 32
        if blk % 2 == 0:
            rotT[i + 32, i] = -1.0
        else:
            rotT[i - 32, i] = 1.0
    N1, N2, L = cf.N1, cf.N2, cf.L
    a1 = 2 * np.pi * np.outer(np.arange(N1), np.arange(N1)).astype(f64) / N1
    t1 = np.concatenate([np.cos(a1), np.sin(a1)], 1).astype(np.float32)
    a2 = 2 * np.pi * np.outer(np.arange(N2), np.arange(N2)).astype(f64) / N2
    C2, S2 = np.cos(a2), np.sin(a2)
    tabR = np.concatenate([C2, S2], 1)
    tabI = np.concatenate([-S2, C2], 1)
    t3 = np.stack([tabR, tabI], 0).reshape(2, 2, cf.N2C, 2 * N2).astype(np.float32)
    at = 2 * np.pi * np.outer(np.arange(N1), np.arange(N2)).astype(f64) / L
    tw = np.stack([np.cos(at), np.sin(at)], 1).astype(np.float32)
    ac = 2 * np.pi * np.outer(np.arange(128), np.arange(128)).astype(f64) / 128
    sc = 1.0 / math.sqrt(L * 128.0)
    cs = np.stack([np.cos(ac) * sc, -np.sin(ac) * sc], 1).astype(np.float32)
    tabs["cmat"] = np.ascontiguousarray(np.concatenate([ident, rotT, cs.reshape(128, 256)], 1))
    tabs["t1"] = t1
    tabs["t3"] = np.ascontiguousarray(t3.transpose(2, 0, 1, 3).reshape(cf.N2C, 4 * 2 * N2))
    tabs["tw"] = np.ascontiguousarray(tw.reshape(N1, 2 * N2))
    return tabs


def core_tables(cf, r):
    inv = 1.0 / (10000.0 ** (np.arange(32, dtype=np.float64) / 32))
    tg = r * cf.TPC + np.arange(cf.TPC)
    rows = (tg // cf.GRIDW).astype(np.float64)
    cols = (tg % cf.GRIDW).astype(np.float64)
    ang = np.zeros((128, cf.NCOL), np.float64)
    ang[0:32, 16:] = inv[:, None] * rows[None]
    ang[32:64, 16:] = inv[:, None] * rows[None]
    ang[64:96, 16:] = inv[:, None] * cols[None]
    ang[96:128, 16:] = inv[:, None] * cols[None]
    rope = np.stack([np.cos(ang), np.sin(ang)], 1).astype(np.float32)
    sel = np.zeros((128, 16), np.float32)
    sel[:, 0] = 1.0 if r == 0 else 0.0
    for j in range(4):
        sel[:, 1 + j] = 1.0 if j == r - 1 else 0.0
        sel[:, 5 + j] = 1.0 if j == r + 1 else 0.0
        sel[:, 9 + j] = 1.0 if j == r else 0.0
    return np.ascontiguousarray(rope.reshape(128, 2 * cf.NCOL)), sel


def prep_inputs(cf, x, meta_tokens, norm_mix, norm_ffn, w_in, b_gate, q_norm, k_norm,
                w_attn_br, w_four, w_out, w_up, w_conv, b_conv, w_down):
    f = lambda a: np.asarray(a, dtype=np.float32)
    x, meta_tokens = f(x), f(meta_tokens)
    w_in, w_attn_br, w_four, w_out, w_up, w_down = map(f, (w_in, w_attn_br, w_four, w_out, w_up, w_down))
    DEPTH = cf.DEPTH
    full = {n: [] for n in WNAMES}
    for l in range(DEPTH):
        full["wqk"].append(lhsT_layout(w_in[l][:, 0:cf.OFF_V]))
        full["wg"].append(lhsT_layout(w_in[l][:, cf.OFF_GA:]))
        full["wvf"].append(rhs_layout(w_in[l][:, cf.OFF_V:cf.OFF_GA]))
        full["wab"].append(lhsT_layout(w_attn_br[l]))
        full["wfr"].append(rhs_layout(w_four[l]))
        full["wout"].append(lhsT_layout(w_out[l]))
        full["wup"].append(lhsT_layout(w_up[l]))
        full["wdn"].append(lhsT_layout(w_down[l]))
    vecs = np.zeros((DEPTH, 128, cf.NV), np.float32)
    for l in range(DEPTH):
        vecs[l, :, cf.V_GMIX:cf.V_GMIX + cf.DC] = f(norm_mix[l]).reshape(cf.DC, 128).T
        vecs[l, :, cf.V_GFFN:cf.V_GFFN + cf.DC] = f(norm_ffn[l]).reshape(cf.DC, 128).T
        vecs[l, :, cf.V_BG:cf.V_BG + 2 * cf.DC] = f(b_gate[l]).reshape(2 * cf.DC, 128).T
        vecs[l, :, cf.V_QN] = f(q_norm[l])
        vecs[l, :, cf.V_KN] = f(k_norm[l])
        wc = f(w_conv[l]).reshape(3, cf.FFC, 128).transpose(2, 1, 0)
        vecs[l, :, cf.V_WCV:cf.V_WCV + 3 * cf.FFC] = wc.reshape(128, 3 * cf.FFC)
        vecs[l, :, cf.V_BCV:cf.V_BCV + cf.FFC] = f(b_conv[l]).reshape(cf.FFC, 128).T
    tabs = host_tables(cf)
    wch = weight_chunks(cf)
    metaT = np.ascontiguousarray(meta_tokens.T)
    in_maps = []
    for c in range(NCORES):
        b, r = c // 4, c % 4
        m = {}
        m["xT"] = np.ascontiguousarray(x[b, r * cf.TPC:(r + 1) * cf.TPC, :].T)
        m["metaT"] = metaT
        for n in WNAMES:
            R = full[n][0].shape[0]
            nch = wch[n]
            rc = R // 4 // nch
            m[n] = np.ascontiguousarray(np.stack(
                [full[n][l].reshape(nch, 4, rc, -1)[:, r].reshape(nch * rc, -1) for l in range(DEPTH)], 0))
        m["vecs"] = vecs
        rope, sel = core_tables(cf, r)
        m["rope"] = rope
        m["sel"] = sel
        for k, v in tabs.items():
            m[k] = v
        in_maps.append(m)
    return in_maps


class _Stop(Exception):
    pass


class Buf:
    def __init__(self, name, t):
        self.name, self.t = name, t
        self.w, self.r = {}, {}


class Eng:
    def __init__(self, name, e, sem):
        self.name, self.e, self.sem = name, e, sem
        self.count = 0
        self.waited = {}


def _merge(d, src):
    for k, (s, v) in src.items():
        if k not in d or d[k][1] < v:
            d[k] = (s, v)


class TR:
    def __init__(self, nc, stack):
        self.nc, self.stack = nc, stack
        mk = lambda n: stack.enter_context(nc.semaphore(n))
        self.pe = Eng("pe", nc.tensor, mk("s_pe"))
        self.act = Eng("act", nc.scalar, mk("s_act"))
        self.dve = Eng("dve", nc.vector, mk("s_dve"))
        self.pool = Eng("pool", nc.gpsimd, mk("s_pool"))
        self.sp = Eng("sp", nc.sync, mk("s_sp"))
        self.engs = [self.pe, self.act, self.dve, self.pool, self.sp]
        self.dsems = {}
        self.csems = []
        self.shsem = {}
        self.nsem = 5
        self.dead = False

    def _sync(self, eng, reads, writes, ignore=None):
        raw = {}
        for b in reads:
            _merge(raw, b.w)
        oth = {}
        for b in writes:
            _merge(oth, b.w)
            _merge(oth, b.r)
        me = id(eng.sem)
        d = dict(raw)
        for k, sv in oth.items():
            if k == me:
                continue
            if k not in d or d[k][1] < sv[1]:
                d[k] = sv
        if eng is self.pe:
            d.pop(me, None)
        if ignore is not None:
            d.pop(ignore, None)
        for k, (s, v) in d.items():
            if eng.waited.get(k, 0) < v:
                eng.e.wait_ge(s, v)
                eng.waited[k] = v

    def _rec(self, ev, reads, writes):
        k, s, v = ev
        for b in reads:
            if k not in b.r or b.r[k][1] < v:
                b.r[k] = (s, v)
        for b in writes:
            if k not in b.w or b.w[k][1] < v:
                b.w[k] = (s, v)

    def op(self, eng, fn, reads=(), writes=()):
        if self.dead:
            return
        self._sync(eng, reads, writes)
        ins = fn()
        eng.count += 1
        ins.then_inc(eng.sem, 1)
        self._rec((id(eng.sem), eng.sem, eng.count), reads, writes)

    def mm(self, fns, reads=(), writes=()):
        if self.dead:
            return
        eng = self.pe
        self._sync(eng, reads, writes)
        ins = None
        for fn in fns:
            ins = fn()
        eng.count += 1
        ins.then_inc(eng.sem, 1)
        self._rec((id(eng.sem), eng.sem, eng.count), reads, writes)

    def dma(self, q, out, in_, key, reads=(), writes=()):
        if self.dead:
            return
        self._sync(q, reads, writes)
        if key not in self.dsems:
            self.nsem += 1
            self.dsems[key] = [self.stack.enter_context(self.nc.semaphore("d_" + key)), 0]
        ent = self.dsems[key]
        ent[1] += 1
        q.e.dma_start(out=out, in_=in_).then_inc(ent[0], 16)
        self._rec((id(ent[0]), ent[0], 16 * ent[1]), reads, writes)

    def coll(self, kind, groups, inb, outb, name, shared=None, in_ap=None, out_ap=None):
        if self.dead:
            return
        q = self.pool
        if shared is not None and shared[0] in self.shsem:
            self._sync(q, [inb], [outb], ignore=id(self.shsem[shared[0]]))
        else:
            self._sync(q, [inb], [outb])
        if shared is None:
            self.nsem += 1
            sem = self.stack.enter_context(self.nc.semaphore("c_" + name))
            val = 1
            self.csems.append((sem, 1))
        else:
            key, val = shared
            if key not in self.shsem:
                self.nsem += 1
                self.shsem[key] = self.stack.enter_context(self.nc.semaphore("c_" + key))
                self.csems.append((self.shsem[key], val))
            sem = self.shsem[key]
        q.e.collective_compute(kind, ALU.bypass, replica_groups=groups,
                               ins=[(in_ap if in_ap is not None else inb.t.ap()).opt()],
                               outs=[(out_ap if out_ap is not None else outb.t.ap()).opt()]).then_inc(sem, 1)
        self._rec((id(sem), sem, val), [inb], [outb])

    def barrier(self, full=False):
        if self.dead:
            return
        evs = [(id(e.sem), e.sem, e.count) for e in self.engs if e.count > 0]
        evs += [(id(s), s, 16 * c) for (s, c) in self.dsems.values() if c > 0]
        if full:
            evs += [(id(s), s, v) for (s, v) in self.csems]
        for eng in self.engs:
            for k, s, v in evs:
                if k == id(eng.sem):
                    continue
                if eng.waited.get(k, 0) < v:
                    eng.e.wait_ge(s, v)
                    eng.waited[k] = v


def build(cf, final_wait=True):
    nc = bass.Bass("TRN2", target_bir_lowering=False)
    D, DC, TT, NT, NCOL, TPC, H, KVH, G = cf.D, cf.DC, cf.TT, cf.NT, cf.NCOL, cf.TPC, cf.H, cf.KVH, cf.G
    FW, FCH, FCB, FFC, KVW, AW, L, N1, N2, N2C = cf.FW, cf.FCH, cf.FCB, cf.FFC, cf.KVW, cf.AW, cf.L, cf.N1, cf.N2, cf.N2C
    DEPTH = cf.DEPTH
    NQK = H + KVH
    VFW = KVW + FW
    PQC = 2 * FW // 128
    wsh = weight_shapes(cf)

    def din(name, shape, dt=F32):
        return nc.dram_tensor(name, list(shape), dt, kind="ExternalInput")

    xT_in = din("xT", [D, TPC])
    metaT_in = din("metaT", [D, 16])
    w_in_sh = {n: din(n, [DEPTH, wsh[n][0] // 4, wsh[n][1]]) for n in WNAMES}
    vecs_in = din("vecs", [DEPTH, 128, cf.NV])
    rope_in = din("rope", [128, 2 * NCOL])
    sel_in = din("sel", [128, 16])
    cmat_in = din("cmat", [128, 512])
    t1_in = din("t1", [N1, 2 * N1])
    t3_in = din("t3", [N2C, 8 * N2])
    tw_in = din("tw", [N1, 2 * N2])
    yT = nc.dram_tensor("yT", [D, TPC], F32, kind="ExternalOutput")

    stack = contextlib.ExitStack()
    with stack:
        stack.enter_context(nc.allow_non_contiguous_dma(reason="small strided scratch transfers"))
        T = TR(nc, stack)
        blk = stack.enter_context(nc.Block())
        PE, ACT, DVE, POOL, SP = T.pe, T.act, T.dve, T.pool, T.sp

        def dram(name, shape, dt):
            return Buf(name, nc.dram_tensor(name, list(shape), dt))

        def ext(tn, name):
            return Buf(name, tn)

        B_xT, B_metaT, B_yT = ext(xT_in, "xT"), ext(metaT_in, "metaT"), ext(yT, "yT")
        B_vecs, B_rope, B_sel, B_cmat = ext(vecs_in, "vecs"), ext(rope_in, "rope"), ext(sel_in, "sel"), ext(cmat_in, "cmat")
        B_t1, B_t3, B_tw = ext(t1_in, "t1"), ext(t3_in, "t3"), ext(tw_in, "tw")
        B_wsh = {n: ext(w_in_sh[n], n) for n in WNAMES}

        wbs = {(n, l): dram(f"wbs_{n}{l}", [wsh[n][0] // 4, wsh[n][1]], BF16) for n in WNAMES for l in range(DEPTH)}
        wbf = {(n, l): dram(f"wbf_{n}{l}", [wsh[n][0], wsh[n][1]], BF16) for n in WNAMES for l in range(DEPTH)}
        wcs_d = [dram(f"wcs{l}", [D, 2 * FW], BF16) for l in range(DEPTH)]
        xa = dram("xa", [D, NCOL], F32)
        xm = dram("xm", [D, NCOL], F32)
        qT_d = dram("qT", [AW, NCOL], BF16)
        gT_d = dram("gT", [2 * D, NCOL], BF16)
        aT_d = dram("aT", [AW, NCOL], BF16)
        pqsel_d = dram("pqsel", [PQC * 128, NCOL], BF16)
        kin = [dram(f"kin{l}", [KVW, TPC], BF16) for l in range(DEPTH)]
        kout = [dram(f"kout{l}", [4 * KVW, TPC], BF16) for l in range(DEPTH)]
        kmeta = dram("kmeta", [KVW, 16], BF16)
        vin = [dram(f"vin{l}", [KVH * TPC, 128], BF16) for l in range(DEPTH)]
        vout = [dram(f"vout{l}", [KVH * 4 * TPC, 128], BF16) for l in range(DEPTH)]
        vmeta = dram("vmeta", [16, KVW], BF16)
        fin = [dram(f"fin{l}", [TPC, FW], BF16) for l in range(DEPTH)]
        fout = [dram(f"fout{l}", [4 * TPC, FW], BF16) for l in range(DEPTH)]
        fmeta = dram("fmeta", [16, FW], BF16)
        yd = dram("yd", [2, N1, N2, FCH], BF16)
        fpos = dram("fpos", [L, FW], BF16)
        pqin = [dram(f"pqin{l}", [cf.NPC * 2 * FCH, cf.LC], BF16) for l in range(DEPTH)]
        pqout = [dram(f"pqout{l}", [cf.NPC * 4 * 2 * FCH, cf.LC], BF16) for l in range(DEPTH)]
        hin = [dram(f"hin{l}", [128, 2 * DC], F32) for l in range(DEPTH)]
        hout = [dram(f"hout{l}", [4 * 128, 2 * DC], F32) for l in range(DEPTH)]

        ps0 = stack.enter_context(nc.psum_tensor("ps0", [128, 1024], F32))
        ps1 = stack.enter_context(nc.psum_tensor("ps1", [128, 1024], F32))
        ps4 = stack.enter_context(nc.psum_tensor("ps4", [128, 512], F32))
        ps5 = stack.enter_context(nc.psum_tensor("ps5", [128, 512], F32))
        ps6 = stack.enter_context(nc.psum_tensor("ps6", [128, 512], F32))
        pst = stack.enter_context(nc.psum_tensor("pst", [128, 1024], BF16))
        BK = [Buf(f"bk{i}", None) for i in range(8)]
        bank_aps = [ps0[:, 0:512], ps0[:, 512:1024], ps1[:, 0:512], ps1[:, 512:1024], ps4[:, :], ps5[:, :], ps6[:, :]]

        def bk(i):
            return bank_aps[i]

        uniq = [0]

        def sb(st, name, shape, dt, key=None):
            uniq[0] += 1
            return Buf(key or name, st.enter_context(nc.sbuf_tensor(f"{name}_{uniq[0]}", list(shape), dt)))

        ident = sb(stack, "ident", [128, 128], BF16)
        rotT = sb(stack, "rotT", [128, 128], BF16)
        csm = sb(stack, "csm", [128, 2, 128], BF16)
        ones = sb(stack, "ones", [128, 128], BF16)
        selb = sb(stack, "selb", [128, 16], F32)
        vec = sb(stack, "vec", [128, cf.NV], F32)

        T.dma(POOL, ident.t[:], cmat_in.ap()[:, 0:128], "c0", [B_cmat], [ident])
        T.dma(POOL, rotT.t[:], cmat_in.ap()[:, 128:256], "c0", [B_cmat], [rotT])
        T.dma(POOL, csm.t[:], cmat_in.ap()[:, 256:512].rearrange("p (a b) -> p a b", a=2), "c0", [B_cmat], [csm])
        T.dma(POOL, selb.t[:], sel_in.ap(), "c0", [B_sel], [selb])
        T.op(DVE, lambda: nc.vector.memset(ones.t[:], 1.0), [], [ones])

        T.dma(POOL, xa.t.ap()[:, 0:16], metaT_in.ap(), "xinit", [B_metaT], [xa])
        T.dma(POOL, xa.t.ap()[:, 16:NCOL], xT_in.ap(), "xinit", [B_xT], [xa])

        def ag_chunks(inb, outb, rows_in, nch, key, total=None):
            rc = rows_in // nch
            for j in range(nch):
                if getattr(cf, 'unshare', False) and total is None:
                    T.coll("AllGather", GRP, inb, outb, f"{key}_{j}",
                           in_ap=inb.t.ap()[j * rc:(j + 1) * rc, :], out_ap=outb.t.ap()[j * 4 * rc:(j + 1) * 4 * rc, :])
                else:
                    T.coll("AllGather", GRP, inb, outb, key, shared=(key, total or nch),
                           in_ap=inb.t.ap()[j * rc:(j + 1) * rc, :], out_ap=outb.t.ap()[j * 4 * rc:(j + 1) * 4 * rc, :])

        wch = weight_chunks(cf)
        wtot = sum(wch.values())
        WGA = ["wqk", "wg", "wvf"]
        wtot_a = sum(wch[n] for n in WGA)
        wtot_b = wtot - wtot_a

        def weight_ags(l):
            for n in WNAMES:
                if n in WGA:
                    ag_chunks(wbs[(n, l)], wbf[(n, l)], wsh[n][0] // 4, wch[n], f"w{l}a", wtot_a)
                else:
                    ag_chunks(wbs[(n, l)], wbf[(n, l)], wsh[n][0] // 4, wch[n], f"w{l}b", wtot_b)

        for l in range(DEPTH):
            for n in WNAMES:
                T.dma(POOL, wbs[(n, l)].t.ap(), w_in_sh[n].ap()[l], "wcast", [B_wsh[n]], [wbs[(n, l)]])
            if l == 0:
                weight_ags(0)

        tiles = [(16 + i * TT, TT, i) for i in range(NT)] + [(0, 16, NT)]

        class WRing:
            def __init__(self, st, name, kcmax, nslots):
                self.slots = [sb(st, f"{name}{i}", [128, kcmax, 128], BF16, key=f"{name[2:]}{i}") for i in range(nslots)]
                self.i = 0

            def load(self, wb, chunk, kc):
                s = self.slots[self.i % len(self.slots)]
                self.i += 1
                src = wb.t.ap()[chunk * 128:(chunk + 1) * 128, :].rearrange("p (k n) -> p k n", n=128)
                T.dma(SP, s.t[:, 0:kc, :], src, s.name, [wb], [s])
                return s

        def rmsnorm(xt, n, gcol, sq, hT, rs, rstd, pbank, pbuf):
            T.op(ACT, lambda: nc.scalar.activation(out=sq.t[:, :, 0:n], in_=xt.t[:, :, 0:n], func=AF.Square), [xt], [sq])
            T.mm([(lambda c=c: nc.tensor.matmul(pbank[:, 0:n], ones.t[:, :], sq.t[:, c, 0:n], start=(c == 0), stop=(c == DC - 1)))
                  for c in range(DC)], [ones, sq], [pbuf])
            T.op(ACT, lambda: nc.scalar.activation(out=rs.t[:, 0:n], in_=pbank[:, 0:n], func=AF.Sqrt, bias=float(cf.EPS), scale=1.0 / D), [pbuf], [rs])
            T.op(DVE, lambda: nc.vector.reciprocal(out=rstd.t[:, 0:n], in_=rs.t[:, 0:n]), [rs], [rstd])
            for c in range(DC):
                T.op(DVE, lambda c=c: nc.vector.scalar_tensor_tensor(out=hT.t[:, c, 0:n], in0=xt.t[:, c, 0:n], scalar=vec.t[:, gcol + c:gcol + c + 1],
                                                                    in1=rstd.t[:, 0:n], op0=ALU.mult, op1=ALU.mult), [xt, vec, rstd], [hT])

        def ck(name):
            if getattr(cf, 'stop', None) == name:
                if getattr(cf, 'exc', False):
                    raise _Stop()
                T.barrier(full=True)
                T.dead = True

        try:
          for l in range(DEPTH):
              last = (l == DEPTH - 1)
              T.barrier()
              ck('pro')
              T.dma(POOL, vec.t[:], vecs_in.ap()[l], "c_vec", [B_vecs], [vec])

              with contextlib.ExitStack() as ph:
                  xt = sb(ph, "p1_xt", [128, DC, TT], F32, key="xt")
                  sq = sb(ph, "p1_sq", [128, DC, TT], BF16)
                  hT = sb(ph, "p1_hT", [128, DC, TT], BF16)
                  rs = sb(ph, "p1_rs", [128, TT], F32)
                  rstd = sb(ph, "p1_rstd", [128, TT], F32)
                  wvf = sb(ph, "p1_wvf", [128, DC, VFW], BF16)
                  ring = WRing(ph, "p1_w", DC, 4)
                  rope = sb(ph, "p1_rope", [128, 2, TT], F32)
                  sqh = sb(ph, "p1_sqh", [128, TT], BF16)
                  qg = sb(ph, "p1_qg", [128, TT], BF16)
                  rsh = sb(ph, "p1_rsh", [128, TT], F32)
                  rstdh = sb(ph, "p1_rstdh", [128, TT], F32)
                  t1b = sb(ph, "p1_t1", [128, TT], F32)
                  t2b = sb(ph, "p1_t2", [128, TT], F32)
                  qo = [sb(ph, f"p1_qo{i}", [128, TT], BF16) for i in range(2)]
                  vo = [sb(ph, f"p1_vo{i}", [128, 512], BF16) for i in range(2)]
                  go = [sb(ph, f"p1_go{i}", [128, 4, TT], BF16) for i in range(2)]
                  T.dma(SP, wvf.t[:], wbf[("wvf", l)].t.ap().rearrange("p (k n) -> p k n", n=VFW), "p1_wvf", [wbf[("wvf", l)]], [wvf])
                  nq = nv = ng = 0
                  frc = TPC // cf.NFC
                  f_ag = [0]
                  f_cp = [0]

                  def f_copy(j):
                      for rr in range(4):
                          p0 = 16 + rr * TPC + j * frc
                          T.dma(POOL, fpos.t.ap()[p0:p0 + frc, :], fout[l].t.ap()[(j * 4 + rr) * frc:(j * 4 + rr + 1) * frc, :], "fpos_a", [fout[l]], [fpos])

                  def f_progress(done_tokens, flush=False):
                      while f_ag[0] < cf.NFC and (f_ag[0] + 1) * frc <= done_tokens:
                          j = f_ag[0]
                          T.coll("AllGather", GRP, fin[l], fout[l], f"f{l}", shared=(f"f{l}", cf.NFC),
                                 in_ap=fin[l].t.ap()[j * frc:(j + 1) * frc, :], out_ap=fout[l].t.ap()[j * 4 * frc:(j + 1) * 4 * frc, :])
                          f_ag[0] += 1
                          while f_cp[0] < f_ag[0] - 1:
                              f_copy(f_cp[0])
                              f_cp[0] += 1
                      if flush:
                          while f_cp[0] < f_ag[0]:
                              f_copy(f_cp[0])
                              f_cp[0] += 1

                  for (c0, n, _) in tiles:
                      meta = (n == 16)
                      T.dma(POOL, xt.t[:, :, 0:n], xa.t.ap()[:, c0:c0 + n].rearrange("(c p) n -> p c n", p=128), xt.name, [xa], [xt])
                      T.dma(POOL, rope.t[:, :, 0:n], rope_in.ap().rearrange("p (a n) -> p a n", a=2)[:, :, c0:c0 + n], "p1_rope", [B_rope], [rope])
                      ck('p1x')
                      rmsnorm(xt, n, cf.V_GMIX, sq, hT, rs, rstd, bk(6), BK[6])
                      ck('p1a')
                      def head_mm(hd):
                          w = ring.load(wbf[("wqk", l)], hd, DC)
                          pa, pab = bk(hd % 2), BK[hd % 2]
                          T.mm([(lambda c=c: nc.tensor.matmul(pa[:, 0:n], w.t[:, c, :], hT.t[:, c, 0:n], start=(c == 0), stop=(c == DC - 1)))
                                for c in range(DC)], [w, hT], [pab])

                      def head_p1(hd):
                          pa, pab = bk(hd % 2), BK[hd % 2]
                          gcol = cf.V_QN if hd < H else cf.V_KN
                          T.op(ACT, lambda: nc.scalar.activation(out=sqh.t[:, 0:n], in_=pa[:, 0:n], func=AF.Square), [pab], [sqh])
                          T.op(ACT, lambda: nc.scalar.activation(out=qg.t[:, 0:n], in_=pa[:, 0:n], func=AF.Copy, scale=vec.t[:, gcol:gcol + 1]), [pab, vec], [qg])

                      def head_p2(hd, q_o):
                          T.mm([lambda: nc.tensor.matmul(bk(4)[:, 0:n], ones.t[:, :], sqh.t[:, 0:n], start=True, stop=True)], [ones, sqh], [BK[4]])
                          T.mm([lambda: nc.tensor.matmul(bk(5)[:, 0:n], rotT.t[:, :], qg.t[:, 0:n], start=True, stop=True)], [rotT, qg], [BK[5]])
                          T.op(ACT, lambda: nc.scalar.activation(out=rsh.t[:, 0:n], in_=bk(4)[:, 0:n], func=AF.Sqrt, bias=float(cf.EPS), scale=1.0 / 128), [BK[4]], [rsh])
                          T.op(DVE, lambda: nc.vector.reciprocal(out=rstdh.t[:, 0:n], in_=rsh.t[:, 0:n]), [rsh], [rstdh])
                          T.op(DVE, lambda: nc.vector.tensor_tensor(out=t1b.t[:, 0:n], in0=qg.t[:, 0:n], in1=rope.t[:, 0, 0:n], op=ALU.mult), [qg, rope], [t1b])
                          T.op(DVE, lambda: nc.vector.tensor_tensor(out=t2b.t[:, 0:n], in0=bk(5)[:, 0:n], in1=rope.t[:, 1, 0:n], op=ALU.mult), [BK[5], rope], [t2b])
                          T.op(DVE, lambda: nc.vector.tensor_tensor(out=t1b.t[:, 0:n], in0=t1b.t[:, 0:n], in1=t2b.t[:, 0:n], op=ALU.add), [t1b, t2b], [t1b])
                          T.op(DVE, lambda: nc.vector.tensor_tensor(out=q_o.t[:, 0:n], in0=t1b.t[:, 0:n], in1=rstdh.t[:, 0:n], op=ALU.mult), [t1b, rstdh], [q_o])
                          if hd < H:
                              T.dma(POOL, qT_d.t.ap()[hd * 128:(hd + 1) * 128, c0:c0 + n], q_o.t[:, 0:n], q_o.name, [q_o], [qT_d])
                          else:
                              kh = hd - H
                              if meta:
                                  T.dma(POOL, kmeta.t.ap()[kh * 128:(kh + 1) * 128, :], q_o.t[:, 0:n], q_o.name, [q_o], [kmeta])
                              else:
                                  T.dma(POOL, kin[l].t.ap()[kh * 128:(kh + 1) * 128, c0 - 16:c0 - 16 + n], q_o.t[:, 0:n], q_o.name, [q_o], [kin[l]])

                      head_mm(0)
                      head_p1(0)
                      for hd in range(1, NQK):
                          head_mm(hd)
                          head_p2(hd - 1, qo[nq % 2])
                          nq += 1
                          head_p1(hd)
                      head_p2(NQK - 1, qo[nq % 2])
                      nq += 1
                      ck('p1b')
                      ntb = max(1, n // 128)
                      for tb in range(ntb):
                          tn = min(128, n)
                          cb0 = 0
                          while cb0 < VFW:
                              if cb0 < KVW:
                                  cw = KVW
                              else:
                                  cw = min(512, VFW - cb0)
                              pv, pvb = bk(2 + nv % 2), BK[2 + nv % 2]
                              T.mm([(lambda c=c: nc.tensor.matmul(pv[0:tn, 0:cw], hT.t[:, c, tb * 128:tb * 128 + tn], wvf.t[:, c, cb0:cb0 + cw],
                                                                  start=(c == 0), stop=(c == DC - 1))) for c in range(DC)], [hT, wvf], [pvb])
                              v_o = vo[nv % 2]
                              nv += 1
                              T.op(ACT, lambda: nc.scalar.copy(out=v_o.t[0:tn, 0:cw], in_=pv[0:tn, 0:cw]), [pvb], [v_o])
                              if cb0 < KVW:
                                  dst, dm = (vmeta, vmeta.t.ap()[:, :]) if meta else (
                                      vin[l], vin[l].t.ap().rearrange("(h t) d -> t h d", h=KVH)[c0 - 16 + tb * 128:c0 - 16 + tb * 128 + tn, :, :])
                              else:
                                  f0 = cb0 - KVW
                                  dst, dm = (fmeta, fmeta.t.ap()[:, f0:f0 + cw]) if meta else (fin[l], fin[l].t.ap()[c0 - 16 + tb * 128:c0 - 16 + tb * 128 + tn, f0:f0 + cw])
                              srcv = v_o.t[0:tn, 0:cw]
                              if cb0 < KVW and not meta:
                                  srcv = srcv.rearrange("p (h d) -> p h d", h=KVH)
                              T.dma(POOL, dm, srcv, v_o.name, [v_o], [dst])
                              cb0 += cw
                      ck('p1c')
                      for gc in range(2 * DC):
                          w = ring.load(wbf[("wg", l)], gc, DC)
                          pa, pab = bk(gc % 2), BK[gc % 2]
                          T.mm([(lambda c=c: nc.tensor.matmul(pa[:, 0:n], w.t[:, c, :], hT.t[:, c, 0:n], start=(c == 0), stop=(c == DC - 1)))
                                for c in range(DC)], [w, hT], [pab])
                          g_o = go[(ng // 4) % 2]
                          T.op(ACT, lambda: nc.scalar.activation(out=g_o.t[:, gc % 4, 0:n], in_=pa[:, 0:n], func=AF.Sigmoid,
                                                                 bias=vec.t[:, cf.V_BG + gc:cf.V_BG + gc + 1], scale=1.0), [pab, vec], [g_o])
                          ng += 1
                          if gc % 4 == 3:
                              g4 = gc // 4
                              T.dma(POOL, gT_d.t.ap()[g4 * 512:(g4 + 1) * 512, c0:c0 + n].rearrange("(a p) n -> p a n", p=128),
                                    g_o.t[:, :, 0:n], g_o.name, [g_o], [gT_d])
              T.barrier()
              ck('p1')
              ag_chunks(fin[l], fout[l], TPC, cf.NFC, f"f{l}")
              ag_chunks(kin[l], kout[l], KVW, KVH, f"k{l}")
              ag_chunks(vin[l], vout[l], KVH * TPC, KVH, f"v{l}")
              ck('ag1')

              P2B = cf.P2B
              with contextlib.ExitStack() as ph:
                  T.dma(POOL, fpos.t.ap()[0:16, :], fmeta.t.ap(), "fpos_a", [fmeta], [fpos])
                  frc_ = TPC // cf.NFC
                  for j in range(cf.NFC):
                      for rr in range(4):
                          p0 = 16 + rr * TPC + j * frc_
                          T.dma(POOL, fpos.t.ap()[p0:p0 + frc_, :], fout[l].t.ap()[(j * 4 + rr) * frc_:(j * 4 + rr + 1) * frc_, :], "fpos_a", [fout[l]], [fpos])
                  Z = sb(ph, "f_Z", [N1, N2, FCH], BF16)
                  PZ = N2 // 4 if N2 % 4 == 0 else N2 // 2
                  zs = [sb(ph, f"f_zs{i}", [N1, PZ, FCH], BF16) for i in range(2)]
                  t1s = sb(ph, "f_t1", [N1, 2 * N1], BF16)
                  tws = sb(ph, "f_tw", [N1, 2, N2], F32)
                  wfr = sb(ph, "f_wfr", [128, cf.FG, D], BF16)
                  wst = [sb(ph, f"f_wst{i}", [128, 512], BF16) for i in range(2)]
                  fa = sb(ph, "f_a", [N1, 512], F32)
                  fb = sb(ph, "f_b", [N1, 512], F32)
                  yo = [sb(ph, f"f_yo{i}", [N1, 2, 512], BF16) for i in range(2)]
                  T.dma(POOL, t1s.t[:], t1_in.ap(), "f_t1", [B_t1], [t1s])
                  T.dma(POOL, tws.t[:], tw_in.ap().rearrange("p (a n) -> p a n", a=2), "f_tw", [B_tw], [tws])
                  T.dma(SP, wfr.t[:], wbf[("wfr", l)].t.ap().rearrange("p (g n) -> p g n", g=cf.FG), "f_wfr", [wbf[("wfr", l)]], [wfr])
                  nw = 0
                  for g in range(cf.FG):
                      for pq in range(2):
                          kc = (g // FCB) * (2 * FCB) + pq * FCB + (g % FCB)
                          for nb in range(D // 512 if D >= 512 else 1):
                              nbw = min(512, D)
                              pw, pwb = bk(nw % 2), BK[nw % 2]
                              T.mm([lambda: nc.tensor.matmul(pw[:, 0:nbw], csm.t[:, pq, :], wfr.t[:, g, nb * nbw:(nb + 1) * nbw], start=True, stop=True)],
                                   [csm, wfr], [pwb])
                              ws_ = wst[nw % 2]
                              nw += 1
                              T.op(ACT, lambda: nc.scalar.copy(out=ws_.t[:, 0:nbw], in_=pw[:, 0:nbw]), [pwb], [ws_])
                              T.dma(POOL, wcs_d[l].t.ap()[nb * nbw:(nb + 1) * nbw, kc * 128:(kc + 1) * 128].rearrange("(j p) n -> p j n", p=128),
                                    ws_.t[:, 0:nbw].rearrange("p (j n) -> p j n", n=128), ws_.name, [ws_], [wcs_d[l]])
                  fview = fpos.t.ap().rearrange("(a b) c -> a b c", b=N2)
                  nz = 0
                  for pz in range(N2 // PZ):
                      for q in range(4):
                          z_ = zs[nz % 2]
                          nz += 1
                          T.dma(SP, z_.t[:], fview[:, pz * PZ:(pz + 1) * PZ, q * FCH:(q + 1) * FCH], z_.name, [fpos], [z_])
                          if q == 0:
                              T.op(DVE, lambda: nc.vector.tensor_scalar(out=Z.t[:, pz * PZ:(pz + 1) * PZ, :], in0=z_.t[:], scalar1=selb.t[0:N1, 9:10], scalar2=None, op0=ALU.mult),
                                   [z_, selb], [Z])
                          else:
                              T.op(DVE, lambda: nc.vector.scalar_tensor_tensor(out=Z.t[:, pz * PZ:(pz + 1) * PZ, :], in0=z_.t[:], scalar=selb.t[0:N1, 9 + q:10 + q],
                                                                              in1=Z.t[:, pz * PZ:(pz + 1) * PZ, :], op0=ALU.mult, op1=ALU.add), [z_, selb, Z], [Z])
                  Zf = Z.t[:].rearrange("p a c -> p (a c)")
                  nblk = N2 // P2B
                  for b_ in range(nblk):
                      pr, prb = bk(2 * (b_ % 2)), BK[2 * (b_ % 2)]
                      pi, pib = bk(2 * (b_ % 2) + 1), BK[2 * (b_ % 2) + 1]
                      T.mm([lambda: nc.tensor.matmul(pr[0:N1, :], t1s.t[:, 0:N1], Zf[:, b_ * 512:(b_ + 1) * 512], start=True, stop=True)], [t1s, Z], [prb])
                      T.mm([lambda: nc.tensor.matmul(pi[0:N1, :], t1s.t[:, N1:2 * N1], Zf[:, b_ * 512:(b_ + 1) * 512], start=True, stop=True)], [t1s, Z], [pib])
                      y_ = yo[b_ % 2]
                      tcb = tws.t[:, 0, b_ * P2B:(b_ + 1) * P2B].unsqueeze(2).to_broadcast([N1, P2B, FCH])
                      tsb = tws.t[:, 1, b_ * P2B:(b_ + 1) * P2B].unsqueeze(2).to_broadcast([N1, P2B, FCH])
                      v3 = lambda ap: ap.rearrange("p (a c) -> p a c", c=FCH)
                      T.op(DVE, lambda: nc.vector.tensor_tensor(out=v3(fa.t[:, :]), in0=v3(pr[0:N1, :]), in1=tcb, op=ALU.mult), [prb, tws], [fa])
                      T.op(DVE, lambda: nc.vector.tensor_tensor(out=v3(fb.t[:, :]), in0=v3(pi[0:N1, :]), in1=tsb, op=ALU.mult), [pib, tws], [fb])
                      T.op(DVE, lambda: nc.vector.tensor_tensor(out=y_.t[:, 0, :], in0=fa.t[:, :], in1=fb.t[:, :], op=ALU.subtract), [fa, fb], [y_])
                      T.op(DVE, lambda: nc.vector.tensor_tensor(out=v3(fa.t[:, :]), in0=v3(pr[0:N1, :]), in1=tsb, op=ALU.mult), [prb, tws], [fa])
                      T.op(DVE, lambda: nc.vector.tensor_tensor(out=v3(fb.t[:, :]), in0=v3(pi[0:N1, :]), in1=tcb, op=ALU.mult), [pib, tws], [fb])
                      T.op(DVE, lambda: nc.vector.tensor_tensor(out=y_.t[:, 1, :], in0=fa.t[:, :], in1=fb.t[:, :], op=ALU.add), [fa, fb], [y_])
                      ydv = yd.t.ap().rearrange("r k a c -> k r (a c)")
                      T.dma(POOL, ydv[:, :, b_ * 512:(b_ + 1) * 512], y_.t[:], y_.name, [y_], [yd])
              T.barrier()
              ck('f1')
              with contextlib.ExitStack() as ph:
                  t3s = sb(ph, "f_t3", [N2C, 2, 2, 2 * N2], BF16)
                  T.dma(POOL, t3s.t[:], t3_in.ap().rearrange("p (r k n) -> p r k n", r=2, k=2), "f_t3", [B_t3], [t3s])
                  yt = [[sb(ph, f"f_yt{ri}{ch}", [N2C, N1, 128], BF16) for ch in range(2)] for ri in range(2)]
                  pqs = sb(ph, "f_pqs", [128, 2, L], BF16)
                  for cb in range(FCB):
                      for ri in range(2):
                          for ch in range(2):
                              src = yd.t.ap()[ri].rearrange("k a c -> a k c")[ch * N2C:(ch + 1) * N2C, :, cb * 128:(cb + 1) * 128]
                              T.dma(SP, yt[ri][ch].t[:], src, yt[ri][ch].name, [yd], [yt[ri][ch]])
                      for k1 in range(N1):
                          po, pob = bk(4 + k1 % 2), BK[4 + k1 % 2]
                          fns = []
                          idx = 0
                          for ri in range(2):
                              for ch in range(2):
                                  fns.append(lambda ri=ri, ch=ch, idx=idx: nc.tensor.matmul(po[:, 0:2 * N2], yt[ri][ch].t[:, k1, :], t3s.t[:, ri, ch, :],
                                                                                           start=(idx == 0), stop=(idx == 3)))
                                  idx += 1
                          T.mm(fns, [yt[0][0], yt[0][1], yt[1][0], yt[1][1], t3s], [pob])
                          dst = pqs.t[:, :, k1:k1 + N1 * (N2 - 1) + 1:N1]
                          T.op(ACT, lambda: nc.scalar.copy(out=dst, in_=po[:, 0:2 * N2].rearrange("p (a n) -> p a n", a=2)), [pob], [pqs])
                      for j in range(cf.NPC):
                          T.dma(POOL, pqin[l].t.ap()[j * 2 * FCH:(j + 1) * 2 * FCH, :].rearrange("(a c) n -> c a n", a=2)[cb * 128:(cb + 1) * 128, :, :],
                                pqs.t[:, :, j * cf.LC:(j + 1) * cf.LC], "f_pqs", [pqs], [pqin[l]])
              T.barrier()
              ck('f3')
              ag_chunks(pqin[l], pqout[l], cf.NPC * 2 * FCH, cf.NPC, f"pq{l}")
              ck('pq')

              scale = 1.0 / math.sqrt(128.0)
              NKC = 4 * TPC // 128
              with contextlib.ExitStack() as ph:
                  KT = sb(ph, "a_KT", [128, 4 * TPC + 16], BF16)
                  Vt = sb(ph, "a_Vt", [128, NKC + 1, 132], BF16)
                  QT = [sb(ph, f"a_QT{i}", [128, TT], BF16) for i in range(2)]
                  Pb = [sb(ph, f"a_P{i}", [128, 2, TT], BF16) for i in range(2)]
                  rinv = sb(ph, "a_rinv", [128, 4], F32)
                  on = [sb(ph, f"a_on{i}", [128, 128], BF16) for i in range(2)]
                  aT = [sb(ph, f"a_aT{i}", [128, TT], BF16) for i in range(2)]
                  pqa = sb(ph, "a_pq", [128, PQC, TT], BF16)
                  pqc_a = [sb(ph, f"a_pqc{i}", [128, PQC, TT], BF16) for i in range(2)]
                  npc_a = [0]

                  def pq_load(dst_t, pos0, nn, key, dbuf):
                      a = pos0
                      while a < pos0 + nn:
                          j = a // cf.LC
                          b = min(pos0 + nn, (j + 1) * cf.LC)
                          v_ = pqout[l].t.ap()[j * 8 * FCH:(j + 1) * 8 * FCH, :].rearrange("(c p) n -> p c n", p=128)
                          T.dma(SP, dst_t[:, :, a - pos0:b - pos0], v_[:, :, a - j * cf.LC:b - j * cf.LC], key, [pqout[l]], [dbuf])
                          a = b

                  na = 0
                  if l + 1 < DEPTH:
                      weight_ags(l + 1)
                  for kvh in range(KVH):
                      for rr in range(4):
                          T.dma(SP, KT.t[:, rr * TPC:(rr + 1) * TPC], kout[l].t.ap()[(kvh * 4 + rr) * 128:(kvh * 4 + rr + 1) * 128, :], "a_KT", [kout[l]], [KT])
                      T.dma(SP, KT.t[:, 4 * TPC:4 * TPC + 16], kmeta.t.ap()[kvh * 128:(kvh + 1) * 128, :], "a_KT", [kmeta], [KT])
                      T.dma(SP, Vt.t[:, 0:NKC, 0:128], vout[l].t.ap()[kvh * 4 * TPC:(kvh + 1) * 4 * TPC, :].rearrange("(c p) d -> p c d", p=128), "a_Vt", [vout[l]], [Vt])
                      T.dma(SP, Vt.t[0:16, NKC, 0:128], vmeta.t.ap()[:, kvh * 128:(kvh + 1) * 128], "a_Vt", [vmeta], [Vt])
                      T.op(DVE, lambda: nc.vector.memset(Vt.t[:, :, 128:129], 1.0), [], [Vt])
                      items = [(2 * j, 2) for j in range(NKC // 2)] + [(NKC, 1)]
                      for (c0, n, _) in tiles:
                          nqb = max(1, n // 128)
                          qn_ = min(128, n)
                          if kvh == KVH - 1:
                              if n == 16:
                                  pq_load(pqa.t, 0, 16, "a_pq", pqa)
                              else:
                                  for q in range(4):
                                      pc = pqc_a[npc_a[0] % 2]
                                      npc_a[0] += 1
                                      pq_load(pc.t, q * TPC + c0, n, pc.name, pc)
                                      if q == 0:
                                          T.op(DVE, lambda: nc.vector.tensor_scalar(out=pqa.t[:, :, 0:n], in0=pc.t[:, :, 0:n], scalar1=selb.t[:, 9:10], scalar2=None, op0=ALU.mult), [pc, selb], [pqa])
                                      else:
                                          T.op(DVE, lambda: nc.vector.scalar_tensor_tensor(out=pqa.t[:, :, 0:n], in0=pc.t[:, :, 0:n], scalar=selb.t[:, 9 + q:10 + q], in1=pqa.t[:, :, 0:n],
                                                                                          op0=ALU.mult, op1=ALU.add), [pc, selb, pqa], [pqa])
                              T.dma(SP, pqsel_d.t.ap()[:, c0:c0 + n].rearrange("(c p) n -> p c n", p=128), pqa.t[:, :, 0:n], "a_pq", [pqa], [pqsel_d])
                          for gq in range(G):
                              hd = kvh * G + gq
                              Q = QT[na % 2]
                              a_T = aT[na % 2]
                              na += 1
                              T.dma(POOL, Q.t[:, 0:n], qT_d.t.ap()[hd * 128:(hd + 1) * 128, c0:c0 + n], Q.name, [qT_d], [Q])

                              def qk(it, slot):
                                  kc0, cnt = it
                                  for u in range(cnt):
                                      kk = 16 if kc0 == NKC else 128
                                      pS = bk(2 * slot + u)
                                      T.mm([lambda: nc.tensor.matmul(pS[0:kk, 0:n], KT.t[:, (kc0 + u) * 128:(kc0 + u) * 128 + kk], Q.t[:, 0:n], start=True, stop=True)],
                                           [KT, Q], [BK[2 * slot + u]])

                              qk(items[0], 0)
                              for ii, it in enumerate(items):
                                  slot = ii % 2
                                  if ii + 1 < len(items):
                                      qk(items[ii + 1], (ii + 1) % 2)
                                  kc0, cnt = it
                                  kk = 16 if kc0 == NKC else 128
                                  Pt = Pb[slot]
                                  src = (ps0 if slot == 0 else ps1)[0:kk, :].rearrange("p (a n) -> p a n", a=2)[:, 0:cnt, 0:n]
                                  T.op(ACT, lambda: nc.scalar.activation(out=Pt.t[0:kk, 0:cnt, 0:n], in_=src, func=AF.Exp, scale=scale),
                                       [BK[2 * slot], BK[2 * slot + 1]], [Pt])
                                  fns = []
                                  for u in range(cnt):
                                      for qb in range(nqb):
                                          ob = 4 + qb // 2
                                          oc0 = (qb % 2) * 256
                                          first = (ii == 0 and u == 0 and qb % 2 == 0)
                                          lastm = (ii == len(items) - 1 and u == cnt - 1)
                                          fns.append(lambda u=u, qb=qb, ob=ob, oc0=oc0, first=first, lastm=lastm: nc.tensor.matmul(
                                              bk(ob)[0:qn_, oc0:oc0 + 129], Pt.t[0:kk, u, qb * 128:qb * 128 + qn_], Vt.t[0:kk, kc0 + u, 0:129],
                                              start=first, stop=lastm, skip_group_check=True))
                                  T.mm(fns, [Pt, Vt], [BK[4], BK[5]])
                              for qb in range(nqb):
                                  ob = 4 + qb // 2
                                  oc0 = (qb % 2) * 256
                                  T.op(DVE, lambda: nc.vector.reciprocal(out=rinv.t[0:qn_, qb:qb + 1], in_=bk(ob)[0:qn_, oc0 + 128:oc0 + 129]), [BK[ob]], [rinv])
                                  o_n = on[qb % 2]
                                  T.op(DVE, lambda: nc.vector.tensor_scalar(out=o_n.t[0:qn_, :], in0=bk(ob)[0:qn_, oc0:oc0 + 128], scalar1=rinv.t[0:qn_, qb:qb + 1], scalar2=None, op0=ALU.mult),
                                       [BK[ob], rinv], [o_n])
                                  T.mm([lambda: nc.tensor.transpose(pst[:, qb * 128:qb * 128 + qn_], o_n.t[0:qn_, :], ident.t[0:qn_, 0:qn_])], [o_n, ident], [BK[7]])
                              T.op(DVE, lambda: nc.vector.tensor_copy(out=a_T.t[:, 0:n], in_=pst[:, 0:n]), [BK[7]], [a_T])
                              T.dma(POOL, aT_d.t.ap()[hd * 128:(hd + 1) * 128, c0:c0 + n], a_T.t[:, 0:n], a_T.name, [a_T], [aT_d])
              T.barrier()
              ck('attn')
              xh_st = contextlib.ExitStack()
              xh = sb(xh_st, "xh", [128, DC, cf.NH], F32)
              hsb = sb(xh_st, "hsb", [128, 2, DC], F32)
              with contextlib.ExitStack() as ph:
                  xt = sb(ph, "m_xt", [128, DC, TT], F32, key="xt")
                  at = sb(ph, "m_at", [128, H, TT], BF16)
                  pq = sb(ph, "m_pq", [128, PQC, TT], BF16)
                  gt = sb(ph, "m_gt", [128, 2 * DC, TT], BF16)
                  mg = sb(ph, "m_mg", [128, DC, TT], BF16)
                  ta = sb(ph, "m_ta", [128, TT], F32)
                  tb_ = sb(ph, "m_tb", [128, TT], F32)
                  xo = xt
                  ring = WRing(ph, "m_w", max(DC, PQC, H), 4)
                  T.op(DVE, lambda: nc.vector.memset(xh.t[:], 0.0), [], [xh])
                  npc = 0
                  for (c0, n, hj) in tiles:
                      meta = (n == 16)
                      T.dma(POOL, xt.t[:, :, 0:n], xa.t.ap()[:, c0:c0 + n].rearrange("(c p) n -> p c n", p=128), xt.name, [xa], [xt])
                      T.dma(POOL, at.t[:, :, 0:n], aT_d.t.ap()[:, c0:c0 + n].rearrange("(c p) n -> p c n", p=128), "m_at", [aT_d], [at])
                      T.dma(POOL, gt.t[:, :, 0:n], gT_d.t.ap()[:, c0:c0 + n].rearrange("(c p) n -> p c n", p=128), "m_gt", [gT_d], [gt])
                      T.dma(POOL, pq.t[:, :, 0:n], pqsel_d.t.ap()[:, c0:c0 + n].rearrange("(c p) n -> p c n", p=128), "m_pq", [pqsel_d], [pq])
                      for oc in range(DC):
                          w1 = ring.load(wbf[("wab", l)], oc, H)
                          w2 = ring.load(wcs_d[l], oc, PQC)
                          ba, bb = 2 * (oc % 2), 2 * (oc % 2) + 1
                          T.mm([(lambda c=c: nc.tensor.matmul(bk(ba)[:, 0:n], w1.t[:, c, :], at.t[:, c, 0:n], start=(c == 0), stop=(c == H - 1))) for c in range(H)], [w1, at], [BK[ba]])
                          T.mm([(lambda c=c: nc.tensor.matmul(bk(bb)[:, 0:n], w2.t[:, c, :], pq.t[:, c, 0:n], start=(c == 0), stop=(c == PQC - 1))) for c in range(PQC)], [w2, pq], [BK[bb]])
                          T.op(DVE, lambda: nc.vector.tensor_tensor(out=ta.t[:, 0:n], in0=bk(ba)[:, 0:n], in1=gt.t[:, oc, 0:n], op=ALU.mult), [BK[ba], gt], [ta])
                          T.op(DVE, lambda: nc.vector.tensor_tensor(out=tb_.t[:, 0:n], in0=bk(bb)[:, 0:n], in1=gt.t[:, DC + oc, 0:n], op=ALU.mult), [BK[bb], gt], [tb_])
                          T.op(DVE, lambda: nc.vector.tensor_tensor(out=mg.t[:, oc, 0:n], in0=ta.t[:, 0:n], in1=tb_.t[:, 0:n], op=ALU.add), [ta, tb_], [mg])
                      for oc in range(DC):
                          w = ring.load(wbf[("wout", l)], oc, DC)
                          pb_, pbb = bk(4 + oc % 2), BK[4 + oc % 2]
                          T.mm([(lambda c=c: nc.tensor.matmul(pb_[:, 0:n], w.t[:, c, :], mg.t[:, c, 0:n], start=(c == 0), stop=(c == DC - 1))) for c in range(DC)], [w, mg], [pbb])
                          T.op(DVE, lambda: nc.vector.tensor_tensor(out=xo.t[:, oc, 0:n], in0=pb_[:, 0:n], in1=xt.t[:, oc, 0:n], op=ALU.add), [pbb, xt], [xo])
                      T.dma(POOL, xm.t.ap()[:, c0:c0 + n].rearrange("(c p) n -> p c n", p=128), xo.t[:, :, 0:n], "xo", [xo], [xm])
                      if meta:
                          T.op(DVE, lambda: nc.vector.tensor_copy(out=xh.t[:, :, 0], in_=xo.t[:, :, 15]), [xo], [xh])
                      else:
                          if hj + 1 < NT:
                              T.op(DVE, lambda: nc.vector.tensor_copy(out=xh.t[:, :, 2 * (hj + 1)], in_=xo.t[:, :, n - 1]), [xo], [xh])
                          else:
                              T.op(DVE, lambda: nc.vector.tensor_copy(out=hsb.t[:, 1, :], in_=xo.t[:, :, n - 1]), [xo], [hsb])
                          if hj >= 1:
                              T.op(DVE, lambda: nc.vector.tensor_copy(out=xh.t[:, :, 2 * (hj - 1) + 1], in_=xo.t[:, :, 0]), [xo], [xh])
                          else:
                              T.op(DVE, lambda: nc.vector.tensor_copy(out=hsb.t[:, 0, :], in_=xo.t[:, :, 0]), [xo], [hsb])
                  T.dma(POOL, hin[l].t.ap().rearrange("p (a c) -> p a c", a=2), hsb.t[:], "hsb", [hsb], [hin[l]])
              T.barrier()
              T.coll("AllGather", GRP, hin[l], hout[l], f"h{l}")
              ck('halo')
              with contextlib.ExitStack() as ph:
                  hb = sb(ph, "n_hb", [128, 4, 2, DC], F32)
                  T.dma(POOL, hb.t[:], hout[l].t.ap().rearrange("(r p) (a c) -> p r a c", p=128, a=2), "n_hb", [hout[l]], [hb])
                  jl, jr, jm = 0, 2 * (NT - 1) + 1, 2 * NT + 1
                  T.op(DVE, lambda: nc.vector.tensor_scalar(out=xh.t[:, :, jl], in0=xh.t[:, :, jl], scalar1=selb.t[:, 0:1], scalar2=None, op0=ALU.mult), [xh, selb], [xh])
                  for j in range(4):
                      T.op(DVE, lambda j=j: nc.vector.scalar_tensor_tensor(out=xh.t[:, :, jl], in0=hb.t[:, j, 1, :], scalar=selb.t[:, 1 + j:2 + j], in1=xh.t[:, :, jl],
                                                                          op0=ALU.mult, op1=ALU.add), [hb, selb, xh], [xh])
                      T.op(DVE, lambda j=j: nc.vector.scalar_tensor_tensor(out=xh.t[:, :, jr], in0=hb.t[:, j, 0, :], scalar=selb.t[:, 5 + j:6 + j], in1=xh.t[:, :, jr],
                                                                          op0=ALU.mult, op1=ALU.add), [hb, selb, xh], [xh])
                  T.op(DVE, lambda: nc.vector.tensor_copy(out=xh.t[:, :, jm], in_=hb.t[:, 0, 0, :]), [hb], [xh])
                  NH = cf.NH
                  sqh_ = sb(ph, "n_sqh", [128, DC, NH], BF16)
                  h2h = sb(ph, "n_h2h", [128, DC, NH], BF16)
                  rsx = sb(ph, "n_rsx", [128, NH], F32)
                  rstx = sb(ph, "n_rstx", [128, NH], F32)
                  ugh = sb(ph, "n_ugh", [128, FFC, NH], F32)
                  xt = sb(ph, "n_xt", [128, DC, TT], F32, key="xt")
                  sq = sb(ph, "n_sq", [128, DC, TT], BF16)
                  h2 = sb(ph, "n_h2", [128, DC, TT], BF16)
                  rs = sb(ph, "n_rs", [128, TT], F32)
                  rstd = sb(ph, "n_rstd", [128, TT], F32)
                  cc = sb(ph, "n_cc", [128, TT], F32)
                  sg = sb(ph, "n_sg", [128, TT], F32)
                  uT = sb(ph, "n_uT", [128, FFC, TT], BF16)
                  xo = xt
                  ring = WRing(ph, "n_w", DC, 4)
                  ringd = WRing(ph, "n_wd", FFC, 2)
                  rmsnorm(xh, NH, cf.V_GFFN, sqh_, h2h, rsx, rstx, bk(6), BK[6])
                  for fc in range(FFC):
                      w = ring.load(wbf[("wup", l)], fc, DC)
                      T.mm([(lambda c=c: nc.tensor.matmul(bk(fc % 2)[:, 0:NH], w.t[:, c, :], h2h.t[:, c, :], start=(c == 0), stop=(c == DC - 1))) for c in range(DC)], [w, h2h], [BK[fc % 2]])
                      T.op(ACT, lambda: nc.scalar.copy(out=ugh.t[:, fc, :], in_=bk(fc % 2)[:, 0:NH]), [BK[fc % 2]], [ugh])
                  wv = lambda fc, k: vec.t[:, cf.V_WCV + 3 * fc + k:cf.V_WCV + 3 * fc + k + 1]
                  for (c0, n, hj) in tiles:
                      meta = (n == 16)
                      if last and meta:
                          continue
                      T.dma(POOL, xt.t[:, :, 0:n], xm.t.ap()[:, c0:c0 + n].rearrange("(c p) n -> p c n", p=128), xt.name, [xm], [xt])
                      rmsnorm(xt, n, cf.V_GFFN, sq, h2, rs, rstd, bk(6), BK[6])
                      for fc in range(FFC):
                          wg_ = ring.load(wbf[("wup", l)], fc, DC)
                          wv_ = ring.load(wbf[("wup", l)], FFC + fc, DC)
                          pg, pgb = bk(2 * (fc % 2)), BK[2 * (fc % 2)]
                          pu, pub = bk(2 * (fc % 2) + 1), BK[2 * (fc % 2) + 1]
                          T.mm([(lambda c=c: nc.tensor.matmul(pg[:, 0:n], wg_.t[:, c, :], h2.t[:, c, 0:n], start=(c == 0), stop=(c == DC - 1))) for c in range(DC)], [wg_, h2], [pgb])
                          T.mm([(lambda c=c: nc.tensor.matmul(pu[:, 0:n], wv_.t[:, c, :], h2.t[:, c, 0:n], start=(c == 0), stop=(c == DC - 1))) for c in range(DC)], [wv_, h2], [pub])
                          bcol = vec.t[:, cf.V_BCV + fc:cf.V_BCV + fc + 1]
                          T.op(DVE, lambda: nc.vector.tensor_scalar(out=cc.t[:, 0:n], in0=pg[:, 0:n], scalar1=wv(fc, 1), scalar2=bcol, op0=ALU.mult, op1=ALU.add), [pgb, vec], [cc])
                          T.op(DVE, lambda: nc.vector.scalar_tensor_tensor(out=cc.t[:, 1:n], in0=pg[:, 0:n - 1], scalar=wv(fc, 0), in1=cc.t[:, 1:n], op0=ALU.mult, op1=ALU.add), [pgb, vec, cc], [cc])
                          T.op(DVE, lambda: nc.vector.scalar_tensor_tensor(out=cc.t[:, 0:n - 1], in0=pg[:, 1:n], scalar=wv(fc, 2), in1=cc.t[:, 0:n - 1], op0=ALU.mult, op1=ALU.add), [pgb, vec, cc], [cc])
                          T.op(DVE, lambda: nc.vector.scalar_tensor_tensor(out=cc.t[:, 0:1], in0=ugh.t[:, fc, 2 * hj:2 * hj + 1], scalar=wv(fc, 0), in1=cc.t[:, 0:1], op0=ALU.mult, op1=ALU.add), [ugh, vec, cc], [cc])
                          T.op(DVE, lambda: nc.vector.scalar_tensor_tensor(out=cc.t[:, n - 1:n], in0=ugh.t[:, fc, 2 * hj + 1:2 * hj + 2], scalar=wv(fc, 2), in1=cc.t[:, n - 1:n], op0=ALU.mult, op1=ALU.add), [ugh, vec, cc], [cc])
                          T.op(ACT, lambda: nc.scalar.activation(out=sg.t[:, 0:n], in_=cc.t[:, 0:n], func=AF.Silu), [cc], [sg])
                          T.op(DVE, lambda: nc.vector.tensor_tensor(out=uT.t[:, fc, 0:n], in0=sg.t[:, 0:n], in1=pu[:, 0:n], op=ALU.mult), [sg, pub], [uT])
                      for oc in range(DC):
                          w = ringd.load(wbf[("wdn", l)], oc, FFC)
                          pb_, pbb = bk(4 + oc % 2), BK[4 + oc % 2]
                          T.mm([(lambda c=c: nc.tensor.matmul(pb_[:, 0:n], w.t[:, c, :], uT.t[:, c, 0:n], start=(c == 0), stop=(c == FFC - 1))) for c in range(FFC)], [w, uT], [pbb])
                          T.op(DVE, lambda: nc.vector.tensor_tensor(out=xo.t[:, oc, 0:n], in0=pb_[:, 0:n], in1=xt.t[:, oc, 0:n], op=ALU.add), [pbb, xt], [xo])
                      if last and not meta:
                          T.dma(POOL, yT.ap()[:, c0 - 16:c0 - 16 + n].rearrange("(c p) n -> p c n", p=128), xo.t[:, :, 0:n], "xo", [xo], [B_yT])
                      elif not last:
                          T.dma(POOL, xa.t.ap()[:, c0:c0 + n].rearrange("(c p) n -> p c n", p=128), xo.t[:, :, 0:n], "xo", [xo], [xa])
              xh_st.close()
        except _Stop:
            pass
        T.dead = False
        T.barrier(full=True)
        if getattr(cf, 'endclear', False):
            fin_sem = stack.enter_context(nc.semaphore("s_fin"))
            for eng in (T.pe, T.act, T.dve, T.sp):
                eng.e.sem_inc(fin_sem, 1)
            nc.gpsimd.wait_ge(fin_sem, 4)
            allsems = [e.sem for e in T.engs] + [v[0] for v in T.dsems.values()] + [sv[0] for sv in T.csems] + [fin_sem]
            for sm in allsems:
                nc.gpsimd.sem_clear(sm)
    return nc, T.nsem


_CACHE = {}


def run(cf, inputs):
    in_maps = prep_inputs(cf, **inputs)
    if "nc" not in _CACHE or _CACHE.get("cf") is not cf:
        _CACHE["nc"], nsem = build(cf)
        _CACHE["cf"] = cf
    res = run_bass_kernel_spmd(_CACHE["nc"], in_maps, core_ids=list(range(NCORES)))
    out = np.empty((2, cf.SEQ, cf.D), np.float32)
    for c in range(NCORES):
        b, r = c // 4, c % 4
        out[b, r * cf.TPC:(r + 1) * cf.TPC, :] = res.results[c]["yT"].T
    return out


def kernel(**inputs):
    return run(FULL, inputs)
```

```python
import contextlib
import math
import numpy as np
import concourse.bass as bass
import concourse.mybir as mybir
from concourse.bass_utils import run_bass_kernel_spmd

F32, BF16 = mybir.dt.float32, mybir.dt.bfloat16
AF = mybir.ActivationFunctionType
ALU = mybir.AluOpType
NCORES = 8
GRP = [[0, 1, 2, 3], [4, 5, 6, 7]]
ALL8 = [list(range(8))]


def nchunks_for(n, unit_bytes, limit=1 << 20):
    for k in range(1, n + 1):
        if n % k == 0 and (n // k) * unit_bytes <= limit:
            return k
    raise ValueError


class Cfg:
    def __init__(self, D=2048, SEQ=16384, H=8, KVH=2, FG=8, DFF=5632, TT=512, N1=100, N2=164, DEPTH=2):
        self.D, self.SEQ, self.H, self.KVH, self.FG, self.DFF, self.TT = D, SEQ, H, KVH, FG, DFF, TT
        self.N1, self.N2, self.DEPTH = N1, N2, DEPTH
        self.HD = 128
        self.G = H // KVH
        self.AW = H * 128
        self.KVW = KVH * 128
        self.FW = FG * 128
        self.NMETA = 16
        self.GRIDW = 64
        self.L = SEQ + 16
        assert N1 * N2 == self.L
        self.TPC = SEQ // 4
        self.NT = self.TPC // TT
        self.NCOL = 16 + self.TPC
        self.DC = D // 128
        self.FFC = DFF // 128
        self.FCH = self.FW // 4
        self.FCB = self.FCH // 128
        self.OFF_K = self.AW
        self.OFF_V = self.AW + self.KVW
        self.OFF_F = self.OFF_V + self.KVW
        self.OFF_GA = self.OFF_F + self.FW
        self.INW = self.OFF_GA + 2 * D
        self.EPS = 1e-6
        self.N2C = N2 // 2
        assert self.N2C * 2 == N2 and self.N2C <= 128 and 2 * N2 <= 512 and N1 <= 128
        self.P2B = 512 // self.FCH
        assert N2 % self.P2B == 0
        self.NH = 2 * (self.NT + 1)
        MB = 1 << 20
        self.NFC = nchunks_for(self.TPC, self.FW * 2)
        self.NPC = nchunks_for(self.L, 2 * self.FCH * 2)
        self.LC = self.L // self.NPC
        assert 128 * self.TPC * 2 <= MB
        o = 0
        self.V_GMIX = o; o += self.DC
        self.V_GFFN = o; o += self.DC
        self.V_BG = o; o += 2 * self.DC
        self.V_QN = o; o += 1
        self.V_KN = o; o += 1
        self.V_WCV = o; o += 3 * self.FFC
        self.V_BCV = o; o += self.FFC
        self.NV = o


FULL = Cfg()


def lhsT_layout(W):
    K, N = W.shape
    return np.ascontiguousarray(W.reshape(K // 128, 128, N // 128, 128).transpose(2, 1, 0, 3).reshape(N, K))


def rhs_layout(W):
    K, N = W.shape
    return np.ascontiguousarray(W.reshape(K // 128, 128, N).transpose(1, 0, 2).reshape(128, (K // 128) * N))


WNAMES = ["wqk", "wg", "wvf", "wab", "wfr", "wout", "wup", "wdn"]


def weight_shapes(cf):
    return {
        "wqk": (cf.AW + cf.KVW, cf.D), "wg": (2 * cf.D, cf.D), "wvf": (128, cf.DC * (cf.KVW + cf.FW)),
        "wab": (cf.D, cf.AW), "wfr": (128, cf.FG * cf.D), "wout": (cf.D, cf.D),
        "wup": (2 * cf.DFF, cf.D), "wdn": (cf.D, cf.DFF),
    }


def weight_chunks(cf):
    out = {}
    for n, (R, C) in weight_shapes(cf).items():
        out[n] = nchunks_for(R // 4, C * 2)
    return out


def host_tables(cf):
    f64 = np.float64
    tabs = {}
    ident = np.eye(128, dtype=np.float32)
    rotT = np.zeros((128, 128), np.float32)
    for i in range(128):
        blk = i // 32
        if blk % 2 == 0:
            rotT[i + 32, i] = -1.0
        else:
            rotT[i - 32, i] = 1.0
    N1, N2, L = cf.N1, cf.N2, cf.L
    a1 = 2 * np.pi * np.outer(np.arange(N1), np.arange(N1)).astype(f64) / N1
    t1 = np.concatenate([np.cos(a1), np.sin(a1)], 1).astype(np.float32)
    a2 = 2 * np.pi * np.outer(np.arange(N2), np.arange(N2)).astype(f64) / N2
    C2, S2 = np.cos(a2), np.sin(a2)
    tabR = np.concatenate([C2, S2], 1)
    tabI = np.concatenate([-S2, C2], 1)
    t3 = np.stack([tabR, tabI], 0).reshape(2, 2, cf.N2C, 2 * N2).astype(np.float32)
    at = 2 * np.pi * np.outer(np.arange(N1), np.arange(N2)).astype(f64) / L
    tw = np.stack([np.cos(at), np.sin(at)], 1).astype(np.float32)
    ac = 2 * np.pi * np.outer(np.arange(128), np.arange(128)).astype(f64) / 128
    sc = 1.0 / math.sqrt(L * 128.0)
    cs = np.stack([np.cos(ac) * sc, -np.sin(ac) * sc], 1).astype(np.float32)
    tabs["cmat"] = np.ascontiguousarray(np.concatenate([ident, rotT, cs.reshape(128, 256)], 1))
    tabs["t1"] = t1
    tabs["t3"] = np.ascontiguousarray(t3.transpose(2, 0, 1, 3).reshape(cf.N2C, 4 * 2 * N2))
    tabs["tw"] = np.ascontiguousarray(tw.reshape(N1, 2 * N2))
    return tabs


def core_tables(cf, r):
    inv = 1.0 / (10000.0 ** (np.arange(32, dtype=np.float64) / 32))
    tg = r * cf.TPC + np.arange(cf.TPC)
    rows = (tg // cf.GRIDW).astype(np.float64)
    cols = (tg % cf.GRIDW).astype(np.float64)
    ang = np.zeros((128, cf.NCOL), np.float64)
    ang[0:32, 16:] = inv[:, None] * rows[None]
    ang[32:64, 16:] = inv[:, None] * rows[None]
    ang[64:96, 16:] = inv[:, None] * cols[None]
    ang[96:128, 16:] = inv[:, None] * cols[None]
    rope = np.stack([np.cos(ang), np.sin(ang)], 1).astype(np.float32)
    sel = np.zeros((128, 16), np.float32)
    sel[:, 0] = 1.0 if r == 0 else 0.0
    for j in range(4):
        sel[:, 1 + j] = 1.0 if j == r - 1 else 0.0
        sel[:, 5 + j] = 1.0 if j == r + 1 else 0.0
        sel[:, 9 + j] = 1.0 if j == r else 0.0
    return np.ascontiguousarray(rope.reshape(128, 2 * cf.NCOL)), sel


def prep_inputs(cf, x, meta_tokens, norm_mix, norm_ffn, w_in, b_gate, q_norm, k_norm,
                w_attn_br, w_four, w_out, w_up, w_conv, b_conv, w_down):
    f = lambda a: np.asarray(a, dtype=np.float32)
    x, meta_tokens = f(x), f(meta_tokens)
    w_in, w_attn_br, w_four, w_out, w_up, w_down = map(f, (w_in, w_attn_br, w_four, w_out, w_up, w_down))
    DEPTH = cf.DEPTH
    full = {n: [] for n in WNAMES}
    for l in range(DEPTH):
        full["wqk"].append(lhsT_layout(w_in[l][:, 0:cf.OFF_V]))
        full["wg"].append(lhsT_layout(w_in[l][:, cf.OFF_GA:]))
        full["wvf"].append(rhs_layout(w_in[l][:, cf.OFF_V:cf.OFF_GA]))
        full["wab"].append(lhsT_layout(w_attn_br[l]))
        full["wfr"].append(rhs_layout(w_four[l]))
        full["wout"].append(lhsT_layout(w_out[l]))
        full["wup"].append(lhsT_layout(w_up[l]))
        full["wdn"].append(lhsT_layout(w_down[l]))
    vecs = np.zeros((DEPTH, 128, cf.NV), np.float32)
    for l in range(DEPTH):
        vecs[l, :, cf.V_GMIX:cf.V_GMIX + cf.DC] = f(norm_mix[l]).reshape(cf.DC, 128).T
        vecs[l, :, cf.V_GFFN:cf.V_GFFN + cf.DC] = f(norm_ffn[l]).reshape(cf.DC, 128).T
        vecs[l, :, cf.V_BG:cf.V_BG + 2 * cf.DC] = f(b_gate[l]).reshape(2 * cf.DC, 128).T
        vecs[l, :, cf.V_QN] = f(q_norm[l])
        vecs[l, :, cf.V_KN] = f(k_norm[l])
        wc = f(w_conv[l]).reshape(3, cf.FFC, 128).transpose(2, 1, 0)
        vecs[l, :, cf.V_WCV:cf.V_WCV + 3 * cf.FFC] = wc.reshape(128, 3 * cf.FFC)
        vecs[l, :, cf.V_BCV:cf.V_BCV + cf.FFC] = f(b_conv[l]).reshape(cf.FFC, 128).T
    tabs = host_tables(cf)
    wch = weight_chunks(cf)
    metaT = np.ascontiguousarray(meta_tokens.T)
    in_maps = []
    for c in range(NCORES):
        b, r = c // 4, c % 4
        m = {}
        m["xT"] = np.ascontiguousarray(x[b, r * cf.TPC:(r + 1) * cf.TPC, :].T)
        m["metaT"] = metaT
        for n in WNAMES:
            R = full[n][0].shape[0]
            nch = wch[n]
            rc = R // 4 // nch
            m[n] = np.ascontiguousarray(np.stack(
                [full[n][l].reshape(nch, 4, rc, -1)[:, r].reshape(nch * rc, -1) for l in range(DEPTH)], 0))
        m["vecs"] = vecs
        rope, sel = core_tables(cf, r)
        m["rope"] = rope
        m["sel"] = sel
        for k, v in tabs.items():
            m[k] = v
        in_maps.append(m)
    return in_maps


class _Stop(Exception):
    pass


class Buf:
    def __init__(self, name, t):
        self.name, self.t = name, t
        self.w, self.r = {}, {}


class Eng:
    def __init__(self, name, e, sem):
        self.name, self.e, self.sem = name, e, sem
        self.count = 0
        self.waited = {}


def _merge(d, src):
    for k, (s, v) in src.items():
        if k not in d or d[k][1] < v:
            d[k] = (s, v)


class TR:
    def __init__(self, nc, stack):
        self.nc, self.stack = nc, stack
        mk = lambda n: stack.enter_context(nc.semaphore(n))
        self.pe = Eng("pe", nc.tensor, mk("s_pe"))
        self.act = Eng("act", nc.scalar, mk("s_act"))
        self.dve = Eng("dve", nc.vector, mk("s_dve"))
        self.pool = Eng("pool", nc.gpsimd, mk("s_pool"))
        self.sp = Eng("sp", nc.sync, mk("s_sp"))
        self.engs = [self.pe, self.act, self.dve, self.pool, self.sp]
        self.dsems = {}
        self.csems = []
        self.shsem = {}
        self.nsem = 5
        self.dead = False

    def _sync(self, eng, reads, writes, ignore=None):
        raw = {}
        for b in reads:
            _merge(raw, b.w)
        oth = {}
        for b in writes:
            _merge(oth, b.w)
            _merge(oth, b.r)
        me = id(eng.sem)
        d = dict(raw)
        for k, sv in oth.items():
            if k == me:
                continue
            if k not in d or d[k][1] < sv[1]:
                d[k] = sv
        if eng is self.pe:
            d.pop(me, None)
        if ignore is not None:
            d.pop(ignore, None)
        for k, (s, v) in d.items():
            if eng.waited.get(k, 0) < v:
                eng.e.wait_ge(s, v)
                eng.waited[k] = v

    def _rec(self, ev, reads, writes):
        k, s, v = ev
        for b in reads:
            if k not in b.r or b.r[k][1] < v:
                b.r[k] = (s, v)
        for b in writes:
            if k not in b.w or b.w[k][1] < v:
                b.w[k] = (s, v)

    def op(self, eng, fn, reads=(), writes=()):
        if self.dead:
            return
        self._sync(eng, reads, writes)
        ins = fn()
        eng.count += 1
        ins.then_inc(eng.sem, 1)
        self._rec((id(eng.sem), eng.sem, eng.count), reads, writes)

    def mm(self, fns, reads=(), writes=()):
        if self.dead:
            return
        eng = self.pe
        self._sync(eng, reads, writes)
        ins = None
        for fn in fns:
            ins = fn()
        eng.count += 1
        ins.then_inc(eng.sem, 1)
        self._rec((id(eng.sem), eng.sem, eng.count), reads, writes)

    def dma(self, q, out, in_, key, reads=(), writes=()):
        if self.dead:
            return
        self._sync(q, reads, writes)
        if key not in self.dsems:
            self.nsem += 1
            self.dsems[key] = [self.stack.enter_context(self.nc.semaphore("d_" + key)), 0]
        ent = self.dsems[key]
        ent[1] += 1
        q.e.dma_start(out=out, in_=in_).then_inc(ent[0], 16)
        self._rec((id(ent[0]), ent[0], 16 * ent[1]), reads, writes)

    def coll(self, kind, groups, inb, outb, name, shared=None, in_ap=None, out_ap=None):
        if self.dead:
            return
        q = self.pool
        if shared is not None and shared[0] in self.shsem:
            self._sync(q, [inb], [outb], ignore=id(self.shsem[shared[0]]))
        else:
            self._sync(q, [inb], [outb])
        if shared is None:
            self.nsem += 1
            sem = self.stack.enter_context(self.nc.semaphore("c_" + name))
            val = 1
            self.csems.append((sem, 1))
        else:
            key, val = shared
            if key not in self.shsem:
                self.nsem += 1
                self.shsem[key] = self.stack.enter_context(self.nc.semaphore("c_" + key))
                self.csems.append((self.shsem[key], val))
            sem = self.shsem[key]
        q.e.collective_compute(kind, ALU.bypass, replica_groups=groups,
                               ins=[(in_ap if in_ap is not None else inb.t.ap()).opt()],
                               outs=[(out_ap if out_ap is not None else outb.t.ap()).opt()]).then_inc(sem, 1)
        self._rec((id(sem), sem, val), [inb], [outb])

    def barrier(self, full=False):
        if self.dead:
            return
        evs = [(id(e.sem), e.sem, e.count) for e in self.engs if e.count > 0]
        evs += [(id(s), s, 16 * c) for (s, c) in self.dsems.values() if c > 0]
        if full:
            evs += [(id(s), s, v) for (s, v) in self.csems]
        for eng in self.engs:
            for k, s, v in evs:
                if k == id(eng.sem):
                    continue
                if eng.waited.get(k, 0) < v:
                    eng.e.wait_ge(s, v)
                    eng.waited[k] = v


def build(cf, final_wait=True):
    nc = bass.Bass("TRN2", target_bir_lowering=False)
    D, DC, TT, NT, NCOL, TPC, H, KVH, G = cf.D, cf.DC, cf.TT, cf.NT, cf.NCOL, cf.TPC, cf.H, cf.KVH, cf.G
    FW, FCH, FCB, FFC, KVW, AW, L, N1, N2, N2C = cf.FW, cf.FCH, cf.FCB, cf.FFC, cf.KVW, cf.AW, cf.L, cf.N1, cf.N2, cf.N2C
    DEPTH = cf.DEPTH
    NQK = H + KVH
    VFW = KVW + FW
    PQC = 2 * FW // 128
    wsh = weight_shapes(cf)

    def din(name, shape, dt=F32):
        return nc.dram_tensor(name, list(shape), dt, kind="ExternalInput")

    xT_in = din("xT", [D, TPC])
    metaT_in = din("metaT", [D, 16])
    w_in_sh = {n: din(n, [DEPTH, wsh[n][0] // 4, wsh[n][1]]) for n in WNAMES}
    vecs_in = din("vecs", [DEPTH, 128, cf.NV])
    rope_in = din("rope", [128, 2 * NCOL])
    sel_in = din("sel", [128, 16])
    cmat_in = din("cmat", [128, 512])
    t1_in = din("t1", [N1, 2 * N1])
    t3_in = din("t3", [N2C, 8 * N2])
    tw_in = din("tw", [N1, 2 * N2])
    yT = nc.dram_tensor("yT", [D, TPC], F32, kind="ExternalOutput")

    stack = contextlib.ExitStack()
    with stack:
        stack.enter_context(nc.allow_non_contiguous_dma(reason="small strided scratch transfers"))
        T = TR(nc, stack)
        blk = stack.enter_context(nc.Block())
        PE, ACT, DVE, POOL, SP = T.pe, T.act, T.dve, T.pool, T.sp

        def dram(name, shape, dt):
            return Buf(name, nc.dram_tensor(name, list(shape), dt))

        def ext(tn, name):
            return Buf(name, tn)

        B_xT, B_metaT, B_yT = ext(xT_in, "xT"), ext(metaT_in, "metaT"), ext(yT, "yT")
        B_vecs, B_rope, B_sel, B_cmat = ext(vecs_in, "vecs"), ext(rope_in, "rope"), ext(sel_in, "sel"), ext(cmat_in, "cmat")
        B_t1, B_t3, B_tw = ext(t1_in, "t1"), ext(t3_in, "t3"), ext(tw_in, "tw")
        B_wsh = {n: ext(w_in_sh[n], n) for n in WNAMES}

        wbs = {(n, l): dram(f"wbs_{n}{l}", [wsh[n][0] // 4, wsh[n][1]], BF16) for n in WNAMES for l in range(DEPTH)}
        wbf = {(n, l): dram(f"wbf_{n}{l}", [wsh[n][0], wsh[n][1]], BF16) for n in WNAMES for l in range(DEPTH)}
        wcs_d = [dram(f"wcs{l}", [D, 2 * FW], BF16) for l in range(DEPTH)]
        xa = dram("xa", [D, NCOL], F32)
        xm = dram("xm", [D, NCOL], F32)
        qT_d = dram("qT", [AW, NCOL], BF16)
        gT_d = dram("gT", [2 * D, NCOL], BF16)
        aT_d = dram("aT", [AW, NCOL], BF16)
        pqsel_d = dram("pqsel", [PQC * 128, NCOL], BF16)
        kin = [dram(f"kin{l}", [KVW, TPC], BF16) for l in range(DEPTH)]
        kout = [dram(f"kout{l}", [4 * KVW, TPC], BF16) for l in range(DEPTH)]
        kmeta = dram("kmeta", [KVW, 16], BF16)
        vin = [dram(f"vin{l}", [KVH * TPC, 128], BF16) for l in range(DEPTH)]
        vout = [dram(f"vout{l}", [KVH * 4 * TPC, 128], BF16) for l in range(DEPTH)]
        vmeta = dram("vmeta", [16, KVW], BF16)
        fin = [dram(f"fin{l}", [TPC, FW], BF16) for l in range(DEPTH)]
        fout = [dram(f"fout{l}", [4 * TPC, FW], BF16) for l in range(DEPTH)]
        fmeta = dram("fmeta", [16, FW], BF16)
        yd = dram("yd", [2, N1, N2, FCH], BF16)
        fpos = dram("fpos", [L, FW], BF16)
        pqin = [dram(f"pqin{l}", [cf.NPC * 2 * FCH, cf.LC], BF16) for l in range(DEPTH)]
        pqout = [dram(f"pqout{l}", [cf.NPC * 4 * 2 * FCH, cf.LC], BF16) for l in range(DEPTH)]
        hin = [dram(f"hin{l}", [128, 2 * DC], F32) for l in range(DEPTH)]
        hout = [dram(f"hout{l}", [4 * 128, 2 * DC], F32) for l in range(DEPTH)]

        ps0 = stack.enter_context(nc.psum_tensor("ps0", [128, 1024], F32))
        ps1 = stack.enter_context(nc.psum_tensor("ps1", [128, 1024], F32))
        ps4 = stack.enter_context(nc.psum_tensor("ps4", [128, 512], F32))
        ps5 = stack.enter_context(nc.psum_tensor("ps5", [128, 512], F32))
        ps6 = stack.enter_context(nc.psum_tensor("ps6", [128, 512], F32))
        pst = stack.enter_context(nc.psum_tensor("pst", [128, 1024], BF16))
        BK = [Buf(f"bk{i}", None) for i in range(8)]
        bank_aps = [ps0[:, 0:512], ps0[:, 512:1024], ps1[:, 0:512], ps1[:, 512:1024], ps4[:, :], ps5[:, :], ps6[:, :]]

        def bk(i):
            return bank_aps[i]

        uniq = [0]

        def sb(st, name, shape, dt, key=None):
            uniq[0] += 1
            return Buf(key or name, st.enter_context(nc.sbuf_tensor(f"{name}_{uniq[0]}", list(shape), dt)))

        ident = sb(stack, "ident", [128, 128], BF16)
        rotT = sb(stack, "rotT", [128, 128], BF16)
        csm = sb(stack, "csm", [128, 2, 128], BF16)
        ones = sb(stack, "ones", [128, 128], BF16)
        selb = sb(stack, "selb", [128, 16], F32)
        vec = sb(stack, "vec", [128, cf.NV], F32)

        T.dma(POOL, ident.t[:], cmat_in.ap()[:, 0:128], "c0", [B_cmat], [ident])
        T.dma(POOL, rotT.t[:], cmat_in.ap()[:, 128:256], "c0", [B_cmat], [rotT])
        T.dma(POOL, csm.t[:], cmat_in.ap()[:, 256:512].rearrange("p (a b) -> p a b", a=2), "c0", [B_cmat], [csm])
        T.dma(POOL, selb.t[:], sel_in.ap(), "c0", [B_sel], [selb])
        T.op(DVE, lambda: nc.vector.memset(ones.t[:], 1.0), [], [ones])

        T.dma(POOL, xa.t.ap()[:, 0:16], metaT_in.ap(), "xinit", [B_metaT], [xa])
        T.dma(POOL, xa.t.ap()[:, 16:NCOL], xT_in.ap(), "xinit", [B_xT], [xa])

        def ag_chunks(inb, outb, rows_in, nch, key, total=None):
            rc = rows_in // nch
            for j in range(nch):
                if getattr(cf, 'unshare', False) and total is None:
                    T.coll("AllGather", GRP, inb, outb, f"{key}_{j}",
                           in_ap=inb.t.ap()[j * rc:(j + 1) * rc, :], out_ap=outb.t.ap()[j * 4 * rc:(j + 1) * 4 * rc, :])
                else:
                    T.coll("AllGather", GRP, inb, outb, key, shared=(key, total or nch),
                           in_ap=inb.t.ap()[j * rc:(j + 1) * rc, :], out_ap=outb.t.ap()[j * 4 * rc:(j + 1) * 4 * rc, :])

        wch = weight_chunks(cf)
        wtot = sum(wch.values())
        WGA = ["wqk", "wg", "wvf"]
        wtot_a = sum(wch[n] for n in WGA)
        wtot_b = wtot - wtot_a

        def weight_ags(l):
            for n in WNAMES:
                if n in WGA:
                    ag_chunks(wbs[(n, l)], wbf[(n, l)], wsh[n][0] // 4, wch[n], f"w{l}a", wtot_a)
                else:
                    ag_chunks(wbs[(n, l)], wbf[(n, l)], wsh[n][0] // 4, wch[n], f"w{l}b", wtot_b)

        for l in range(DEPTH):
            for n in WNAMES:
                T.dma(POOL, wbs[(n, l)].t.ap(), w_in_sh[n].ap()[l], "wcast", [B_wsh[n]], [wbs[(n, l)]])
            if l == 0:
                weight_ags(0)

        tiles = [(16 + i * TT, TT, i) for i in range(NT)] + [(0, 16, NT)]

        class WRing:
            def __init__(self, st, name, kcmax, nslots):
                self.slots = [sb(st, f"{name}{i}", [128, kcmax, 128], BF16, key=f"{name[2:]}{i}") for i in range(nslots)]
                self.i = 0

            def load(self, wb, chunk, kc):
                s = self.slots[self.i % len(self.slots)]
                self.i += 1
                src = wb.t.ap()[chunk * 128:(chunk + 1) * 128, :].rearrange("p (k n) -> p k n", n=128)
                T.dma(SP, s.t[:, 0:kc, :], src, s.name, [wb], [s])
                return s

        def rmsnorm(xt, n, gcol, sq, hT, rs, rstd, pbank, pbuf):
            T.op(ACT, lambda: nc.scalar.activation(out=sq.t[:, :, 0:n], in_=xt.t[:, :, 0:n], func=AF.Square), [xt], [sq])
            T.mm([(lambda c=c: nc.tensor.matmul(pbank[:, 0:n], ones.t[:, :], sq.t[:, c, 0:n], start=(c == 0), stop=(c == DC - 1)))
                  for c in range(DC)], [ones, sq], [pbuf])
            T.op(ACT, lambda: nc.scalar.activation(out=rs.t[:, 0:n], in_=pbank[:, 0:n], func=AF.Sqrt, bias=float(cf.EPS), scale=1.0 / D), [pbuf], [rs])
            T.op(DVE, lambda: nc.vector.reciprocal(out=rstd.t[:, 0:n], in_=rs.t[:, 0:n]), [rs], [rstd])
            for c in range(DC):
                T.op(DVE, lambda c=c: nc.vector.scalar_tensor_tensor(out=hT.t[:, c, 0:n], in0=xt.t[:, c, 0:n], scalar=vec.t[:, gcol + c:gcol + c + 1],
                                                                    in1=rstd.t[:, 0:n], op0=ALU.mult, op1=ALU.mult), [xt, vec, rstd], [hT])

        def ck(name):
            if getattr(cf, 'stop', None) == name:
                if getattr(cf, 'exc', False):
                    raise _Stop()
                T.barrier(full=True)
                T.dead = True

        try:
          for l in range(DEPTH):
              last = (l == DEPTH - 1)
              T.barrier()
              ck('pro')
              T.dma(POOL, vec.t[:], vecs_in.ap()[l], "c_vec", [B_vecs], [vec])

              with contextlib.ExitStack() as ph:
                  xt = sb(ph, "p1_xt", [128, DC, TT], F32, key="xt")
                  sq = sb(ph, "p1_sq", [128, DC, TT], BF16)
                  hT = sb(ph, "p1_hT", [128, DC, TT], BF16)
                  rs = sb(ph, "p1_rs", [128, TT], F32)
                  rstd = sb(ph, "p1_rstd", [128, TT], F32)
                  wvf = sb(ph, "p1_wvf", [128, DC, VFW], BF16)
                  ring = WRing(ph, "p1_w", DC, 4)
                  rope = sb(ph, "p1_rope", [128, 2, TT], F32)
                  sqh = sb(ph, "p1_sqh", [128, TT], BF16)
                  qg = sb(ph, "p1_qg", [128, TT], BF16)
                  rsh = sb(ph, "p1_rsh", [128, TT], F32)
                  rstdh = sb(ph, "p1_rstdh", [128, TT], F32)
                  t1b = sb(ph, "p1_t1", [128, TT], F32)
                  t2b = sb(ph, "p1_t2", [128, TT], F32)
                  qo = [sb(ph, f"p1_qo{i}", [128, TT], BF16) for i in range(2)]
                  vo = [sb(ph, f"p1_vo{i}", [128, 512], BF16) for i in range(2)]
                  go = [sb(ph, f"p1_go{i}", [128, 4, TT], BF16) for i in range(2)]
                  T.dma(SP, wvf.t[:], wbf[("wvf", l)].t.ap().rearrange("p (k n) -> p k n", n=VFW), "p1_wvf", [wbf[("wvf", l)]], [wvf])
                  nq = nv = ng = 0
                  frc = TPC // cf.NFC
                  f_ag = [0]
                  f_cp = [0]

                  def f_copy(j):
                      for rr in range(4):
                          p0 = 16 + rr * TPC + j * frc
                          T.dma(POOL, fpos.t.ap()[p0:p0 + frc, :], fout[l].t.ap()[(j * 4 + rr) * frc:(j * 4 + rr + 1) * frc, :], "fpos_a", [fout[l]], [fpos])

                  def f_progress(done_tokens, flush=False):
                      while f_ag[0] < cf.NFC and (f_ag[0] + 1) * frc <= done_tokens:
                          j = f_ag[0]
                          T.coll("AllGather", GRP, fin[l], fout[l], f"f{l}", shared=(f"f{l}", cf.NFC),
                                 in_ap=fin[l].t.ap()[j * frc:(j + 1) * frc, :], out_ap=fout[l].t.ap()[j * 4 * frc:(j + 1) * 4 * frc, :])
                          f_ag[0] += 1
                          while f_cp[0] < f_ag[0] - 1:
                              f_copy(f_cp[0])
                              f_cp[0] += 1
                      if flush:
                          while f_cp[0] < f_ag[0]:
                              f_copy(f_cp[0])
                              f_cp[0] += 1

                  for (c0, n, _) in tiles:
                      meta = (n == 16)
                      T.dma(POOL, xt.t[:, :, 0:n], xa.t.ap()[:, c0:c0 + n].rearrange("(c p) n -> p c n", p=128), xt.name, [xa], [xt])
                      T.dma(POOL, rope.t[:, :, 0:n], rope_in.ap().rearrange("p (a n) -> p a n", a=2)[:, :, c0:c0 + n], "p1_rope", [B_rope], [rope])
                      ck('p1x')
                      rmsnorm(xt, n, cf.V_GMIX, sq, hT, rs, rstd, bk(6), BK[6])
                      ck('p1a')
                      def head_mm(hd):
                          w = ring.load(wbf[("wqk", l)], hd, DC)
                          pa, pab = bk(hd % 2), BK[hd % 2]
                          T.mm([(lambda c=c: nc.tensor.matmul(pa[:, 0:n], w.t[:, c, :], hT.t[:, c, 0:n], start=(c == 0), stop=(c == DC - 1)))
                                for c in range(DC)], [w, hT], [pab])

                      def head_p1(hd):
                          pa, pab = bk(hd % 2), BK[hd % 2]
                          gcol = cf.V_QN if hd < H else cf.V_KN
                          T.op(ACT, lambda: nc.scalar.activation(out=sqh.t[:, 0:n], in_=pa[:, 0:n], func=AF.Square), [pab], [sqh])
                          T.op(ACT, lambda: nc.scalar.activation(out=qg.t[:, 0:n], in_=pa[:, 0:n], func=AF.Copy, scale=vec.t[:, gcol:gcol + 1]), [pab, vec], [qg])

                      def head_p2(hd, q_o):
                          T.mm([lambda: nc.tensor.matmul(bk(4)[:, 0:n], ones.t[:, :], sqh.t[:, 0:n], start=True, stop=True)], [ones, sqh], [BK[4]])
                          T.mm([lambda: nc.tensor.matmul(bk(5)[:, 0:n], rotT.t[:, :], qg.t[:, 0:n], start=True, stop=True)], [rotT, qg], [BK[5]])
                          T.op(ACT, lambda: nc.scalar.activation(out=rsh.t[:, 0:n], in_=bk(4)[:, 0:n], func=AF.Sqrt, bias=float(cf.EPS), scale=1.0 / 128), [BK[4]], [rsh])
                          T.op(DVE, lambda: nc.vector.reciprocal(out=rstdh.t[:, 0:n], in_=rsh.t[:, 0:n]), [rsh], [rstdh])
                          T.op(DVE, lambda: nc.vector.tensor_tensor(out=t1b.t[:, 0:n], in0=qg.t[:, 0:n], in1=rope.t[:, 0, 0:n], op=ALU.mult), [qg, rope], [t1b])
                          T.op(DVE, lambda: nc.vector.tensor_tensor(out=t2b.t[:, 0:n], in0=bk(5)[:, 0:n], in1=rope.t[:, 1, 0:n], op=ALU.mult), [BK[5], rope], [t2b])
                          T.op(DVE, lambda: nc.vector.tensor_tensor(out=t1b.t[:, 0:n], in0=t1b.t[:, 0:n], in1=t2b.t[:, 0:n], op=ALU.add), [t1b, t2b], [t1b])
                          T.op(DVE, lambda: nc.vector.tensor_tensor(out=q_o.t[:, 0:n], in0=t1b.t[:, 0:n], in1=rstdh.t[:, 0:n], op=ALU.mult), [t1b, rstdh], [q_o])
                          if hd < H:
                              T.dma(POOL, qT_d.t.ap()[hd * 128:(hd + 1) * 128, c0:c0 + n], q_o.t[:, 0:n], q_o.name, [q_o], [qT_d])
                          else:
                              kh = hd - H
                              if meta:
                                  T.dma(POOL, kmeta.t.ap()[kh * 128:(kh + 1) * 128, :], q_o.t[:, 0:n], q_o.name, [q_o], [kmeta])
                              else:
                                  T.dma(POOL, kin[l].t.ap()[kh * 128:(kh + 1) * 128, c0 - 16:c0 - 16 + n], q_o.t[:, 0:n], q_o.name, [q_o], [kin[l]])

                      head_mm(0)
                      head_p1(0)
                      for hd in range(1, NQK):
                          head_mm(hd)
                          head_p2(hd - 1, qo[nq % 2])
                          nq += 1
                          head_p1(hd)
                      head_p2(NQK - 1, qo[nq % 2])
                      nq += 1
                      ck('p1b')
                      ntb = max(1, n // 128)
                      for tb in range(ntb):
                          tn = min(128, n)
                          cb0 = 0
                          while cb0 < VFW:
                              if cb0 < KVW:
                                  cw = KVW
                              else:
                                  cw = min(512, VFW - cb0)
                              pv, pvb = bk(2 + nv % 2), BK[2 + nv % 2]
                              T.mm([(lambda c=c: nc.tensor.matmul(pv[0:tn, 0:cw], hT.t[:, c, tb * 128:tb * 128 + tn], wvf.t[:, c, cb0:cb0 + cw],
                                                                  start=(c == 0), stop=(c == DC - 1))) for c in range(DC)], [hT, wvf], [pvb])
                              v_o = vo[nv % 2]
                              nv += 1
                              T.op(ACT, lambda: nc.scalar.copy(out=v_o.t[0:tn, 0:cw], in_=pv[0:tn, 0:cw]), [pvb], [v_o])
                              if cb0 < KVW:
                                  dst, dm = (vmeta, vmeta.t.ap()[:, :]) if meta else (
                                      vin[l], vin[l].t.ap().rearrange("(h t) d -> t h d", h=KVH)[c0 - 16 + tb * 128:c0 - 16 + tb * 128 + tn, :, :])
                              else:
                                  f0 = cb0 - KVW
                                  dst, dm = (fmeta, fmeta.t.ap()[:, f0:f0 + cw]) if meta else (fin[l], fin[l].t.ap()[c0 - 16 + tb * 128:c0 - 16 + tb * 128 + tn, f0:f0 + cw])
                              srcv = v_o.t[0:tn, 0:cw]
                              if cb0 < KVW and not meta:
                                  srcv = srcv.rearrange("p (h d) -> p h d", h=KVH)
                              T.dma(POOL, dm, srcv, v_o.name, [v_o], [dst])
                              cb0 += cw
                      ck('p1c')
                      for gc in range(2 * DC):
                          w = ring.load(wbf[("wg", l)], gc, DC)
                          pa, pab = bk(gc % 2), BK[gc % 2]
                          T.mm([(lambda c=c: nc.tensor.matmul(pa[:, 0:n], w.t[:, c, :], hT.t[:, c, 0:n], start=(c == 0), stop=(c == DC - 1)))
                                for c in range(DC)], [w, hT], [pab])
                          g_o = go[(ng // 4) % 2]
                          T.op(ACT, lambda: nc.scalar.activation(out=g_o.t[:, gc % 4, 0:n], in_=pa[:, 0:n], func=AF.Sigmoid,
                                                                 bias=vec.t[:, cf.V_BG + gc:cf.V_BG + gc + 1], scale=1.0), [pab, vec], [g_o])
                          ng += 1
                          if gc % 4 == 3:
                              g4 = gc // 4
                              T.dma(POOL, gT_d.t.ap()[g4 * 512:(g4 + 1) * 512, c0:c0 + n].rearrange("(a p) n -> p a n", p=128),
                                    g_o.t[:, :, 0:n], g_o.name, [g_o], [gT_d])
              T.barrier()
              ck('p1')
              ag_chunks(fin[l], fout[l], TPC, cf.NFC, f"f{l}")
              ag_chunks(kin[l], kout[l], KVW, KVH, f"k{l}")
              ag_chunks(vin[l], vout[l], KVH * TPC, KVH, f"v{l}")
              ck('ag1')

              P2B = cf.P2B
              with contextlib.ExitStack() as ph:
                  T.dma(POOL, fpos.t.ap()[0:16, :], fmeta.t.ap(), "fpos_a", [fmeta], [fpos])
                  frc_ = TPC // cf.NFC
                  for j in range(cf.NFC):
                      for rr in range(4):
                          p0 = 16 + rr * TPC + j * frc_
                          T.dma(POOL, fpos.t.ap()[p0:p0 + frc_, :], fout[l].t.ap()[(j * 4 + rr) * frc_:(j * 4 + rr + 1) * frc_, :], "fpos_a", [fout[l]], [fpos])
                  Z = sb(ph, "f_Z", [N1, N2, FCH], BF16)
                  PZ = N2 // 4 if N2 % 4 == 0 else N2 // 2
                  zs = [sb(ph, f"f_zs{i}", [N1, PZ, FCH], BF16) for i in range(2)]
                  t1s = sb(ph, "f_t1", [N1, 2 * N1], BF16)
                  tws = sb(ph, "f_tw", [N1, 2, N2], F32)
                  wfr = sb(ph, "f_wfr", [128, cf.FG, D], BF16)
                  wst = [sb(ph, f"f_wst{i}", [128, 512], BF16) for i in range(2)]
                  fa = sb(ph, "f_a", [N1, 512], F32)
                  fb = sb(ph, "f_b", [N1, 512], F32)
                  yo = [sb(ph, f"f_yo{i}", [N1, 2, 512], BF16) for i in range(2)]
                  T.dma(POOL, t1s.t[:], t1_in.ap(), "f_t1", [B_t1], [t1s])
                  T.dma(POOL, tws.t[:], tw_in.ap().rearrange("p (a n) -> p a n", a=2), "f_tw", [B_tw], [tws])
                  T.dma(SP, wfr.t[:], wbf[("wfr", l)].t.ap().rearrange("p (g n) -> p g n", g=cf.FG), "f_wfr", [wbf[("wfr", l)]], [wfr])
                  nw = 0
                  for g in range(cf.FG):
                      for pq in range(2):
                          kc = (g // FCB) * (2 * FCB) + pq * FCB + (g % FCB)
                          for nb in range(D // 512 if D >= 512 else 1):
                              nbw = min(512, D)
                              pw, pwb = bk(nw % 2), BK[nw % 2]
                              T.mm([lambda: nc.tensor.matmul(pw[:, 0:nbw], csm.t[:, pq, :], wfr.t[:, g, nb * nbw:(nb + 1) * nbw], start=True, stop=True)],
                                   [csm, wfr], [pwb])
                              ws_ = wst[nw % 2]
                              nw += 1
                              T.op(ACT, lambda: nc.scalar.copy(out=ws_.t[:, 0:nbw], in_=pw[:, 0:nbw]), [pwb], [ws_])
                              T.dma(POOL, wcs_d[l].t.ap()[nb * nbw:(nb + 1) * nbw, kc * 128:(kc + 1) * 128].rearrange("(j p) n -> p j n", p=128),
                                    ws_.t[:, 0:nbw].rearrange("p (j n) -> p j n", n=128), ws_.name, [ws_], [wcs_d[l]])
                  fview = fpos.t.ap().rearrange("(a b) c -> a b c", b=N2)
                  nz = 0
                  for pz in range(N2 // PZ):
                      for q in range(4):
                          z_ = zs[nz % 2]
                          nz += 1
                          T.dma(SP, z_.t[:], fview[:, pz * PZ:(pz + 1) * PZ, q * FCH:(q + 1) * FCH], z_.name, [fpos], [z_])
                          if q == 0:
                              T.op(DVE, lambda: nc.vector.tensor_scalar(out=Z.t[:, pz * PZ:(pz + 1) * PZ, :], in0=z_.t[:], scalar1=selb.t[0:N1, 9:10], scalar2=None, op0=ALU.mult),
                                   [z_, selb], [Z])
                          else:
                              T.op(DVE, lambda: nc.vector.scalar_tensor_tensor(out=Z.t[:, pz * PZ:(pz + 1) * PZ, :], in0=z_.t[:], scalar=selb.t[0:N1, 9 + q:10 + q],
                                                                              in1=Z.t[:, pz * PZ:(pz + 1) * PZ, :], op0=ALU.mult, op1=ALU.add), [z_, selb, Z], [Z])
                  Zf = Z.t[:].rearrange("p a c -> p (a c)")
                  nblk = N2 // P2B
                  for b_ in range(nblk):
                      pr, prb = bk(2 * (b_ % 2)), BK[2 * (b_ % 2)]
                      pi, pib = bk(2 * (b_ % 2) + 1), BK[2 * (b_ % 2) + 1]
                      T.mm([lambda: nc.tensor.matmul(pr[0:N1, :], t1s.t[:, 0:N1], Zf[:, b_ * 512:(b_ + 1) * 512], start=True, stop=True)], [t1s, Z], [prb])
                      T.mm([lambda: nc.tensor.matmul(pi[0:N1, :], t1s.t[:, N1:2 * N1], Zf[:, b_ * 512:(b_ + 1) * 512], start=True, stop=True)], [t1s, Z], [pib])
                      y_ = yo[b_ % 2]
                      tcb = tws.t[:, 0, b_ * P2B:(b_ + 1) * P2B].unsqueeze(2).to_broadcast([N1, P2B, FCH])
                      tsb = tws.t[:, 1, b_ * P2B:(b_ + 1) * P2B].unsqueeze(2).to_broadcast([N1, P2B, FCH])
                      v3 = lambda ap: ap.rearrange("p (a c) -> p a c", c=FCH)
                      T.op(DVE, lambda: nc.vector.tensor_tensor(out=v3(fa.t[:, :]), in0=v3(pr[0:N1, :]), in1=tcb, op=ALU.mult), [prb, tws], [fa])
                      T.op(DVE, lambda: nc.vector.tensor_tensor(out=v3(fb.t[:, :]), in0=v3(pi[0:N1, :]), in1=tsb, op=ALU.mult), [pib, tws], [fb])
                      T.op(DVE, lambda: nc.vector.tensor_tensor(out=y_.t[:, 0, :], in0=fa.t[:, :], in1=fb.t[:, :], op=ALU.subtract), [fa, fb], [y_])
                      T.op(DVE, lambda: nc.vector.tensor_tensor(out=v3(fa.t[:, :]), in0=v3(pr[0:N1, :]), in1=tsb, op=ALU.mult), [prb, tws], [fa])
                      T.op(DVE, lambda: nc.vector.tensor_tensor(out=v3(fb.t[:, :]), in0=v3(pi[0:N1, :]), in1=tcb, op=ALU.mult), [pib, tws], [fb])
                      T.op(DVE, lambda: nc.vector.tensor_tensor(out=y_.t[:, 1, :], in0=fa.t[:, :], in1=fb.t[:, :], op=ALU.add), [fa, fb], [y_])
                      ydv = yd.t.ap().rearrange("r k a c -> k r (a c)")
                      T.dma(POOL, ydv[:, :, b_ * 512:(b_ + 1) * 512], y_.t[:], y_.name, [y_], [yd])
              T.barrier()
              ck('f1')
              with contextlib.ExitStack() as ph:
                  t3s = sb(ph, "f_t3", [N2C, 2, 2, 2 * N2], BF16)
                  T.dma(POOL, t3s.t[:], t3_in.ap().rearrange("p (r k n) -> p r k n", r=2, k=2), "f_t3", [B_t3], [t3s])
                  yt = [[sb(ph, f"f_yt{ri}{ch}", [N2C, N1, 128], BF16) for ch in range(2)] for ri in range(2)]
                  pqs = sb(ph, "f_pqs", [128, 2, L], BF16)
                  for cb in range(FCB):
                      for ri in range(2):
                          for ch in range(2):
                              src = yd.t.ap()[ri].rearrange("k a c -> a k c")[ch * N2C:(ch + 1) * N2C, :, cb * 128:(cb + 1) * 128]
                              T.dma(SP, yt[ri][ch].t[:], src, yt[ri][ch].name, [yd], [yt[ri][ch]])
                      for k1 in range(N1):
                          po, pob = bk(4 + k1 % 2), BK[4 + k1 % 2]
                          fns = []
                          idx = 0
                          for ri in range(2):
                              for ch in range(2):
                                  fns.append(lambda ri=ri, ch=ch, idx=idx: nc.tensor.matmul(po[:, 0:2 * N2], yt[ri][ch].t[:, k1, :], t3s.t[:, ri, ch, :],
                                                                                           start=(idx == 0), stop=(idx == 3)))
                                  idx += 1
                          T.mm(fns, [yt[0][0], yt[0][1], yt[1][0], yt[1][1], t3s], [pob])
                          dst = pqs.t[:, :, k1:k1 + N1 * (N2 - 1) + 1:N1]
                          T.op(ACT, lambda: nc.scalar.copy(out=dst, in_=po[:, 0:2 * N2].rearrange("p (a n) -> p a n", a=2)), [pob], [pqs])
                      for j in range(cf.NPC):
                          T.dma(POOL, pqin[l].t.ap()[j * 2 * FCH:(j + 1) * 2 * FCH, :].rearrange("(a c) n -> c a n", a=2)[cb * 128:(cb + 1) * 128, :, :],
                                pqs.t[:, :, j * cf.LC:(j + 1) * cf.LC], "f_pqs", [pqs], [pqin[l]])
              T.barrier()
              ck('f3')
              ag_chunks(pqin[l], pqout[l], cf.NPC * 2 * FCH, cf.NPC, f"pq{l}")
              ck('pq')

              scale = 1.0 / math.sqrt(128.0)
              NKC = 4 * TPC // 128
              with contextlib.ExitStack() as ph:
                  KT = sb(ph, "a_KT", [128, 4 * TPC + 16], BF16)
                  Vt = sb(ph, "a_Vt", [128, NKC + 1, 132], BF16)
                  QT = [sb(ph, f"a_QT{i}", [128, TT], BF16) for i in range(2)]
                  Pb = [sb(ph, f"a_P{i}", [128, 2, TT], BF16) for i in range(2)]
                  rinv = sb(ph, "a_rinv", [128, 4], F32)
                  on = [sb(ph, f"a_on{i}", [128, 128], BF16) for i in range(2)]
                  aT = [sb(ph, f"a_aT{i}", [128, TT], BF16) for i in range(2)]
                  pqa = sb(ph, "a_pq", [128, PQC, TT], BF16)
                  pqc_a = [sb(ph, f"a_pqc{i}", [128, PQC, TT], BF16) for i in range(2)]
                  npc_a = [0]

                  def pq_load(dst_t, pos0, nn, key, dbuf):
                      a = pos0
                      while a < pos0 + nn:
                          j = a // cf.LC
                          b = min(pos0 + nn, (j + 1) * cf.LC)
                          v_ = pqout[l].t.ap()[j * 8 * FCH:(j + 1) * 8 * FCH, :].rearrange("(c p) n -> p c n", p=128)
                          T.dma(SP, dst_t[:, :, a - pos0:b - pos0], v_[:, :, a - j * cf.LC:b - j * cf.LC], key, [pqout[l]], [dbuf])
                          a = b

                  na = 0
                  if l + 1 < DEPTH:
                      weight_ags(l + 1)
                  for kvh in range(KVH):
                      for rr in range(4):
                          T.dma(SP, KT.t[:, rr * TPC:(rr + 1) * TPC], kout[l].t.ap()[(kvh * 4 + rr) * 128:(kvh * 4 + rr + 1) * 128, :], "a_KT", [kout[l]], [KT])
                      T.dma(SP, KT.t[:, 4 * TPC:4 * TPC + 16], kmeta.t.ap()[kvh * 128:(kvh + 1) * 128, :], "a_KT", [kmeta], [KT])
                      T.dma(SP, Vt.t[:, 0:NKC, 0:128], vout[l].t.ap()[kvh * 4 * TPC:(kvh + 1) * 4 * TPC, :].rearrange("(c p) d -> p c d", p=128), "a_Vt", [vout[l]], [Vt])
                      T.dma(SP, Vt.t[0:16, NKC, 0:128], vmeta.t.ap()[:, kvh * 128:(kvh + 1) * 128], "a_Vt", [vmeta], [Vt])
                      T.op(DVE, lambda: nc.vector.memset(Vt.t[:, :, 128:129], 1.0), [], [Vt])
                      items = [(2 * j, 2) for j in range(NKC // 2)] + [(NKC, 1)]
                      for (c0, n, _) in tiles:
                          nqb = max(1, n // 128)
                          qn_ = min(128, n)
                          if kvh == KVH - 1:
                              if n == 16:
                                  pq_load(pqa.t, 0, 16, "a_pq", pqa)
                              else:
                                  for q in range(4):
                                      pc = pqc_a[npc_a[0] % 2]
                                      npc_a[0] += 1
                                      pq_load(pc.t, q * TPC + c0, n, pc.name, pc)
                                      if q == 0:
                                          T.op(DVE, lambda: nc.vector.tensor_scalar(out=pqa.t[:, :, 0:n], in0=pc.t[:, :, 0:n], scalar1=selb.t[:, 9:10], scalar2=None, op0=ALU.mult), [pc, selb], [pqa])
                                      else:
                                          T.op(DVE, lambda: nc.vector.scalar_tensor_tensor(out=pqa.t[:, :, 0:n], in0=pc.t[:, :, 0:n], scalar=selb.t[:, 9 + q:10 + q], in1=pqa.t[:, :, 0:n],
                                                                                          op0=ALU.mult, op1=ALU.add), [pc, selb, pqa], [pqa])
                              T.dma(SP, pqsel_d.t.ap()[:, c0:c0 + n].rearrange("(c p) n -> p c n", p=128), pqa.t[:, :, 0:n], "a_pq", [pqa], [pqsel_d])
                          for gq in range(G):
                              hd = kvh * G + gq
                              Q = QT[na % 2]
                              a_T = aT[na % 2]
                              na += 1
                              T.dma(POOL, Q.t[:, 0:n], qT_d.t.ap()[hd * 128:(hd + 1) * 128, c0:c0 + n], Q.name, [qT_d], [Q])

                              def qk(it, slot):
                                  kc0, cnt = it
                                  for u in range(cnt):
                                      kk = 16 if kc0 == NKC else 128
                                      pS = bk(2 * slot + u)
                                      T.mm([lambda: nc.tensor.matmul(pS[0:kk, 0:n], KT.t[:, (kc0 + u) * 128:(kc0 + u) * 128 + kk], Q.t[:, 0:n], start=True, stop=True)],
                                           [KT, Q], [BK[2 * slot + u]])

                              qk(items[0], 0)
                              for ii, it in enumerate(items):
                                  slot = ii % 2
                                  if ii + 1 < len(items):
                                      qk(items[ii + 1], (ii + 1) % 2)
                                  kc0, cnt = it
                                  kk = 16 if kc0 == NKC else 128
                                  Pt = Pb[slot]
                                  src = (ps0 if slot == 0 else ps1)[0:kk, :].rearrange("p (a n) -> p a n", a=2)[:, 0:cnt, 0:n]
                                  T.op(ACT, lambda: nc.scalar.activation(out=Pt.t[0:kk, 0:cnt, 0:n], in_=src, func=AF.Exp, scale=scale),
                                       [BK[2 * slot], BK[2 * slot + 1]], [Pt])
                                  fns = []
                                  for u in range(cnt):
                                      for qb in range(nqb):
                                          ob = 4 + qb // 2
                                          oc0 = (qb % 2) * 256
                                          first = (ii == 0 and u == 0 and qb % 2 == 0)
                                          lastm = (ii == len(items) - 1 and u == cnt - 1)
                                          fns.append(lambda u=u, qb=qb, ob=ob, oc0=oc0, first=first, lastm=lastm: nc.tensor.matmul(
                                              bk(ob)[0:qn_, oc0:oc0 + 129], Pt.t[0:kk, u, qb * 128:qb * 128 + qn_], Vt.t[0:kk, kc0 + u, 0:129],
                                              start=first, stop=lastm, skip_group_check=True))
                                  T.mm(fns, [Pt, Vt], [BK[4], BK[5]])
                              for qb in range(nqb):
                                  ob = 4 + qb // 2
                                  oc0 = (qb % 2) * 256
                                  T.op(DVE, lambda: nc.vector.reciprocal(out=rinv.t[0:qn_, qb:qb + 1], in_=bk(ob)[0:qn_, oc0 + 128:oc0 + 129]), [BK[ob]], [rinv])
                                  o_n = on[qb % 2]
                                  T.op(DVE, lambda: nc.vector.tensor_scalar(out=o_n.t[0:qn_, :], in0=bk(ob)[0:qn_, oc0:oc0 + 128], scalar1=rinv.t[0:qn_, qb:qb + 1], scalar2=None, op0=ALU.mult),
                                       [BK[ob], rinv], [o_n])
                                  T.mm([lambda: nc.tensor.transpose(pst[:, qb * 128:qb * 128 + qn_], o_n.t[0:qn_, :], ident.t[0:qn_, 0:qn_])], [o_n, ident], [BK[7]])
                              T.op(DVE, lambda: nc.vector.tensor_copy(out=a_T.t[:, 0:n], in_=pst[:, 0:n]), [BK[7]], [a_T])
                              T.dma(POOL, aT_d.t.ap()[hd * 128:(hd + 1) * 128, c0:c0 + n], a_T.t[:, 0:n], a_T.name, [a_T], [aT_d])
              T.barrier()
              ck('attn')
              xh_st = contextlib.ExitStack()
              xh = sb(xh_st, "xh", [128, DC, cf.NH], F32)
              hsb = sb(xh_st, "hsb", [128, 2, DC], F32)
              with contextlib.ExitStack() as ph:
                  xt = sb(ph, "m_xt", [128, DC, TT], F32, key="xt")
                  at = sb(ph, "m_at", [128, H, TT], BF16)
                  pq = sb(ph, "m_pq", [128, PQC, TT], BF16)
                  gt = sb(ph, "m_gt", [128, 2 * DC, TT], BF16)
                  mg = sb(ph, "m_mg", [128, DC, TT], BF16)
                  ta = sb(ph, "m_ta", [128, TT], F32)
                  tb_ = sb(ph, "m_tb", [128, TT], F32)
                  xo = xt
                  ring = WRing(ph, "m_w", max(DC, PQC, H), 4)
                  T.op(DVE, lambda: nc.vector.memset(xh.t[:], 0.0), [], [xh])
                  npc = 0
                  for (c0, n, hj) in tiles:
                      meta = (n == 16)
                      T.dma(POOL, xt.t[:, :, 0:n], xa.t.ap()[:, c0:c0 + n].rearrange("(c p) n -> p c n", p=128), xt.name, [xa], [xt])
                      T.dma(ACT, at.t[:, :, 0:n], aT_d.t.ap()[:, c0:c0 + n].rearrange("(c p) n -> p c n", p=128), "m_at", [aT_d], [at])
                      T.dma(ACT, gt.t[:, :, 0:n], gT_d.t.ap()[:, c0:c0 + n].rearrange("(c p) n -> p c n", p=128), "m_gt", [gT_d], [gt])
                      T.dma(ACT, pq.t[:, :, 0:n], pqsel_d.t.ap()[:, c0:c0 + n].rearrange("(c p) n -> p c n", p=128), "m_pq", [pqsel_d], [pq])
                      for oc in range(DC):
                          w1 = ring.load(wbf[("wab", l)], oc, H)
                          w2 = ring.load(wcs_d[l], oc, PQC)
                          ba, bb = 2 * (oc % 2), 2 * (oc % 2) + 1
                          T.mm([(lambda c=c: nc.tensor.matmul(bk(ba)[:, 0:n], w1.t[:, c, :], at.t[:, c, 0:n], start=(c == 0), stop=(c == H - 1))) for c in range(H)], [w1, at], [BK[ba]])
                          T.mm([(lambda c=c: nc.tensor.matmul(bk(bb)[:, 0:n], w2.t[:, c, :], pq.t[:, c, 0:n], start=(c == 0), stop=(c == PQC - 1))) for c in range(PQC)], [w2, pq], [BK[bb]])
                          T.op(DVE, lambda: nc.vector.tensor_tensor(out=ta.t[:, 0:n], in0=bk(ba)[:, 0:n], in1=gt.t[:, oc, 0:n], op=ALU.mult), [BK[ba], gt], [ta])
                          T.op(DVE, lambda: nc.vector.tensor_tensor(out=tb_.t[:, 0:n], in0=bk(bb)[:, 0:n], in1=gt.t[:, DC + oc, 0:n], op=ALU.mult), [BK[bb], gt], [tb_])
                          T.op(DVE, lambda: nc.vector.tensor_tensor(out=mg.t[:, oc, 0:n], in0=ta.t[:, 0:n], in1=tb_.t[:, 0:n], op=ALU.add), [ta, tb_], [mg])
                      for oc in range(DC):
                          w = ring.load(wbf[("wout", l)], oc, DC)
                          pb_, pbb = bk(4 + oc % 2), BK[4 + oc % 2]
                          T.mm([(lambda c=c: nc.tensor.matmul(pb_[:, 0:n], w.t[:, c, :], mg.t[:, c, 0:n], start=(c == 0), stop=(c == DC - 1))) for c in range(DC)], [w, mg], [pbb])
                          T.op(DVE, lambda: nc.vector.tensor_tensor(out=xo.t[:, oc, 0:n], in0=pb_[:, 0:n], in1=xt.t[:, oc, 0:n], op=ALU.add), [pbb, xt], [xo])
                      T.dma(POOL, xm.t.ap()[:, c0:c0 + n].rearrange("(c p) n -> p c n", p=128), xo.t[:, :, 0:n], "xo", [xo], [xm])
                      if meta:
                          T.op(DVE, lambda: nc.vector.tensor_copy(out=xh.t[:, :, 0], in_=xo.t[:, :, 15]), [xo], [xh])
                      else:
                          if hj + 1 < NT:
                              T.op(DVE, lambda: nc.vector.tensor_copy(out=xh.t[:, :, 2 * (hj + 1)], in_=xo.t[:, :, n - 1]), [xo], [xh])
                          else:
                              T.op(DVE, lambda: nc.vector.tensor_copy(out=hsb.t[:, 1, :], in_=xo.t[:, :, n - 1]), [xo], [hsb])
                          if hj >= 1:
                              T.op(DVE, lambda: nc.vector.tensor_copy(out=xh.t[:, :, 2 * (hj - 1) + 1], in_=xo.t[:, :, 0]), [xo], [xh])
                          else:
                              T.op(DVE, lambda: nc.vector.tensor_copy(out=hsb.t[:, 0, :], in_=xo.t[:, :, 0]), [xo], [hsb])
                  T.dma(POOL, hin[l].t.ap().rearrange("p (a c) -> p a c", a=2), hsb.t[:], "hsb", [hsb], [hin[l]])
              T.barrier()
              T.coll("AllGather", GRP, hin[l], hout[l], f"h{l}")
              ck('halo')
              with contextlib.ExitStack() as ph:
                  hb = sb(ph, "n_hb", [128, 4, 2, DC], F32)
                  T.dma(POOL, hb.t[:], hout[l].t.ap().rearrange("(r p) (a c) -> p r a c", p=128, a=2), "n_hb", [hout[l]], [hb])
                  jl, jr, jm = 0, 2 * (NT - 1) + 1, 2 * NT + 1
                  T.op(DVE, lambda: nc.vector.tensor_scalar(out=xh.t[:, :, jl], in0=xh.t[:, :, jl], scalar1=selb.t[:, 0:1], scalar2=None, op0=ALU.mult), [xh, selb], [xh])
                  for j in range(4):
                      T.op(DVE, lambda j=j: nc.vector.scalar_tensor_tensor(out=xh.t[:, :, jl], in0=hb.t[:, j, 1, :], scalar=selb.t[:, 1 + j:2 + j], in1=xh.t[:, :, jl],
                                                                          op0=ALU.mult, op1=ALU.add), [hb, selb, xh], [xh])
                      T.op(DVE, lambda j=j: nc.vector.scalar_tensor_tensor(out=xh.t[:, :, jr], in0=hb.t[:, j, 0, :], scalar=selb.t[:, 5 + j:6 + j], in1=xh.t[:, :, jr],
                                                                          op0=ALU.mult, op1=ALU.add), [hb, selb, xh], [xh])
                  T.op(DVE, lambda: nc.vector.tensor_copy(out=xh.t[:, :, jm], in_=hb.t[:, 0, 0, :]), [hb], [xh])
                  NH = cf.NH
                  sqh_ = sb(ph, "n_sqh", [128, DC, NH], BF16)
                  h2h = sb(ph, "n_h2h", [128, DC, NH], BF16)
                  rsx = sb(ph, "n_rsx", [128, NH], F32)
                  rstx = sb(ph, "n_rstx", [128, NH], F32)
                  ugh = sb(ph, "n_ugh", [128, FFC, NH], F32)
                  xt = sb(ph, "n_xt", [128, DC, TT], F32, key="xt")
                  sq = sb(ph, "n_sq", [128, DC, TT], BF16)
                  h2 = sb(ph, "n_h2", [128, DC, TT], BF16)
                  rs = sb(ph, "n_rs", [128, TT], F32)
                  rstd = sb(ph, "n_rstd", [128, TT], F32)
                  cc = sb(ph, "n_cc", [128, TT], F32)
                  sg = sb(ph, "n_sg", [128, TT], F32)
                  uT = sb(ph, "n_uT", [128, FFC, TT], BF16)
                  xo = xt
                  ring = WRing(ph, "n_w", DC, 4)
                  ringd = WRing(ph, "n_wd", FFC, 2)
                  rmsnorm(xh, NH, cf.V_GFFN, sqh_, h2h, rsx, rstx, bk(6), BK[6])
                  for fc in range(FFC):
                      w = ring.load(wbf[("wup", l)], fc, DC)
                      T.mm([(lambda c=c: nc.tensor.matmul(bk(fc % 2)[:, 0:NH], w.t[:, c, :], h2h.t[:, c, :], start=(c == 0), stop=(c == DC - 1))) for c in range(DC)], [w, h2h], [BK[fc % 2]])
                      T.op(ACT, lambda: nc.scalar.copy(out=ugh.t[:, fc, :], in_=bk(fc % 2)[:, 0:NH]), [BK[fc % 2]], [ugh])
                  wv = lambda fc, k: vec.t[:, cf.V_WCV + 3 * fc + k:cf.V_WCV + 3 * fc + k + 1]
                  for (c0, n, hj) in tiles:
                      meta = (n == 16)
                      if last and meta:
                          continue
                      T.dma(POOL, xt.t[:, :, 0:n], xm.t.ap()[:, c0:c0 + n].rearrange("(c p) n -> p c n", p=128), xt.name, [xm], [xt])
                      rmsnorm(xt, n, cf.V_GFFN, sq, h2, rs, rstd, bk(6), BK[6])
                      for fc in range(FFC):
                          wg_ = ring.load(wbf[("wup", l)], fc, DC)
                          wv_ = ring.load(wbf[("wup", l)], FFC + fc, DC)
                          pg, pgb = bk(2 * (fc % 2)), BK[2 * (fc % 2)]
                          pu, pub = bk(2 * (fc % 2) + 1), BK[2 * (fc % 2) + 1]
                          T.mm([(lambda c=c: nc.tensor.matmul(pg[:, 0:n], wg_.t[:, c, :], h2.t[:, c, 0:n], start=(c == 0), stop=(c == DC - 1))) for c in range(DC)], [wg_, h2], [pgb])
                          T.mm([(lambda c=c: nc.tensor.matmul(pu[:, 0:n], wv_.t[:, c, :], h2.t[:, c, 0:n], start=(c == 0), stop=(c == DC - 1))) for c in range(DC)], [wv_, h2], [pub])
                          bcol = vec.t[:, cf.V_BCV + fc:cf.V_BCV + fc + 1]
                          T.op(DVE, lambda: nc.vector.tensor_scalar(out=cc.t[:, 0:n], in0=pg[:, 0:n], scalar1=wv(fc, 1), scalar2=bcol, op0=ALU.mult, op1=ALU.add), [pgb, vec], [cc])
                          T.op(DVE, lambda: nc.vector.scalar_tensor_tensor(out=cc.t[:, 1:n], in0=pg[:, 0:n - 1], scalar=wv(fc, 0), in1=cc.t[:, 1:n], op0=ALU.mult, op1=ALU.add), [pgb, vec, cc], [cc])
                          T.op(DVE, lambda: nc.vector.scalar_tensor_tensor(out=cc.t[:, 0:n - 1], in0=pg[:, 1:n], scalar=wv(fc, 2), in1=cc.t[:, 0:n - 1], op0=ALU.mult, op1=ALU.add), [pgb, vec, cc], [cc])
                          T.op(DVE, lambda: nc.vector.scalar_tensor_tensor(out=cc.t[:, 0:1], in0=ugh.t[:, fc, 2 * hj:2 * hj + 1], scalar=wv(fc, 0), in1=cc.t[:, 0:1], op0=ALU.mult, op1=ALU.add), [ugh, vec, cc], [cc])
                          T.op(DVE, lambda: nc.vector.scalar_tensor_tensor(out=cc.t[:, n - 1:n], in0=ugh.t[:, fc, 2 * hj + 1:2 * hj + 2], scalar=wv(fc, 2), in1=cc.t[:, n - 1:n], op0=ALU.mult, op1=ALU.add), [ugh, vec, cc], [cc])
                          T.op(ACT, lambda: nc.scalar.activation(out=sg.t[:, 0:n], in_=cc.t[:, 0:n], func=AF.Silu), [cc], [sg])
                          T.op(DVE, lambda: nc.vector.tensor_tensor(out=uT.t[:, fc, 0:n], in0=sg.t[:, 0:n], in1=pu[:, 0:n], op=ALU.mult), [sg, pub], [uT])
                      for oc in range(DC):
                          w = ringd.load(wbf[("wdn", l)], oc, FFC)
                          pb_, pbb = bk(4 + oc % 2), BK[4 + oc % 2]
                          T.mm([(lambda c=c: nc.tensor.matmul(pb_[:, 0:n], w.t[:, c, :], uT.t[:, c, 0:n], start=(c == 0), stop=(c == FFC - 1))) for c in range(FFC)], [w, uT], [pbb])
                          T.op(DVE, lambda: nc.vector.tensor_tensor(out=xo.t[:, oc, 0:n], in0=pb_[:, 0:n], in1=xt.t[:, oc, 0:n], op=ALU.add), [pbb, xt], [xo])
                      if last and not meta:
                          T.dma(POOL, yT.ap()[:, c0 - 16:c0 - 16 + n].rearrange("(c p) n -> p c n", p=128), xo.t[:, :, 0:n], "xo", [xo], [B_yT])
                      elif not last:
                          T.dma(POOL, xa.t.ap()[:, c0:c0 + n].rearrange("(c p) n -> p c n", p=128), xo.t[:, :, 0:n], "xo", [xo], [xa])
              xh_st.close()
        except _Stop:
            pass
        T.dead = False
        T.barrier(full=True)
        if getattr(cf, 'endclear', False):
            fin_sem = stack.enter_context(nc.semaphore("s_fin"))
            for eng in (T.pe, T.act, T.dve, T.sp):
                eng.e.sem_inc(fin_sem, 1)
            nc.gpsimd.wait_ge(fin_sem, 4)
            allsems = [e.sem for e in T.engs] + [v[0] for v in T.dsems.values()] + [sv[0] for sv in T.csems] + [fin_sem]
            for sm in allsems:
                nc.gpsimd.sem_clear(sm)
    return nc, T.nsem


_CACHE = {}


def run(cf, inputs):
    in_maps = prep_inputs(cf, **inputs)
    if "nc" not in _CACHE or _CACHE.get("cf") is not cf:
        _CACHE["nc"], nsem = build(cf)
        _CACHE["cf"] = cf
    res = run_bass_kernel_spmd(_CACHE["nc"], in_maps, core_ids=list(range(NCORES)))
    out = np.empty((2, cf.SEQ, cf.D), np.float32)
    for c in range(NCORES):
        b, r = c // 4, c % 4
        out[b, r * cf.TPC:(r + 1) * cf.TPC, :] = res.results[c]["yT"].T
    return out


def kernel(**inputs):
    return run(FULL, inputs)
```
